# Optimizing a Trainium2 kernel written in Bass

```python
import jax, jax.numpy as jnp
from jax import lax
import numpy as np

D_MODEL = 1024
BATCH = 8
SEQ = 2048
DEPTH = 2
DEC_BATCH = 128
DEC_SEQ = 1
PAST_LEN = 16384
PAGE_SIZE = 128

N_EVEN = (DEPTH + 1) // 2
N_ODD = DEPTH // 2
EPS = 1e-6
MEM_LEN = 256
X_HEADS = 4
X_HEAD_DIM = D_MODEL // X_HEADS
D_A = D_MODEL // 2
CONV_A_W = 31
D_B = D_MODEL // 2
POOL_WINDOWS = (2, 4, 8, 16)
MAX_POOL = 16
POOL_GROUP_DIM = D_B // len(POOL_WINDOWS)
D_C = D_MODEL // 2
C_HEADS = 4
C_HEAD_DIM = D_C // C_HEADS
CHUNK = 128
D_D = D_MODEL // 2
CONV_D_W = 3
D_FF = ((8 * D_MODEL + 3 * 256 - 1) // (3 * 256)) * 256

kernel_name = 'hybrid_conv_pool_gmlp_shortconv_decoder_step'


def _rmsnorm(x, g):
    xf = x.astype(jnp.float32)
    y = xf * lax.rsqrt(jnp.mean(xf * xf, axis=-1, keepdims=True) + EPS)
    return (y * g.astype(jnp.float32)).astype(x.dtype)


def _layernorm(x, g, b):
    xf = x.astype(jnp.float32)
    mu = jnp.mean(xf, axis=-1, keepdims=True)
    var = jnp.mean(jnp.square(xf - mu), axis=-1, keepdims=True)
    y = (xf - mu) * lax.rsqrt(var + EPS) * g.astype(jnp.float32) + b.astype(jnp.float32)
    return y.astype(x.dtype)


def _causal_dwconv(z, hist, w, b):
    k = w.shape[0]
    ext = jnp.concatenate([hist.astype(z.dtype), z], axis=1)
    y = lax.conv_general_dilated(ext, w[:, None, :].astype(z.dtype), window_strides=(1,),
                                 padding='VALID', dimension_numbers=('NWC', 'WIO', 'NWC'),
                                 feature_group_count=z.shape[-1])
    if b is not None:
        y = y + b
    return y, ext[:, z.shape[1]:]


def _multiscale_pool(z, hist, pos0, w_grp, scale):
    n, t, c = z.shape
    p = MAX_POOL - 1
    ext = jnp.concatenate([hist.astype(z.dtype), z], axis=1)
    cs = jnp.cumsum(ext.astype(jnp.float32), axis=1)
    cs = jnp.concatenate([jnp.zeros((n, 1, c), jnp.float32), cs], axis=1)
    pos = (pos0 + jnp.arange(t, dtype=jnp.int32))[None, :, None]
    end = cs[:, p + 1:p + 1 + t]
    zf = z.astype(jnp.float32)
    parts = []
    for g, w in enumerate(POOL_WINDOWS):
        sl = slice(g * POOL_GROUP_DIM, (g + 1) * POOL_GROUP_DIM)
        start = cs[:, p + 1 - w:p + 1 - w + t, sl]
        cnt = jnp.minimum(pos + 1, w).astype(jnp.float32)
        parts.append((end[..., sl] - start) / cnt - zf[..., sl])
    pooled = jnp.stack(parts, axis=2).astype(z.dtype)
    mixed = jnp.einsum('ntgc,gce->ntge', pooled, w_grp).reshape(n, t, c)
    return mixed * scale, ext[:, t:]


def _chunk_spatial_gate(u, v, w_s, b_s):
    n, t, h, dh = v.shape
    n_chunks = -(-t // CHUNK)
    pad = n_chunks * CHUNK - t
    vp = jnp.pad(v, ((0, 0), (0, pad), (0, 0), (0, 0))).reshape(n, n_chunks, CHUNK, h, dh)
    mask = jnp.tril(jnp.ones((CHUNK, CHUNK), dtype=bool))
    wm = jnp.where(mask[None], w_s, jnp.zeros_like(w_s))
    s = jnp.einsum('hij,ncjhd->ncihd', wm, vp) + b_s.T[None, None, :, :, None]
    s = s.reshape(n, n_chunks * CHUNK, h, dh)[:, :t]
    return u * s


def _cross_attn(h, k, v, wq, wo):
    n, t, _ = h.shape
    q = (h @ wq).reshape(n, t, X_HEADS, X_HEAD_DIM)
    s = jnp.einsum('nthd,nmhd->nhtm', q, k).astype(jnp.float32) * (X_HEAD_DIM ** -0.5)
    a = jax.nn.softmax(s, axis=-1).astype(v.dtype)
    o = jnp.einsum('nhtm,nmhd->nthd', a, v).reshape(n, t, D_MODEL)
    return o @ wo


def _swiglu(h, wg, wu, wd):
    return (jax.nn.silu(h @ wg) * (h @ wu)) @ wd


def _trunk(x, mem_k, mem_v, hist_a, hist_b, hist_d, pos0, W):
    n, t, _ = x.shape
    open_start = max(((pos0 + t - 1) // CHUNK) * CHUNK - pos0, 0)
    new_a, new_b, new_c, new_d = [], [], [], []
    for layer in range(DEPTH):
        i = layer // 2
        h = _rmsnorm(x, W['norm_mix'][layer])
        if layer % 2 == 0:
            p = h @ W['w_in_even'][i]
            za = p[..., :D_A] * jax.nn.sigmoid(p[..., D_A:2 * D_A])
            zb = p[..., 2 * D_A:]
            ca, ha = _causal_dwconv(za, hist_a[i], W['conv_a_w'][i], W['conv_a_b'][i])
            ya = jax.nn.silu(_layernorm(ca, W['ln_a_g'][i], W['ln_a_b'][i]))
            yb, hb = _multiscale_pool(zb, hist_b[i], pos0, W['pool_b_w'][i], W['pool_b_scale'][i])
            x = x + jnp.concatenate([ya, yb], axis=-1) @ W['w_out_even'][i]
            new_a.append(ha)
            new_b.append(hb)
        else:
            p = h @ W['w_in_odd'][i]
            u = p[..., :D_C].reshape(n, t, C_HEADS, C_HEAD_DIM)
            v = _layernorm(p[..., D_C:2 * D_C], W['ln_c_g'][i], W['ln_c_b'][i]).reshape(n, t, C_HEADS, C_HEAD_DIM)
            o = 2 * D_C
            gb = p[..., o:o + D_D]
            gc = p[..., o + D_D:o + 2 * D_D]
            hd = p[..., o + 2 * D_D:]
            yc = _chunk_spatial_gate(u, v, W['ws_c'][i], W['bs_c'][i]).reshape(n, t, D_C)
            cd, hdn = _causal_dwconv(gc * hd, hist_d[i], W['conv_d_w'][i], None)
            yd = gb * cd
            x = x + jnp.concatenate([yc, yd], axis=-1) @ W['w_out_odd'][i]
            new_c.append(v[:, open_start:])
            new_d.append(hdn)
        h = _rmsnorm(x, W['norm_x'][layer])
        x = x + _cross_attn(h, mem_k[layer], mem_v[layer], W['wq_x'][layer], W['wo_x'][layer])
        h = _rmsnorm(x, W['norm_ffn'][layer])
        x = x + _swiglu(h, W['w_gate'][layer], W['w_up'][layer], W['w_down'][layer])
    y = _rmsnorm(x, W['norm_final'])
    return y, jnp.stack(new_a), jnp.stack(new_b), jnp.stack(new_c), jnp.stack(new_d)


def _nrm(k, shape, scale):
    return jax.random.normal(k, shape, jnp.float32) * scale


def setup_inputs(seed: int = 0) -> dict:
    key = jax.random.key(seed)
    ks = iter(jax.random.split(key, 40))
    f_in_even = 2 * D_A + D_B
    f_in_odd = 2 * D_C + 3 * D_D
    return {
        'x_prompt': _nrm(next(ks), (BATCH, SEQ, D_MODEL), 1.0),
        'x_sample': _nrm(next(ks), (DEC_BATCH, DEC_SEQ, D_MODEL), 1.0),
        'mem_prompt': _nrm(next(ks), (BATCH, MEM_LEN, D_MODEL), 1.0),
        'state_convA': _nrm(next(ks), (N_EVEN, DEC_BATCH, CONV_A_W - 1, D_A), 0.5),
        'state_poolB': _nrm(next(ks), (N_EVEN, DEC_BATCH, MAX_POOL - 1, D_B), 1.0),
        'state_convD': _nrm(next(ks), (N_ODD, DEC_BATCH, CONV_D_W - 1, D_D), 0.5),
        'cache_mem_k': _nrm(next(ks), (DEPTH, DEC_BATCH, MEM_LEN, X_HEADS, X_HEAD_DIM), 1.0),
        'cache_mem_v': _nrm(next(ks), (DEPTH, DEC_BATCH, MEM_LEN, X_HEADS, X_HEAD_DIM), 1.0),
        'norm_mix': 1.0 + _nrm(next(ks), (DEPTH, D_MODEL), 0.01),
        'norm_x': 1.0 + _nrm(next(ks), (DEPTH, D_MODEL), 0.01),
        'norm_ffn': 1.0 + _nrm(next(ks), (DEPTH, D_MODEL), 0.01),
        'norm_final': 1.0 + _nrm(next(ks), (D_MODEL,), 0.01),
        'w_in_even': _nrm(next(ks), (N_EVEN, D_MODEL, f_in_even), D_MODEL ** -0.5),
        'conv_a_w': _nrm(next(ks), (N_EVEN, CONV_A_W, D_A), CONV_A_W ** -0.5),
        'conv_a_b': _nrm(next(ks), (N_EVEN, D_A), 0.01),
        'ln_a_g': 1.0 + _nrm(next(ks), (N_EVEN, D_A), 0.01),
        'ln_a_b': _nrm(next(ks), (N_EVEN, D_A), 0.01),
        'pool_b_w': _nrm(next(ks), (N_EVEN, len(POOL_WINDOWS), POOL_GROUP_DIM, POOL_GROUP_DIM), POOL_GROUP_DIM ** -0.5),
        'pool_b_scale': 1.0 + _nrm(next(ks), (N_EVEN, D_B), 0.1),
        'w_out_even': _nrm(next(ks), (N_EVEN, D_A + D_B, D_MODEL), (D_A + D_B) ** -0.5),
        'w_in_odd': _nrm(next(ks), (N_ODD, D_MODEL, f_in_odd), D_MODEL ** -0.5),
        'ln_c_g': 1.0 + _nrm(next(ks), (N_ODD, D_C), 0.01),
        'ln_c_b': _nrm(next(ks), (N_ODD, D_C), 0.01),
        'ws_c': _nrm(next(ks), (N_ODD, C_HEADS, CHUNK, CHUNK), CHUNK ** -0.5),
        'bs_c': 1.0 + _nrm(next(ks), (N_ODD, C_HEADS, CHUNK), 0.01),
        'conv_d_w': _nrm(next(ks), (N_ODD, CONV_D_W, D_D), CONV_D_W ** -0.5),
        'w_out_odd': _nrm(next(ks), (N_ODD, D_C + D_D, D_MODEL), (D_C + D_D) ** -0.5),
        'wq_x': _nrm(next(ks), (DEPTH, D_MODEL, D_MODEL), D_MODEL ** -0.5),
        'wk_x': _nrm(next(ks), (DEPTH, D_MODEL, D_MODEL), D_MODEL ** -0.5),
        'wv_x': _nrm(next(ks), (DEPTH, D_MODEL, D_MODEL), D_MODEL ** -0.5),
        'wo_x': _nrm(next(ks), (DEPTH, D_MODEL, D_MODEL), D_MODEL ** -0.5),
        'w_gate': _nrm(next(ks), (DEPTH, D_MODEL, D_FF), D_MODEL ** -0.5),
        'w_up': _nrm(next(ks), (DEPTH, D_MODEL, D_FF), D_MODEL ** -0.5),
        'w_down': _nrm(next(ks), (DEPTH, D_FF, D_MODEL), D_FF ** -0.5),
    }


def reference(x_prompt, x_sample, mem_prompt, state_convA, state_poolB, state_convD,
              cache_mem_k, cache_mem_v, norm_mix, norm_x, norm_ffn, norm_final,
              w_in_even, conv_a_w, conv_a_b, ln_a_g, ln_a_b, pool_b_w, pool_b_scale, w_out_even,
              w_in_odd, ln_c_g, ln_c_b, ws_c, bs_c, conv_d_w, w_out_odd,
              wq_x, wk_x, wv_x, wo_x, w_gate, w_up, w_down):
    W = {'norm_mix': norm_mix, 'norm_x': norm_x, 'norm_ffn': norm_ffn, 'norm_final': norm_final,
         'w_in_even': w_in_even, 'conv_a_w': conv_a_w, 'conv_a_b': conv_a_b,
         'ln_a_g': ln_a_g, 'ln_a_b': ln_a_b, 'pool_b_w': pool_b_w, 'pool_b_scale': pool_b_scale,
         'w_out_even': w_out_even, 'w_in_odd': w_in_odd, 'ln_c_g': ln_c_g, 'ln_c_b': ln_c_b,
         'ws_c': ws_c, 'bs_c': bs_c, 'conv_d_w': conv_d_w, 'w_out_odd': w_out_odd,
         'wq_x': wq_x, 'wo_x': wo_x, 'w_gate': w_gate, 'w_up': w_up, 'w_down': w_down}
    nb = x_prompt.shape[0]
    p_mem_k = jnp.einsum('nmd,lde->lnme', mem_prompt, wk_x).reshape(DEPTH, nb, MEM_LEN, X_HEADS, X_HEAD_DIM)
    p_mem_v = jnp.einsum('nmd,lde->lnme', mem_prompt, wv_x).reshape(DEPTH, nb, MEM_LEN, X_HEADS, X_HEAD_DIM)
    ha0 = jnp.zeros((N_EVEN, nb, CONV_A_W - 1, D_A), x_prompt.dtype)
    hb0 = jnp.zeros((N_EVEN, nb, MAX_POOL - 1, D_B), x_prompt.dtype)
    hd0 = jnp.zeros((N_ODD, nb, CONV_D_W - 1, D_D), x_prompt.dtype)
    y_prompt, p_convA, p_poolB, p_chunkC_v, p_convD = _trunk(
        x_prompt, p_mem_k, p_mem_v, ha0, hb0, hd0, 0, W)
    y_sample, s_convA, s_poolB, s_chunkC_v, s_convD = _trunk(
        x_sample, cache_mem_k, cache_mem_v, state_convA, state_poolB, state_convD, PAST_LEN, W)
    return (y_prompt, y_sample, p_convA, p_poolB, p_chunkC_v, p_convD, p_mem_k, p_mem_v,
            s_convA, s_poolB, s_chunkC_v, s_convD)
```

```python
import numpy as np
import concourse.bass as bass
import concourse.mybir as mybir
from concourse.bass_utils import run_bass_kernel_spmd
from contextlib import ExitStack

F32 = mybir.dt.float32
BF16 = mybir.dt.bfloat16
I32 = mybir.dt.int32
ALU = mybir.AluOpType
AF = mybir.ActivationFunctionType
AX = mybir.AxisListType

NCORE = 8
D = 1024
SEQ = 2048
NS = 16
DFF = 2816
NFC = 22
EPS = 1e-6


class Dep:
    __slots__ = ("w", "r", "excl")

    def __init__(self, excl=False):
        self.w = None
        self.r = []
        self.excl = excl


class Buf:
    def __init__(self, ap, deps):
        self.ap = ap
        self.deps = list(deps)

    def __getitem__(self, idx):
        return self.ap[idx]


def _flat(xs):
    out = []
    for x in xs:
        if x is None:
            continue
        if isinstance(x, Dep):
            out.append(x)
        elif isinstance(x, Buf):
            out.extend(x.deps)
        else:
            out.extend(_flat(x))
    return out


class Prog:
    CE = ("pe", "act", "dve", "pool")
    ENGS = ("pe", "act", "dve", "pool", "sp")

    def __init__(self, nc, es, ring=16):
        self.nc = nc
        self.rec = {e: [] for e in self.ENGS}
        self.nops = {e: 0 for e in self.CE}
        self.semobj = {}
        for e in self.CE:
            self.semobj[("e", e)] = es.enter_context(nc.semaphore("s_" + e))
        self.ring = ring
        self.dq = ("sp", "pool")
        for q in self.dq:
            for i in range(ring):
                self.semobj[("d", q, i)] = es.enter_context(nc.semaphore("d_%s%d" % (q, i)))
        self.dcnt = {q: 0 for q in self.dq}
        self.out_tokens = []

    @staticmethod
    def _needs(reads, writes, extra=()):
        need = {}

        def add(tok):
            if tok is None:
                return
            k, v = tok
            if need.get(k, 0) < v:
                need[k] = v
        for d in reads:
            add(d.w)
        for d in writes:
            add(d.w)
            for t in d.r:
                add(t)
        for t in extra:
            add(t)
        return need

    @staticmethod
    def _commit(tok, reads, writes):
        for d in reads:
            d.r.append(tok)
        for d in writes:
            d.w = tok
            d.r = []

    def op(self, eng, fn, reads=(), writes=()):
        reads = _flat(reads)
        writes = _flat(writes)
        writes = writes + [d for d in reads if d.excl]
        reads = [d for d in reads if not d.excl]
        need = self._needs(reads, writes)
        self.nops[eng] += 1
        tok = (("c", eng), self.nops[eng])
        self.rec[eng].append({"kind": "op", "fn": fn, "need": need, "ord": self.nops[eng]})
        self._commit(tok, reads, writes)
        return tok

    def dma(self, q, out, in_, reads=(), writes=(), is_output=False, **kw):
        reads = _flat(reads)
        writes = _flat(writes)
        j = self.dcnt[q]
        self.dcnt[q] += 1
        slot = j % self.ring
        rnd = j // self.ring
        key = ("d", q, slot)
        extra = [(key, 16 * rnd)] if rnd > 0 else []
        need = self._needs(reads, writes, extra)
        tok = (key, 16 * (rnd + 1))
        self.rec[q].append({"kind": "dma", "out": out, "in": in_, "kw": kw, "need": need, "sem": key})
        self._commit(tok, reads, writes)
        if is_output:
            self.out_tokens.append(tok)
        return tok

    def finish(self):
        allt = list(self.out_tokens)
        for q in self.dq:
            n = self.dcnt[q]
            for slot in range(self.ring):
                k = (n - slot + self.ring - 1) // self.ring if n > slot else 0
                if k > 0:
                    allt.append((("d", q, slot), 16 * k))
        self.rec["sp"].append({"kind": "wait", "need": self._needs((), (), allt)})

    def emit_all(self):
        signal = {e: set() for e in self.CE}
        for E in self.ENGS:
            waited = {}
            for r in self.rec[E]:
                w = []
                for k, v in r["need"].items():
                    if k[0] == "c" and k[1] == "pe" and E == "pe":
                        continue
                    if waited.get(k, 0) < v:
                        waited[k] = v
                        w.append((k, v))
                        if k[0] == "c":
                            signal[k[1]].add(v)
                r["waits"] = w
        val = {}
        for e in self.CE:
            c = 0
            m = {}
            for o in range(1, self.nops[e] + 1):
                if o in signal[e]:
                    c += 1
                m[o] = c
            val[e] = m
        self.n_signals = {e: len(signal[e]) for e in self.CE}

        def run(E, e):
            for r in self.rec[E]:
                for k, v in r["waits"]:
                    if k[0] == "c":
                        e.wait_ge(self.semobj[("e", k[1])], val[k[1]][v])
                    else:
                        e.wait_ge(self.semobj[k], v)
                if r["kind"] == "op":
                    inst = r["fn"](e)
                    if r["ord"] in signal[E]:
                        inst.then_inc(self.semobj[("e", E)], 1)
                elif r["kind"] == "dma":
                    e.dma_start(out=r["out"], in_=r["in"], **r["kw"]).then_inc(self.semobj[r["sem"]], 16)

        with self.nc.Block() as block:
            @block.sync
            def _(e):
                run("sp", e)

            @block.tensor
            def _(e):
                run("pe", e)

            @block.scalar
            def _(e):
                run("act", e)

            @block.vector
            def _(e):
                run("dve", e)

            @block.gpsimd
            def _(e):
                run("pool", e)


G_MIX, G_X, G_FFN, G_FIN = 0, 16, 32, 48
C_AB, C_LAG, C_LAB, C_PBS, C_LCG, C_LCB, C_DW = 56, 60, 64, 68, 72, 76, 80
NPRM = 92


def build_program(stage=99):
    import os
    nc = bass.Bass("TRN2", target_bir_lowering=False)

    def din(name, shape):
        return nc.dram_tensor(name, list(shape), F32, kind="ExternalInput").ap()

    def dout(name, shape):
        return nc.dram_tensor(name, list(shape), F32, kind="ExternalOutput").ap()

    x_p = din("x_p", [SEQ, D]); x_s = din("x_s", [NS, D]); mem = din("mem", [256, D])
    st_a = din("st_a", [NS, 30, 512]); st_b = din("st_b", [NS, 15, 512]); st_d = din("st_d", [NS, 2, 512])
    ck = din("ck", [2, NS, 256, D]); cv = din("cv", [2, NS, 256, D])
    norm_mix = din("norm_mix", [2, D]); norm_x = din("norm_x", [2, D]); norm_ffn = din("norm_ffn", [2, D])
    norm_final = din("norm_final", [D])
    w_in_even = din("w_in_even", [1, D, 1536]); conv_a_w = din("conv_a_w", [1, 31, 512])
    conv_a_b = din("conv_a_b", [1, 512]); ln_a_g = din("ln_a_g", [1, 512]); ln_a_b = din("ln_a_b", [1, 512])
    pool_b_w = din("pool_b_w", [1, 4, 128, 128]); pool_b_scale = din("pool_b_scale", [1, 512])
    w_out_even = din("w_out_even", [1, D, D]); w_in_odd = din("w_in_odd", [1, D, 2560])
    ln_c_g = din("ln_c_g", [1, 512]); ln_c_b = din("ln_c_b", [1, 512]); ws_c = din("ws_c", [1, 4, 128, 128])
    bs_c = din("bs_c", [1, 4, 128]); conv_d_w = din("conv_d_w", [1, 3, 512]); w_out_odd = din("w_out_odd", [1, D, D])
    wq_x = din("wq_x", [2, D, D]); wk_x = din("wk_x", [2, D, D]); wv_x = din("wv_x", [2, D, D]); wo_x = din("wo_x", [2, D, D])
    w_gate = din("w_gate", [2, D, DFF]); w_up = din("w_up", [2, D, DFF]); w_down = din("w_down", [2, DFF, D])

    y_p = dout("y_p", [SEQ, D]); y_s = dout("y_s", [NS, D])
    o_pa = dout("o_pa", [30, 512]); o_pb = dout("o_pb", [15, 512]); o_pc = dout("o_pc", [128, 512]); o_pd = dout("o_pd", [2, 512])
    o_pk = dout("o_pk", [2, 256, D]); o_pv = dout("o_pv", [2, 256, D])
    o_sa = dout("o_sa", [NS, 30, 512]); o_sb = dout("o_sb", [NS, 15, 512]); o_sc = dout("o_sc", [NS, 512]); o_sd = dout("o_sd", [NS, 2, 512])

    with ExitStack() as es:
        P = Prog(nc, es)

        def sbt(name, shape, dt, ndeps=1):
            t = es.enter_context(nc.sbuf_tensor(name, list(shape), dt))
            return Buf(t, [Dep() for _ in range(ndeps)])

        NT = 1040
        xT = sbt("xT", [128, 8, NT], F32)
        xd = [[Dep() for _ in range(3)] for _ in range(8)]
        hT = sbt("hT", [128, 8, NT], BF16)
        hdp = [Dep() for _ in range(3)]
        KTp = [sbt("KTp%d" % l, [128, 8, 256], BF16) for l in range(2)]
        Vp = [sbt("Vp%d" % l, [128, 2, D], BF16) for l in range(2)]
        ident_f = sbt("ident_f", [128, 128], F32)
        ident_b = sbt("ident_b", [128, 128], BF16)
        ones_rms = sbt("ones_rms", [128, 128], BF16)
        ones_ln = sbt("ones_ln", [128, 128], BF16)
        ones_row = sbt("ones_row", [1, 128], BF16)
        prm = sbt("prm", [128, NPRM], F32)
        cw = sbt("cw", [128, 124], F32)
        eps_t = sbt("eps_t", [128, 1], F32)
        gC_bc = sbt("gC_bc", [128, 512], F32)
        bC_bc = sbt("bC_bc", [128, 512], F32)
        wsT = sbt("wsT", [128, 4, 128], BF16)
        poolw = sbt("poolw", [128, 4, 128], BF16)
        bsF = sbt("bsF", [1, 512], F32)
        bsH = sbt("bsH", [1, 512], BF16)
        bsL = sbt("bsL", [1, 512], BF16)
        ws00 = sbt("ws00", [128, 4], F32)
        bs0 = sbt("bs0", [128, 4], F32)
        invc = sbt("invc", [128, 16], F32)
        zaH = sbt("zaH", [128, 4, 30], F32)
        zbH = sbt("zbH", [128, 4, 15], F32)
        gdH = sbt("gdH", [128, 4, 2], F32)
        slots = [sbt("slot%d" % i, [128, 12288], BF16, ndeps=4) for i in range(2)]
        DgA = Buf(slots[1].ap[:, 8192:10240].rearrange("p (k f) -> p k f", f=128), [Dep()])
        DgB = Buf(slots[1].ap[:, 10240:12288].rearrange("p (k f) -> p k f", f=128), [Dep()])
        slot_extra = {1: [DgA, DgB], 0: []}
        AW = 17568
        arena_t = es.enter_context(nc.sbuf_tensor("arena", [128, AW], F32))
        CH = 32
        adeps = [Dep() for _ in range((AW + CH - 1) // CH)]

        def av(off, shape, dt):
            n = 1
            for s in shape[1:]:
                n *= s
            words = n if dt in (F32, I32) else (n + 1) // 2
            assert off + words <= AW, (off, words)
            ap = arena_t[0:shape[0], off:off + words]
            if dt != F32:
                ap = ap.bitcast(dt)
            if len(shape) >= 3:
                names = "abcdef"[:len(shape) - 1]
                kw = {names[i]: shape[i + 1] for i in range(1, len(names))}
                ap = ap.rearrange("p (%s) -> p %s" % (" ".join(names), " ".join(names)), **kw)
            assert off % CH == 0, off
            deps = adeps[off // CH:(off + words - 1) // CH + 1]
            return Buf(ap, deps)

        sq_sep = sbt("sq_sep", [128, 8, 512], BF16) if os.environ.get("SQSEP") else None
        banks = []
        for i in range(8):
            t = es.enter_context(nc.psum_tensor("pb%d" % i, [128, 512], F32))
            banks.append(Buf(t, [Dep(excl=True)]))
        bank_i = [0]
        pinned = set()

        def nb(pin=False):
            while (bank_i[0] % 8) in pinned:
                bank_i[0] += 1
            i = bank_i[0] % 8
            bank_i[0] += 1
            if pin:
                pinned.add(i)
            return banks[i]

        def unpin(b):
            pinned.discard(banks.index(b))

        def mm(out_ap, pairs, reads, writes):
            def fn(e, pairs=pairs, out_ap=out_ap):
                n = len(pairs)
                inst = None
                for i, (l, r) in enumerate(pairs):
                    inst = e.matmul(out_ap, lhsT=l, rhs=r, start=(i == 0), stop=(i == n - 1))
                return inst
            P.op("pe", fn, reads, writes)

        def mmg(out_ap, pairs, first, last, reads, writes):
            def fn(e, pairs=pairs, out_ap=out_ap, first=first, last=last):
                n = len(pairs)
                inst = None
                for i, (l, r) in enumerate(pairs):
                    inst = e.matmul(out_ap, lhsT=l, rhs=r, start=(first and i == 0), stop=(last and i == n - 1))
                return inst
            P.op("pe", fn, reads, writes)

        def mm1(out_ap, lhsT, rhs, start, stop, reads, writes):
            P.op("pe", lambda e: e.matmul(out_ap, lhsT=lhsT, rhs=rhs, start=start, stop=stop), reads, writes)

        def tr(out_ap, in_ap, ident_ap, reads, writes):
            P.op("pe", lambda e: e.transpose(out=out_ap, in_=in_ap, identity=ident_ap), reads, writes)

        def act(out, in_, func, reads, writes, **kw):
            P.op("act", lambda e: e.activation(out=out, in_=in_, func=func, **kw), reads, writes)

        def tt(out, in0, in1, op, reads, writes):
            P.op("dve", lambda e: e.tensor_tensor(out=out, in0=in0, in1=in1, op=op), reads, writes)

        def ts(out, in0, s1, s2, op0, op1, reads, writes):
            if op1 is None:
                P.op("dve", lambda e: e.tensor_scalar(out=out, in0=in0, scalar1=s1, scalar2=None, op0=op0), reads, writes)
            else:
                P.op("dve", lambda e: e.tensor_scalar(out=out, in0=in0, scalar1=s1, scalar2=s2, op0=op0, op1=op1), reads, writes)

        def stt(out, in0, scalar, in1, op0, op1, reads, writes, **kw):
            P.op("dve", lambda e: e.scalar_tensor_tensor(out=out, in0=in0, scalar=scalar, in1=in1, op0=op0, op1=op1, **kw), reads, writes)

        def vcopy(out, in_, reads, writes):
            P.op("dve", lambda e: e.tensor_copy(out=out, in_=in_), reads, writes)

        if stage < 0:
            tt_ = av(0, [NS, D], F32)
            P.dma("sp", tt_[:], x_s[:, :], writes=[tt_])
            P.dma("sp", y_s[:, :], tt_[:], reads=[tt_], is_output=True)
            P.finish()
            P.emit_all()
            return nc
        P.op("pool", lambda e: e.memset(ident_f[:], 1.0), [], [ident_f])
        P.op("pool", lambda e: e.affine_select(out=ident_f[:], in_=ident_f[:], pattern=[[-1, 128]], compare_op=ALU.is_equal,
                                               fill=0.0, base=0, channel_multiplier=1), [ident_f], [ident_f])
        vcopy(ident_b[:], ident_f[:], [ident_f], [ident_b])
        P.op("pool", lambda e: e.memset(ones_rms[:], 1.0 / 1024.0), [], [ones_rms])
        P.op("pool", lambda e: e.memset(ones_ln[:], 1.0 / 512.0), [], [ones_ln])
        P.op("pool", lambda e: e.memset(ones_row[:], 1.0), [], [ones_row])
        P.op("pool", lambda e: e.memset(eps_t[:], EPS), [], [eps_t])
        P.op("pool", lambda e: e.memset(zaH[:], 0.0), [], [zaH])
        P.op("pool", lambda e: e.memset(zbH[:], 0.0), [], [zbH])
        P.op("pool", lambda e: e.memset(gdH[:], 0.0), [], [gdH])
        ii = av(0, [128, 16], I32)
        P.op("pool", lambda e: e.iota(ii[:], pattern=[[1, 16]], base=1, channel_multiplier=0), [], [ii])
        vcopy(invc[:], ii[:], [ii], [invc])
        P.op("dve", lambda e: e.reciprocal(out=invc[:], in_=invc[:]), [invc], [invc])

        sel = sbt("sel", [NS, NS, 128], BF16)
        maskM = sbt("maskM", [128, NS, NS], BF16)
        P.op("pool", lambda e: e.memset(sel[:], 1.0), [], [sel])
        P.op("pool", lambda e: e.affine_select(out=sel[:], in_=sel[:], pattern=[[-1, NS], [0, 128]], compare_op=ALU.is_equal,
                                               fill=0.0, base=0, channel_multiplier=1), [sel], [sel])
        P.op("pool", lambda e: e.memset(maskM[:], 1.0), [], [maskM])
        P.op("pool", lambda e: e.affine_select(out=maskM[:], in_=maskM[:], pattern=[[1, NS], [-1, NS]], compare_op=ALU.is_equal,
                                               fill=0.0, base=0, channel_multiplier=0), [maskM], [maskM])
        R1 = av(512, [128, 128], F32)
        R2 = av(1024, [128, 128], F32)
        P.op("pool", lambda e: e.memset(R1[:], 0.0), [], [R1])
        P.op("pool", lambda e: e.memset(R2[:], 0.0), [], [R2])

        def rows(dst, r0, src, n):
            P.dma("sp", dst[r0:r0 + n, :], src.rearrange("(c p) -> c p", p=128), writes=[dst])
        for l in range(2):
            rows(R1, G_MIX + 8 * l, norm_mix[l], 8)
            rows(R1, G_X + 8 * l, norm_x[l], 8)
            rows(R1, G_FFN + 8 * l, norm_ffn[l], 8)
        rows(R1, G_FIN, norm_final, 8)
        rows(R1, C_AB, conv_a_b[0], 4); rows(R1, C_LAG, ln_a_g[0], 4); rows(R1, C_LAB, ln_a_b[0], 4)
        rows(R1, C_PBS, pool_b_scale[0], 4); rows(R1, C_LCG, ln_c_g[0], 4); rows(R1, C_LCB, ln_c_b[0], 4)
        for k in range(3):
            rows(R1, C_DW + 4 * k, conv_d_w[0, k], 4)
        P.dma("sp", R2[0:124, :], conv_a_w[0].rearrange("k (c p) -> (k c) p", p=128), writes=[R2])
        b = nb()
        tr(b[:, 0:128], R1[:], ident_f[:], [R1, ident_f], [b])
        vcopy(prm[:], b[:, 0:NPRM], [b], [prm])
        b = nb()
        tr(b[:, 0:128], R2[:], ident_f[:], [R2, ident_f], [b])
        vcopy(cw[:], b[:, 0:124], [b], [cw])
        P.dma("sp", gC_bc[:], ln_c_g[0].partition_broadcast(128), writes=[gC_bc])
        P.dma("sp", bC_bc[:], ln_c_b[0].partition_broadcast(128), writes=[bC_bc])
        P.dma("sp", bsF[:], bs_c[0].rearrange("h i -> (h i)").partition_broadcast(1), writes=[bsF])
        vcopy(bsH[:], bsF[:], [bsF], [bsH])
        tt(bsF[:], bsF[:], bsH[:], ALU.subtract, [bsF, bsH], [bsF])
        vcopy(bsL[:], bsF[:], [bsF], [bsL])
        if not os.environ.get("SKIPX"):
            P.dma("sp", ws00[:], ws_c[0, :, 0, 0].partition_broadcast(128), writes=[ws00], allow_slow_non_contiguous=True)
            P.dma("sp", bs0[:], bs_c[0, :, 0].partition_broadcast(128), writes=[bs0], allow_slow_non_contiguous=True)
        wsf = av(1536, [128, 4, 128], F32)
        P.dma("sp", wsf[:], ws_c[0].rearrange("h i j -> i h j"), writes=[wsf])
        P.op("pool", lambda e: e.affine_select(out=wsf[:], in_=wsf[:], pattern=[[0, 4], [-1, 128]], compare_op=ALU.is_ge,
                                               fill=0.0, base=0, channel_multiplier=1), [wsf], [wsf])
        for h in range(4):
            b = nb()
            tr(b[:, 0:128], wsf[:, h, :], ident_f[:], [wsf, ident_f], [b])
            vcopy(wsT[:, h, :], b[:, 0:128], [b], [wsT])
        P.dma("pool", poolw[:], pool_b_w[0].rearrange("g c e -> c g e"), writes=[poolw])

        memf = av(2048, [128, 2, D], F32)
        memT = av(4096, [128, 8, 256], BF16)
        stg = av(5120, [128, D], F32)
        P.dma("sp", memf[:], mem.rearrange("(c p) f -> p c f", p=128), writes=[memf])
        for kc in range(8):
            b = nb()
            for mc in range(2):
                tr(b[:, mc * 128:(mc + 1) * 128], memf[:, mc, kc * 128:(kc + 1) * 128], ident_f[:], [memf, ident_f], [b])
            act(memT[:, kc, :], b[:, 0:256], AF.Copy, [b], [memT])
        for l in range(2 if not os.environ.get("SKIPKV") else 0):
            wk = Buf(slots[0].ap[:, 0:8192].rearrange("p (k f) -> p k f", f=D), slots[0].deps)
            wv = Buf(slots[1].ap[:, 0:8192].rearrange("p (k f) -> p k f", f=D), slots[1].deps)
            P.dma("pool", wk[:], wk_x[l].rearrange("(k p) f -> p k f", p=128), writes=[wk])
            P.dma("pool", wv[:], wv_x[l].rearrange("(k p) f -> p k f", p=128), writes=[wv])
            for ec in range(8):
                b = nb()
                mm(b[:, 0:256], [(wk[:, kc, ec * 128:(ec + 1) * 128], memT[:, kc, :]) for kc in range(8)], [wk, memT], [b])
                act(KTp[l][:, ec, :], b[:, 0:256], AF.Copy, [b], [KTp[l]])
            for (w_, dst, isv) in ((wk, o_pk, False), (wv, o_pv, True)):
                for mc in range(2):
                    for hf in range(2):
                        b = nb()
                        mm(b[:, :], [(memT[:, kc, mc * 128:(mc + 1) * 128], w_[:, kc, hf * 512:(hf + 1) * 512]) for kc in range(8)],
                           [w_, memT], [b])
                        act(stg[:, hf * 512:(hf + 1) * 512], b[:, :], AF.Copy, [b], [stg])
                        if isv:
                            vcopy(Vp[l][:, mc, hf * 512:(hf + 1) * 512], b[:, :], [b], [Vp[l]])
                    P.dma("sp", dst[l, mc * 128:(mc + 1) * 128, :], stg[:], reads=[stg], is_output=True)

        def xdeps(ti):
            return [xd[k][ti] for k in range(8)]

        SQ_OFF, RS_OFF = 0, 2048

        def rmsnorm(ti, c0, w, gcol, final_dst=None):
            sq = av(SQ_OFF, [128, 8, 512], BF16)
            rstd = av(RS_OFF, [128, 512], F32)
            if os.environ.get("SQSEP"):
                sq = sq_sep
            RL = int(os.environ.get("RMS_LEVEL", "9"))
            if os.environ.get("SQ2D"):
                for k in range(8):
                    act(sq[:, k, 0:w], xT[:, k, c0:c0 + w], AF.Square, [xd[k][ti]], [sq])
            else:
                act(sq[:, :, 0:w], xT[:, :, c0:c0 + w], AF.Square, xdeps(ti), [sq])
            if RL < 2:
                return
            b = nb()
            mm(b[:, 0:w], [(ones_rms[:], sq[:, k, 0:w]) for k in range(8)], [sq, ones_rms], [b])
            if RL < 3:
                return
            act(rstd[:, 0:w], b[:, 0:w], AF.Ln, [b, eps_t], [rstd], bias=eps_t[:, 0:1], scale=1.0)
            if RL < 4:
                return
            act(rstd[:, 0:w], rstd[:, 0:w], AF.Exp, [rstd], [rstd], scale=-0.5)
            if RL < 5:
                return
            for k in range(8):
                if final_dst is None:
                    stt(hT[:, k, c0:c0 + w], xT[:, k, c0:c0 + w], prm[:, gcol + k:gcol + k + 1], rstd[:, 0:w], ALU.mult, ALU.mult,
                        [xd[k][ti], rstd, prm], [hdp[ti]])
                else:
                    stt(final_dst[:, k, 0:w], xT[:, k, c0:c0 + w], prm[:, gcol + k:gcol + k + 1], rstd[:, 0:w], ALU.mult, ALU.mult,
                        [xd[k][ti], rstd, prm], [final_dst])

        def proj(Wb, f0, ti, c0, w, src=None, srcdeps=None):
            src = hT if src is None else src
            sd = [hdp[ti]] if srcdeps is None else srcdeps
            b = nb()
            mm(b[:, 0:w], [(Wb[:, kc, f0:f0 + 128], src[:, kc, c0:c0 + w]) for kc in range(8)], [Wb] + sd, [b])
            return b

        def wview(slot, ncols):
            return Buf(slot.ap[:, 0:8 * ncols].rearrange("p (k f) -> p k f", f=ncols), slot.deps)

        def load_w(slot, src2d, c_lo, c_hi):
            n = c_hi - c_lo
            Wb = wview(slot, n)
            s = src2d.rearrange("(k p) f -> p k f", p=128)
            for k0 in range(0, 8, 2):
                P.dma("pool", Wb[:, k0:k0 + 2, :], s[:, k0:k0 + 2, c_lo:c_hi], reads=[],
                      writes=[slot.deps[k0 // 2]] + (slot_extra[slots.index(slot)] if 8 * n > 8192 else []))
            return Wb

        def add_to_x(ti, c0, w, oc, b):
            tt(xT[:, oc, c0:c0 + w], xT[:, oc, c0:c0 + w], b[:, 0:w], ALU.add, [xd[oc][ti], b], [xd[oc][ti]])

        FUSE = stage >= 7

        def out_proj(Wb, yT, tiles, next_gcol=None):
            for (ti, c0, w, kind, g0) in tiles:
                for oc in range(8):
                    b = nb()
                    mm(b[:, 0:w], [(Wb[:, kc, oc * 128:(oc + 1) * 128], yT[:, kc, c0:c0 + w]) for kc in range(8)], [Wb, yT], [b])
                    add_to_x(ti, c0, w, oc, b)
                if FUSE and next_gcol is not None:
                    rmsnorm(ti, c0, w, next_gcol)

        YT_OFF = 2560
        T0 = YT_OFF + 6240


        cwv = Buf(cw.ap.rearrange("p (k c) -> p c k", c=4), cw.deps)
        O_ZA, O_CA, O_ZB, O_T1, O_T2, O_CB = T0, T0 + 2176, T0 + 4224, T0 + 6336, T0 + 6848, T0 + 7360

        def ln_feat(get_src, srcbuf, w, gcol, bcol, func, get_dst, dstbuf):
            tmp1 = av(O_T1, [128, 512], F32)
            tmp2 = av(O_T2, [128, 512], F32)
            cbs = [av(O_CB + 256 * i, [128, 512], BF16) for i in range(4)]
            bm = nb()
            bq = nb()
            for c in range(4):
                cab, csq = cbs[(c % 2) * 2], cbs[(c % 2) * 2 + 1]
                act(cab[:, 0:w], get_src(c), AF.Copy, [srcbuf], [cab])
                act(csq[:, 0:w], get_src(c), AF.Square, [srcbuf], [csq])
                mm1(bm[:, 0:w], ones_ln[:], cab[:, 0:w], c == 0, c == 3, [ones_ln, cab], [bm])
                mm1(bq[:, 0:w], ones_ln[:], csq[:, 0:w], c == 0, c == 3, [ones_ln, csq], [bq])
            act(tmp1[:, 0:w], bm[:, 0:w], AF.Square, [bm], [tmp1])
            tt(tmp1[:, 0:w], bq[:, 0:w], tmp1[:, 0:w], ALU.subtract, [bq, tmp1], [tmp1])
            act(tmp1[:, 0:w], tmp1[:, 0:w], AF.Ln, [tmp1, eps_t], [tmp1], bias=eps_t[:, 0:1], scale=1.0)
            act(tmp1[:, 0:w], tmp1[:, 0:w], AF.Exp, [tmp1], [tmp1], scale=-0.5)
            act(tmp2[:, 0:w], bm[:, 0:w], AF.Copy, [bm], [tmp2])
            for c in range(4):
                tt(get_src(c), get_src(c), tmp2[:, 0:w], ALU.subtract, [srcbuf, tmp2], [srcbuf])
                tt(get_src(c), get_src(c), tmp1[:, 0:w], ALU.mult, [srcbuf, tmp1], [srcbuf])
                act(get_dst(c), get_src(c), func, [srcbuf, prm], [dstbuf], scale=prm[:, gcol + c:gcol + c + 1], bias=prm[:, bcol + c:bcol + c + 1])

        def to_tokmajor(get_src, srcbuf, nrow, dst_dram, stg_off):
            b = nb()
            for c in range(4):
                tr(b[0:nrow, c * 128:(c + 1) * 128], get_src(c), ident_f[:], [srcbuf, ident_f], [b])
            sg = av(stg_off, [32, 512], F32)
            act(sg[0:nrow, :], b[0:nrow, :], AF.Copy, [b], [sg])
            P.dma("sp", dst_dram, sg[0:nrow, :], reads=[sg], is_output=True)

        def load_hist_T(src_dram, nk, dst, dstbuf):
            per = 120 // nk if nk > 8 else 16
            per = min(per, NS)
            while NS % per:
                per -= 1
            rows = per * nk
            for j in range(NS // per):
                raw = av(O_T1, [128, 512], F32)
                P.dma("sp", raw[0:rows, :], src_dram[j * per:(j + 1) * per].rearrange("n k f -> (n k) f"), writes=[raw])
                for c in range(4):
                    b = nb()
                    tr(b[:, 0:rows], raw[0:rows, c * 128:(c + 1) * 128], ident_f[0:rows, 0:rows], [raw, ident_f], [b])
                    act(dst(c)[:, j * per:(j + 1) * per, 0:nk], b[:, 0:rows].rearrange("p (n k) -> p n k", k=nk), AF.Copy, [b], [dstbuf])

        def mixer_even(st, tiles):
            Win = load_w(slots[0], w_in_even[0], 0, 1536)
            Wout = load_w(slots[1], w_out_even[0], 0, 1024)
            yT = av(YT_OFF, [128, 8, NT], BF16)
            tmp1 = av(O_T1, [128, 512], F32)
            tmp2 = av(O_T2, [128, 512], F32)
            cbs = [av(O_CB + 256 * i, [128, 512], BF16) for i in range(4)]
            for (ti, c0, w, kind, g0) in tiles:
                if not FUSE:
                    rmsnorm(ti, c0, w, G_MIX + 0)
            for (ti, c0, w, kind, g0) in tiles:
                if kind == "p":
                    za = av(O_ZA, [128, 4, 542], BF16)
                    ca = av(O_CA, [128, 4, 512], F32)
                    zb = av(O_ZB, [128, 4, 527], F32)
                    vcopy(za[:, :, 0:30], zaH[:], [zaH], [za])
                    vcopy(zb[:, :, 0:15], zbH[:], [zbH], [zb])
                    zcur = lambda c: za[:, c, 30:30 + w]
                    bcur = lambda c: zb[:, c, 15:15 + w]
                else:
                    za = av(O_ZA, [128, 4, NS, 31], F32)
                    ca = av(O_CA, [128, 4, NS], F32)
                    zb = av(O_ZB, [128, 4, NS, 16], F32)
                    load_hist_T(st_a, 30, lambda c: za[:, c, :, :], za)
                    load_hist_T(st_b, 15, lambda c: zb[:, c, :, :], zb)
                    zcur = lambda c: za[:, c, :, 30]
                    bcur = lambda c: zb[:, c, :, 15]
                for c in range(4):
                    ba = proj(Win, c * 128, ti, c0, w)
                    bg = proj(Win, 512 + c * 128, ti, c0, w)
                    act(tmp1[:, 0:w], bg[:, 0:w], AF.Sigmoid, [bg], [tmp1])
                    tt(zcur(c), ba[:, 0:w], tmp1[:, 0:w], ALU.mult, [ba, tmp1], [za])
                    if kind == "p":
                        tt(zaH[:, c, :], ba[:, w - 30:w], tmp1[:, w - 30:w], ALU.mult, [ba, tmp1], [zaH])
                    bz = proj(Win, 1024 + c * 128, ti, c0, w)
                    act(bcur(c), bz[:, 0:w], AF.Copy, [bz], [zb])
                for c in range(4):
                    if kind == "p":
                        for k in range(16):
                            ts(DgA[:, k, :], ident_b[:], cwv[:, c, k:k + 1], None, ALU.mult, None, [ident_b, cw], [DgA])
                        for k in range(16, 31):
                            ts(DgB[:, k - 16, :], ident_b[:], cwv[:, c, k:k + 1], None, ALU.mult, None, [ident_b, cw], [DgB])
                        bc_ = nb(pin=True)
                        mmg(bc_[:, 0:w], [(DgA[:, k, :], za[:, c, k:k + w]) for k in range(16)], True, False, [DgA, za], [bc_])
                        mmg(bc_[:, 0:w], [(DgB[:, k - 16, :], za[:, c, k:k + w]) for k in range(16, 31)], False, True, [DgB, za], [bc_])
                        act(ca[:, c, 0:w], bc_[:, 0:w], AF.Identity, [bc_, prm], [ca], bias=prm[:, C_AB + c:C_AB + c + 1], scale=1.0)
                        unpin(bc_)
                        continue
                    cac = ca[:, c, :]
                    tap = lambda k, c=c: za[:, c, :, k]
                    ts(cac, tap(30), cwv[:, c, 30:31], prm[:, C_AB + c:C_AB + c + 1], ALU.mult, ALU.add, [za, cw, prm], [ca])
                    for k in range(30):
                        stt(cac, tap(k), cwv[:, c, k:k + 1], cac, ALU.mult, ALU.add, [za, cw, ca], [ca])
                if kind == "p":
                    ln_feat(lambda c: ca[:, c, 0:w], ca, w, C_LAG, C_LAB, AF.Silu, lambda c: yT[:, c, c0:c0 + w], yT)
                else:
                    ln_feat(lambda c: ca[:, c, :], ca, w, C_LAG, C_LAB, AF.Silu, lambda c: yT[:, c, c0:c0 + w], yT)
                for g in range(4):
                    win = 2 << g
                    pooled = cbs[g]
                    if kind == "p":
                        Sa = av(O_CA, [128, 527], F32)
                        Sb = av(O_CA + 544, [128, 527], F32)
                        tt(Sa[:, 1:527], zb[:, g, 1:527], zb[:, g, 0:526], ALU.add, [zb], [Sa])
                        cur = Sa
                        if g >= 1:
                            tt(Sb[:, 3:527], Sa[:, 3:527], Sa[:, 1:525], ALU.add, [Sa], [Sb]); cur = Sb
                        if g >= 2:
                            tt(Sa[:, 7:527], Sb[:, 7:527], Sb[:, 3:523], ALU.add, [Sb], [Sa]); cur = Sa
                        if g >= 3:
                            tt(Sb[:, 15:527], Sa[:, 15:527], Sa[:, 7:519], ALU.add, [Sa], [Sb]); cur = Sb
                        stt(pooled[:, 0:w], cur[:, 15:15 + w], 1.0 / win, zb[:, g, 15:15 + w], ALU.mult, ALU.subtract, [cur, zb], [pooled])
                        if g0 == 0:
                            nfix = win - 1
                            tt(tmp2[:, 0:nfix], cur[:, 15:15 + nfix], invc[:, 0:nfix], ALU.mult, [cur, invc], [tmp2])
                            tt(pooled[:, 0:nfix], tmp2[:, 0:nfix], zb[:, g, 15:15 + nfix], ALU.subtract, [tmp2, zb], [pooled])
                    else:
                        P.op("dve", lambda e, g=g, win=win, zb=zb, tmp2=tmp2: e.tensor_reduce(out=tmp2[:, 0:NS], in_=zb[:, g, :, 16 - win:16], axis=AX.X, op=ALU.add),
                             [zb], [tmp2])
                        stt(pooled[:, 0:w], tmp2[:, 0:NS], 1.0 / win, zb[:, g, :, 15], ALU.mult, ALU.subtract, [tmp2, zb], [pooled])
                    b = nb()
                    mm(b[:, 0:w], [(poolw[:, g, :], pooled[:, 0:w])], [poolw, pooled], [b])
                    act(yT[:, 4 + g, c0:c0 + w], b[:, 0:w], AF.Identity, [b, prm], [yT], scale=prm[:, C_PBS + g:C_PBS + g + 1])
                if kind == "p":
                    vcopy(zbH[:], zb[:, :, 512:527], [zb], [zbH])
                    if g0 + 512 == SEQ:
                        to_tokmajor(lambda c: zaH[:, c, :], zaH, 30, o_pa[:, :], O_T1)
                        to_tokmajor(lambda c: zb[:, c, 512:527], zb, 15, o_pb[:, :], O_T2)
                else:
                    P.dma("sp", o_sa[:, 0:29, :], st_a[:, 1:30, :], is_output=True)
                    P.dma("sp", o_sb[:, 0:14, :], st_b[:, 1:15, :], is_output=True)
                    to_tokmajor(lambda c: za[:, c, :, 30], za, NS, o_sa[:, 29, :], O_T1)
                    to_tokmajor(lambda c: zb[:, c, :, 15], zb, NS, o_sb[:, 14, :], O_T2)
                out_proj(Wout, yT, [(ti, c0, w, kind, g0)], next_gcol=G_X + 0)

        def bf(bank):
            return bank.ap[:, :].bitcast(BF16)

        def attn(st, tiles, layer):
            Wq = load_w(slots[0], wq_x[layer], 0, 1024)
            Wo = load_w(slots[1], wo_x[layer], 0, 1024)
            qT = av(T0, [128, 8, 512], BF16)
            PT = av(T0 + 2048, [128, 8, 512], BF16)
            oT = av(T0 + 4096, [128, 8, 512], BF16)
            Pun2 = [av(T0 + 6144 + i * 1024, [128, 4, 256], F32) for i in range(2)]
            Pn = av(T0 + 8192, [128, 4, 256], BF16)
            sm2 = [av(T0 + 8192 + 512 + i * 32, [128, 16], F32) for i in range(2)]
            for (ti, c0, w, kind, g0) in tiles:
                if not FUSE:
                    rmsnorm(ti, c0, w, G_X + 8 * layer)

            def qproj(tile):
                (ti, c0, w, kind, g0) = tile
                for fc in range(8):
                    b = proj(Wq, fc * 128, ti, c0, w)
                    act(qT[:, fc, 0:w], b[:, 0:w], AF.Identity, [b], [qT], scale=0.0625)
            qproj(tiles[0])
            stile = [t for t in tiles if t[3] == "s"]
            kgen = sample_k_phase(layer, stile[0][0], stile[0][1], stile[0][2]) if stile else iter(())
            for tidx, (ti, c0, w, kind, g0) in enumerate(tiles):
                nxt = tiles[tidx + 1] if tidx + 1 < len(tiles) else None
                if nxt is not None and nxt[3] == "s":
                    nxt = None
                if kind == "p":
                    def stA(sb_):
                        tk = slice(sb_ * 128, (sb_ + 1) * 128)
                        bks = [nb(pin=True), nb(pin=True)]
                        for h in range(4):
                            bk = bks[h // 2]
                            off = (h % 2) * 256
                            mm(bk[:, off:off + 256], [(qT[:, 2 * h + dc, tk], KTp[layer][:, 2 * h + dc, :]) for dc in range(2)], [qT, KTp[layer]], [bk])
                        return bks

                    def stB(sb_, bks):
                        Pun = Pun2[sb_ % 2]
                        sm = sm2[sb_ % 2]
                        for i in range(2):
                            P.op("dve", lambda e, i=i, bks=bks, sm=sm: e.tensor_reduce(out=sm[:, 2 * i:2 * i + 2], in_=bks[i][:, :].rearrange("p (h m) -> p h m", m=256),
                                                                         axis=AX.X, op=ALU.max, negate=True), [bks[i]], [sm])
                        for h in range(4):
                            bk = bks[h // 2]
                            off = (h % 2) * 256
                            act(Pun[:, h, :], bk[:, off:off + 256], AF.Exp, [bk, sm], [Pun, sm], bias=sm[:, h:h + 1], scale=1.0, accum_out=sm[:, 4 + h:5 + h])
                        unpin(bks[0])
                        unpin(bks[1])
                        P.op("dve", lambda e, sm=sm: e.reciprocal(out=sm[:, 8:12], in_=sm[:, 4:8]), [sm], [sm])
                        for h in range(4):
                            act(Pn[:, h, :], Pun[:, h, :], AF.Copy, [Pun, sm], [Pn], scale=sm[:, 8 + h:9 + h])

                    def stC(sb_):
                        tk = slice(sb_ * 128, (sb_ + 1) * 128)
                        bT = nb()
                        for h in range(4):
                            for mc in range(2):
                                j = h * 2 + mc
                                tr(bf(bT)[:, j * 128:(j + 1) * 128], Pn[:, h, mc * 128:(mc + 1) * 128], ident_b[:], [Pn, ident_b], [bT])
                        act(PT[:, :, tk], bf(bT).rearrange("p (a t) -> p a t", t=128), AF.Copy, [bT], [PT])

                    bq_ = {0: stA(0)}
                    for sb_ in range(4):
                        if sb_ + 1 < 4:
                            bq_[sb_ + 1] = stA(sb_ + 1)
                        elif nxt is not None:
                            qproj(nxt)
                            nxt = None
                        stB(sb_, bq_[sb_])
                        stC(sb_)
                        for _ in range(3):
                            next(kgen, None)
                    for h in range(4):
                        for dc in range(2):
                            e_ = 2 * h + dc
                            b = nb()
                            mm(b[:, 0:w], [(Vp[layer][:, mc, e_ * 128:(e_ + 1) * 128], PT[:, 2 * h + mc, 0:w]) for mc in range(2)], [Vp[layer], PT], [b])
                            act(oT[:, e_, 0:w], b[:, 0:w], AF.Copy, [b], [oT])
                else:
                    for _ in kgen:
                        pass
                    attn_sample(layer, oT)
                for oc in range(8):
                    b = nb()
                    mm(b[:, 0:w], [(Wo[:, kc, oc * 128:(oc + 1) * 128], oT[:, kc, 0:w]) for kc in range(8)], [Wo, oT], [b])
                    add_to_x(ti, c0, w, oc, b)
                if FUSE:
                    rmsnorm(ti, c0, w, G_FFN + 8 * layer)

        def sample_k_phase(layer, ti, c0, w):
            Y0 = YT_OFF
            NK = 5
            Kr = [av(Y0 + i * 1024, [128, 2, D], BF16) for i in range(NK)]
            q_tm = av(Y0 + 5120, [NS, D], BF16)
            junk = av(Y0 + 5632, [128, 256], F32)
            STs = av(Y0 + 5888, [128, 2, NS, 4], F32)
            qTs = av(Y0 + 6016, [128, 8, NS], BF16)
            Wq = wview(slots[0], 1024)
            for fc in range(8):
                b = proj(Wq, fc * 128, ti, c0, w)
                act(qTs[:, fc, :], b[:, 0:w], AF.Identity, [b], [qTs], scale=0.0625)
            bq = nb()
            for fc in range(8):
                tr(bf(bq)[0:NS, fc * 128:(fc + 1) * 128], qTs[:, fc, :], ident_b[:], [qTs, ident_b], [bq])
            act(q_tm[:, :], bf(bq)[0:NS, :], AF.Copy, [bq], [q_tm])
            yield
            for n in range(NS):
                Kb = Kr[n % NK]
                P.dma("pool", Kb[:], ck[layer, n].rearrange("(c p) f -> p c f", p=128), writes=[Kb])
                qb = [nb(), nb()]
                for i in range(2):
                    mm(qb[i][:, :], [(sel[:, n, :], q_tm[:, i * 512:(i + 1) * 512])], [sel, q_tm], [qb[i]])
                for mc in range(2):
                    for h in range(4):
                        stt(junk[:, :], Kb[:, mc, h * 256:(h + 1) * 256], 1.0, qb[h // 2][:, (h % 2) * 256:(h % 2) * 256 + 256], ALU.mult, ALU.mult,
                            [Kb, qb[h // 2]], [junk, STs], accum_out=STs[:, mc, n, h:h + 1])
                yield

        def attn_sample(layer, oT):
            A0 = T0 + 2048
            NV = 4
            KV = [av(A0 + i * 1024, [128, 2, D], BF16) for i in range(NV)]
            q_tm = av(YT_OFF + 5120, [NS, D], BF16)
            STs = av(YT_OFF + 5888, [128, 2, NS, 4], F32)
            S_sm = av(T0 + 6144 + 128, [64, 256], F32)
            PTs = av(T0 + 6144 + 384, [128, 2, NS, 4], F32)
            Pm = av(T0 + 6144 + 512, [128, 2, 4, NS, NS], BF16)
            sm = av(T0 + 7680, [128, 16], F32)
            bS = nb()
            for mc in range(2):
                tr(bS[0:64, mc * 128:(mc + 1) * 128], STs[:, mc, :, :].rearrange("p n h -> p (n h)"), ident_f[:], [STs, ident_f], [bS])
            P.op("dve", lambda e: e.tensor_reduce(out=sm[0:64, 0:1], in_=bS[0:64, 0:256], axis=AX.X, op=ALU.max, negate=True), [bS], [sm])
            act(S_sm[:, :], bS[0:64, 0:256], AF.Exp, [bS, sm], [S_sm, sm], bias=sm[0:64, 0:1], scale=1.0, accum_out=sm[0:64, 4:5])
            P.op("dve", lambda e: e.reciprocal(out=sm[0:64, 8:9], in_=sm[0:64, 4:5]), [sm], [sm])
            ts(S_sm[:, :], S_sm[:, :], sm[0:64, 8:9], None, ALU.mult, None, [S_sm, sm], [S_sm])
            bS2 = nb()
            for mc in range(2):
                tr(bS2[:, mc * 64:(mc + 1) * 64], S_sm[:, mc * 128:(mc + 1) * 128], ident_f[0:64, 0:64], [S_sm, ident_f], [bS2])
            act(PTs[:, :, :, :].rearrange("p c n h -> p (c n h)"), bS2[:, 0:128], AF.Copy, [bS2], [PTs])
            for mc in range(2):
                for h in range(4):
                    for n in range(NS):
                        ts(Pm[:, mc, h, n, :], maskM[:, n, :], PTs[:, mc, n, h:h + 1], None, ALU.mult, None, [maskM, PTs], [Pm])
            bo = [nb(pin=True) for _ in range(4)]
            for n in range(NS):
                Vb = KV[n % NV]
                P.dma("pool", Vb[:], cv[layer, n].rearrange("(c p) f -> p c f", p=128), writes=[Vb])
                for h in range(4):
                    for mc in range(2):
                        mm1(bo[h][0:NS, 0:256], Pm[:, mc, h, n, :], Vb[:, mc, h * 256:(h + 1) * 256], n == 0 and mc == 0, n == NS - 1 and mc == 1,
                            [Pm, Vb], [bo[h]])
            o_tm = q_tm
            for h in range(4):
                act(o_tm[:, h * 256:(h + 1) * 256], bo[h][0:NS, 0:256], AF.Copy, [bo[h]], [o_tm])
                unpin(bo[h])
            bq2 = nb()
            for kc in range(8):
                tr(bf(bq2)[:, kc * NS:(kc + 1) * NS], o_tm[:, kc * 128:(kc + 1) * 128], ident_b[0:NS, 0:NS], [o_tm, ident_b], [bq2])
            act(oT[:, :, 0:NS], bf(bq2)[:, 0:8 * NS].rearrange("p (a t) -> p a t", t=NS), AF.Copy, [bq2], [oT])

        def ffn(st, tiles, layer):
            for (ti, c0, w, kind, g0) in tiles:
                if not FUSE:
                    rmsnorm(ti, c0, w, G_FFN + 8 * layer)
            actb = av(YT_OFF, [128, 12, NT], BF16)
            wg_v = w_gate[layer].rearrange("(k p) f -> p k f", p=128)
            wu_v = w_up[layer].rearrange("(k p) f -> p k f", p=128)
            cnt = 0
            pi = 0
            groups = [(0, 12), (12, 10)]
            for gi, (f0, G) in enumerate(groups):
                Wd = Buf(slots[gi].ap[:, 0:G * D].rearrange("p (j f) -> p j f", f=D), slots[gi].deps)
                wd_v = w_down[layer][f0 * 128:(f0 + G) * 128, :].rearrange("(j p) f -> p j f", p=128)
                hG = G // 2
                P.dma("pool", Wd[:, 0:hG, :], wd_v[:, 0:hG, :], writes=slots[gi].deps[0:2])
                P.dma("pool", Wd[:, hG:G, :], wd_v[:, hG:G, :], writes=slots[gi].deps[2:4] + slot_extra[gi])
                for jp in range(0, G, 4):
                    fc = f0 + jp
                    nq = min(4, G - jp)
                    weg = av(T0 + (pi % 2) * 4096, [128, 8, 512], BF16)
                    weu = av(T0 + (pi % 2) * 4096 + 2048, [128, 8, 512], BF16)
                    pi += 1
                    P.dma("pool", weg[:, :, 0:nq * 128], wg_v[:, :, fc * 128:(fc + nq) * 128], writes=[weg])
                    P.dma("pool", weu[:, :, 0:nq * 128], wu_v[:, :, fc * 128:(fc + nq) * 128], writes=[weu])
                    for q in range(nq):
                        j = jp + q
                        for (ti, c0, w, kind, g0) in tiles:
                            bg = nb()
                            mm(bg[:, 0:w], [(weg[:, kc, q * 128:(q + 1) * 128], hT[:, kc, c0:c0 + w]) for kc in range(8)], [weg, hdp[ti]], [bg])
                            bu = nb()
                            mm(bu[:, 0:w], [(weu[:, kc, q * 128:(q + 1) * 128], hT[:, kc, c0:c0 + w]) for kc in range(8)], [weu, hdp[ti]], [bu])
                            sg = av(T0 + 8192, [128, 512], F32)
                            act(sg[:, 0:w], bg[:, 0:w], AF.Silu, [bg], [sg])
                            tt(actb[:, j, c0:c0 + w], sg[:, 0:w], bu[:, 0:w], ALU.mult, [sg, bu], [actb])
                for (ti, c0, w, kind, g0) in tiles:
                    for oc in range(8):
                        b = nb()
                        mm(b[:, 0:w], [(Wd[:, j, oc * 128:(oc + 1) * 128], actb[:, j, c0:c0 + w]) for j in range(G)], [Wd, actb], [b])
                        add_to_x(ti, c0, w, oc, b)
                    if FUSE and gi == len(groups) - 1 and layer == 0:
                        rmsnorm(ti, c0, w, G_MIX + 8)

        def mixer_odd(st, tiles):
            yT = av(YT_OFF, [128, 8, NT], BF16)
            tmp1 = av(O_T1, [128, 512], F32)
            tmp2 = av(O_T2, [128, 512], F32)
            WC = load_w(slots[0], w_in_odd[0], 0, 1024)
            WD = load_w(slots[1], w_in_odd[0], 1024, 2560)
            for (ti, c0, w, kind, g0) in tiles:
                if not FUSE:
                    rmsnorm(ti, c0, w, G_MIX + 8)
            for (ti, c0, w, kind, g0) in tiles:
                u_sb = av(T0, [128, 4, 512], F32)
                for c in range(4):
                    b = proj(WC, c * 128, ti, c0, w)
                    act(u_sb[:, c, 0:w], b[:, 0:w], AF.Copy, [b], [u_sb])
                if kind == "p":
                    vt = av(T0 + 2048, [128, 512], F32)
                    vb = av(T0 + 2560, [128, 4, 512], BF16)
                    sm = av(T0 + 3584, [128, 16], F32)
                    hb_ = [nb(pin=True) for _ in range(4)]

                    def vproj(sb_):
                        tk = slice(c0 + sb_ * 128, c0 + (sb_ + 1) * 128)
                        bv = nb(pin=True)
                        mm(bv[:, :], [(hT[:, kc, tk], WC[:, kc, 512:1024]) for kc in range(8)], [hdp[ti], WC], [bv])
                        return bv
                    bvs = {0: vproj(0)}
                    for sb_ in range(4):
                        if sb_ + 1 < 4:
                            bvs[sb_ + 1] = vproj(sb_ + 1)
                        bv = bvs[sb_]
                        P.op("dve", lambda e, bv=bv, sm=sm: e.bn_stats(out=sm[:, 0:6], in_=bv[:, :]), [bv], [sm])
                        P.op("dve", lambda e, sm=sm: e.bn_aggr(out=sm[:, 8:10], in_=sm[:, 0:6]), [sm], [sm])
                        act(sm[:, 10:11], sm[:, 9:10], AF.Ln, [sm, eps_t], [sm], bias=eps_t[:, 0:1], scale=1.0)
                        act(sm[:, 10:11], sm[:, 10:11], AF.Exp, [sm], [sm], scale=-0.5)
                        ts(vt[:, :], bv[:, :], sm[:, 8:9], sm[:, 10:11], ALU.subtract, ALU.mult, [bv, sm], [vt])
                        unpin(bv)
                        tt(vt[:, :], vt[:, :], gC_bc[:], ALU.mult, [vt, gC_bc], [vt])
                        tt(vt[:, :], vt[:, :], bC_bc[:], ALU.add, [vt, bC_bc], [vt])
                        if g0 + sb_ * 128 == SEQ - 128:
                            P.dma("sp", o_pc[:, :], vt[:, :], reads=[vt], is_output=True)
                        vcopy(vb[:, sb_, :], vt[:, :], [vt], [vb])
                        for h in range(4):
                            mm(hb_[h][:, sb_ * 128:(sb_ + 1) * 128],
                               [(vb[:, sb_, h * 128:(h + 1) * 128], wsT[:, h, :]),
                                (ones_row[0:1, :], bsH[0:1, h * 128:(h + 1) * 128]),
                                (ones_row[0:1, :], bsL[0:1, h * 128:(h + 1) * 128])], [vb, wsT, ones_row, bsH, bsL], [hb_[h]])
                    for h in range(4):
                        tt(yT[:, h, c0:c0 + w], u_sb[:, h, 0:w], hb_[h][:, 0:w], ALU.mult, [u_sb, hb_[h]], [yT])
                        unpin(hb_[h])
                else:
                    vs = av(O_CA, [128, 4, NS], F32)
                    for c in range(4):
                        b = proj(WC, 512 + c * 128, ti, c0, w)
                        act(vs[:, c, :], b[:, 0:w], AF.Copy, [b], [vs])
                    ln_feat(lambda c: vs[:, c, :], vs, w, C_LCG, C_LCB, AF.Identity, lambda c: vs[:, c, :], vs)
                    to_tokmajor(lambda c: vs[:, c, :], vs, NS, o_sc[:, :], O_T1)
                    for c in range(4):
                        ts(tmp2[:, 0:w], vs[:, c, :], ws00[:, c:c + 1], bs0[:, c:c + 1], ALU.mult, ALU.add, [vs, ws00, bs0], [tmp2])
                        tt(yT[:, c, c0:c0 + w], u_sb[:, c, 0:w], tmp2[:, 0:w], ALU.mult, [u_sb, tmp2], [yT])
            for (ti, c0, w, kind, g0) in tiles:
                if kind == "p":
                    gd = av(T0, [128, 4, 514], F32)
                    vcopy(gd[:, :, 0:2], gdH[:], [gdH], [gd])
                    cur = lambda c: gd[:, c, 2:2 + w]
                    tap = lambda c, k: gd[:, c, k:k + w]
                else:
                    gd = av(T0, [128, 4, NS, 3], F32)
                    load_hist_T(st_d, 2, lambda c: gd[:, c, :, :], gd)
                    cur = lambda c: gd[:, c, :, 2]
                    tap = lambda c, k: gd[:, c, :, k]
                for c in range(4):
                    bgc = proj(WD, 512 + c * 128, ti, c0, w)
                    bhd = proj(WD, 1024 + c * 128, ti, c0, w)
                    act(tmp1[:, 0:w], bgc[:, 0:w], AF.Copy, [bgc], [tmp1])
                    tt(cur(c), tmp1[:, 0:w], bhd[:, 0:w], ALU.mult, [tmp1, bhd], [gd])
                    ts(tmp2[:, 0:w], tap(c, 0), prm[:, C_DW + c:C_DW + c + 1], None, ALU.mult, None, [gd, prm], [tmp2])
                    stt(tmp2[:, 0:w], tap(c, 1), prm[:, C_DW + 4 + c:C_DW + 5 + c], tmp2[:, 0:w], ALU.mult, ALU.add, [gd, prm, tmp2], [tmp2])
                    stt(tmp2[:, 0:w], tap(c, 2), prm[:, C_DW + 8 + c:C_DW + 9 + c], tmp2[:, 0:w], ALU.mult, ALU.add, [gd, prm, tmp2], [tmp2])
                    bgb = proj(WD, c * 128, ti, c0, w)
                    tt(yT[:, 4 + c, c0:c0 + w], tmp2[:, 0:w], bgb[:, 0:w], ALU.mult, [tmp2, bgb], [yT])
                if kind == "p":
                    vcopy(gdH[:], gd[:, :, 512:514], [gd], [gdH])
                    if g0 + 512 == SEQ:
                        to_tokmajor(lambda c: gd[:, c, 512:514], gd, 2, o_pd[:, :], O_T1)
                else:
                    P.dma("sp", o_sd[:, 0:1, :], st_d[:, 1:2, :], is_output=True)
                    to_tokmajor(lambda c: gd[:, c, :, 2], gd, NS, o_sd[:, 1, :], O_T1)
            Wout = load_w(slots[0], w_out_odd[0], 0, 1024)
            out_proj(Wout, yT, tiles, next_gcol=G_X + 8)

        ST_TILES = [
            [(0, 0, 512, "p", 0), (1, 512, 512, "p", 512)],
            [(0, 0, 512, "p", 1024), (1, 512, 512, "p", 1536), (2, 1024, NS, "s", 0)],
        ]
        if stage < 1:
            ST_TILES = []
        import os
        if os.environ.get("NOSAMPLE"):
            ST_TILES = [[t for t in tl if t[3] == "p"] for tl in ST_TILES]

        for st, tiles in enumerate(ST_TILES):
            for (ti, c0, w, kind, g0) in tiles:
                if kind == "p":
                    for sb_ in range(4):
                        xs = av(T0 + (sb_ % 2) * 1024, [128, D], F32)
                        P.dma("sp", xs[:], x_p[g0 + sb_ * 128:g0 + (sb_ + 1) * 128, :], writes=[xs])
                        for kq in range(2):
                            b = nb()
                            for k4 in range(4):
                                kc = kq * 4 + k4
                                tr(b[:, k4 * 128:(k4 + 1) * 128], xs[:, kc * 128:(kc + 1) * 128], ident_f[:], [xs, ident_f], [b])
                            act(xT[:, kq * 4:kq * 4 + 4, c0 + sb_ * 128:c0 + (sb_ + 1) * 128],
                                b[:, :].rearrange("p (a t) -> p a t", t=128), AF.Copy, [b], [xd[kq * 4 + k4][ti] for k4 in range(4)])
                else:
                    xs = av(T0, [NS, D], F32)
                    P.dma("sp", xs[:], x_s[:, :], writes=[xs])
                    b = nb()
                    for kc in range(8):
                        tr(b[:, kc * NS:(kc + 1) * NS], xs[:, kc * 128:(kc + 1) * 128], ident_f[0:NS, 0:NS], [xs, ident_f], [b])
                    act(xT[:, :, c0:c0 + NS], b[:, 0:8 * NS].rearrange("p (a t) -> p a t", t=NS), AF.Copy, [b], xdeps(ti))
                if stage >= 7:
                    rmsnorm(ti, c0, w, G_MIX + 0)

            nlayers = 0 if stage < 2 else (1 if stage < 5 else 2)
            for layer in range(nlayers):
                base = 2 + 3 * layer
                if layer == 0:
                    mixer_even(st, tiles)
                else:
                    mixer_odd(st, tiles)
                if stage >= base + 1:
                    attn(st, tiles, layer)
                if stage >= base + 2:
                    ffn(st, tiles, layer)
            if os.environ.get("RMSONLY"):
                for (ti, c0, w, kind, g0) in tiles:
                    rmsnorm(ti, c0, w, G_FIN)
            for (ti, c0, w, kind, g0) in (tiles if not os.environ.get("NOFINAL") else []):
                yf = av(T0, [128, 8, 512], F32)
                rmsnorm(ti, c0, w, G_FIN, final_dst=yf)
                if kind == "p":
                    for sb_ in range(4):
                        ys = av(T0 + 4096 + (sb_ % 2) * 1024, [128, D], F32)
                        for kq in range(2):
                            b = nb()
                            for k4 in range(4):
                                kc = kq * 4 + k4
                                tr(b[:, k4 * 128:(k4 + 1) * 128], yf[:, kc, sb_ * 128:(sb_ + 1) * 128], ident_f[:], [yf, ident_f], [b])
                            act(ys[:, kq * 512:(kq + 1) * 512], b[:, :], AF.Copy, [b], [ys])
                        P.dma("sp", y_p[g0 + sb_ * 128:g0 + (sb_ + 1) * 128, :], ys[:], reads=[ys], is_output=True)
                else:
                    ys = av(T0 + 4096, [NS, D], F32)
                    for kq in range(2):
                        b = nb()
                        for k4 in range(4):
                            kc = kq * 4 + k4
                            tr(b[0:NS, k4 * 128:(k4 + 1) * 128], yf[:, kc, 0:NS], ident_f[:], [yf, ident_f], [b])
                        act(ys[:, kq * 512:(kq + 1) * 512], b[0:NS, :], AF.Copy, [b], [ys])
                    P.dma("sp", y_s[:, :], ys[:], reads=[ys], is_output=True)

        P.finish()
        P.emit_all()
    return nc


_OUT_ORDER = ["y_p", "y_s", "o_pa", "o_pb", "o_pc", "o_pd", "o_pk", "o_pv", "o_sa", "o_sb", "o_sc", "o_sd"]


def make_in_maps(inp):
    f = lambda a: np.ascontiguousarray(np.asarray(a, dtype=np.float32))
    maps = []
    shared = {k: f(inp[k]) for k in ["norm_mix", "norm_x", "norm_ffn", "norm_final", "w_in_even", "conv_a_w", "conv_a_b",
                                     "ln_a_g", "ln_a_b", "pool_b_w", "pool_b_scale", "w_out_even", "w_in_odd", "ln_c_g",
                                     "ln_c_b", "ws_c", "bs_c", "conv_d_w", "w_out_odd", "wq_x", "wk_x", "wv_x", "wo_x",
                                     "w_gate", "w_up", "w_down"]}
    for c in range(NCORE):
        s = slice(c * NS, (c + 1) * NS)
        m = dict(shared)
        m["x_p"] = f(inp["x_prompt"][c])
        m["x_s"] = f(inp["x_sample"][s, 0])
        m["mem"] = f(inp["mem_prompt"][c])
        m["st_a"] = f(inp["state_convA"][0, s])
        m["st_b"] = f(inp["state_poolB"][0, s])
        m["st_d"] = f(inp["state_convD"][0, s])
        m["ck"] = f(np.asarray(inp["cache_mem_k"])[:, s].reshape(2, NS, 256, D))
        m["cv"] = f(np.asarray(inp["cache_mem_v"])[:, s].reshape(2, NS, 256, D))
        maps.append(m)
    return maps


def assemble(results):
    r = results
    cat = lambda k: np.stack([np.asarray(r[c][k]) for c in range(NCORE)])
    y_prompt = cat("y_p")
    y_sample = np.concatenate([np.asarray(r[c]["y_s"]) for c in range(NCORE)], 0)[:, None, :]
    p_convA = cat("o_pa")[None]
    p_poolB = cat("o_pb")[None]
    p_chunkC = cat("o_pc").reshape(1, NCORE, 128, 4, 128)
    p_convD = cat("o_pd")[None]
    p_mem_k = np.stack([np.asarray(r[c]["o_pk"]) for c in range(NCORE)], 1).reshape(2, NCORE, 256, 4, 256)
    p_mem_v = np.stack([np.asarray(r[c]["o_pv"]) for c in range(NCORE)], 1).reshape(2, NCORE, 256, 4, 256)
    s_convA = np.concatenate([np.asarray(r[c]["o_sa"]) for c in range(NCORE)], 0)[None]
    s_poolB = np.concatenate([np.asarray(r[c]["o_sb"]) for c in range(NCORE)], 0)[None]
    s_chunkC = np.concatenate([np.asarray(r[c]["o_sc"]) for c in range(NCORE)], 0).reshape(1, NCORE * NS, 1, 4, 128)
    s_convD = np.concatenate([np.asarray(r[c]["o_sd"]) for c in range(NCORE)], 0)[None]
    outs = (y_prompt, y_sample, p_convA, p_poolB, p_chunkC, p_convD, p_mem_k, p_mem_v, s_convA, s_poolB, s_chunkC, s_convD)
    return tuple(np.ascontiguousarray(o, dtype=np.float32) for o in outs)


def kernel(**inputs):
    nc = build_program()
    in_maps = make_in_maps(inputs)
    res = run_bass_kernel_spmd(nc, in_maps, core_ids=list(range(NCORE)))
    return assemble(res.results)
```

```python
import numpy as np
import concourse.bass as bass
import concourse.mybir as mybir
from concourse.bass_utils import run_bass_kernel_spmd
from contextlib import ExitStack

F32 = mybir.dt.float32
BF16 = mybir.dt.bfloat16
I32 = mybir.dt.int32
ALU = mybir.AluOpType
AF = mybir.ActivationFunctionType
AX = mybir.AxisListType

NCORE = 8
D = 1024
SEQ = 2048
NS = 16
DFF = 2816
NFC = 22
EPS = 1e-6


class Dep:
    __slots__ = ("w", "r", "excl")

    def __init__(self, excl=False):
        self.w = None
        self.r = []
        self.excl = excl


class Buf:
    def __init__(self, ap, deps):
        self.ap = ap
        self.deps = list(deps)

    def __getitem__(self, idx):
        return self.ap[idx]


def _flat(xs):
    out = []
    for x in xs:
        if x is None:
            continue
        if isinstance(x, Dep):
            out.append(x)
        elif isinstance(x, Buf):
            out.extend(x.deps)
        else:
            out.extend(_flat(x))
    return out


class Prog:
    CE = ("pe", "act", "dve", "pool")
    ENGS = ("pe", "act", "dve", "pool", "sp")

    def __init__(self, nc, es, ring=16):
        self.nc = nc
        self.rec = {e: [] for e in self.ENGS}
        self.nops = {e: 0 for e in self.CE}
        self.semobj = {}
        for e in self.CE:
            self.semobj[("e", e)] = es.enter_context(nc.semaphore("s_" + e))
        self.ring = ring
        self.dq = ("sp", "pool")
        for q in self.dq:
            for i in range(ring):
                self.semobj[("d", q, i)] = es.enter_context(nc.semaphore("d_%s%d" % (q, i)))
        self.dcnt = {q: 0 for q in self.dq}
        self.out_tokens = []

    @staticmethod
    def _needs(reads, writes, extra=()):
        need = {}

        def add(tok):
            if tok is None:
                return
            k, v = tok
            if need.get(k, 0) < v:
                need[k] = v
        for d in reads:
            add(d.w)
        for d in writes:
            add(d.w)
            for t in d.r:
                add(t)
        for t in extra:
            add(t)
        return need

    @staticmethod
    def _commit(tok, reads, writes):
        for d in reads:
            d.r.append(tok)
        for d in writes:
            d.w = tok
            d.r = []

    def op(self, eng, fn, reads=(), writes=()):
        reads = _flat(reads)
        writes = _flat(writes)
        writes = writes + [d for d in reads if d.excl]
        reads = [d for d in reads if not d.excl]
        need = self._needs(reads, writes)
        self.nops[eng] += 1
        tok = (("c", eng), self.nops[eng])
        self.rec[eng].append({"kind": "op", "fn": fn, "need": need, "ord": self.nops[eng]})
        self._commit(tok, reads, writes)
        return tok

    def dma(self, q, out, in_, reads=(), writes=(), is_output=False, **kw):
        reads = _flat(reads)
        writes = _flat(writes)
        j = self.dcnt[q]
        self.dcnt[q] += 1
        slot = j % self.ring
        rnd = j // self.ring
        key = ("d", q, slot)
        extra = [(key, 16 * rnd)] if rnd > 0 else []
        need = self._needs(reads, writes, extra)
        tok = (key, 16 * (rnd + 1))
        self.rec[q].append({"kind": "dma", "out": out, "in": in_, "kw": kw, "need": need, "sem": key})
        self._commit(tok, reads, writes)
        if is_output:
            self.out_tokens.append(tok)
        return tok

    def finish(self):
        allt = list(self.out_tokens)
        for q in self.dq:
            n = self.dcnt[q]
            for slot in range(self.ring):
                k = (n - slot + self.ring - 1) // self.ring if n > slot else 0
                if k > 0:
                    allt.append((("d", q, slot), 16 * k))
        self.rec["sp"].append({"kind": "wait", "need": self._needs((), (), allt)})

    def emit_all(self):
        signal = {e: set() for e in self.CE}
        for E in self.ENGS:
            waited = {}
            for r in self.rec[E]:
                w = []
                for k, v in r["need"].items():
                    if k[0] == "c" and k[1] == "pe" and E == "pe":
                        continue
                    if waited.get(k, 0) < v:
                        waited[k] = v
                        w.append((k, v))
                        if k[0] == "c":
                            signal[k[1]].add(v)
                r["waits"] = w
        val = {}
        for e in self.CE:
            c = 0
            m = {}
            for o in range(1, self.nops[e] + 1):
                if o in signal[e]:
                    c += 1
                m[o] = c
            val[e] = m
        self.n_signals = {e: len(signal[e]) for e in self.CE}

        def run(E, e):
            for r in self.rec[E]:
                for k, v in r["waits"]:
                    if k[0] == "c":
                        e.wait_ge(self.semobj[("e", k[1])], val[k[1]][v])
                    else:
                        e.wait_ge(self.semobj[k], v)
                if r["kind"] == "op":
                    inst = r["fn"](e)
                    if r["ord"] in signal[E]:
                        inst.then_inc(self.semobj[("e", E)], 1)
                elif r["kind"] == "dma":
                    e.dma_start(out=r["out"], in_=r["in"], **r["kw"]).then_inc(self.semobj[r["sem"]], 16)

        with self.nc.Block() as block:
            @block.sync
            def _(e):
                run("sp", e)

            @block.tensor
            def _(e):
                run("pe", e)

            @block.scalar
            def _(e):
                run("act", e)

            @block.vector
            def _(e):
                run("dve", e)

            @block.gpsimd
            def _(e):
                run("pool", e)


G_MIX, G_X, G_FFN, G_FIN = 0, 16, 32, 48
C_AB, C_LAG, C_LAB, C_PBS, C_LCG, C_LCB, C_DW = 56, 60, 64, 68, 72, 76, 80
NPRM = 92


def build_program(stage=99):
    import os
    nc = bass.Bass("TRN2", target_bir_lowering=False)

    def din(name, shape):
        return nc.dram_tensor(name, list(shape), F32, kind="ExternalInput").ap()

    def dout(name, shape):
        return nc.dram_tensor(name, list(shape), F32, kind="ExternalOutput").ap()

    x_p = din("x_p", [SEQ, D]); x_s = din("x_s", [NS, D]); mem = din("mem", [256, D])
    st_a = din("st_a", [NS, 30, 512]); st_b = din("st_b", [NS, 15, 512]); st_d = din("st_d", [NS, 2, 512])
    ck = din("ck", [2, NS, 256, D]); cv = din("cv", [2, NS, 256, D])
    norm_mix = din("norm_mix", [2, D]); norm_x = din("norm_x", [2, D]); norm_ffn = din("norm_ffn", [2, D])
    norm_final = din("norm_final", [D])
    w_in_even = din("w_in_even", [1, D, 1536]); conv_a_w = din("conv_a_w", [1, 31, 512])
    conv_a_b = din("conv_a_b", [1, 512]); ln_a_g = din("ln_a_g", [1, 512]); ln_a_b = din("ln_a_b", [1, 512])
    pool_b_w = din("pool_b_w", [1, 4, 128, 128]); pool_b_scale = din("pool_b_scale", [1, 512])
    w_out_even = din("w_out_even", [1, D, D]); w_in_odd = din("w_in_odd", [1, D, 2560])
    ln_c_g = din("ln_c_g", [1, 512]); ln_c_b = din("ln_c_b", [1, 512]); ws_c = din("ws_c", [1, 4, 128, 128])
    bs_c = din("bs_c", [1, 4, 128]); conv_d_w = din("conv_d_w", [1, 3, 512]); w_out_odd = din("w_out_odd", [1, D, D])
    wq_x = din("wq_x", [2, D, D]); wk_x = din("wk_x", [2, D, D]); wv_x = din("wv_x", [2, D, D]); wo_x = din("wo_x", [2, D, D])
    w_gate = din("w_gate", [2, D, DFF]); w_up = din("w_up", [2, D, DFF]); w_down = din("w_down", [2, DFF, D])

    y_p = dout("y_p", [SEQ, D]); y_s = dout("y_s", [NS, D])
    o_pa = dout("o_pa", [30, 512]); o_pb = dout("o_pb", [15, 512]); o_pc = dout("o_pc", [128, 512]); o_pd = dout("o_pd", [2, 512])
    o_pk = dout("o_pk", [2, 256, D]); o_pv = dout("o_pv", [2, 256, D])
    o_sa = dout("o_sa", [NS, 30, 512]); o_sb = dout("o_sb", [NS, 15, 512]); o_sc = dout("o_sc", [NS, 512]); o_sd = dout("o_sd", [NS, 2, 512])

    with ExitStack() as es:
        P = Prog(nc, es)

        def sbt(name, shape, dt, ndeps=1):
            t = es.enter_context(nc.sbuf_tensor(name, list(shape), dt))
            return Buf(t, [Dep() for _ in range(ndeps)])

        NT = 1040
        xT = sbt("xT", [128, 8, NT], F32)
        xd = [[Dep() for _ in range(3)] for _ in range(8)]
        hT = sbt("hT", [128, 8, NT], BF16)
        hdp = [Dep() for _ in range(3)]
        KTp = [sbt("KTp%d" % l, [128, 8, 256], BF16) for l in range(2)]
        Vp = [sbt("Vp%d" % l, [128, 2, D], BF16) for l in range(2)]
        ident_f = sbt("ident_f", [128, 128], F32)
        ident_b = sbt("ident_b", [128, 128], BF16)
        ones_rms = sbt("ones_rms", [128, 128], BF16)
        ones_ln = sbt("ones_ln", [128, 128], BF16)
        ones_row = sbt("ones_row", [1, 128], BF16)
        prm = sbt("prm", [128, NPRM], F32)
        cw = sbt("cw", [128, 124], F32)
        eps_t = sbt("eps_t", [128, 1], F32)
        gC_bc = sbt("gC_bc", [128, 512], F32)
        bC_bc = sbt("bC_bc", [128, 512], F32)
        wsT = sbt("wsT", [128, 4, 128], BF16)
        poolw = sbt("poolw", [128, 4, 128], BF16)
        bsF = sbt("bsF", [1, 512], F32)
        bsH = sbt("bsH", [1, 512], BF16)
        bsL = sbt("bsL", [1, 512], BF16)
        ws00 = sbt("ws00", [128, 4], F32)
        bs0 = sbt("bs0", [128, 4], F32)
        invc = sbt("invc", [128, 16], F32)
        zaH = sbt("zaH", [128, 4, 30], F32)
        zbH = sbt("zbH", [128, 4, 15], F32)
        gdH = sbt("gdH", [128, 4, 2], F32)
        slots = [sbt("slot%d" % i, [128, 12288], BF16, ndeps=4) for i in range(2)]
        DgA = Buf(slots[1].ap[:, 8192:10240].rearrange("p (k f) -> p k f", f=128), [Dep()])
        DgB = Buf(slots[1].ap[:, 10240:12288].rearrange("p (k f) -> p k f", f=128), [Dep()])
        slot_extra = {1: [DgA, DgB], 0: []}
        AW = 17568
        arena_t = es.enter_context(nc.sbuf_tensor("arena", [128, AW], F32))
        CH = 32
        adeps = [Dep() for _ in range((AW + CH - 1) // CH)]

        def av(off, shape, dt):
            n = 1
            for s in shape[1:]:
                n *= s
            words = n if dt in (F32, I32) else (n + 1) // 2
            assert off + words <= AW, (off, words)
            ap = arena_t[0:shape[0], off:off + words]
            if dt != F32:
                ap = ap.bitcast(dt)
            if len(shape) >= 3:
                names = "abcdef"[:len(shape) - 1]
                kw = {names[i]: shape[i + 1] for i in range(1, len(names))}
                ap = ap.rearrange("p (%s) -> p %s" % (" ".join(names), " ".join(names)), **kw)
            assert off % CH == 0, off
            deps = adeps[off // CH:(off + words - 1) // CH + 1]
            return Buf(ap, deps)

        sq_sep = sbt("sq_sep", [128, 8, 512], BF16) if os.environ.get("SQSEP") else None
        banks = []
        for i in range(8):
            t = es.enter_context(nc.psum_tensor("pb%d" % i, [128, 512], F32))
            banks.append(Buf(t, [Dep(excl=True)]))
        bank_i = [0]
        pinned = set()

        def nb(pin=False):
            while (bank_i[0] % 8) in pinned:
                bank_i[0] += 1
            i = bank_i[0] % 8
            bank_i[0] += 1
            if pin:
                pinned.add(i)
            return banks[i]

        def unpin(b):
            pinned.discard(banks.index(b))

        def mm(out_ap, pairs, reads, writes):
            def fn(e, pairs=pairs, out_ap=out_ap):
                n = len(pairs)
                inst = None
                for i, (l, r) in enumerate(pairs):
                    inst = e.matmul(out_ap, lhsT=l, rhs=r, start=(i == 0), stop=(i == n - 1))
                return inst
            P.op("pe", fn, reads, writes)

        def mmg(out_ap, pairs, first, last, reads, writes):
            def fn(e, pairs=pairs, out_ap=out_ap, first=first, last=last):
                n = len(pairs)
                inst = None
                for i, (l, r) in enumerate(pairs):
                    inst = e.matmul(out_ap, lhsT=l, rhs=r, start=(first and i == 0), stop=(last and i == n - 1))
                return inst
            P.op("pe", fn, reads, writes)

        def mm1(out_ap, lhsT, rhs, start, stop, reads, writes):
            P.op("pe", lambda e: e.matmul(out_ap, lhsT=lhsT, rhs=rhs, start=start, stop=stop), reads, writes)

        def tr(out_ap, in_ap, ident_ap, reads, writes):
            P.op("pe", lambda e: e.transpose(out=out_ap, in_=in_ap, identity=ident_ap), reads, writes)

        def act(out, in_, func, reads, writes, **kw):
            P.op("act", lambda e: e.activation(out=out, in_=in_, func=func, **kw), reads, writes)

        def tt(out, in0, in1, op, reads, writes):
            P.op("dve", lambda e: e.tensor_tensor(out=out, in0=in0, in1=in1, op=op), reads, writes)

        def ts(out, in0, s1, s2, op0, op1, reads, writes):
            if op1 is None:
                P.op("dve", lambda e: e.tensor_scalar(out=out, in0=in0, scalar1=s1, scalar2=None, op0=op0), reads, writes)
            else:
                P.op("dve", lambda e: e.tensor_scalar(out=out, in0=in0, scalar1=s1, scalar2=s2, op0=op0, op1=op1), reads, writes)

        def stt(out, in0, scalar, in1, op0, op1, reads, writes, **kw):
            P.op("dve", lambda e: e.scalar_tensor_tensor(out=out, in0=in0, scalar=scalar, in1=in1, op0=op0, op1=op1, **kw), reads, writes)

        def vcopy(out, in_, reads, writes):
            P.op("dve", lambda e: e.tensor_copy(out=out, in_=in_), reads, writes)

        if stage < 0:
            tt_ = av(0, [NS, D], F32)
            P.dma("sp", tt_[:], x_s[:, :], writes=[tt_])
            P.dma("sp", y_s[:, :], tt_[:], reads=[tt_], is_output=True)
            P.finish()
            P.emit_all()
            return nc
        P.op("pool", lambda e: e.memset(ident_f[:], 1.0), [], [ident_f])
        P.op("pool", lambda e: e.affine_select(out=ident_f[:], in_=ident_f[:], pattern=[[-1, 128]], compare_op=ALU.is_equal,
                                               fill=0.0, base=0, channel_multiplier=1), [ident_f], [ident_f])
        vcopy(ident_b[:], ident_f[:], [ident_f], [ident_b])
        P.op("pool", lambda e: e.memset(ones_rms[:], 1.0 / 1024.0), [], [ones_rms])
        P.op("pool", lambda e: e.memset(ones_ln[:], 1.0 / 512.0), [], [ones_ln])
        P.op("pool", lambda e: e.memset(ones_row[:], 1.0), [], [ones_row])
        P.op("pool", lambda e: e.memset(eps_t[:], EPS), [], [eps_t])
        P.op("pool", lambda e: e.memset(zaH[:], 0.0), [], [zaH])
        P.op("pool", lambda e: e.memset(zbH[:], 0.0), [], [zbH])
        P.op("pool", lambda e: e.memset(gdH[:], 0.0), [], [gdH])
        ii = av(0, [128, 16], I32)
        P.op("pool", lambda e: e.iota(ii[:], pattern=[[1, 16]], base=1, channel_multiplier=0), [], [ii])
        vcopy(invc[:], ii[:], [ii], [invc])
        P.op("dve", lambda e: e.reciprocal(out=invc[:], in_=invc[:]), [invc], [invc])

        sel = sbt("sel", [NS, NS, 128], BF16)
        maskM = sbt("maskM", [128, NS, NS], BF16)
        P.op("pool", lambda e: e.memset(sel[:], 1.0), [], [sel])
        P.op("pool", lambda e: e.affine_select(out=sel[:], in_=sel[:], pattern=[[-1, NS], [0, 128]], compare_op=ALU.is_equal,
                                               fill=0.0, base=0, channel_multiplier=1), [sel], [sel])
        P.op("pool", lambda e: e.memset(maskM[:], 1.0), [], [maskM])
        P.op("pool", lambda e: e.affine_select(out=maskM[:], in_=maskM[:], pattern=[[1, NS], [-1, NS]], compare_op=ALU.is_equal,
                                               fill=0.0, base=0, channel_multiplier=0), [maskM], [maskM])
        R1 = av(512, [128, 128], F32)
        R2 = av(1024, [128, 128], F32)
        P.op("pool", lambda e: e.memset(R1[:], 0.0), [], [R1])
        P.op("pool", lambda e: e.memset(R2[:], 0.0), [], [R2])

        def rows(dst, r0, src, n):
            P.dma("sp", dst[r0:r0 + n, :], src.rearrange("(c p) -> c p", p=128), writes=[dst])
        for l in range(2):
            rows(R1, G_MIX + 8 * l, norm_mix[l], 8)
            rows(R1, G_X + 8 * l, norm_x[l], 8)
            rows(R1, G_FFN + 8 * l, norm_ffn[l], 8)
        rows(R1, G_FIN, norm_final, 8)
        rows(R1, C_AB, conv_a_b[0], 4); rows(R1, C_LAG, ln_a_g[0], 4); rows(R1, C_LAB, ln_a_b[0], 4)
        rows(R1, C_PBS, pool_b_scale[0], 4); rows(R1, C_LCG, ln_c_g[0], 4); rows(R1, C_LCB, ln_c_b[0], 4)
        for k in range(3):
            rows(R1, C_DW + 4 * k, conv_d_w[0, k], 4)
        P.dma("sp", R2[0:124, :], conv_a_w[0].rearrange("k (c p) -> (k c) p", p=128), writes=[R2])
        b = nb()
        tr(b[:, 0:128], R1[:], ident_f[:], [R1, ident_f], [b])
        vcopy(prm[:], b[:, 0:NPRM], [b], [prm])
        b = nb()
        tr(b[:, 0:128], R2[:], ident_f[:], [R2, ident_f], [b])
        vcopy(cw[:], b[:, 0:124], [b], [cw])
        P.dma("sp", gC_bc[:], ln_c_g[0].partition_broadcast(128), writes=[gC_bc])
        P.dma("sp", bC_bc[:], ln_c_b[0].partition_broadcast(128), writes=[bC_bc])
        P.dma("sp", bsF[:], bs_c[0].rearrange("h i -> (h i)").partition_broadcast(1), writes=[bsF])
        vcopy(bsH[:], bsF[:], [bsF], [bsH])
        tt(bsF[:], bsF[:], bsH[:], ALU.subtract, [bsF, bsH], [bsF])
        vcopy(bsL[:], bsF[:], [bsF], [bsL])
        if not os.environ.get("SKIPX"):
            P.dma("sp", ws00[:], ws_c[0, :, 0, 0].partition_broadcast(128), writes=[ws00], allow_slow_non_contiguous=True)
            P.dma("sp", bs0[:], bs_c[0, :, 0].partition_broadcast(128), writes=[bs0], allow_slow_non_contiguous=True)
        wsf = av(1536, [128, 4, 128], F32)
        P.dma("sp", wsf[:], ws_c[0].rearrange("h i j -> i h j"), writes=[wsf])
        P.op("pool", lambda e: e.affine_select(out=wsf[:], in_=wsf[:], pattern=[[0, 4], [-1, 128]], compare_op=ALU.is_ge,
                                               fill=0.0, base=0, channel_multiplier=1), [wsf], [wsf])
        for h in range(4):
            b = nb()
            tr(b[:, 0:128], wsf[:, h, :], ident_f[:], [wsf, ident_f], [b])
            vcopy(wsT[:, h, :], b[:, 0:128], [b], [wsT])
        P.dma("pool", poolw[:], pool_b_w[0].rearrange("g c e -> c g e"), writes=[poolw])

        memf = av(2048, [128, 2, D], F32)
        memT = av(4096, [128, 8, 256], BF16)
        stg = av(5120, [128, D], F32)
        P.dma("sp", memf[:], mem.rearrange("(c p) f -> p c f", p=128), writes=[memf])
        for kc in range(8):
            b = nb()
            for mc in range(2):
                tr(b[:, mc * 128:(mc + 1) * 128], memf[:, mc, kc * 128:(kc + 1) * 128], ident_f[:], [memf, ident_f], [b])
            act(memT[:, kc, :], b[:, 0:256], AF.Copy, [b], [memT])
        for l in range(2 if not os.environ.get("SKIPKV") else 0):
            wk = Buf(slots[0].ap[:, 0:8192].rearrange("p (k f) -> p k f", f=D), slots[0].deps)
            wv = Buf(slots[1].ap[:, 0:8192].rearrange("p (k f) -> p k f", f=D), slots[1].deps)
            P.dma("pool", wk[:], wk_x[l].rearrange("(k p) f -> p k f", p=128), writes=[wk])
            P.dma("pool", wv[:], wv_x[l].rearrange("(k p) f -> p k f", p=128), writes=[wv])
            for ec in range(8):
                b = nb()
                mm(b[:, 0:256], [(wk[:, kc, ec * 128:(ec + 1) * 128], memT[:, kc, :]) for kc in range(8)], [wk, memT], [b])
                act(KTp[l][:, ec, :], b[:, 0:256], AF.Copy, [b], [KTp[l]])
            for (w_, dst, isv) in ((wk, o_pk, False), (wv, o_pv, True)):
                for mc in range(2):
                    for hf in range(2):
                        b = nb()
                        mm(b[:, :], [(memT[:, kc, mc * 128:(mc + 1) * 128], w_[:, kc, hf * 512:(hf + 1) * 512]) for kc in range(8)],
                           [w_, memT], [b])
                        act(stg[:, hf * 512:(hf + 1) * 512], b[:, :], AF.Copy, [b], [stg])
                        if isv:
                            vcopy(Vp[l][:, mc, hf * 512:(hf + 1) * 512], b[:, :], [b], [Vp[l]])
                    P.dma("sp", dst[l, mc * 128:(mc + 1) * 128, :], stg[:], reads=[stg], is_output=True)

        def xdeps(ti):
            return [xd[k][ti] for k in range(8)]

        SQ_OFF, RS_OFF = 0, 2048

        def rmsnorm(ti, c0, w, gcol, final_dst=None):
            sq = av(SQ_OFF, [128, 8, 512], BF16)
            rstd = av(RS_OFF, [128, 512], F32)
            if os.environ.get("SQSEP"):
                sq = sq_sep
            RL = int(os.environ.get("RMS_LEVEL", "9"))
            if os.environ.get("SQ2D"):
                for k in range(8):
                    act(sq[:, k, 0:w], xT[:, k, c0:c0 + w], AF.Square, [xd[k][ti]], [sq])
            else:
                act(sq[:, :, 0:w], xT[:, :, c0:c0 + w], AF.Square, xdeps(ti), [sq])
            if RL < 2:
                return
            b = nb()
            mm(b[:, 0:w], [(ones_rms[:], sq[:, k, 0:w]) for k in range(8)], [sq, ones_rms], [b])
            if RL < 3:
                return
            act(rstd[:, 0:w], b[:, 0:w], AF.Ln, [b, eps_t], [rstd], bias=eps_t[:, 0:1], scale=1.0)
            if RL < 4:
                return
            act(rstd[:, 0:w], rstd[:, 0:w], AF.Exp, [rstd], [rstd], scale=-0.5)
            if RL < 5:
                return
            for k in range(8):
                if final_dst is None:
                    stt(hT[:, k, c0:c0 + w], xT[:, k, c0:c0 + w], prm[:, gcol + k:gcol + k + 1], rstd[:, 0:w], ALU.mult, ALU.mult,
                        [xd[k][ti], rstd, prm], [hdp[ti]])
                else:
                    stt(final_dst[:, k, 0:w], xT[:, k, c0:c0 + w], prm[:, gcol + k:gcol + k + 1], rstd[:, 0:w], ALU.mult, ALU.mult,
                        [xd[k][ti], rstd, prm], [final_dst])

        def proj(Wb, f0, ti, c0, w, src=None, srcdeps=None):
            src = hT if src is None else src
            sd = [hdp[ti]] if srcdeps is None else srcdeps
            b = nb()
            mm(b[:, 0:w], [(Wb[:, kc, f0:f0 + 128], src[:, kc, c0:c0 + w]) for kc in range(8)], [Wb] + sd, [b])
            return b

        def wview(slot, ncols):
            return Buf(slot.ap[:, 0:8 * ncols].rearrange("p (k f) -> p k f", f=ncols), slot.deps)

        def load_w(slot, src2d, c_lo, c_hi):
            n = c_hi - c_lo
            Wb = wview(slot, n)
            s = src2d.rearrange("(k p) f -> p k f", p=128)
            for k0 in range(0, 8, 2):
                P.dma("pool", Wb[:, k0:k0 + 2, :], s[:, k0:k0 + 2, c_lo:c_hi], reads=[],
                      writes=[slot.deps[k0 // 2]] + (slot_extra[slots.index(slot)] if 8 * n > 8192 else []))
            return Wb

        def add_to_x(ti, c0, w, oc, b):
            tt(xT[:, oc, c0:c0 + w], xT[:, oc, c0:c0 + w], b[:, 0:w], ALU.add, [xd[oc][ti], b], [xd[oc][ti]])

        def out_proj(Wb, yT, tiles):
            for (ti, c0, w, kind, g0) in tiles:
                for oc in range(8):
                    b = nb()
                    mm(b[:, 0:w], [(Wb[:, kc, oc * 128:(oc + 1) * 128], yT[:, kc, c0:c0 + w]) for kc in range(8)], [Wb, yT], [b])
                    add_to_x(ti, c0, w, oc, b)

        YT_OFF = 2560
        T0 = YT_OFF + 6240


        cwv = Buf(cw.ap.rearrange("p (k c) -> p c k", c=4), cw.deps)
        O_ZA, O_CA, O_ZB, O_T1, O_T2, O_CB = T0, T0 + 2176, T0 + 4224, T0 + 6336, T0 + 6848, T0 + 7360

        def ln_feat(get_src, srcbuf, w, gcol, bcol, func, get_dst, dstbuf):
            tmp1 = av(O_T1, [128, 512], F32)
            tmp2 = av(O_T2, [128, 512], F32)
            cbs = [av(O_CB + 256 * i, [128, 512], BF16) for i in range(4)]
            bm = nb()
            bq = nb()
            for c in range(4):
                cab, csq = cbs[(c % 2) * 2], cbs[(c % 2) * 2 + 1]
                act(cab[:, 0:w], get_src(c), AF.Copy, [srcbuf], [cab])
                act(csq[:, 0:w], get_src(c), AF.Square, [srcbuf], [csq])
                mm1(bm[:, 0:w], ones_ln[:], cab[:, 0:w], c == 0, c == 3, [ones_ln, cab], [bm])
                mm1(bq[:, 0:w], ones_ln[:], csq[:, 0:w], c == 0, c == 3, [ones_ln, csq], [bq])
            act(tmp1[:, 0:w], bm[:, 0:w], AF.Square, [bm], [tmp1])
            tt(tmp1[:, 0:w], bq[:, 0:w], tmp1[:, 0:w], ALU.subtract, [bq, tmp1], [tmp1])
            act(tmp1[:, 0:w], tmp1[:, 0:w], AF.Ln, [tmp1, eps_t], [tmp1], bias=eps_t[:, 0:1], scale=1.0)
            act(tmp1[:, 0:w], tmp1[:, 0:w], AF.Exp, [tmp1], [tmp1], scale=-0.5)
            act(tmp2[:, 0:w], bm[:, 0:w], AF.Copy, [bm], [tmp2])
            for c in range(4):
                tt(get_src(c), get_src(c), tmp2[:, 0:w], ALU.subtract, [srcbuf, tmp2], [srcbuf])
                tt(get_src(c), get_src(c), tmp1[:, 0:w], ALU.mult, [srcbuf, tmp1], [srcbuf])
                act(get_dst(c), get_src(c), func, [srcbuf, prm], [dstbuf], scale=prm[:, gcol + c:gcol + c + 1], bias=prm[:, bcol + c:bcol + c + 1])

        def to_tokmajor(get_src, srcbuf, nrow, dst_dram, stg_off):
            b = nb()
            for c in range(4):
                tr(b[0:nrow, c * 128:(c + 1) * 128], get_src(c), ident_f[:], [srcbuf, ident_f], [b])
            sg = av(stg_off, [32, 512], F32)
            act(sg[0:nrow, :], b[0:nrow, :], AF.Copy, [b], [sg])
            P.dma("sp", dst_dram, sg[0:nrow, :], reads=[sg], is_output=True)

        def load_hist_T(src_dram, nk, dst, dstbuf):
            per = 120 // nk if nk > 8 else 16
            per = min(per, NS)
            while NS % per:
                per -= 1
            rows = per * nk
            for j in range(NS // per):
                raw = av(O_T1, [128, 512], F32)
                P.dma("sp", raw[0:rows, :], src_dram[j * per:(j + 1) * per].rearrange("n k f -> (n k) f"), writes=[raw])
                for c in range(4):
                    b = nb()
                    tr(b[:, 0:rows], raw[0:rows, c * 128:(c + 1) * 128], ident_f[0:rows, 0:rows], [raw, ident_f], [b])
                    act(dst(c)[:, j * per:(j + 1) * per, 0:nk], b[:, 0:rows].rearrange("p (n k) -> p n k", k=nk), AF.Copy, [b], [dstbuf])

        def mixer_even(st, tiles):
            Win = load_w(slots[0], w_in_even[0], 0, 1536)
            Wout = load_w(slots[1], w_out_even[0], 0, 1024)
            yT = av(YT_OFF, [128, 8, NT], BF16)
            tmp1 = av(O_T1, [128, 512], F32)
            tmp2 = av(O_T2, [128, 512], F32)
            cbs = [av(O_CB + 256 * i, [128, 512], BF16) for i in range(4)]
            for (ti, c0, w, kind, g0) in tiles:
                rmsnorm(ti, c0, w, G_MIX + 0)
            for (ti, c0, w, kind, g0) in tiles:
                if kind == "p":
                    za = av(O_ZA, [128, 4, 542], BF16)
                    ca = av(O_CA, [128, 4, 512], F32)
                    zb = av(O_ZB, [128, 4, 527], F32)
                    vcopy(za[:, :, 0:30], zaH[:], [zaH], [za])
                    vcopy(zb[:, :, 0:15], zbH[:], [zbH], [zb])
                    zcur = lambda c: za[:, c, 30:30 + w]
                    bcur = lambda c: zb[:, c, 15:15 + w]
                else:
                    za = av(O_ZA, [128, 4, NS, 31], F32)
                    ca = av(O_CA, [128, 4, NS], F32)
                    zb = av(O_ZB, [128, 4, NS, 16], F32)
                    load_hist_T(st_a, 30, lambda c: za[:, c, :, :], za)
                    load_hist_T(st_b, 15, lambda c: zb[:, c, :, :], zb)
                    zcur = lambda c: za[:, c, :, 30]
                    bcur = lambda c: zb[:, c, :, 15]
                for c in range(4):
                    ba = proj(Win, c * 128, ti, c0, w)
                    bg = proj(Win, 512 + c * 128, ti, c0, w)
                    act(tmp1[:, 0:w], bg[:, 0:w], AF.Sigmoid, [bg], [tmp1])
                    tt(zcur(c), ba[:, 0:w], tmp1[:, 0:w], ALU.mult, [ba, tmp1], [za])
                    if kind == "p":
                        tt(zaH[:, c, :], ba[:, w - 30:w], tmp1[:, w - 30:w], ALU.mult, [ba, tmp1], [zaH])
                    bz = proj(Win, 1024 + c * 128, ti, c0, w)
                    act(bcur(c), bz[:, 0:w], AF.Copy, [bz], [zb])
                for c in range(4):
                    if kind == "p":
                        for k in range(16):
                            ts(DgA[:, k, :], ident_b[:], cwv[:, c, k:k + 1], None, ALU.mult, None, [ident_b, cw], [DgA])
                        for k in range(16, 31):
                            ts(DgB[:, k - 16, :], ident_b[:], cwv[:, c, k:k + 1], None, ALU.mult, None, [ident_b, cw], [DgB])
                        bc_ = nb(pin=True)
                        mmg(bc_[:, 0:w], [(DgA[:, k, :], za[:, c, k:k + w]) for k in range(16)], True, False, [DgA, za], [bc_])
                        mmg(bc_[:, 0:w], [(DgB[:, k - 16, :], za[:, c, k:k + w]) for k in range(16, 31)], False, True, [DgB, za], [bc_])
                        act(ca[:, c, 0:w], bc_[:, 0:w], AF.Identity, [bc_, prm], [ca], bias=prm[:, C_AB + c:C_AB + c + 1], scale=1.0)
                        unpin(bc_)
                        continue
                    cac = ca[:, c, :]
                    tap = lambda k, c=c: za[:, c, :, k]
                    ts(cac, tap(30), cwv[:, c, 30:31], prm[:, C_AB + c:C_AB + c + 1], ALU.mult, ALU.add, [za, cw, prm], [ca])
                    for k in range(30):
                        stt(cac, tap(k), cwv[:, c, k:k + 1], cac, ALU.mult, ALU.add, [za, cw, ca], [ca])
                if kind == "p":
                    ln_feat(lambda c: ca[:, c, 0:w], ca, w, C_LAG, C_LAB, AF.Silu, lambda c: yT[:, c, c0:c0 + w], yT)
                else:
                    ln_feat(lambda c: ca[:, c, :], ca, w, C_LAG, C_LAB, AF.Silu, lambda c: yT[:, c, c0:c0 + w], yT)
                for g in range(4):
                    win = 2 << g
                    pooled = cbs[g]
                    if kind == "p":
                        Sa = av(O_CA, [128, 527], F32)
                        Sb = av(O_CA + 544, [128, 527], F32)
                        tt(Sa[:, 1:527], zb[:, g, 1:527], zb[:, g, 0:526], ALU.add, [zb], [Sa])
                        cur = Sa
                        if g >= 1:
                            tt(Sb[:, 3:527], Sa[:, 3:527], Sa[:, 1:525], ALU.add, [Sa], [Sb]); cur = Sb
                        if g >= 2:
                            tt(Sa[:, 7:527], Sb[:, 7:527], Sb[:, 3:523], ALU.add, [Sb], [Sa]); cur = Sa
                        if g >= 3:
                            tt(Sb[:, 15:527], Sa[:, 15:527], Sa[:, 7:519], ALU.add, [Sa], [Sb]); cur = Sb
                        stt(pooled[:, 0:w], cur[:, 15:15 + w], 1.0 / win, zb[:, g, 15:15 + w], ALU.mult, ALU.subtract, [cur, zb], [pooled])
                        if g0 == 0:
                            nfix = win - 1
                            tt(tmp2[:, 0:nfix], cur[:, 15:15 + nfix], invc[:, 0:nfix], ALU.mult, [cur, invc], [tmp2])
                            tt(pooled[:, 0:nfix], tmp2[:, 0:nfix], zb[:, g, 15:15 + nfix], ALU.subtract, [tmp2, zb], [pooled])
                    else:
                        P.op("dve", lambda e, g=g, win=win, zb=zb, tmp2=tmp2: e.tensor_reduce(out=tmp2[:, 0:NS], in_=zb[:, g, :, 16 - win:16], axis=AX.X, op=ALU.add),
                             [zb], [tmp2])
                        stt(pooled[:, 0:w], tmp2[:, 0:NS], 1.0 / win, zb[:, g, :, 15], ALU.mult, ALU.subtract, [tmp2, zb], [pooled])
                    b = nb()
                    mm(b[:, 0:w], [(poolw[:, g, :], pooled[:, 0:w])], [poolw, pooled], [b])
                    act(yT[:, 4 + g, c0:c0 + w], b[:, 0:w], AF.Identity, [b, prm], [yT], scale=prm[:, C_PBS + g:C_PBS + g + 1])
                if kind == "p":
                    vcopy(zbH[:], zb[:, :, 512:527], [zb], [zbH])
                    if g0 + 512 == SEQ:
                        to_tokmajor(lambda c: zaH[:, c, :], zaH, 30, o_pa[:, :], O_T1)
                        to_tokmajor(lambda c: zb[:, c, 512:527], zb, 15, o_pb[:, :], O_T2)
                else:
                    P.dma("sp", o_sa[:, 0:29, :], st_a[:, 1:30, :], is_output=True)
                    P.dma("sp", o_sb[:, 0:14, :], st_b[:, 1:15, :], is_output=True)
                    to_tokmajor(lambda c: za[:, c, :, 30], za, NS, o_sa[:, 29, :], O_T1)
                    to_tokmajor(lambda c: zb[:, c, :, 15], zb, NS, o_sb[:, 14, :], O_T2)
                out_proj(Wout, yT, [(ti, c0, w, kind, g0)])

        def bf(bank):
            return bank.ap[:, :].bitcast(BF16)

        def attn(st, tiles, layer):
            Wq = load_w(slots[0], wq_x[layer], 0, 1024)
            Wo = load_w(slots[1], wo_x[layer], 0, 1024)
            qT = av(T0, [128, 8, 512], BF16)
            PT = av(T0 + 2048, [128, 8, 512], BF16)
            oT = av(T0 + 4096, [128, 8, 512], BF16)
            Pun2 = [av(T0 + 6144 + i * 1024, [128, 4, 256], F32) for i in range(2)]
            Pn = av(T0 + 8192, [128, 4, 256], BF16)
            sm2 = [av(T0 + 8192 + 512 + i * 32, [128, 16], F32) for i in range(2)]
            for (ti, c0, w, kind, g0) in tiles:
                rmsnorm(ti, c0, w, G_X + 8 * layer)

            def qproj(tile):
                (ti, c0, w, kind, g0) = tile
                for fc in range(8):
                    b = proj(Wq, fc * 128, ti, c0, w)
                    act(qT[:, fc, 0:w], b[:, 0:w], AF.Identity, [b], [qT], scale=0.0625)
            qproj(tiles[0])
            pre_scores = []
            stile = [t for t in tiles if t[3] == "s"]
            kgen = sample_k_phase(layer, stile[0][0], stile[0][1], stile[0][2]) if stile else iter(())
            for tidx, (ti, c0, w, kind, g0) in enumerate(tiles):
                nxt = tiles[tidx + 1] if tidx + 1 < len(tiles) else None
                if nxt is not None and nxt[3] == "s":
                    nxt = None
                if kind == "p":
                    def stA(sb_):
                        tk = slice(sb_ * 128, (sb_ + 1) * 128)
                        bks = [nb(pin=True), nb(pin=True)]
                        for h in range(4):
                            bk = bks[h // 2]
                            off = (h % 2) * 256
                            mm(bk[:, off:off + 256], [(qT[:, 2 * h + dc, tk], KTp[layer][:, 2 * h + dc, :]) for dc in range(2)], [qT, KTp[layer]], [bk])
                        return bks

                    def stB(sb_, bks):
                        Pun = Pun2[sb_ % 2]
                        sm = sm2[sb_ % 2]
                        for i in range(2):
                            P.op("dve", lambda e, i=i, bks=bks, sm=sm: e.tensor_reduce(out=sm[:, 2 * i:2 * i + 2], in_=bks[i][:, :].rearrange("p (h m) -> p h m", m=256),
                                                                         axis=AX.X, op=ALU.max, negate=True), [bks[i]], [sm])
                        for h in range(4):
                            bk = bks[h // 2]
                            off = (h % 2) * 256
                            act(Pun[:, h, :], bk[:, off:off + 256], AF.Exp, [bk, sm], [Pun, sm], bias=sm[:, h:h + 1], scale=1.0, accum_out=sm[:, 4 + h:5 + h])
                        unpin(bks[0])
                        unpin(bks[1])
                        P.op("dve", lambda e, sm=sm: e.reciprocal(out=sm[:, 8:12], in_=sm[:, 4:8]), [sm], [sm])
                        for h in range(4):
                            act(Pn[:, h, :], Pun[:, h, :], AF.Copy, [Pun, sm], [Pn], scale=sm[:, 8 + h:9 + h])

                    def stC(sb_):
                        tk = slice(sb_ * 128, (sb_ + 1) * 128)
                        bT = nb()
                        for h in range(4):
                            for mc in range(2):
                                j = h * 2 + mc
                                tr(bf(bT)[:, j * 128:(j + 1) * 128], Pn[:, h, mc * 128:(mc + 1) * 128], ident_b[:], [Pn, ident_b], [bT])
                        act(PT[:, :, tk], bf(bT).rearrange("p (a t) -> p a t", t=128), AF.Copy, [bT], [PT])

                    bq_ = {0: pre_scores.pop() if pre_scores else stA(0)}
                    for sb_ in range(4):
                        if sb_ + 1 < 4:
                            bq_[sb_ + 1] = stA(sb_ + 1)
                        elif nxt is not None:
                            qproj(nxt)
                            nxt = None
                        stB(sb_, bq_[sb_])
                        stC(sb_)
                        for _ in range(3):
                            next(kgen, None)
                    if tidx + 1 < len(tiles) and tiles[tidx + 1][3] == "p":
                        pre_scores.append(stA(0))
                    for h in range(4):
                        for dc in range(2):
                            e_ = 2 * h + dc
                            b = nb()
                            mm(b[:, 0:w], [(Vp[layer][:, mc, e_ * 128:(e_ + 1) * 128], PT[:, 2 * h + mc, 0:w]) for mc in range(2)], [Vp[layer], PT], [b])
                            act(oT[:, e_, 0:w], b[:, 0:w], AF.Copy, [b], [oT])
                else:
                    for _ in kgen:
                        pass
                    attn_sample(layer, oT)
                for oc in range(8):
                    b = nb()
                    mm(b[:, 0:w], [(Wo[:, kc, oc * 128:(oc + 1) * 128], oT[:, kc, 0:w]) for kc in range(8)], [Wo, oT], [b])
                    add_to_x(ti, c0, w, oc, b)

        def sample_k_phase(layer, ti, c0, w):
            Y0 = YT_OFF
            NK = 5
            Kr = [av(Y0 + i * 1024, [128, 2, D], BF16) for i in range(NK)]
            q_tm = av(Y0 + 5120, [NS, D], BF16)
            junk = av(Y0 + 5632, [128, 256], F32)
            STs = av(Y0 + 5888, [128, 2, NS, 4], F32)
            qTs = av(Y0 + 6016, [128, 8, NS], BF16)
            Wq = wview(slots[0], 1024)
            for fc in range(8):
                b = proj(Wq, fc * 128, ti, c0, w)
                act(qTs[:, fc, :], b[:, 0:w], AF.Identity, [b], [qTs], scale=0.0625)
            bq = nb()
            for fc in range(8):
                tr(bf(bq)[0:NS, fc * 128:(fc + 1) * 128], qTs[:, fc, :], ident_b[:], [qTs, ident_b], [bq])
            act(q_tm[:, :], bf(bq)[0:NS, :], AF.Copy, [bq], [q_tm])
            yield
            for n in range(NS):
                Kb = Kr[n % NK]
                P.dma("pool", Kb[:], ck[layer, n].rearrange("(c p) f -> p c f", p=128), writes=[Kb])
                qb = [nb(), nb()]
                for i in range(2):
                    mm(qb[i][:, :], [(sel[:, n, :], q_tm[:, i * 512:(i + 1) * 512])], [sel, q_tm], [qb[i]])
                for mc in range(2):
                    for h in range(4):
                        stt(junk[:, :], Kb[:, mc, h * 256:(h + 1) * 256], 1.0, qb[h // 2][:, (h % 2) * 256:(h % 2) * 256 + 256], ALU.mult, ALU.mult,
                            [Kb, qb[h // 2]], [junk, STs], accum_out=STs[:, mc, n, h:h + 1])
                yield

        def attn_sample(layer, oT):
            A0 = T0 + 2048
            NV = 4
            KV = [av(A0 + i * 1024, [128, 2, D], BF16) for i in range(NV)]
            q_tm = av(YT_OFF + 5120, [NS, D], BF16)
            STs = av(YT_OFF + 5888, [128, 2, NS, 4], F32)
            S_sm = av(T0 + 6144 + 128, [64, 256], F32)
            PTs = av(T0 + 6144 + 384, [128, 2, NS, 4], F32)
            Pm = av(T0 + 6144 + 512, [128, 2, 4, NS, NS], BF16)
            sm = av(T0 + 7680, [128, 16], F32)
            bS = nb()
            for mc in range(2):
                tr(bS[0:64, mc * 128:(mc + 1) * 128], STs[:, mc, :, :].rearrange("p n h -> p (n h)"), ident_f[:], [STs, ident_f], [bS])
            P.op("dve", lambda e: e.tensor_reduce(out=sm[0:64, 0:1], in_=bS[0:64, 0:256], axis=AX.X, op=ALU.max, negate=True), [bS], [sm])
            act(S_sm[:, :], bS[0:64, 0:256], AF.Exp, [bS, sm], [S_sm, sm], bias=sm[0:64, 0:1], scale=1.0, accum_out=sm[0:64, 4:5])
            P.op("dve", lambda e: e.reciprocal(out=sm[0:64, 8:9], in_=sm[0:64, 4:5]), [sm], [sm])
            ts(S_sm[:, :], S_sm[:, :], sm[0:64, 8:9], None, ALU.mult, None, [S_sm, sm], [S_sm])
            bS2 = nb()
            for mc in range(2):
                tr(bS2[:, mc * 64:(mc + 1) * 64], S_sm[:, mc * 128:(mc + 1) * 128], ident_f[0:64, 0:64], [S_sm, ident_f], [bS2])
            act(PTs[:, :, :, :].rearrange("p c n h -> p (c n h)"), bS2[:, 0:128], AF.Copy, [bS2], [PTs])
            for mc in range(2):
                for h in range(4):
                    for n in range(NS):
                        ts(Pm[:, mc, h, n, :], maskM[:, n, :], PTs[:, mc, n, h:h + 1], None, ALU.mult, None, [maskM, PTs], [Pm])
            bo = [nb(pin=True) for _ in range(4)]
            for n in range(NS):
                Vb = KV[n % NV]
                P.dma("pool", Vb[:], cv[layer, n].rearrange("(c p) f -> p c f", p=128), writes=[Vb])
                for h in range(4):
                    for mc in range(2):
                        mm1(bo[h][0:NS, 0:256], Pm[:, mc, h, n, :], Vb[:, mc, h * 256:(h + 1) * 256], n == 0 and mc == 0, n == NS - 1 and mc == 1,
                            [Pm, Vb], [bo[h]])
            o_tm = q_tm
            for h in range(4):
                act(o_tm[:, h * 256:(h + 1) * 256], bo[h][0:NS, 0:256], AF.Copy, [bo[h]], [o_tm])
                unpin(bo[h])
            bq2 = nb()
            for kc in range(8):
                tr(bf(bq2)[:, kc * NS:(kc + 1) * NS], o_tm[:, kc * 128:(kc + 1) * 128], ident_b[0:NS, 0:NS], [o_tm, ident_b], [bq2])
            act(oT[:, :, 0:NS], bf(bq2)[:, 0:8 * NS].rearrange("p (a t) -> p a t", t=NS), AF.Copy, [bq2], [oT])

        def ffn(st, tiles, layer):
            for (ti, c0, w, kind, g0) in tiles:
                rmsnorm(ti, c0, w, G_FFN + 8 * layer)
            actb = av(YT_OFF, [128, 12, NT], BF16)
            wg_v = w_gate[layer].rearrange("(k p) f -> p k f", p=128)
            wu_v = w_up[layer].rearrange("(k p) f -> p k f", p=128)
            cnt = 0
            pi = 0
            groups = [(0, 12), (12, 10)]
            for gi, (f0, G) in enumerate(groups):
                Wd = Buf(slots[gi].ap[:, 0:G * D].rearrange("p (j f) -> p j f", f=D), slots[gi].deps)
                wd_v = w_down[layer][f0 * 128:(f0 + G) * 128, :].rearrange("(j p) f -> p j f", p=128)
                hG = G // 2
                P.dma("pool", Wd[:, 0:hG, :], wd_v[:, 0:hG, :], writes=slots[gi].deps[0:2])
                P.dma("pool", Wd[:, hG:G, :], wd_v[:, hG:G, :], writes=slots[gi].deps[2:4] + slot_extra[gi])
                for jp in range(0, G, 4):
                    fc = f0 + jp
                    nq = min(4, G - jp)
                    weg = av(T0 + (pi % 2) * 4096, [128, 8, 512], BF16)
                    weu = av(T0 + (pi % 2) * 4096 + 2048, [128, 8, 512], BF16)
                    pi += 1
                    P.dma("pool", weg[:, :, 0:nq * 128], wg_v[:, :, fc * 128:(fc + nq) * 128], writes=[weg])
                    P.dma("pool", weu[:, :, 0:nq * 128], wu_v[:, :, fc * 128:(fc + nq) * 128], writes=[weu])
                    for q in range(nq):
                        j = jp + q
                        for (ti, c0, w, kind, g0) in tiles:
                            bg = nb()
                            mm(bg[:, 0:w], [(weg[:, kc, q * 128:(q + 1) * 128], hT[:, kc, c0:c0 + w]) for kc in range(8)], [weg, hdp[ti]], [bg])
                            bu = nb()
                            mm(bu[:, 0:w], [(weu[:, kc, q * 128:(q + 1) * 128], hT[:, kc, c0:c0 + w]) for kc in range(8)], [weu, hdp[ti]], [bu])
                            sg = av(T0 + 8192, [128, 512], F32)
                            act(sg[:, 0:w], bg[:, 0:w], AF.Silu, [bg], [sg])
                            tt(actb[:, j, c0:c0 + w], sg[:, 0:w], bu[:, 0:w], ALU.mult, [sg, bu], [actb])
                for (ti, c0, w, kind, g0) in tiles:
                    for oc in range(8):
                        b = nb()
                        mm(b[:, 0:w], [(Wd[:, j, oc * 128:(oc + 1) * 128], actb[:, j, c0:c0 + w]) for j in range(G)], [Wd, actb], [b])
                        add_to_x(ti, c0, w, oc, b)

        def mixer_odd(st, tiles):
            yT = av(YT_OFF, [128, 8, NT], BF16)
            tmp1 = av(O_T1, [128, 512], F32)
            tmp2 = av(O_T2, [128, 512], F32)
            WC = load_w(slots[0], w_in_odd[0], 0, 1024)
            WD = load_w(slots[1], w_in_odd[0], 1024, 2560)
            for (ti, c0, w, kind, g0) in tiles:
                rmsnorm(ti, c0, w, G_MIX + 8)
            for (ti, c0, w, kind, g0) in tiles:
                u_sb = av(T0, [128, 4, 512], F32)
                for c in range(4):
                    b = proj(WC, c * 128, ti, c0, w)
                    act(u_sb[:, c, 0:w], b[:, 0:w], AF.Copy, [b], [u_sb])
                if kind == "p":
                    vt = av(T0 + 2048, [128, 512], F32)
                    vb = av(T0 + 2560, [128, 4, 512], BF16)
                    sm = av(T0 + 3584, [128, 16], F32)
                    hb_ = [nb(pin=True) for _ in range(4)]

                    def vproj(sb_):
                        tk = slice(c0 + sb_ * 128, c0 + (sb_ + 1) * 128)
                        bv = nb(pin=True)
                        mm(bv[:, :], [(hT[:, kc, tk], WC[:, kc, 512:1024]) for kc in range(8)], [hdp[ti], WC], [bv])
                        return bv
                    bvs = {0: vproj(0)}
                    for sb_ in range(4):
                        if sb_ + 1 < 4:
                            bvs[sb_ + 1] = vproj(sb_ + 1)
                        bv = bvs[sb_]
                        P.op("dve", lambda e, bv=bv, sm=sm: e.bn_stats(out=sm[:, 0:6], in_=bv[:, :]), [bv], [sm])
                        P.op("dve", lambda e, sm=sm: e.bn_aggr(out=sm[:, 8:10], in_=sm[:, 0:6]), [sm], [sm])
                        act(sm[:, 10:11], sm[:, 9:10], AF.Ln, [sm, eps_t], [sm], bias=eps_t[:, 0:1], scale=1.0)
                        act(sm[:, 10:11], sm[:, 10:11], AF.Exp, [sm], [sm], scale=-0.5)
                        ts(vt[:, :], bv[:, :], sm[:, 8:9], sm[:, 10:11], ALU.subtract, ALU.mult, [bv, sm], [vt])
                        unpin(bv)
                        tt(vt[:, :], vt[:, :], gC_bc[:], ALU.mult, [vt, gC_bc], [vt])
                        tt(vt[:, :], vt[:, :], bC_bc[:], ALU.add, [vt, bC_bc], [vt])
                        if g0 + sb_ * 128 == SEQ - 128:
                            P.dma("sp", o_pc[:, :], vt[:, :], reads=[vt], is_output=True)
                        vcopy(vb[:, sb_, :], vt[:, :], [vt], [vb])
                        for h in range(4):
                            mm(hb_[h][:, sb_ * 128:(sb_ + 1) * 128],
                               [(vb[:, sb_, h * 128:(h + 1) * 128], wsT[:, h, :]),
                                (ones_row[0:1, :], bsH[0:1, h * 128:(h + 1) * 128]),
                                (ones_row[0:1, :], bsL[0:1, h * 128:(h + 1) * 128])], [vb, wsT, ones_row, bsH, bsL], [hb_[h]])
                    for h in range(4):
                        tt(yT[:, h, c0:c0 + w], u_sb[:, h, 0:w], hb_[h][:, 0:w], ALU.mult, [u_sb, hb_[h]], [yT])
                        unpin(hb_[h])
                else:
                    vs = av(O_CA, [128, 4, NS], F32)
                    for c in range(4):
                        b = proj(WC, 512 + c * 128, ti, c0, w)
                        act(vs[:, c, :], b[:, 0:w], AF.Copy, [b], [vs])
                    ln_feat(lambda c: vs[:, c, :], vs, w, C_LCG, C_LCB, AF.Identity, lambda c: vs[:, c, :], vs)
                    to_tokmajor(lambda c: vs[:, c, :], vs, NS, o_sc[:, :], O_T1)
                    for c in range(4):
                        ts(tmp2[:, 0:w], vs[:, c, :], ws00[:, c:c + 1], bs0[:, c:c + 1], ALU.mult, ALU.add, [vs, ws00, bs0], [tmp2])
                        tt(yT[:, c, c0:c0 + w], u_sb[:, c, 0:w], tmp2[:, 0:w], ALU.mult, [u_sb, tmp2], [yT])
            for (ti, c0, w, kind, g0) in tiles:
                if kind == "p":
                    gd = av(T0, [128, 4, 514], F32)
                    vcopy(gd[:, :, 0:2], gdH[:], [gdH], [gd])
                    cur = lambda c: gd[:, c, 2:2 + w]
                    tap = lambda c, k: gd[:, c, k:k + w]
                else:
                    gd = av(T0, [128, 4, NS, 3], F32)
                    load_hist_T(st_d, 2, lambda c: gd[:, c, :, :], gd)
                    cur = lambda c: gd[:, c, :, 2]
                    tap = lambda c, k: gd[:, c, :, k]
                for c in range(4):
                    bgc = proj(WD, 512 + c * 128, ti, c0, w)
                    bhd = proj(WD, 1024 + c * 128, ti, c0, w)
                    act(tmp1[:, 0:w], bgc[:, 0:w], AF.Copy, [bgc], [tmp1])
                    tt(cur(c), tmp1[:, 0:w], bhd[:, 0:w], ALU.mult, [tmp1, bhd], [gd])
                    ts(tmp2[:, 0:w], tap(c, 0), prm[:, C_DW + c:C_DW + c + 1], None, ALU.mult, None, [gd, prm], [tmp2])
                    stt(tmp2[:, 0:w], tap(c, 1), prm[:, C_DW + 4 + c:C_DW + 5 + c], tmp2[:, 0:w], ALU.mult, ALU.add, [gd, prm, tmp2], [tmp2])
                    stt(tmp2[:, 0:w], tap(c, 2), prm[:, C_DW + 8 + c:C_DW + 9 + c], tmp2[:, 0:w], ALU.mult, ALU.add, [gd, prm, tmp2], [tmp2])
                    bgb = proj(WD, c * 128, ti, c0, w)
                    tt(yT[:, 4 + c, c0:c0 + w], tmp2[:, 0:w], bgb[:, 0:w], ALU.mult, [tmp2, bgb], [yT])
                if kind == "p":
                    vcopy(gdH[:], gd[:, :, 512:514], [gd], [gdH])
                    if g0 + 512 == SEQ:
                        to_tokmajor(lambda c: gd[:, c, 512:514], gd, 2, o_pd[:, :], O_T1)
                else:
                    P.dma("sp", o_sd[:, 0:1, :], st_d[:, 1:2, :], is_output=True)
                    to_tokmajor(lambda c: gd[:, c, :, 2], gd, NS, o_sd[:, 1, :], O_T1)
            Wout = load_w(slots[0], w_out_odd[0], 0, 1024)
            out_proj(Wout, yT, tiles)

        ST_TILES = [
            [(0, 0, 512, "p", 0), (1, 512, 512, "p", 512)],
            [(0, 0, 512, "p", 1024), (1, 512, 512, "p", 1536), (2, 1024, NS, "s", 0)],
        ]
        if stage < 1:
            ST_TILES = []
        import os
        if os.environ.get("NOSAMPLE"):
            ST_TILES = [[t for t in tl if t[3] == "p"] for tl in ST_TILES]

        for st, tiles in enumerate(ST_TILES):
            for (ti, c0, w, kind, g0) in tiles:
                if kind == "p":
                    for sb_ in range(4):
                        xs = av(T0 + (sb_ % 2) * 1024, [128, D], F32)
                        P.dma("sp", xs[:], x_p[g0 + sb_ * 128:g0 + (sb_ + 1) * 128, :], writes=[xs])
                        for kq in range(2):
                            b = nb()
                            for k4 in range(4):
                                kc = kq * 4 + k4
                                tr(b[:, k4 * 128:(k4 + 1) * 128], xs[:, kc * 128:(kc + 1) * 128], ident_f[:], [xs, ident_f], [b])
                            act(xT[:, kq * 4:kq * 4 + 4, c0 + sb_ * 128:c0 + (sb_ + 1) * 128],
                                b[:, :].rearrange("p (a t) -> p a t", t=128), AF.Copy, [b], [xd[kq * 4 + k4][ti] for k4 in range(4)])
                else:
                    xs = av(T0, [NS, D], F32)
                    P.dma("sp", xs[:], x_s[:, :], writes=[xs])
                    b = nb()
                    for kc in range(8):
                        tr(b[:, kc * NS:(kc + 1) * NS], xs[:, kc * 128:(kc + 1) * 128], ident_f[0:NS, 0:NS], [xs, ident_f], [b])
                    act(xT[:, :, c0:c0 + NS], b[:, 0:8 * NS].rearrange("p (a t) -> p a t", t=NS), AF.Copy, [b], xdeps(ti))

            nlayers = 0 if stage < 2 else (1 if stage < 5 else 2)
            for layer in range(nlayers):
                base = 2 + 3 * layer
                if layer == 0:
                    mixer_even(st, tiles)
                else:
                    mixer_odd(st, tiles)
                if stage >= base + 1:
                    attn(st, tiles, layer)
                if stage >= base + 2:
                    ffn(st, tiles, layer)
            if os.environ.get("RMSONLY"):
                for (ti, c0, w, kind, g0) in tiles:
                    rmsnorm(ti, c0, w, G_FIN)
            for (ti, c0, w, kind, g0) in (tiles if not os.environ.get("NOFINAL") else []):
                yf = av(T0, [128, 8, 512], F32)
                rmsnorm(ti, c0, w, G_FIN, final_dst=yf)
                if kind == "p":
                    for sb_ in range(4):
                        ys = av(T0 + 4096 + (sb_ % 2) * 1024, [128, D], F32)
                        for kq in range(2):
                            b = nb()
                            for k4 in range(4):
                                kc = kq * 4 + k4
                                tr(b[:, k4 * 128:(k4 + 1) * 128], yf[:, kc, sb_ * 128:(sb_ + 1) * 128], ident_f[:], [yf, ident_f], [b])
                            act(ys[:, kq * 512:(kq + 1) * 512], b[:, :], AF.Copy, [b], [ys])
                        P.dma("sp", y_p[g0 + sb_ * 128:g0 + (sb_ + 1) * 128, :], ys[:], reads=[ys], is_output=True)
                else:
                    ys = av(T0 + 4096, [NS, D], F32)
                    for kq in range(2):
                        b = nb()
                        for k4 in range(4):
                            kc = kq * 4 + k4
                            tr(b[0:NS, k4 * 128:(k4 + 1) * 128], yf[:, kc, 0:NS], ident_f[:], [yf, ident_f], [b])
                        act(ys[:, kq * 512:(kq + 1) * 512], b[0:NS, :], AF.Copy, [b], [ys])
                    P.dma("sp", y_s[:, :], ys[:], reads=[ys], is_output=True)

        P.finish()
        P.emit_all()
    return nc


_OUT_ORDER = ["y_p", "y_s", "o_pa", "o_pb", "o_pc", "o_pd", "o_pk", "o_pv", "o_sa", "o_sb", "o_sc", "o_sd"]


def make_in_maps(inp):
    f = lambda a: np.ascontiguousarray(np.asarray(a, dtype=np.float32))
    maps = []
    shared = {k: f(inp[k]) for k in ["norm_mix", "norm_x", "norm_ffn", "norm_final", "w_in_even", "conv_a_w", "conv_a_b",
                                     "ln_a_g", "ln_a_b", "pool_b_w", "pool_b_scale", "w_out_even", "w_in_odd", "ln_c_g",
                                     "ln_c_b", "ws_c", "bs_c", "conv_d_w", "w_out_odd", "wq_x", "wk_x", "wv_x", "wo_x",
                                     "w_gate", "w_up", "w_down"]}
    for c in range(NCORE):
        s = slice(c * NS, (c + 1) * NS)
        m = dict(shared)
        m["x_p"] = f(inp["x_prompt"][c])
        m["x_s"] = f(inp["x_sample"][s, 0])
        m["mem"] = f(inp["mem_prompt"][c])
        m["st_a"] = f(inp["state_convA"][0, s])
        m["st_b"] = f(inp["state_poolB"][0, s])
        m["st_d"] = f(inp["state_convD"][0, s])
        m["ck"] = f(np.asarray(inp["cache_mem_k"])[:, s].reshape(2, NS, 256, D))
        m["cv"] = f(np.asarray(inp["cache_mem_v"])[:, s].reshape(2, NS, 256, D))
        maps.append(m)
    return maps


def assemble(results):
    r = results
    cat = lambda k: np.stack([np.asarray(r[c][k]) for c in range(NCORE)])
    y_prompt = cat("y_p")
    y_sample = np.concatenate([np.asarray(r[c]["y_s"]) for c in range(NCORE)], 0)[:, None, :]
    p_convA = cat("o_pa")[None]
    p_poolB = cat("o_pb")[None]
    p_chunkC = cat("o_pc").reshape(1, NCORE, 128, 4, 128)
    p_convD = cat("o_pd")[None]
    p_mem_k = np.stack([np.asarray(r[c]["o_pk"]) for c in range(NCORE)], 1).reshape(2, NCORE, 256, 4, 256)
    p_mem_v = np.stack([np.asarray(r[c]["o_pv"]) for c in range(NCORE)], 1).reshape(2, NCORE, 256, 4, 256)
    s_convA = np.concatenate([np.asarray(r[c]["o_sa"]) for c in range(NCORE)], 0)[None]
    s_poolB = np.concatenate([np.asarray(r[c]["o_sb"]) for c in range(NCORE)], 0)[None]
    s_chunkC = np.concatenate([np.asarray(r[c]["o_sc"]) for c in range(NCORE)], 0).reshape(1, NCORE * NS, 1, 4, 128)
    s_convD = np.concatenate([np.asarray(r[c]["o_sd"]) for c in range(NCORE)], 0)[None]
    outs = (y_prompt, y_sample, p_convA, p_poolB, p_chunkC, p_convD, p_mem_k, p_mem_v, s_convA, s_poolB, s_chunkC, s_convD)
    return tuple(np.ascontiguousarray(o, dtype=np.float32) for o in outs)


def kernel(**inputs):
    nc = build_program()
    in_maps = make_in_maps(inputs)
    res = run_bass_kernel_spmd(nc, in_maps, core_ids=list(range(NCORE)))
    return assemble(res.results)
```

```python
import numpy as np
import concourse.bass as bass
import concourse.mybir as mybir
from concourse.bass_utils import run_bass_kernel_spmd
from contextlib import ExitStack

F32 = mybir.dt.float32
BF16 = mybir.dt.bfloat16
I32 = mybir.dt.int32
ALU = mybir.AluOpType
AF = mybir.ActivationFunctionType
AX = mybir.AxisListType

NCORE = 8
D = 1024
SEQ = 2048
NS = 16
DFF = 2816
NFC = 22
EPS = 1e-6


class Dep:
    __slots__ = ("w", "r", "excl")

    def __init__(self, excl=False):
        self.w = None
        self.r = []
        self.excl = excl


class Buf:
    def __init__(self, ap, deps):
        self.ap = ap
        self.deps = list(deps)

    def __getitem__(self, idx):
        return self.ap[idx]


def _flat(xs):
    out = []
    for x in xs:
        if x is None:
            continue
        if isinstance(x, Dep):
            out.append(x)
        elif isinstance(x, Buf):
            out.extend(x.deps)
        else:
            out.extend(_flat(x))
    return out


class Prog:
    CE = ("pe", "act", "dve", "pool")
    ENGS = ("pe", "act", "dve", "pool", "sp")

    def __init__(self, nc, es, ring=16):
        self.nc = nc
        self.rec = {e: [] for e in self.ENGS}
        self.nops = {e: 0 for e in self.CE}
        self.semobj = {}
        for e in self.CE:
            self.semobj[("e", e)] = es.enter_context(nc.semaphore("s_" + e))
        self.ring = ring
        self.dq = ("sp", "pool")
        for q in self.dq:
            for i in range(ring):
                self.semobj[("d", q, i)] = es.enter_context(nc.semaphore("d_%s%d" % (q, i)))
        self.dcnt = {q: 0 for q in self.dq}
        self.out_tokens = []

    @staticmethod
    def _needs(reads, writes, extra=()):
        need = {}

        def add(tok):
            if tok is None:
                return
            k, v = tok
            if need.get(k, 0) < v:
                need[k] = v
        for d in reads:
            add(d.w)
        for d in writes:
            add(d.w)
            for t in d.r:
                add(t)
        for t in extra:
            add(t)
        return need

    @staticmethod
    def _commit(tok, reads, writes):
        for d in reads:
            d.r.append(tok)
        for d in writes:
            d.w = tok
            d.r = []

    def op(self, eng, fn, reads=(), writes=()):
        reads = _flat(reads)
        writes = _flat(writes)
        writes = writes + [d for d in reads if d.excl]
        reads = [d for d in reads if not d.excl]
        need = self._needs(reads, writes)
        self.nops[eng] += 1
        tok = (("c", eng), self.nops[eng])
        self.rec[eng].append({"kind": "op", "fn": fn, "need": need, "ord": self.nops[eng]})
        self._commit(tok, reads, writes)
        return tok

    def dma(self, q, out, in_, reads=(), writes=(), is_output=False, **kw):
        reads = _flat(reads)
        writes = _flat(writes)
        j = self.dcnt[q]
        self.dcnt[q] += 1
        slot = j % self.ring
        rnd = j // self.ring
        key = ("d", q, slot)
        extra = [(key, 16 * rnd)] if rnd > 0 else []
        need = self._needs(reads, writes, extra)
        tok = (key, 16 * (rnd + 1))
        self.rec[q].append({"kind": "dma", "out": out, "in": in_, "kw": kw, "need": need, "sem": key})
        self._commit(tok, reads, writes)
        if is_output:
            self.out_tokens.append(tok)
        return tok

    def finish(self):
        allt = list(self.out_tokens)
        for q in self.dq:
            n = self.dcnt[q]
            for slot in range(self.ring):
                k = (n - slot + self.ring - 1) // self.ring if n > slot else 0
                if k > 0:
                    allt.append((("d", q, slot), 16 * k))
        self.rec["sp"].append({"kind": "wait", "need": self._needs((), (), allt)})

    def emit_all(self):
        signal = {e: set() for e in self.CE}
        for E in self.ENGS:
            waited = {}
            for r in self.rec[E]:
                w = []
                for k, v in r["need"].items():
                    if k[0] == "c" and k[1] == "pe" and E == "pe":
                        continue
                    if waited.get(k, 0) < v:
                        waited[k] = v
                        w.append((k, v))
                        if k[0] == "c":
                            signal[k[1]].add(v)
                r["waits"] = w
        val = {}
        for e in self.CE:
            c = 0
            m = {}
            for o in range(1, self.nops[e] + 1):
                if o in signal[e]:
                    c += 1
                m[o] = c
            val[e] = m
        self.n_signals = {e: len(signal[e]) for e in self.CE}

        def run(E, e):
            for r in self.rec[E]:
                for k, v in r["waits"]:
                    if k[0] == "c":
                        e.wait_ge(self.semobj[("e", k[1])], val[k[1]][v])
                    else:
                        e.wait_ge(self.semobj[k], v)
                if r["kind"] == "op":
                    inst = r["fn"](e)
                    if r["ord"] in signal[E]:
                        inst.then_inc(self.semobj[("e", E)], 1)
                elif r["kind"] == "dma":
                    e.dma_start(out=r["out"], in_=r["in"], **r["kw"]).then_inc(self.semobj[r["sem"]], 16)

        with self.nc.Block() as block:
            @block.sync
            def _(e):
                run("sp", e)

            @block.tensor
            def _(e):
                run("pe", e)

            @block.scalar
            def _(e):
                run("act", e)

            @block.vector
            def _(e):
                run("dve", e)

            @block.gpsimd
            def _(e):
                run("pool", e)


G_MIX, G_X, G_FFN, G_FIN = 0, 16, 32, 48
C_AB, C_LAG, C_LAB, C_PBS, C_LCG, C_LCB, C_DW = 56, 60, 64, 68, 72, 76, 80
NPRM = 92


def build_program(stage=99):
    import os
    nc = bass.Bass("TRN2", target_bir_lowering=False)

    def din(name, shape):
        return nc.dram_tensor(name, list(shape), F32, kind="ExternalInput").ap()

    def dout(name, shape):
        return nc.dram_tensor(name, list(shape), F32, kind="ExternalOutput").ap()

    x_p = din("x_p", [SEQ, D]); x_s = din("x_s", [NS, D]); mem = din("mem", [256, D])
    st_a = din("st_a", [NS, 30, 512]); st_b = din("st_b", [NS, 15, 512]); st_d = din("st_d", [NS, 2, 512])
    ck = din("ck", [2, NS, 256, D]); cv = din("cv", [2, NS, 256, D])
    norm_mix = din("norm_mix", [2, D]); norm_x = din("norm_x", [2, D]); norm_ffn = din("norm_ffn", [2, D])
    norm_final = din("norm_final", [D])
    w_in_even = din("w_in_even", [1, D, 1536]); conv_a_w = din("conv_a_w", [1, 31, 512])
    conv_a_b = din("conv_a_b", [1, 512]); ln_a_g = din("ln_a_g", [1, 512]); ln_a_b = din("ln_a_b", [1, 512])
    pool_b_w = din("pool_b_w", [1, 4, 128, 128]); pool_b_scale = din("pool_b_scale", [1, 512])
    w_out_even = din("w_out_even", [1, D, D]); w_in_odd = din("w_in_odd", [1, D, 2560])
    ln_c_g = din("ln_c_g", [1, 512]); ln_c_b = din("ln_c_b", [1, 512]); ws_c = din("ws_c", [1, 4, 128, 128])
    bs_c = din("bs_c", [1, 4, 128]); conv_d_w = din("conv_d_w", [1, 3, 512]); w_out_odd = din("w_out_odd", [1, D, D])
    wq_x = din("wq_x", [2, D, D]); wk_x = din("wk_x", [2, D, D]); wv_x = din("wv_x", [2, D, D]); wo_x = din("wo_x", [2, D, D])
    w_gate = din("w_gate", [2, D, DFF]); w_up = din("w_up", [2, D, DFF]); w_down = din("w_down", [2, DFF, D])

    y_p = dout("y_p", [SEQ, D]); y_s = dout("y_s", [NS, D])
    o_pa = dout("o_pa", [30, 512]); o_pb = dout("o_pb", [15, 512]); o_pc = dout("o_pc", [128, 512]); o_pd = dout("o_pd", [2, 512])
    o_pk = dout("o_pk", [2, 256, D]); o_pv = dout("o_pv", [2, 256, D])
    o_sa = dout("o_sa", [NS, 30, 512]); o_sb = dout("o_sb", [NS, 15, 512]); o_sc = dout("o_sc", [NS, 512]); o_sd = dout("o_sd", [NS, 2, 512])

    with ExitStack() as es:
        P = Prog(nc, es)

        def sbt(name, shape, dt, ndeps=1):
            t = es.enter_context(nc.sbuf_tensor(name, list(shape), dt))
            return Buf(t, [Dep() for _ in range(ndeps)])

        NT = 1040
        xT = sbt("xT", [128, 8, NT], F32)
        xd = [[Dep() for _ in range(3)] for _ in range(8)]
        hT = sbt("hT", [128, 8, NT], BF16)
        hdp = [Dep() for _ in range(3)]
        KTp = [sbt("KTp%d" % l, [128, 8, 256], BF16) for l in range(2)]
        Vp = [sbt("Vp%d" % l, [128, 2, D], BF16) for l in range(2)]
        ident_f = sbt("ident_f", [128, 128], F32)
        ident_b = sbt("ident_b", [128, 128], BF16)
        ones_rms = sbt("ones_rms", [128, 128], BF16)
        ones_ln = sbt("ones_ln", [128, 128], BF16)
        ones_row = sbt("ones_row", [1, 128], BF16)
        prm = sbt("prm", [128, NPRM], F32)
        cw = sbt("cw", [128, 124], F32)
        eps_t = sbt("eps_t", [128, 1], F32)
        gC_bc = sbt("gC_bc", [128, 512], F32)
        bC_bc = sbt("bC_bc", [128, 512], F32)
        wsT = sbt("wsT", [128, 4, 128], BF16)
        poolw = sbt("poolw", [128, 4, 128], BF16)
        bsF = sbt("bsF", [1, 512], F32)
        bsH = sbt("bsH", [1, 512], BF16)
        bsL = sbt("bsL", [1, 512], BF16)
        ws00 = sbt("ws00", [128, 4], F32)
        bs0 = sbt("bs0", [128, 4], F32)
        invc = sbt("invc", [128, 16], F32)
        zaH = sbt("zaH", [128, 4, 30], F32)
        zbH = sbt("zbH", [128, 4, 15], F32)
        gdH = sbt("gdH", [128, 4, 2], F32)
        slots = [sbt("slot%d" % i, [128, 12288], BF16, ndeps=4) for i in range(2)]
        DgA = Buf(slots[1].ap[:, 8192:10240].rearrange("p (k f) -> p k f", f=128), [Dep()])
        DgB = Buf(slots[1].ap[:, 10240:12288].rearrange("p (k f) -> p k f", f=128), [Dep()])
        slot_extra = {1: [DgA, DgB], 0: []}
        AW = 17568
        arena_t = es.enter_context(nc.sbuf_tensor("arena", [128, AW], F32))
        CH = 32
        adeps = [Dep() for _ in range((AW + CH - 1) // CH)]

        def av(off, shape, dt):
            n = 1
            for s in shape[1:]:
                n *= s
            words = n if dt in (F32, I32) else (n + 1) // 2
            assert off + words <= AW, (off, words)
            ap = arena_t[0:shape[0], off:off + words]
            if dt != F32:
                ap = ap.bitcast(dt)
            if len(shape) >= 3:
                names = "abcdef"[:len(shape) - 1]
                kw = {names[i]: shape[i + 1] for i in range(1, len(names))}
                ap = ap.rearrange("p (%s) -> p %s" % (" ".join(names), " ".join(names)), **kw)
            assert off % CH == 0, off
            deps = adeps[off // CH:(off + words - 1) // CH + 1]
            return Buf(ap, deps)

        sq_sep = sbt("sq_sep", [128, 8, 512], BF16) if os.environ.get("SQSEP") else None
        banks = []
        for i in range(8):
            t = es.enter_context(nc.psum_tensor("pb%d" % i, [128, 512], F32))
            banks.append(Buf(t, [Dep(excl=True)]))
        bank_i = [0]
        pinned = set()

        def nb(pin=False):
            while (bank_i[0] % 8) in pinned:
                bank_i[0] += 1
            i = bank_i[0] % 8
            bank_i[0] += 1
            if pin:
                pinned.add(i)
            return banks[i]

        def unpin(b):
            pinned.discard(banks.index(b))

        def mm(out_ap, pairs, reads, writes):
            def fn(e, pairs=pairs, out_ap=out_ap):
                n = len(pairs)
                inst = None
                for i, (l, r) in enumerate(pairs):
                    inst = e.matmul(out_ap, lhsT=l, rhs=r, start=(i == 0), stop=(i == n - 1))
                return inst
            P.op("pe", fn, reads, writes)

        def mmg(out_ap, pairs, first, last, reads, writes):
            def fn(e, pairs=pairs, out_ap=out_ap, first=first, last=last):
                n = len(pairs)
                inst = None
                for i, (l, r) in enumerate(pairs):
                    inst = e.matmul(out_ap, lhsT=l, rhs=r, start=(first and i == 0), stop=(last and i == n - 1))
                return inst
            P.op("pe", fn, reads, writes)

        def mm1(out_ap, lhsT, rhs, start, stop, reads, writes):
            P.op("pe", lambda e: e.matmul(out_ap, lhsT=lhsT, rhs=rhs, start=start, stop=stop), reads, writes)

        def tr(out_ap, in_ap, ident_ap, reads, writes):
            P.op("pe", lambda e: e.transpose(out=out_ap, in_=in_ap, identity=ident_ap), reads, writes)

        def act(out, in_, func, reads, writes, **kw):
            P.op("act", lambda e: e.activation(out=out, in_=in_, func=func, **kw), reads, writes)

        def tt(out, in0, in1, op, reads, writes):
            P.op("dve", lambda e: e.tensor_tensor(out=out, in0=in0, in1=in1, op=op), reads, writes)

        def ts(out, in0, s1, s2, op0, op1, reads, writes):
            if op1 is None:
                P.op("dve", lambda e: e.tensor_scalar(out=out, in0=in0, scalar1=s1, scalar2=None, op0=op0), reads, writes)
            else:
                P.op("dve", lambda e: e.tensor_scalar(out=out, in0=in0, scalar1=s1, scalar2=s2, op0=op0, op1=op1), reads, writes)

        def stt(out, in0, scalar, in1, op0, op1, reads, writes, **kw):
            P.op("dve", lambda e: e.scalar_tensor_tensor(out=out, in0=in0, scalar=scalar, in1=in1, op0=op0, op1=op1, **kw), reads, writes)

        def vcopy(out, in_, reads, writes):
            P.op("dve", lambda e: e.tensor_copy(out=out, in_=in_), reads, writes)

        if stage < 0:
            tt_ = av(0, [NS, D], F32)
            P.dma("sp", tt_[:], x_s[:, :], writes=[tt_])
            P.dma("sp", y_s[:, :], tt_[:], reads=[tt_], is_output=True)
            P.finish()
            P.emit_all()
            return nc
        P.op("pool", lambda e: e.memset(ident_f[:], 1.0), [], [ident_f])
        P.op("pool", lambda e: e.affine_select(out=ident_f[:], in_=ident_f[:], pattern=[[-1, 128]], compare_op=ALU.is_equal,
                                               fill=0.0, base=0, channel_multiplier=1), [ident_f], [ident_f])
        vcopy(ident_b[:], ident_f[:], [ident_f], [ident_b])
        P.op("pool", lambda e: e.memset(ones_rms[:], 1.0 / 1024.0), [], [ones_rms])
        P.op("pool", lambda e: e.memset(ones_ln[:], 1.0 / 512.0), [], [ones_ln])
        P.op("pool", lambda e: e.memset(ones_row[:], 1.0), [], [ones_row])
        P.op("pool", lambda e: e.memset(eps_t[:], EPS), [], [eps_t])
        P.op("pool", lambda e: e.memset(zaH[:], 0.0), [], [zaH])
        P.op("pool", lambda e: e.memset(zbH[:], 0.0), [], [zbH])
        P.op("pool", lambda e: e.memset(gdH[:], 0.0), [], [gdH])
        ii = av(0, [128, 16], I32)
        P.op("pool", lambda e: e.iota(ii[:], pattern=[[1, 16]], base=1, channel_multiplier=0), [], [ii])
        vcopy(invc[:], ii[:], [ii], [invc])
        P.op("dve", lambda e: e.reciprocal(out=invc[:], in_=invc[:]), [invc], [invc])

        sel = sbt("sel", [NS, NS, 128], BF16)
        maskM = sbt("maskM", [128, NS, NS], BF16)
        P.op("pool", lambda e: e.memset(sel[:], 1.0), [], [sel])
        P.op("pool", lambda e: e.affine_select(out=sel[:], in_=sel[:], pattern=[[-1, NS], [0, 128]], compare_op=ALU.is_equal,
                                               fill=0.0, base=0, channel_multiplier=1), [sel], [sel])
        P.op("pool", lambda e: e.memset(maskM[:], 1.0), [], [maskM])
        P.op("pool", lambda e: e.affine_select(out=maskM[:], in_=maskM[:], pattern=[[1, NS], [-1, NS]], compare_op=ALU.is_equal,
                                               fill=0.0, base=0, channel_multiplier=0), [maskM], [maskM])
        R1 = av(512, [128, 128], F32)
        R2 = av(1024, [128, 128], F32)
        P.op("pool", lambda e: e.memset(R1[:], 0.0), [], [R1])
        P.op("pool", lambda e: e.memset(R2[:], 0.0), [], [R2])

        def rows(dst, r0, src, n):
            P.dma("sp", dst[r0:r0 + n, :], src.rearrange("(c p) -> c p", p=128), writes=[dst])
        for l in range(2):
            rows(R1, G_MIX + 8 * l, norm_mix[l], 8)
            rows(R1, G_X + 8 * l, norm_x[l], 8)
            rows(R1, G_FFN + 8 * l, norm_ffn[l], 8)
        rows(R1, G_FIN, norm_final, 8)
        rows(R1, C_AB, conv_a_b[0], 4); rows(R1, C_LAG, ln_a_g[0], 4); rows(R1, C_LAB, ln_a_b[0], 4)
        rows(R1, C_PBS, pool_b_scale[0], 4); rows(R1, C_LCG, ln_c_g[0], 4); rows(R1, C_LCB, ln_c_b[0], 4)
        for k in range(3):
            rows(R1, C_DW + 4 * k, conv_d_w[0, k], 4)
        P.dma("sp", R2[0:124, :], conv_a_w[0].rearrange("k (c p) -> (k c) p", p=128), writes=[R2])
        b = nb()
        tr(b[:, 0:128], R1[:], ident_f[:], [R1, ident_f], [b])
        vcopy(prm[:], b[:, 0:NPRM], [b], [prm])
        b = nb()
        tr(b[:, 0:128], R2[:], ident_f[:], [R2, ident_f], [b])
        vcopy(cw[:], b[:, 0:124], [b], [cw])
        P.dma("sp", gC_bc[:], ln_c_g[0].partition_broadcast(128), writes=[gC_bc])
        P.dma("sp", bC_bc[:], ln_c_b[0].partition_broadcast(128), writes=[bC_bc])
        P.dma("sp", bsF[:], bs_c[0].rearrange("h i -> (h i)").partition_broadcast(1), writes=[bsF])
        vcopy(bsH[:], bsF[:], [bsF], [bsH])
        tt(bsF[:], bsF[:], bsH[:], ALU.subtract, [bsF, bsH], [bsF])
        vcopy(bsL[:], bsF[:], [bsF], [bsL])
        if not os.environ.get("SKIPX"):
            P.dma("sp", ws00[:], ws_c[0, :, 0, 0].partition_broadcast(128), writes=[ws00], allow_slow_non_contiguous=True)
            P.dma("sp", bs0[:], bs_c[0, :, 0].partition_broadcast(128), writes=[bs0], allow_slow_non_contiguous=True)
        wsf = av(1536, [128, 4, 128], F32)
        P.dma("sp", wsf[:], ws_c[0].rearrange("h i j -> i h j"), writes=[wsf])
        P.op("pool", lambda e: e.affine_select(out=wsf[:], in_=wsf[:], pattern=[[0, 4], [-1, 128]], compare_op=ALU.is_ge,
                                               fill=0.0, base=0, channel_multiplier=1), [wsf], [wsf])
        for h in range(4):
            b = nb()
            tr(b[:, 0:128], wsf[:, h, :], ident_f[:], [wsf, ident_f], [b])
            vcopy(wsT[:, h, :], b[:, 0:128], [b], [wsT])
        P.dma("pool", poolw[:], pool_b_w[0].rearrange("g c e -> c g e"), writes=[poolw])

        memf = av(2048, [128, 2, D], F32)
        memT = av(4096, [128, 8, 256], BF16)
        stg = av(5120, [128, D], F32)
        P.dma("sp", memf[:], mem.rearrange("(c p) f -> p c f", p=128), writes=[memf])
        for kc in range(8):
            b = nb()
            for mc in range(2):
                tr(b[:, mc * 128:(mc + 1) * 128], memf[:, mc, kc * 128:(kc + 1) * 128], ident_f[:], [memf, ident_f], [b])
            act(memT[:, kc, :], b[:, 0:256], AF.Copy, [b], [memT])
        for l in range(2 if not os.environ.get("SKIPKV") else 0):
            wk = Buf(slots[0].ap[:, 0:8192].rearrange("p (k f) -> p k f", f=D), slots[0].deps)
            wv = Buf(slots[1].ap[:, 0:8192].rearrange("p (k f) -> p k f", f=D), slots[1].deps)
            P.dma("pool", wk[:], wk_x[l].rearrange("(k p) f -> p k f", p=128), writes=[wk])
            P.dma("pool", wv[:], wv_x[l].rearrange("(k p) f -> p k f", p=128), writes=[wv])
            for ec in range(8):
                b = nb()
                mm(b[:, 0:256], [(wk[:, kc, ec * 128:(ec + 1) * 128], memT[:, kc, :]) for kc in range(8)], [wk, memT], [b])
                act(KTp[l][:, ec, :], b[:, 0:256], AF.Copy, [b], [KTp[l]])
            for (w_, dst, isv) in ((wk, o_pk, False), (wv, o_pv, True)):
                for mc in range(2):
                    for hf in range(2):
                        b = nb()
                        mm(b[:, :], [(memT[:, kc, mc * 128:(mc + 1) * 128], w_[:, kc, hf * 512:(hf + 1) * 512]) for kc in range(8)],
                           [w_, memT], [b])
                        act(stg[:, hf * 512:(hf + 1) * 512], b[:, :], AF.Copy, [b], [stg])
                        if isv:
                            vcopy(Vp[l][:, mc, hf * 512:(hf + 1) * 512], b[:, :], [b], [Vp[l]])
                    P.dma("sp", dst[l, mc * 128:(mc + 1) * 128, :], stg[:], reads=[stg], is_output=True)

        def xdeps(ti):
            return [xd[k][ti] for k in range(8)]

        SQ_OFF, RS_OFF = 0, 2048

        def rmsnorm(ti, c0, w, gcol, final_dst=None):
            sq = av(SQ_OFF, [128, 8, 512], BF16)
            rstd = av(RS_OFF, [128, 512], F32)
            if os.environ.get("SQSEP"):
                sq = sq_sep
            RL = int(os.environ.get("RMS_LEVEL", "9"))
            if os.environ.get("SQ2D"):
                for k in range(8):
                    act(sq[:, k, 0:w], xT[:, k, c0:c0 + w], AF.Square, [xd[k][ti]], [sq])
            else:
                act(sq[:, :, 0:w], xT[:, :, c0:c0 + w], AF.Square, xdeps(ti), [sq])
            if RL < 2:
                return
            b = nb()
            mm(b[:, 0:w], [(ones_rms[:], sq[:, k, 0:w]) for k in range(8)], [sq, ones_rms], [b])
            if RL < 3:
                return
            act(rstd[:, 0:w], b[:, 0:w], AF.Ln, [b, eps_t], [rstd], bias=eps_t[:, 0:1], scale=1.0)
            if RL < 4:
                return
            act(rstd[:, 0:w], rstd[:, 0:w], AF.Exp, [rstd], [rstd], scale=-0.5)
            if RL < 5:
                return
            for k in range(8):
                if final_dst is None:
                    stt(hT[:, k, c0:c0 + w], xT[:, k, c0:c0 + w], prm[:, gcol + k:gcol + k + 1], rstd[:, 0:w], ALU.mult, ALU.mult,
                        [xd[k][ti], rstd, prm], [hdp[ti]])
                else:
                    stt(final_dst[:, k, 0:w], xT[:, k, c0:c0 + w], prm[:, gcol + k:gcol + k + 1], rstd[:, 0:w], ALU.mult, ALU.mult,
                        [xd[k][ti], rstd, prm], [final_dst])

        def proj(Wb, f0, ti, c0, w, src=None, srcdeps=None):
            src = hT if src is None else src
            sd = [hdp[ti]] if srcdeps is None else srcdeps
            b = nb()
            mm(b[:, 0:w], [(Wb[:, kc, f0:f0 + 128], src[:, kc, c0:c0 + w]) for kc in range(8)], [Wb] + sd, [b])
            return b

        def wview(slot, ncols):
            return Buf(slot.ap[:, 0:8 * ncols].rearrange("p (k f) -> p k f", f=ncols), slot.deps)

        def load_w(slot, src2d, c_lo, c_hi):
            n = c_hi - c_lo
            Wb = wview(slot, n)
            s = src2d.rearrange("(k p) f -> p k f", p=128)
            for k0 in range(0, 8, 2):
                P.dma("pool", Wb[:, k0:k0 + 2, :], s[:, k0:k0 + 2, c_lo:c_hi], reads=[],
                      writes=[slot.deps[k0 // 2]] + (slot_extra[slots.index(slot)] if 8 * n > 8192 else []))
            return Wb

        def add_to_x(ti, c0, w, oc, b):
            tt(xT[:, oc, c0:c0 + w], xT[:, oc, c0:c0 + w], b[:, 0:w], ALU.add, [xd[oc][ti], b], [xd[oc][ti]])

        def out_proj(Wb, yT, tiles):
            for (ti, c0, w, kind, g0) in tiles:
                for oc in range(8):
                    b = nb()
                    mm(b[:, 0:w], [(Wb[:, kc, oc * 128:(oc + 1) * 128], yT[:, kc, c0:c0 + w]) for kc in range(8)], [Wb, yT], [b])
                    add_to_x(ti, c0, w, oc, b)

        YT_OFF = 2560
        T0 = YT_OFF + 6240


        cwv = Buf(cw.ap.rearrange("p (k c) -> p c k", c=4), cw.deps)
        O_ZA, O_CA, O_ZB, O_T1, O_T2, O_CB = T0, T0 + 2176, T0 + 4224, T0 + 6336, T0 + 6848, T0 + 7360

        def ln_feat(get_src, srcbuf, w, gcol, bcol, func, get_dst, dstbuf):
            tmp1 = av(O_T1, [128, 512], F32)
            tmp2 = av(O_T2, [128, 512], F32)
            cbs = [av(O_CB + 256 * i, [128, 512], BF16) for i in range(4)]
            bm = nb()
            bq = nb()
            for c in range(4):
                cab, csq = cbs[(c % 2) * 2], cbs[(c % 2) * 2 + 1]
                act(cab[:, 0:w], get_src(c), AF.Copy, [srcbuf], [cab])
                act(csq[:, 0:w], get_src(c), AF.Square, [srcbuf], [csq])
                mm1(bm[:, 0:w], ones_ln[:], cab[:, 0:w], c == 0, c == 3, [ones_ln, cab], [bm])
                mm1(bq[:, 0:w], ones_ln[:], csq[:, 0:w], c == 0, c == 3, [ones_ln, csq], [bq])
            act(tmp1[:, 0:w], bm[:, 0:w], AF.Square, [bm], [tmp1])
            tt(tmp1[:, 0:w], bq[:, 0:w], tmp1[:, 0:w], ALU.subtract, [bq, tmp1], [tmp1])
            act(tmp1[:, 0:w], tmp1[:, 0:w], AF.Ln, [tmp1, eps_t], [tmp1], bias=eps_t[:, 0:1], scale=1.0)
            act(tmp1[:, 0:w], tmp1[:, 0:w], AF.Exp, [tmp1], [tmp1], scale=-0.5)
            act(tmp2[:, 0:w], bm[:, 0:w], AF.Copy, [bm], [tmp2])
            for c in range(4):
                tt(get_src(c), get_src(c), tmp2[:, 0:w], ALU.subtract, [srcbuf, tmp2], [srcbuf])
                tt(get_src(c), get_src(c), tmp1[:, 0:w], ALU.mult, [srcbuf, tmp1], [srcbuf])
                act(get_dst(c), get_src(c), func, [srcbuf, prm], [dstbuf], scale=prm[:, gcol + c:gcol + c + 1], bias=prm[:, bcol + c:bcol + c + 1])

        def to_tokmajor(get_src, srcbuf, nrow, dst_dram, stg_off):
            b = nb()
            for c in range(4):
                tr(b[0:nrow, c * 128:(c + 1) * 128], get_src(c), ident_f[:], [srcbuf, ident_f], [b])
            sg = av(stg_off, [32, 512], F32)
            act(sg[0:nrow, :], b[0:nrow, :], AF.Copy, [b], [sg])
            P.dma("sp", dst_dram, sg[0:nrow, :], reads=[sg], is_output=True)

        def load_hist_T(src_dram, nk, dst, dstbuf):
            per = 120 // nk if nk > 8 else 16
            per = min(per, NS)
            while NS % per:
                per -= 1
            rows = per * nk
            for j in range(NS // per):
                raw = av(O_T1, [128, 512], F32)
                P.dma("sp", raw[0:rows, :], src_dram[j * per:(j + 1) * per].rearrange("n k f -> (n k) f"), writes=[raw])
                for c in range(4):
                    b = nb()
                    tr(b[:, 0:rows], raw[0:rows, c * 128:(c + 1) * 128], ident_f[0:rows, 0:rows], [raw, ident_f], [b])
                    act(dst(c)[:, j * per:(j + 1) * per, 0:nk], b[:, 0:rows].rearrange("p (n k) -> p n k", k=nk), AF.Copy, [b], [dstbuf])

        def mixer_even(st, tiles):
            Win = load_w(slots[0], w_in_even[0], 0, 1536)
            Wout = load_w(slots[1], w_out_even[0], 0, 1024)
            yT = av(YT_OFF, [128, 8, NT], BF16)
            tmp1 = av(O_T1, [128, 512], F32)
            tmp2 = av(O_T2, [128, 512], F32)
            cbs = [av(O_CB + 256 * i, [128, 512], BF16) for i in range(4)]
            for (ti, c0, w, kind, g0) in tiles:
                rmsnorm(ti, c0, w, G_MIX + 0)
            for (ti, c0, w, kind, g0) in tiles:
                if kind == "p":
                    za = av(O_ZA, [128, 4, 542], BF16)
                    ca = av(O_CA, [128, 4, 512], F32)
                    zb = av(O_ZB, [128, 4, 527], F32)
                    vcopy(za[:, :, 0:30], zaH[:], [zaH], [za])
                    vcopy(zb[:, :, 0:15], zbH[:], [zbH], [zb])
                    zcur = lambda c: za[:, c, 30:30 + w]
                    bcur = lambda c: zb[:, c, 15:15 + w]
                else:
                    za = av(O_ZA, [128, 4, NS, 31], F32)
                    ca = av(O_CA, [128, 4, NS], F32)
                    zb = av(O_ZB, [128, 4, NS, 16], F32)
                    load_hist_T(st_a, 30, lambda c: za[:, c, :, :], za)
                    load_hist_T(st_b, 15, lambda c: zb[:, c, :, :], zb)
                    zcur = lambda c: za[:, c, :, 30]
                    bcur = lambda c: zb[:, c, :, 15]
                for c in range(4):
                    ba = proj(Win, c * 128, ti, c0, w)
                    bg = proj(Win, 512 + c * 128, ti, c0, w)
                    act(tmp1[:, 0:w], bg[:, 0:w], AF.Sigmoid, [bg], [tmp1])
                    tt(zcur(c), ba[:, 0:w], tmp1[:, 0:w], ALU.mult, [ba, tmp1], [za])
                    if kind == "p":
                        tt(zaH[:, c, :], ba[:, w - 30:w], tmp1[:, w - 30:w], ALU.mult, [ba, tmp1], [zaH])
                    bz = proj(Win, 1024 + c * 128, ti, c0, w)
                    act(bcur(c), bz[:, 0:w], AF.Copy, [bz], [zb])
                for c in range(4):
                    if kind == "p":
                        for k in range(16):
                            ts(DgA[:, k, :], ident_b[:], cwv[:, c, k:k + 1], None, ALU.mult, None, [ident_b, cw], [DgA])
                        for k in range(16, 31):
                            ts(DgB[:, k - 16, :], ident_b[:], cwv[:, c, k:k + 1], None, ALU.mult, None, [ident_b, cw], [DgB])
                        bc_ = nb(pin=True)
                        mmg(bc_[:, 0:w], [(DgA[:, k, :], za[:, c, k:k + w]) for k in range(16)], True, False, [DgA, za], [bc_])
                        mmg(bc_[:, 0:w], [(DgB[:, k - 16, :], za[:, c, k:k + w]) for k in range(16, 31)], False, True, [DgB, za], [bc_])
                        act(ca[:, c, 0:w], bc_[:, 0:w], AF.Identity, [bc_, prm], [ca], bias=prm[:, C_AB + c:C_AB + c + 1], scale=1.0)
                        unpin(bc_)
                        continue
                    cac = ca[:, c, :]
                    tap = lambda k, c=c: za[:, c, :, k]
                    ts(cac, tap(30), cwv[:, c, 30:31], prm[:, C_AB + c:C_AB + c + 1], ALU.mult, ALU.add, [za, cw, prm], [ca])
                    for k in range(30):
                        stt(cac, tap(k), cwv[:, c, k:k + 1], cac, ALU.mult, ALU.add, [za, cw, ca], [ca])
                if kind == "p":
                    ln_feat(lambda c: ca[:, c, 0:w], ca, w, C_LAG, C_LAB, AF.Silu, lambda c: yT[:, c, c0:c0 + w], yT)
                else:
                    ln_feat(lambda c: ca[:, c, :], ca, w, C_LAG, C_LAB, AF.Silu, lambda c: yT[:, c, c0:c0 + w], yT)
                for g in range(4):
                    win = 2 << g
                    pooled = cbs[g]
                    if kind == "p":
                        Sa = av(O_CA, [128, 527], F32)
                        Sb = av(O_CA + 544, [128, 527], F32)
                        tt(Sa[:, 1:527], zb[:, g, 1:527], zb[:, g, 0:526], ALU.add, [zb], [Sa])
                        cur = Sa
                        if g >= 1:
                            tt(Sb[:, 3:527], Sa[:, 3:527], Sa[:, 1:525], ALU.add, [Sa], [Sb]); cur = Sb
                        if g >= 2:
                            tt(Sa[:, 7:527], Sb[:, 7:527], Sb[:, 3:523], ALU.add, [Sb], [Sa]); cur = Sa
                        if g >= 3:
                            tt(Sb[:, 15:527], Sa[:, 15:527], Sa[:, 7:519], ALU.add, [Sa], [Sb]); cur = Sb
                        stt(pooled[:, 0:w], cur[:, 15:15 + w], 1.0 / win, zb[:, g, 15:15 + w], ALU.mult, ALU.subtract, [cur, zb], [pooled])
                        if g0 == 0:
                            nfix = win - 1
                            tt(tmp2[:, 0:nfix], cur[:, 15:15 + nfix], invc[:, 0:nfix], ALU.mult, [cur, invc], [tmp2])
                            tt(pooled[:, 0:nfix], tmp2[:, 0:nfix], zb[:, g, 15:15 + nfix], ALU.subtract, [tmp2, zb], [pooled])
                    else:
                        P.op("dve", lambda e, g=g, win=win, zb=zb, tmp2=tmp2: e.tensor_reduce(out=tmp2[:, 0:NS], in_=zb[:, g, :, 16 - win:16], axis=AX.X, op=ALU.add),
                             [zb], [tmp2])
                        stt(pooled[:, 0:w], tmp2[:, 0:NS], 1.0 / win, zb[:, g, :, 15], ALU.mult, ALU.subtract, [tmp2, zb], [pooled])
                    b = nb()
                    mm(b[:, 0:w], [(poolw[:, g, :], pooled[:, 0:w])], [poolw, pooled], [b])
                    act(yT[:, 4 + g, c0:c0 + w], b[:, 0:w], AF.Identity, [b, prm], [yT], scale=prm[:, C_PBS + g:C_PBS + g + 1])
                if kind == "p":
                    vcopy(zbH[:], zb[:, :, 512:527], [zb], [zbH])
                    if g0 + 512 == SEQ:
                        to_tokmajor(lambda c: zaH[:, c, :], zaH, 30, o_pa[:, :], O_T1)
                        to_tokmajor(lambda c: zb[:, c, 512:527], zb, 15, o_pb[:, :], O_T2)
                else:
                    P.dma("sp", o_sa[:, 0:29, :], st_a[:, 1:30, :], is_output=True)
                    P.dma("sp", o_sb[:, 0:14, :], st_b[:, 1:15, :], is_output=True)
                    to_tokmajor(lambda c: za[:, c, :, 30], za, NS, o_sa[:, 29, :], O_T1)
                    to_tokmajor(lambda c: zb[:, c, :, 15], zb, NS, o_sb[:, 14, :], O_T2)
                out_proj(Wout, yT, [(ti, c0, w, kind, g0)])

        def bf(bank):
            return bank.ap[:, :].bitcast(BF16)

        def attn(st, tiles, layer):
            Wq = load_w(slots[0], wq_x[layer], 0, 1024)
            Wo = load_w(slots[1], wo_x[layer], 0, 1024)
            qT = av(T0, [128, 8, 512], BF16)
            PT = av(T0 + 2048, [128, 8, 512], BF16)
            oT = av(T0 + 4096, [128, 8, 512], BF16)
            Pun2 = [av(T0 + 6144 + i * 1024, [128, 4, 256], F32) for i in range(2)]
            Pn = av(T0 + 8192, [128, 4, 256], BF16)
            sm2 = [av(T0 + 8192 + 512 + i * 32, [128, 16], F32) for i in range(2)]
            for (ti, c0, w, kind, g0) in tiles:
                rmsnorm(ti, c0, w, G_X + 8 * layer)

            def qproj(tile):
                (ti, c0, w, kind, g0) = tile
                for fc in range(8):
                    b = proj(Wq, fc * 128, ti, c0, w)
                    act(qT[:, fc, 0:w], b[:, 0:w], AF.Identity, [b], [qT], scale=0.0625)
            qproj(tiles[0])
            pre_scores = []
            stile = [t for t in tiles if t[3] == "s"]
            kgen = sample_k_phase(layer, stile[0][0], stile[0][1], stile[0][2]) if stile else iter(())
            for tidx, (ti, c0, w, kind, g0) in enumerate(tiles):
                nxt = tiles[tidx + 1] if tidx + 1 < len(tiles) else None
                if nxt is not None and nxt[3] == "s":
                    nxt = None
                if kind == "p":
                    def stA(sb_):
                        tk = slice(sb_ * 128, (sb_ + 1) * 128)
                        bks = [nb(pin=True), nb(pin=True)]
                        for h in range(4):
                            bk = bks[h // 2]
                            off = (h % 2) * 256
                            mm(bk[:, off:off + 256], [(qT[:, 2 * h + dc, tk], KTp[layer][:, 2 * h + dc, :]) for dc in range(2)], [qT, KTp[layer]], [bk])
                        return bks

                    def stB(sb_, bks):
                        Pun = Pun2[sb_ % 2]
                        sm = sm2[sb_ % 2]
                        for i in range(2):
                            P.op("dve", lambda e, i=i, bks=bks, sm=sm: e.tensor_reduce(out=sm[:, 2 * i:2 * i + 2], in_=bks[i][:, :].rearrange("p (h m) -> p h m", m=256),
                                                                         axis=AX.X, op=ALU.max, negate=True), [bks[i]], [sm])
                        for h in range(4):
                            bk = bks[h // 2]
                            off = (h % 2) * 256
                            act(Pun[:, h, :], bk[:, off:off + 256], AF.Exp, [bk, sm], [Pun, sm], bias=sm[:, h:h + 1], scale=1.0, accum_out=sm[:, 4 + h:5 + h])
                        unpin(bks[0])
                        unpin(bks[1])
                        P.op("dve", lambda e, sm=sm: e.reciprocal(out=sm[:, 8:12], in_=sm[:, 4:8]), [sm], [sm])
                        for h in range(4):
                            if stile:
                                act(Pn[:, h, :], Pun[:, h, :], AF.Copy, [Pun, sm], [Pn], scale=sm[:, 8 + h:9 + h])
                            else:
                                ts(Pn[:, h, :], Pun[:, h, :], sm[:, 8 + h:9 + h], None, ALU.mult, None, [Pun, sm], [Pn])

                    def stC(sb_):
                        tk = slice(sb_ * 128, (sb_ + 1) * 128)
                        bT = nb()
                        for h in range(4):
                            for mc in range(2):
                                j = h * 2 + mc
                                tr(bf(bT)[:, j * 128:(j + 1) * 128], Pn[:, h, mc * 128:(mc + 1) * 128], ident_b[:], [Pn, ident_b], [bT])
                        if stile:
                            act(PT[:, :, tk], bf(bT).rearrange("p (a t) -> p a t", t=128), AF.Copy, [bT], [PT])
                        else:
                            vcopy(PT[:, :, tk], bf(bT).rearrange("p (a t) -> p a t", t=128), [bT], [PT])

                    bq_ = {0: pre_scores.pop() if pre_scores else stA(0)}
                    for sb_ in range(4):
                        if sb_ + 1 < 4:
                            bq_[sb_ + 1] = stA(sb_ + 1)
                        elif nxt is not None:
                            qproj(nxt)
                            nxt = None
                        stB(sb_, bq_[sb_])
                        stC(sb_)
                        for _ in range(3):
                            next(kgen, None)
                    if tidx + 1 < len(tiles) and tiles[tidx + 1][3] == "p":
                        pre_scores.append(stA(0))
                    for h in range(4):
                        for dc in range(2):
                            e_ = 2 * h + dc
                            b = nb()
                            mm(b[:, 0:w], [(Vp[layer][:, mc, e_ * 128:(e_ + 1) * 128], PT[:, 2 * h + mc, 0:w]) for mc in range(2)], [Vp[layer], PT], [b])
                            act(oT[:, e_, 0:w], b[:, 0:w], AF.Copy, [b], [oT])
                else:
                    for _ in kgen:
                        pass
                    attn_sample(layer, oT)
                for oc in range(8):
                    b = nb()
                    mm(b[:, 0:w], [(Wo[:, kc, oc * 128:(oc + 1) * 128], oT[:, kc, 0:w]) for kc in range(8)], [Wo, oT], [b])
                    add_to_x(ti, c0, w, oc, b)

        def sample_k_phase(layer, ti, c0, w):
            Y0 = YT_OFF
            NK = 5
            Kr = [av(Y0 + i * 1024, [128, 2, D], BF16) for i in range(NK)]
            q_tm = av(Y0 + 5120, [NS, D], BF16)
            junk = av(Y0 + 5632, [128, 256], F32)
            STs = av(Y0 + 5888, [128, 2, NS, 4], F32)
            qTs = av(Y0 + 6016, [128, 8, NS], BF16)
            Wq = wview(slots[0], 1024)
            for fc in range(8):
                b = proj(Wq, fc * 128, ti, c0, w)
                act(qTs[:, fc, :], b[:, 0:w], AF.Identity, [b], [qTs], scale=0.0625)
            bq = nb()
            for fc in range(8):
                tr(bf(bq)[0:NS, fc * 128:(fc + 1) * 128], qTs[:, fc, :], ident_b[:], [qTs, ident_b], [bq])
            act(q_tm[:, :], bf(bq)[0:NS, :], AF.Copy, [bq], [q_tm])
            yield
            for n in range(NS):
                Kb = Kr[n % NK]
                P.dma("pool", Kb[:], ck[layer, n].rearrange("(c p) f -> p c f", p=128), writes=[Kb])
                qb = [nb(), nb()]
                for i in range(2):
                    mm(qb[i][:, :], [(sel[:, n, :], q_tm[:, i * 512:(i + 1) * 512])], [sel, q_tm], [qb[i]])
                for mc in range(2):
                    for h in range(4):
                        stt(junk[:, :], Kb[:, mc, h * 256:(h + 1) * 256], 1.0, qb[h // 2][:, (h % 2) * 256:(h % 2) * 256 + 256], ALU.mult, ALU.mult,
                            [Kb, qb[h // 2]], [junk, STs], accum_out=STs[:, mc, n, h:h + 1])
                yield

        def attn_sample(layer, oT):
            A0 = T0 + 2048
            NV = 4
            KV = [av(A0 + i * 1024, [128, 2, D], BF16) for i in range(NV)]
            q_tm = av(YT_OFF + 5120, [NS, D], BF16)
            STs = av(YT_OFF + 5888, [128, 2, NS, 4], F32)
            S_sm = av(T0 + 6144 + 128, [64, 256], F32)
            PTs = av(T0 + 6144 + 384, [128, 2, NS, 4], F32)
            Pm = av(T0 + 6144 + 512, [128, 2, 4, NS, NS], BF16)
            sm = av(T0 + 7680, [128, 16], F32)
            bS = nb()
            for mc in range(2):
                tr(bS[0:64, mc * 128:(mc + 1) * 128], STs[:, mc, :, :].rearrange("p n h -> p (n h)"), ident_f[:], [STs, ident_f], [bS])
            P.op("dve", lambda e: e.tensor_reduce(out=sm[0:64, 0:1], in_=bS[0:64, 0:256], axis=AX.X, op=ALU.max, negate=True), [bS], [sm])
            act(S_sm[:, :], bS[0:64, 0:256], AF.Exp, [bS, sm], [S_sm, sm], bias=sm[0:64, 0:1], scale=1.0, accum_out=sm[0:64, 4:5])
            P.op("dve", lambda e: e.reciprocal(out=sm[0:64, 8:9], in_=sm[0:64, 4:5]), [sm], [sm])
            ts(S_sm[:, :], S_sm[:, :], sm[0:64, 8:9], None, ALU.mult, None, [S_sm, sm], [S_sm])
            bS2 = nb()
            for mc in range(2):
                tr(bS2[:, mc * 64:(mc + 1) * 64], S_sm[:, mc * 128:(mc + 1) * 128], ident_f[0:64, 0:64], [S_sm, ident_f], [bS2])
            act(PTs[:, :, :, :].rearrange("p c n h -> p (c n h)"), bS2[:, 0:128], AF.Copy, [bS2], [PTs])
            for mc in range(2):
                for h in range(4):
                    for n in range(NS):
                        ts(Pm[:, mc, h, n, :], maskM[:, n, :], PTs[:, mc, n, h:h + 1], None, ALU.mult, None, [maskM, PTs], [Pm])
            bo = [nb(pin=True) for _ in range(4)]
            for n in range(NS):
                Vb = KV[n % NV]
                P.dma("pool", Vb[:], cv[layer, n].rearrange("(c p) f -> p c f", p=128), writes=[Vb])
                for h in range(4):
                    for mc in range(2):
                        mm1(bo[h][0:NS, 0:256], Pm[:, mc, h, n, :], Vb[:, mc, h * 256:(h + 1) * 256], n == 0 and mc == 0, n == NS - 1 and mc == 1,
                            [Pm, Vb], [bo[h]])
            o_tm = q_tm
            for h in range(4):
                act(o_tm[:, h * 256:(h + 1) * 256], bo[h][0:NS, 0:256], AF.Copy, [bo[h]], [o_tm])
                unpin(bo[h])
            bq2 = nb()
            for kc in range(8):
                tr(bf(bq2)[:, kc * NS:(kc + 1) * NS], o_tm[:, kc * 128:(kc + 1) * 128], ident_b[0:NS, 0:NS], [o_tm, ident_b], [bq2])
            act(oT[:, :, 0:NS], bf(bq2)[:, 0:8 * NS].rearrange("p (a t) -> p a t", t=NS), AF.Copy, [bq2], [oT])

        def ffn(st, tiles, layer):
            for (ti, c0, w, kind, g0) in tiles:
                rmsnorm(ti, c0, w, G_FFN + 8 * layer)
            actb = av(YT_OFF, [128, 12, NT], BF16)
            wg_v = w_gate[layer].rearrange("(k p) f -> p k f", p=128)
            wu_v = w_up[layer].rearrange("(k p) f -> p k f", p=128)
            cnt = 0
            pi = 0
            groups = [(0, 12), (12, 10)]
            for gi, (f0, G) in enumerate(groups):
                Wd = Buf(slots[gi].ap[:, 0:G * D].rearrange("p (j f) -> p j f", f=D), slots[gi].deps)
                wd_v = w_down[layer][f0 * 128:(f0 + G) * 128, :].rearrange("(j p) f -> p j f", p=128)
                hG = G // 2
                P.dma("pool", Wd[:, 0:hG, :], wd_v[:, 0:hG, :], writes=slots[gi].deps[0:2])
                P.dma("pool", Wd[:, hG:G, :], wd_v[:, hG:G, :], writes=slots[gi].deps[2:4] + slot_extra[gi])
                for jp in range(0, G, 4):
                    fc = f0 + jp
                    nq = min(4, G - jp)
                    weg = av(T0 + (pi % 2) * 4096, [128, 8, 512], BF16)
                    weu = av(T0 + (pi % 2) * 4096 + 2048, [128, 8, 512], BF16)
                    pi += 1
                    P.dma("pool", weg[:, :, 0:nq * 128], wg_v[:, :, fc * 128:(fc + nq) * 128], writes=[weg])
                    P.dma("pool", weu[:, :, 0:nq * 128], wu_v[:, :, fc * 128:(fc + nq) * 128], writes=[weu])
                    for q in range(nq):
                        j = jp + q
                        for (ti, c0, w, kind, g0) in tiles:
                            bg = nb()
                            mm(bg[:, 0:w], [(weg[:, kc, q * 128:(q + 1) * 128], hT[:, kc, c0:c0 + w]) for kc in range(8)], [weg, hdp[ti]], [bg])
                            bu = nb()
                            mm(bu[:, 0:w], [(weu[:, kc, q * 128:(q + 1) * 128], hT[:, kc, c0:c0 + w]) for kc in range(8)], [weu, hdp[ti]], [bu])
                            sg = av(T0 + 8192, [128, 512], F32)
                            act(sg[:, 0:w], bg[:, 0:w], AF.Silu, [bg], [sg])
                            tt(actb[:, j, c0:c0 + w], sg[:, 0:w], bu[:, 0:w], ALU.mult, [sg, bu], [actb])
                for (ti, c0, w, kind, g0) in tiles:
                    for oc in range(8):
                        b = nb()
                        mm(b[:, 0:w], [(Wd[:, j, oc * 128:(oc + 1) * 128], actb[:, j, c0:c0 + w]) for j in range(G)], [Wd, actb], [b])
                        add_to_x(ti, c0, w, oc, b)

        def mixer_odd(st, tiles):
            yT = av(YT_OFF, [128, 8, NT], BF16)
            tmp1 = av(O_T1, [128, 512], F32)
            tmp2 = av(O_T2, [128, 512], F32)
            WC = load_w(slots[0], w_in_odd[0], 0, 1024)
            WD = load_w(slots[1], w_in_odd[0], 1024, 2560)
            for (ti, c0, w, kind, g0) in tiles:
                rmsnorm(ti, c0, w, G_MIX + 8)
            for (ti, c0, w, kind, g0) in tiles:
                u_sb = av(T0, [128, 4, 512], F32)
                for c in range(4):
                    b = proj(WC, c * 128, ti, c0, w)
                    act(u_sb[:, c, 0:w], b[:, 0:w], AF.Copy, [b], [u_sb])
                if kind == "p":
                    vt = av(T0 + 2048, [128, 512], F32)
                    vb = av(T0 + 2560, [128, 4, 512], BF16)
                    sm = av(T0 + 3584, [128, 16], F32)
                    hb_ = [nb(pin=True) for _ in range(4)]

                    def vproj(sb_):
                        tk = slice(c0 + sb_ * 128, c0 + (sb_ + 1) * 128)
                        bv = nb(pin=True)
                        mm(bv[:, :], [(hT[:, kc, tk], WC[:, kc, 512:1024]) for kc in range(8)], [hdp[ti], WC], [bv])
                        return bv
                    bvs = {0: vproj(0)}
                    for sb_ in range(4):
                        if sb_ + 1 < 4:
                            bvs[sb_ + 1] = vproj(sb_ + 1)
                        bv = bvs[sb_]
                        P.op("dve", lambda e, bv=bv, sm=sm: e.bn_stats(out=sm[:, 0:6], in_=bv[:, :]), [bv], [sm])
                        P.op("dve", lambda e, sm=sm: e.bn_aggr(out=sm[:, 8:10], in_=sm[:, 0:6]), [sm], [sm])
                        act(sm[:, 10:11], sm[:, 9:10], AF.Ln, [sm, eps_t], [sm], bias=eps_t[:, 0:1], scale=1.0)
                        act(sm[:, 10:11], sm[:, 10:11], AF.Exp, [sm], [sm], scale=-0.5)
                        ts(vt[:, :], bv[:, :], sm[:, 8:9], sm[:, 10:11], ALU.subtract, ALU.mult, [bv, sm], [vt])
                        unpin(bv)
                        tt(vt[:, :], vt[:, :], gC_bc[:], ALU.mult, [vt, gC_bc], [vt])
                        tt(vt[:, :], vt[:, :], bC_bc[:], ALU.add, [vt, bC_bc], [vt])
                        if g0 + sb_ * 128 == SEQ - 128:
                            P.dma("sp", o_pc[:, :], vt[:, :], reads=[vt], is_output=True)
                        vcopy(vb[:, sb_, :], vt[:, :], [vt], [vb])
                        for h in range(4):
                            mm(hb_[h][:, sb_ * 128:(sb_ + 1) * 128],
                               [(vb[:, sb_, h * 128:(h + 1) * 128], wsT[:, h, :]),
                                (ones_row[0:1, :], bsH[0:1, h * 128:(h + 1) * 128]),
                                (ones_row[0:1, :], bsL[0:1, h * 128:(h + 1) * 128])], [vb, wsT, ones_row, bsH, bsL], [hb_[h]])
                    for h in range(4):
                        tt(yT[:, h, c0:c0 + w], u_sb[:, h, 0:w], hb_[h][:, 0:w], ALU.mult, [u_sb, hb_[h]], [yT])
                        unpin(hb_[h])
                else:
                    vs = av(O_CA, [128, 4, NS], F32)
                    for c in range(4):
                        b = proj(WC, 512 + c * 128, ti, c0, w)
                        act(vs[:, c, :], b[:, 0:w], AF.Copy, [b], [vs])
                    ln_feat(lambda c: vs[:, c, :], vs, w, C_LCG, C_LCB, AF.Identity, lambda c: vs[:, c, :], vs)
                    to_tokmajor(lambda c: vs[:, c, :], vs, NS, o_sc[:, :], O_T1)
                    for c in range(4):
                        ts(tmp2[:, 0:w], vs[:, c, :], ws00[:, c:c + 1], bs0[:, c:c + 1], ALU.mult, ALU.add, [vs, ws00, bs0], [tmp2])
                        tt(yT[:, c, c0:c0 + w], u_sb[:, c, 0:w], tmp2[:, 0:w], ALU.mult, [u_sb, tmp2], [yT])
            for (ti, c0, w, kind, g0) in tiles:
                if kind == "p":
                    gd = av(T0, [128, 4, 514], F32)
                    vcopy(gd[:, :, 0:2], gdH[:], [gdH], [gd])
                    cur = lambda c: gd[:, c, 2:2 + w]
                    tap = lambda c, k: gd[:, c, k:k + w]
                else:
                    gd = av(T0, [128, 4, NS, 3], F32)
                    load_hist_T(st_d, 2, lambda c: gd[:, c, :, :], gd)
                    cur = lambda c: gd[:, c, :, 2]
                    tap = lambda c, k: gd[:, c, :, k]
                for c in range(4):
                    bgc = proj(WD, 512 + c * 128, ti, c0, w)
                    bhd = proj(WD, 1024 + c * 128, ti, c0, w)
                    act(tmp1[:, 0:w], bgc[:, 0:w], AF.Copy, [bgc], [tmp1])
                    tt(cur(c), tmp1[:, 0:w], bhd[:, 0:w], ALU.mult, [tmp1, bhd], [gd])
                    ts(tmp2[:, 0:w], tap(c, 0), prm[:, C_DW + c:C_DW + c + 1], None, ALU.mult, None, [gd, prm], [tmp2])
                    stt(tmp2[:, 0:w], tap(c, 1), prm[:, C_DW + 4 + c:C_DW + 5 + c], tmp2[:, 0:w], ALU.mult, ALU.add, [gd, prm, tmp2], [tmp2])
                    stt(tmp2[:, 0:w], tap(c, 2), prm[:, C_DW + 8 + c:C_DW + 9 + c], tmp2[:, 0:w], ALU.mult, ALU.add, [gd, prm, tmp2], [tmp2])
                    bgb = proj(WD, c * 128, ti, c0, w)
                    tt(yT[:, 4 + c, c0:c0 + w], tmp2[:, 0:w], bgb[:, 0:w], ALU.mult, [tmp2, bgb], [yT])
                if kind == "p":
                    vcopy(gdH[:], gd[:, :, 512:514], [gd], [gdH])
                    if g0 + 512 == SEQ:
                        to_tokmajor(lambda c: gd[:, c, 512:514], gd, 2, o_pd[:, :], O_T1)
                else:
                    P.dma("sp", o_sd[:, 0:1, :], st_d[:, 1:2, :], is_output=True)
                    to_tokmajor(lambda c: gd[:, c, :, 2], gd, NS, o_sd[:, 1, :], O_T1)
            Wout = load_w(slots[0], w_out_odd[0], 0, 1024)
            out_proj(Wout, yT, tiles)

        ST_TILES = [
            [(0, 0, 512, "p", 0), (1, 512, 512, "p", 512)],
            [(0, 0, 512, "p", 1024), (1, 512, 512, "p", 1536), (2, 1024, NS, "s", 0)],
        ]
        if stage < 1:
            ST_TILES = []
        import os
        if os.environ.get("NOSAMPLE"):
            ST_TILES = [[t for t in tl if t[3] == "p"] for tl in ST_TILES]

        for st, tiles in enumerate(ST_TILES):
            for (ti, c0, w, kind, g0) in tiles:
                if kind == "p":
                    for sb_ in range(4):
                        xs = av(T0 + (sb_ % 2) * 1024, [128, D], F32)
                        P.dma("sp", xs[:], x_p[g0 + sb_ * 128:g0 + (sb_ + 1) * 128, :], writes=[xs])
                        for kq in range(2):
                            b = nb()
                            for k4 in range(4):
                                kc = kq * 4 + k4
                                tr(b[:, k4 * 128:(k4 + 1) * 128], xs[:, kc * 128:(kc + 1) * 128], ident_f[:], [xs, ident_f], [b])
                            act(xT[:, kq * 4:kq * 4 + 4, c0 + sb_ * 128:c0 + (sb_ + 1) * 128],
                                b[:, :].rearrange("p (a t) -> p a t", t=128), AF.Copy, [b], [xd[kq * 4 + k4][ti] for k4 in range(4)])
                else:
                    xs = av(T0, [NS, D], F32)
                    P.dma("sp", xs[:], x_s[:, :], writes=[xs])
                    b = nb()
                    for kc in range(8):
                        tr(b[:, kc * NS:(kc + 1) * NS], xs[:, kc * 128:(kc + 1) * 128], ident_f[0:NS, 0:NS], [xs, ident_f], [b])
                    act(xT[:, :, c0:c0 + NS], b[:, 0:8 * NS].rearrange("p (a t) -> p a t", t=NS), AF.Copy, [b], xdeps(ti))

            nlayers = 0 if stage < 2 else (1 if stage < 5 else 2)
            for layer in range(nlayers):
                base = 2 + 3 * layer
                if layer == 0:
                    mixer_even(st, tiles)
                else:
                    mixer_odd(st, tiles)
                if stage >= base + 1:
                    attn(st, tiles, layer)
                if stage >= base + 2:
                    ffn(st, tiles, layer)
            if os.environ.get("RMSONLY"):
                for (ti, c0, w, kind, g0) in tiles:
                    rmsnorm(ti, c0, w, G_FIN)
            for (ti, c0, w, kind, g0) in (tiles if not os.environ.get("NOFINAL") else []):
                yf = av(T0, [128, 8, 512], F32)
                rmsnorm(ti, c0, w, G_FIN, final_dst=yf)
                if kind == "p":
                    for sb_ in range(4):
                        ys = av(T0 + 4096 + (sb_ % 2) * 1024, [128, D], F32)
                        for kq in range(2):
                            b = nb()
                            for k4 in range(4):
                                kc = kq * 4 + k4
                                tr(b[:, k4 * 128:(k4 + 1) * 128], yf[:, kc, sb_ * 128:(sb_ + 1) * 128], ident_f[:], [yf, ident_f], [b])
                            act(ys[:, kq * 512:(kq + 1) * 512], b[:, :], AF.Copy, [b], [ys])
                        P.dma("sp", y_p[g0 + sb_ * 128:g0 + (sb_ + 1) * 128, :], ys[:], reads=[ys], is_output=True)
                else:
                    ys = av(T0 + 4096, [NS, D], F32)
                    for kq in range(2):
                        b = nb()
                        for k4 in range(4):
                            kc = kq * 4 + k4
                            tr(b[0:NS, k4 * 128:(k4 + 1) * 128], yf[:, kc, 0:NS], ident_f[:], [yf, ident_f], [b])
                        act(ys[:, kq * 512:(kq + 1) * 512], b[0:NS, :], AF.Copy, [b], [ys])
                    P.dma("sp", y_s[:, :], ys[:], reads=[ys], is_output=True)

        P.finish()
        P.emit_all()
    return nc


_OUT_ORDER = ["y_p", "y_s", "o_pa", "o_pb", "o_pc", "o_pd", "o_pk", "o_pv", "o_sa", "o_sb", "o_sc", "o_sd"]


def make_in_maps(inp):
    f = lambda a: np.ascontiguousarray(np.asarray(a, dtype=np.float32))
    maps = []
    shared = {k: f(inp[k]) for k in ["norm_mix", "norm_x", "norm_ffn", "norm_final", "w_in_even", "conv_a_w", "conv_a_b",
                                     "ln_a_g", "ln_a_b", "pool_b_w", "pool_b_scale", "w_out_even", "w_in_odd", "ln_c_g",
                                     "ln_c_b", "ws_c", "bs_c", "conv_d_w", "w_out_odd", "wq_x", "wk_x", "wv_x", "wo_x",
                                     "w_gate", "w_up", "w_down"]}
    for c in range(NCORE):
        s = slice(c * NS, (c + 1) * NS)
        m = dict(shared)
        m["x_p"] = f(inp["x_prompt"][c])
        m["x_s"] = f(inp["x_sample"][s, 0])
        m["mem"] = f(inp["mem_prompt"][c])
        m["st_a"] = f(inp["state_convA"][0, s])
        m["st_b"] = f(inp["state_poolB"][0, s])
        m["st_d"] = f(inp["state_convD"][0, s])
        m["ck"] = f(np.asarray(inp["cache_mem_k"])[:, s].reshape(2, NS, 256, D))
        m["cv"] = f(np.asarray(inp["cache_mem_v"])[:, s].reshape(2, NS, 256, D))
        maps.append(m)
    return maps


def assemble(results):
    r = results
    cat = lambda k: np.stack([np.asarray(r[c][k]) for c in range(NCORE)])
    y_prompt = cat("y_p")
    y_sample = np.concatenate([np.asarray(r[c]["y_s"]) for c in range(NCORE)], 0)[:, None, :]
    p_convA = cat("o_pa")[None]
    p_poolB = cat("o_pb")[None]
    p_chunkC = cat("o_pc").reshape(1, NCORE, 128, 4, 128)
    p_convD = cat("o_pd")[None]
    p_mem_k = np.stack([np.asarray(r[c]["o_pk"]) for c in range(NCORE)], 1).reshape(2, NCORE, 256, 4, 256)
    p_mem_v = np.stack([np.asarray(r[c]["o_pv"]) for c in range(NCORE)], 1).reshape(2, NCORE, 256, 4, 256)
    s_convA = np.concatenate([np.asarray(r[c]["o_sa"]) for c in range(NCORE)], 0)[None]
    s_poolB = np.concatenate([np.asarray(r[c]["o_sb"]) for c in range(NCORE)], 0)[None]
    s_chunkC = np.concatenate([np.asarray(r[c]["o_sc"]) for c in range(NCORE)], 0).reshape(1, NCORE * NS, 1, 4, 128)
    s_convD = np.concatenate([np.asarray(r[c]["o_sd"]) for c in range(NCORE)], 0)[None]
    outs = (y_prompt, y_sample, p_convA, p_poolB, p_chunkC, p_convD, p_mem_k, p_mem_v, s_convA, s_poolB, s_chunkC, s_convD)
    return tuple(np.ascontiguousarray(o, dtype=np.float32) for o in outs)


def kernel(**inputs):
    nc = build_program()
    in_maps = make_in_maps(inputs)
    res = run_bass_kernel_spmd(nc, in_maps, core_ids=list(range(NCORE)))
    return assemble(res.results)
```

```python
import numpy as np
import concourse.bass as bass
import concourse.mybir as mybir
from concourse.bass_utils import run_bass_kernel_spmd
from contextlib import ExitStack

F32 = mybir.dt.float32
BF16 = mybir.dt.bfloat16
I32 = mybir.dt.int32
ALU = mybir.AluOpType
AF = mybir.ActivationFunctionType
AX = mybir.AxisListType

NCORE = 8
D = 1024
SEQ = 2048
NS = 16
DFF = 2816
NFC = 22
EPS = 1e-6


class Dep:
    __slots__ = ("w", "r", "excl")

    def __init__(self, excl=False):
        self.w = None
        self.r = []
        self.excl = excl


class Buf:
    def __init__(self, ap, deps):
        self.ap = ap
        self.deps = list(deps)

    def __getitem__(self, idx):
        return self.ap[idx]


def _flat(xs):
    out = []
    for x in xs:
        if x is None:
            continue
        if isinstance(x, Dep):
            out.append(x)
        elif isinstance(x, Buf):
            out.extend(x.deps)
        else:
            out.extend(_flat(x))
    return out


class Prog:
    CE = ("pe", "act", "dve", "pool")
    ENGS = ("pe", "act", "dve", "pool", "sp")

    def __init__(self, nc, es, ring=16):
        self.nc = nc
        self.rec = {e: [] for e in self.ENGS}
        self.nops = {e: 0 for e in self.CE}
        self.semobj = {}
        for e in self.CE:
            self.semobj[("e", e)] = es.enter_context(nc.semaphore("s_" + e))
        self.ring = ring
        self.dq = ("sp", "pool")
        for q in self.dq:
            for i in range(ring):
                self.semobj[("d", q, i)] = es.enter_context(nc.semaphore("d_%s%d" % (q, i)))
        self.dcnt = {q: 0 for q in self.dq}
        self.out_tokens = []

    @staticmethod
    def _needs(reads, writes, extra=()):
        need = {}

        def add(tok):
            if tok is None:
                return
            k, v = tok
            if need.get(k, 0) < v:
                need[k] = v
        for d in reads:
            add(d.w)
        for d in writes:
            add(d.w)
            for t in d.r:
                add(t)
        for t in extra:
            add(t)
        return need

    @staticmethod
    def _commit(tok, reads, writes):
        for d in reads:
            d.r.append(tok)
        for d in writes:
            d.w = tok
            d.r = []

    def op(self, eng, fn, reads=(), writes=()):
        reads = _flat(reads)
        writes = _flat(writes)
        writes = writes + [d for d in reads if d.excl]
        reads = [d for d in reads if not d.excl]
        need = self._needs(reads, writes)
        self.nops[eng] += 1
        tok = (("c", eng), self.nops[eng])
        self.rec[eng].append({"kind": "op", "fn": fn, "need": need, "ord": self.nops[eng]})
        self._commit(tok, reads, writes)
        return tok

    def dma(self, q, out, in_, reads=(), writes=(), is_output=False, **kw):
        reads = _flat(reads)
        writes = _flat(writes)
        j = self.dcnt[q]
        self.dcnt[q] += 1
        slot = j % self.ring
        rnd = j // self.ring
        key = ("d", q, slot)
        extra = [(key, 16 * rnd)] if rnd > 0 else []
        need = self._needs(reads, writes, extra)
        tok = (key, 16 * (rnd + 1))
        self.rec[q].append({"kind": "dma", "out": out, "in": in_, "kw": kw, "need": need, "sem": key})
        self._commit(tok, reads, writes)
        if is_output:
            self.out_tokens.append(tok)
        return tok

    def finish(self):
        allt = list(self.out_tokens)
        for q in self.dq:
            n = self.dcnt[q]
            for slot in range(self.ring):
                k = (n - slot + self.ring - 1) // self.ring if n > slot else 0
                if k > 0:
                    allt.append((("d", q, slot), 16 * k))
        self.rec["sp"].append({"kind": "wait", "need": self._needs((), (), allt)})

    def emit_all(self):
        signal = {e: set() for e in self.CE}
        for E in self.ENGS:
            waited = {}
            for r in self.rec[E]:
                w = []
                for k, v in r["need"].items():
                    if k[0] == "c" and k[1] == "pe" and E == "pe":
                        continue
                    if waited.get(k, 0) < v:
                        waited[k] = v
                        w.append((k, v))
                        if k[0] == "c":
                            signal[k[1]].add(v)
                r["waits"] = w
        val = {}
        for e in self.CE:
            c = 0
            m = {}
            for o in range(1, self.nops[e] + 1):
                if o in signal[e]:
                    c += 1
                m[o] = c
            val[e] = m
        self.n_signals = {e: len(signal[e]) for e in self.CE}

        def run(E, e):
            for r in self.rec[E]:
                for k, v in r["waits"]:
                    if k[0] == "c":
                        e.wait_ge(self.semobj[("e", k[1])], val[k[1]][v])
                    else:
                        e.wait_ge(self.semobj[k], v)
                if r["kind"] == "op":
                    inst = r["fn"](e)
                    if r["ord"] in signal[E]:
                        inst.then_inc(self.semobj[("e", E)], 1)
                elif r["kind"] == "dma":
                    e.dma_start(out=r["out"], in_=r["in"], **r["kw"]).then_inc(self.semobj[r["sem"]], 16)

        with self.nc.Block() as block:
            @block.sync
            def _(e):
                run("sp", e)

            @block.tensor
            def _(e):
                run("pe", e)

            @block.scalar
            def _(e):
                run("act", e)

            @block.vector
            def _(e):
                run("dve", e)

            @block.gpsimd
            def _(e):
                run("pool", e)


G_MIX, G_X, G_FFN, G_FIN = 0, 16, 32, 48
C_AB, C_LAG, C_LAB, C_PBS, C_LCG, C_LCB, C_DW = 56, 60, 64, 68, 72, 76, 80
NPRM = 92


def build_program(stage=99):
    import os
    nc = bass.Bass("TRN2", target_bir_lowering=False)

    def din(name, shape):
        return nc.dram_tensor(name, list(shape), F32, kind="ExternalInput").ap()

    def dout(name, shape):
        return nc.dram_tensor(name, list(shape), F32, kind="ExternalOutput").ap()

    x_p = din("x_p", [SEQ, D]); x_s = din("x_s", [NS, D]); mem = din("mem", [256, D])
    st_a = din("st_a", [NS, 30, 512]); st_b = din("st_b", [NS, 15, 512]); st_d = din("st_d", [NS, 2, 512])
    ck = din("ck", [2, NS, 256, D]); cv = din("cv", [2, NS, 256, D])
    norm_mix = din("norm_mix", [2, D]); norm_x = din("norm_x", [2, D]); norm_ffn = din("norm_ffn", [2, D])
    norm_final = din("norm_final", [D])
    w_in_even = din("w_in_even", [1, D, 1536]); conv_a_w = din("conv_a_w", [1, 31, 512])
    conv_a_b = din("conv_a_b", [1, 512]); ln_a_g = din("ln_a_g", [1, 512]); ln_a_b = din("ln_a_b", [1, 512])
    pool_b_w = din("pool_b_w", [1, 4, 128, 128]); pool_b_scale = din("pool_b_scale", [1, 512])
    w_out_even = din("w_out_even", [1, D, D]); w_in_odd = din("w_in_odd", [1, D, 2560])
    ln_c_g = din("ln_c_g", [1, 512]); ln_c_b = din("ln_c_b", [1, 512]); ws_c = din("ws_c", [1, 4, 128, 128])
    bs_c = din("bs_c", [1, 4, 128]); conv_d_w = din("conv_d_w", [1, 3, 512]); w_out_odd = din("w_out_odd", [1, D, D])
    wq_x = din("wq_x", [2, D, D]); wk_x = din("wk_x", [2, D, D]); wv_x = din("wv_x", [2, D, D]); wo_x = din("wo_x", [2, D, D])
    w_gate = din("w_gate", [2, D, DFF]); w_up = din("w_up", [2, D, DFF]); w_down = din("w_down", [2, DFF, D])

    y_p = dout("y_p", [SEQ, D]); y_s = dout("y_s", [NS, D])
    o_pa = dout("o_pa", [30, 512]); o_pb = dout("o_pb", [15, 512]); o_pc = dout("o_pc", [128, 512]); o_pd = dout("o_pd", [2, 512])
    o_pk = dout("o_pk", [2, 256, D]); o_pv = dout("o_pv", [2, 256, D])
    o_sa = dout("o_sa", [NS, 30, 512]); o_sb = dout("o_sb", [NS, 15, 512]); o_sc = dout("o_sc", [NS, 512]); o_sd = dout("o_sd", [NS, 2, 512])

    with ExitStack() as es:
        P = Prog(nc, es)

        def sbt(name, shape, dt, ndeps=1):
            t = es.enter_context(nc.sbuf_tensor(name, list(shape), dt))
            return Buf(t, [Dep() for _ in range(ndeps)])

        NT = 1040
        xT = sbt("xT", [128, 8, NT], F32)
        xd = [[Dep() for _ in range(3)] for _ in range(8)]
        hT = sbt("hT", [128, 8, NT], BF16)
        hdp = [Dep() for _ in range(3)]
        KTp = [sbt("KTp%d" % l, [128, 8, 256], BF16) for l in range(2)]
        Vp = [sbt("Vp%d" % l, [128, 2, D], BF16) for l in range(2)]
        ident_f = sbt("ident_f", [128, 128], F32)
        ident_b = sbt("ident_b", [128, 128], BF16)
        ones_rms = sbt("ones_rms", [128, 128], BF16)
        ones_ln = sbt("ones_ln", [128, 128], BF16)
        ones_row = sbt("ones_row", [1, 128], BF16)
        prm = sbt("prm", [128, NPRM], F32)
        cw = sbt("cw", [128, 124], F32)
        eps_t = sbt("eps_t", [128, 1], F32)
        gC_bc = sbt("gC_bc", [128, 512], F32)
        bC_bc = sbt("bC_bc", [128, 512], F32)
        wsT = sbt("wsT", [128, 4, 128], BF16)
        poolw = sbt("poolw", [128, 4, 128], BF16)
        bsF = sbt("bsF", [1, 512], F32)
        bsH = sbt("bsH", [1, 512], BF16)
        bsL = sbt("bsL", [1, 512], BF16)
        ws00 = sbt("ws00", [128, 4], F32)
        bs0 = sbt("bs0", [128, 4], F32)
        invc = sbt("invc", [128, 16], F32)
        zaH = sbt("zaH", [128, 4, 30], F32)
        zbH = sbt("zbH", [128, 4, 15], F32)
        gdH = sbt("gdH", [128, 4, 2], F32)
        slots = [sbt("slot%d" % i, [128, 12288], BF16, ndeps=4) for i in range(2)]
        DgA = Buf(slots[1].ap[:, 8192:10240].rearrange("p (k f) -> p k f", f=128), [Dep()])
        DgB = Buf(slots[1].ap[:, 10240:12288].rearrange("p (k f) -> p k f", f=128), [Dep()])
        slot_extra = {1: [DgA, DgB], 0: []}
        AW = 18112
        arena_t = es.enter_context(nc.sbuf_tensor("arena", [128, AW], F32))
        CH = 32
        adeps = [Dep() for _ in range((AW + CH - 1) // CH)]

        def av(off, shape, dt):
            n = 1
            for s in shape[1:]:
                n *= s
            words = n if dt in (F32, I32) else (n + 1) // 2
            assert off + words <= AW, (off, words)
            ap = arena_t[0:shape[0], off:off + words]
            if dt != F32:
                ap = ap.bitcast(dt)
            if len(shape) >= 3:
                names = "abcdef"[:len(shape) - 1]
                kw = {names[i]: shape[i + 1] for i in range(1, len(names))}
                ap = ap.rearrange("p (%s) -> p %s" % (" ".join(names), " ".join(names)), **kw)
            assert off % CH == 0, off
            deps = adeps[off // CH:(off + words - 1) // CH + 1]
            return Buf(ap, deps)

        sq_sep = sbt("sq_sep", [128, 8, 512], BF16) if os.environ.get("SQSEP") else None
        banks = []
        for i in range(8):
            t = es.enter_context(nc.psum_tensor("pb%d" % i, [128, 512], F32))
            banks.append(Buf(t, [Dep(excl=True)]))
        bank_i = [0]
        pinned = set()

        def nb(pin=False):
            while (bank_i[0] % 8) in pinned:
                bank_i[0] += 1
            i = bank_i[0] % 8
            bank_i[0] += 1
            if pin:
                pinned.add(i)
            return banks[i]

        def unpin(b):
            pinned.discard(banks.index(b))

        def mm(out_ap, pairs, reads, writes):
            def fn(e, pairs=pairs, out_ap=out_ap):
                n = len(pairs)
                inst = None
                for i, (l, r) in enumerate(pairs):
                    inst = e.matmul(out_ap, lhsT=l, rhs=r, start=(i == 0), stop=(i == n - 1))
                return inst
            P.op("pe", fn, reads, writes)

        def mmg(out_ap, pairs, first, last, reads, writes):
            def fn(e, pairs=pairs, out_ap=out_ap, first=first, last=last):
                n = len(pairs)
                inst = None
                for i, (l, r) in enumerate(pairs):
                    inst = e.matmul(out_ap, lhsT=l, rhs=r, start=(first and i == 0), stop=(last and i == n - 1))
                return inst
            P.op("pe", fn, reads, writes)

        def mm1(out_ap, lhsT, rhs, start, stop, reads, writes):
            P.op("pe", lambda e: e.matmul(out_ap, lhsT=lhsT, rhs=rhs, start=start, stop=stop), reads, writes)

        def tr(out_ap, in_ap, ident_ap, reads, writes):
            P.op("pe", lambda e: e.transpose(out=out_ap, in_=in_ap, identity=ident_ap), reads, writes)

        def act(out, in_, func, reads, writes, **kw):
            P.op("act", lambda e: e.activation(out=out, in_=in_, func=func, **kw), reads, writes)

        def tt(out, in0, in1, op, reads, writes):
            P.op("dve", lambda e: e.tensor_tensor(out=out, in0=in0, in1=in1, op=op), reads, writes)

        def ts(out, in0, s1, s2, op0, op1, reads, writes):
            if op1 is None:
                P.op("dve", lambda e: e.tensor_scalar(out=out, in0=in0, scalar1=s1, scalar2=None, op0=op0), reads, writes)
            else:
                P.op("dve", lambda e: e.tensor_scalar(out=out, in0=in0, scalar1=s1, scalar2=s2, op0=op0, op1=op1), reads, writes)

        def stt(out, in0, scalar, in1, op0, op1, reads, writes, **kw):
            P.op("dve", lambda e: e.scalar_tensor_tensor(out=out, in0=in0, scalar=scalar, in1=in1, op0=op0, op1=op1, **kw), reads, writes)

        def vcopy(out, in_, reads, writes):
            P.op("dve", lambda e: e.tensor_copy(out=out, in_=in_), reads, writes)

        if stage < 0:
            tt_ = av(0, [NS, D], F32)
            P.dma("sp", tt_[:], x_s[:, :], writes=[tt_])
            P.dma("sp", y_s[:, :], tt_[:], reads=[tt_], is_output=True)
            P.finish()
            P.emit_all()
            return nc
        P.op("pool", lambda e: e.memset(ident_f[:], 1.0), [], [ident_f])
        P.op("pool", lambda e: e.affine_select(out=ident_f[:], in_=ident_f[:], pattern=[[-1, 128]], compare_op=ALU.is_equal,
                                               fill=0.0, base=0, channel_multiplier=1), [ident_f], [ident_f])
        vcopy(ident_b[:], ident_f[:], [ident_f], [ident_b])
        P.op("pool", lambda e: e.memset(ones_rms[:], 1.0 / 1024.0), [], [ones_rms])
        P.op("pool", lambda e: e.memset(ones_ln[:], 1.0 / 512.0), [], [ones_ln])
        P.op("pool", lambda e: e.memset(ones_row[:], 1.0), [], [ones_row])
        P.op("pool", lambda e: e.memset(eps_t[:], EPS), [], [eps_t])
        P.op("pool", lambda e: e.memset(zaH[:], 0.0), [], [zaH])
        P.op("pool", lambda e: e.memset(zbH[:], 0.0), [], [zbH])
        P.op("pool", lambda e: e.memset(gdH[:], 0.0), [], [gdH])
        ii = av(0, [128, 16], I32)
        P.op("pool", lambda e: e.iota(ii[:], pattern=[[1, 16]], base=1, channel_multiplier=0), [], [ii])
        vcopy(invc[:], ii[:], [ii], [invc])
        P.op("dve", lambda e: e.reciprocal(out=invc[:], in_=invc[:]), [invc], [invc])

        sel = sbt("sel", [NS, NS, 128], BF16)
        maskM = sbt("maskM", [128, NS, NS], BF16)
        P.op("pool", lambda e: e.memset(sel[:], 1.0), [], [sel])
        P.op("pool", lambda e: e.affine_select(out=sel[:], in_=sel[:], pattern=[[-1, NS], [0, 128]], compare_op=ALU.is_equal,
                                               fill=0.0, base=0, channel_multiplier=1), [sel], [sel])
        P.op("pool", lambda e: e.memset(maskM[:], 1.0), [], [maskM])
        P.op("pool", lambda e: e.affine_select(out=maskM[:], in_=maskM[:], pattern=[[1, NS], [-1, NS]], compare_op=ALU.is_equal,
                                               fill=0.0, base=0, channel_multiplier=0), [maskM], [maskM])
        R1 = av(512, [128, 128], F32)
        R2 = av(1024, [128, 128], F32)
        P.op("pool", lambda e: e.memset(R1[:], 0.0), [], [R1])
        P.op("pool", lambda e: e.memset(R2[:], 0.0), [], [R2])

        def rows(dst, r0, src, n):
            P.dma("sp", dst[r0:r0 + n, :], src.rearrange("(c p) -> c p", p=128), writes=[dst])
        for l in range(2):
            rows(R1, G_MIX + 8 * l, norm_mix[l], 8)
            rows(R1, G_X + 8 * l, norm_x[l], 8)
            rows(R1, G_FFN + 8 * l, norm_ffn[l], 8)
        rows(R1, G_FIN, norm_final, 8)
        rows(R1, C_AB, conv_a_b[0], 4); rows(R1, C_LAG, ln_a_g[0], 4); rows(R1, C_LAB, ln_a_b[0], 4)
        rows(R1, C_PBS, pool_b_scale[0], 4); rows(R1, C_LCG, ln_c_g[0], 4); rows(R1, C_LCB, ln_c_b[0], 4)
        for k in range(3):
            rows(R1, C_DW + 4 * k, conv_d_w[0, k], 4)
        P.dma("sp", R2[0:124, :], conv_a_w[0].rearrange("k (c p) -> (k c) p", p=128), writes=[R2])
        b = nb()
        tr(b[:, 0:128], R1[:], ident_f[:], [R1, ident_f], [b])
        vcopy(prm[:], b[:, 0:NPRM], [b], [prm])
        b = nb()
        tr(b[:, 0:128], R2[:], ident_f[:], [R2, ident_f], [b])
        vcopy(cw[:], b[:, 0:124], [b], [cw])
        P.dma("sp", gC_bc[:], ln_c_g[0].partition_broadcast(128), writes=[gC_bc])
        P.dma("sp", bC_bc[:], ln_c_b[0].partition_broadcast(128), writes=[bC_bc])
        P.dma("sp", bsF[:], bs_c[0].rearrange("h i -> (h i)").partition_broadcast(1), writes=[bsF])
        vcopy(bsH[:], bsF[:], [bsF], [bsH])
        tt(bsF[:], bsF[:], bsH[:], ALU.subtract, [bsF, bsH], [bsF])
        vcopy(bsL[:], bsF[:], [bsF], [bsL])
        if not os.environ.get("SKIPX"):
            P.dma("sp", ws00[:], ws_c[0, :, 0, 0].partition_broadcast(128), writes=[ws00], allow_slow_non_contiguous=True)
            P.dma("sp", bs0[:], bs_c[0, :, 0].partition_broadcast(128), writes=[bs0], allow_slow_non_contiguous=True)
        wsf = av(1536, [128, 4, 128], F32)
        P.dma("sp", wsf[:], ws_c[0].rearrange("h i j -> i h j"), writes=[wsf])
        P.op("pool", lambda e: e.affine_select(out=wsf[:], in_=wsf[:], pattern=[[0, 4], [-1, 128]], compare_op=ALU.is_ge,
                                               fill=0.0, base=0, channel_multiplier=1), [wsf], [wsf])
        for h in range(4):
            b = nb()
            tr(b[:, 0:128], wsf[:, h, :], ident_f[:], [wsf, ident_f], [b])
            vcopy(wsT[:, h, :], b[:, 0:128], [b], [wsT])
        P.dma("pool", poolw[:], pool_b_w[0].rearrange("g c e -> c g e"), writes=[poolw])

        memf = av(2048, [128, 2, D], F32)
        memT = av(4096, [128, 8, 256], BF16)
        stg = av(5120, [128, D], F32)
        P.dma("sp", memf[:], mem.rearrange("(c p) f -> p c f", p=128), writes=[memf])
        for kc in range(8):
            b = nb()
            for mc in range(2):
                tr(b[:, mc * 128:(mc + 1) * 128], memf[:, mc, kc * 128:(kc + 1) * 128], ident_f[:], [memf, ident_f], [b])
            act(memT[:, kc, :], b[:, 0:256], AF.Copy, [b], [memT])
        for l in range(2 if not os.environ.get("SKIPKV") else 0):
            wk = Buf(slots[0].ap[:, 0:8192].rearrange("p (k f) -> p k f", f=D), slots[0].deps)
            wv = Buf(slots[1].ap[:, 0:8192].rearrange("p (k f) -> p k f", f=D), slots[1].deps)
            P.dma("pool", wk[:], wk_x[l].rearrange("(k p) f -> p k f", p=128), writes=[wk])
            P.dma("pool", wv[:], wv_x[l].rearrange("(k p) f -> p k f", p=128), writes=[wv])
            for ec in range(8):
                b = nb()
                mm(b[:, 0:256], [(wk[:, kc, ec * 128:(ec + 1) * 128], memT[:, kc, :]) for kc in range(8)], [wk, memT], [b])
                act(KTp[l][:, ec, :], b[:, 0:256], AF.Copy, [b], [KTp[l]])
            for (w_, dst, isv) in ((wk, o_pk, False), (wv, o_pv, True)):
                for mc in range(2):
                    for hf in range(2):
                        b = nb()
                        mm(b[:, :], [(memT[:, kc, mc * 128:(mc + 1) * 128], w_[:, kc, hf * 512:(hf + 1) * 512]) for kc in range(8)],
                           [w_, memT], [b])
                        act(stg[:, hf * 512:(hf + 1) * 512], b[:, :], AF.Copy, [b], [stg])
                        if isv:
                            vcopy(Vp[l][:, mc, hf * 512:(hf + 1) * 512], b[:, :], [b], [Vp[l]])
                    P.dma("sp", dst[l, mc * 128:(mc + 1) * 128, :], stg[:], reads=[stg], is_output=True)

        def xdeps(ti):
            return [xd[k][ti] for k in range(8)]

        SQ_OFF, RS_OFF = 0, 2048

        def rmsnorm(ti, c0, w, gcol, final_dst=None):
            sq = av(SQ_OFF, [128, 8, 512], BF16)
            rstd = av(RS_OFF, [128, 512], F32)
            if os.environ.get("SQSEP"):
                sq = sq_sep
            RL = int(os.environ.get("RMS_LEVEL", "9"))
            if os.environ.get("SQ2D"):
                for k in range(8):
                    act(sq[:, k, 0:w], xT[:, k, c0:c0 + w], AF.Square, [xd[k][ti]], [sq])
            else:
                act(sq[:, :, 0:w], xT[:, :, c0:c0 + w], AF.Square, xdeps(ti), [sq])
            if RL < 2:
                return
            b = nb()
            mm(b[:, 0:w], [(ones_rms[:], sq[:, k, 0:w]) for k in range(8)], [sq, ones_rms], [b])
            if RL < 3:
                return
            act(rstd[:, 0:w], b[:, 0:w], AF.Ln, [b, eps_t], [rstd], bias=eps_t[:, 0:1], scale=1.0)
            if RL < 4:
                return
            act(rstd[:, 0:w], rstd[:, 0:w], AF.Exp, [rstd], [rstd], scale=-0.5)
            if RL < 5:
                return
            for k in range(8):
                if final_dst is None:
                    stt(hT[:, k, c0:c0 + w], xT[:, k, c0:c0 + w], prm[:, gcol + k:gcol + k + 1], rstd[:, 0:w], ALU.mult, ALU.mult,
                        [xd[k][ti], rstd, prm], [hdp[ti]])
                else:
                    stt(final_dst[:, k, 0:w], xT[:, k, c0:c0 + w], prm[:, gcol + k:gcol + k + 1], rstd[:, 0:w], ALU.mult, ALU.mult,
                        [xd[k][ti], rstd, prm], [final_dst])

        def proj(Wb, f0, ti, c0, w, src=None, srcdeps=None):
            src = hT if src is None else src
            sd = [hdp[ti]] if srcdeps is None else srcdeps
            b = nb()
            mm(b[:, 0:w], [(Wb[:, kc, f0:f0 + 128], src[:, kc, c0:c0 + w]) for kc in range(8)], [Wb] + sd, [b])
            return b

        def wview(slot, ncols):
            return Buf(slot.ap[:, 0:8 * ncols].rearrange("p (k f) -> p k f", f=ncols), slot.deps)

        def load_w(slot, src2d, c_lo, c_hi):
            n = c_hi - c_lo
            Wb = wview(slot, n)
            s = src2d.rearrange("(k p) f -> p k f", p=128)
            for k0 in range(0, 8, 2):
                P.dma("pool", Wb[:, k0:k0 + 2, :], s[:, k0:k0 + 2, c_lo:c_hi], reads=[],
                      writes=[slot.deps[k0 // 2]] + (slot_extra[slots.index(slot)] if 8 * n > 8192 else []))
            return Wb

        def add_to_x(ti, c0, w, oc, b):
            tt(xT[:, oc, c0:c0 + w], xT[:, oc, c0:c0 + w], b[:, 0:w], ALU.add, [xd[oc][ti], b], [xd[oc][ti]])

        def out_proj(Wb, yT, tiles):
            for (ti, c0, w, kind, g0) in tiles:
                for oc in range(8):
                    b = nb()
                    mm(b[:, 0:w], [(Wb[:, kc, oc * 128:(oc + 1) * 128], yT[:, kc, c0:c0 + w]) for kc in range(8)], [Wb, yT], [b])
                    add_to_x(ti, c0, w, oc, b)

        YT_OFF = 2560
        T0 = YT_OFF + 6240


        cwv = Buf(cw.ap.rearrange("p (k c) -> p c k", c=4), cw.deps)
        O_ZA, O_CA, O_ZB, O_T1, O_T2, O_CB = T0, T0 + 2176, T0 + 4224, T0 + 6336, T0 + 6848, T0 + 7360

        def ln_feat(get_src, srcbuf, w, gcol, bcol, func, get_dst, dstbuf):
            tmp1 = av(O_T1, [128, 512], F32)
            tmp2 = av(O_T2, [128, 512], F32)
            cbs = [av(O_CB + 256 * i, [128, 512], BF16) for i in range(4)]
            bm = nb()
            bq = nb()
            for c in range(4):
                cab, csq = cbs[(c % 2) * 2], cbs[(c % 2) * 2 + 1]
                act(cab[:, 0:w], get_src(c), AF.Copy, [srcbuf], [cab])
                act(csq[:, 0:w], get_src(c), AF.Square, [srcbuf], [csq])
                mm1(bm[:, 0:w], ones_ln[:], cab[:, 0:w], c == 0, c == 3, [ones_ln, cab], [bm])
                mm1(bq[:, 0:w], ones_ln[:], csq[:, 0:w], c == 0, c == 3, [ones_ln, csq], [bq])
            act(tmp1[:, 0:w], bm[:, 0:w], AF.Square, [bm], [tmp1])
            tt(tmp1[:, 0:w], bq[:, 0:w], tmp1[:, 0:w], ALU.subtract, [bq, tmp1], [tmp1])
            act(tmp1[:, 0:w], tmp1[:, 0:w], AF.Ln, [tmp1, eps_t], [tmp1], bias=eps_t[:, 0:1], scale=1.0)
            act(tmp1[:, 0:w], tmp1[:, 0:w], AF.Exp, [tmp1], [tmp1], scale=-0.5)
            act(tmp2[:, 0:w], bm[:, 0:w], AF.Copy, [bm], [tmp2])
            for c in range(4):
                tt(get_src(c), get_src(c), tmp2[:, 0:w], ALU.subtract, [srcbuf, tmp2], [srcbuf])
                tt(get_src(c), get_src(c), tmp1[:, 0:w], ALU.mult, [srcbuf, tmp1], [srcbuf])
                act(get_dst(c), get_src(c), func, [srcbuf, prm], [dstbuf], scale=prm[:, gcol + c:gcol + c + 1], bias=prm[:, bcol + c:bcol + c + 1])

        def to_tokmajor(get_src, srcbuf, nrow, dst_dram, stg_off):
            b = nb()
            for c in range(4):
                tr(b[0:nrow, c * 128:(c + 1) * 128], get_src(c), ident_f[:], [srcbuf, ident_f], [b])
            sg = av(stg_off, [32, 512], F32)
            act(sg[0:nrow, :], b[0:nrow, :], AF.Copy, [b], [sg])
            P.dma("sp", dst_dram, sg[0:nrow, :], reads=[sg], is_output=True)

        def load_hist_T(src_dram, nk, dst, dstbuf):
            per = 120 // nk if nk > 8 else 16
            per = min(per, NS)
            while NS % per:
                per -= 1
            rows = per * nk
            for j in range(NS // per):
                raw = av(O_T1, [128, 512], F32)
                P.dma("sp", raw[0:rows, :], src_dram[j * per:(j + 1) * per].rearrange("n k f -> (n k) f"), writes=[raw])
                for c in range(4):
                    b = nb()
                    tr(b[:, 0:rows], raw[0:rows, c * 128:(c + 1) * 128], ident_f[0:rows, 0:rows], [raw, ident_f], [b])
                    act(dst(c)[:, j * per:(j + 1) * per, 0:nk], b[:, 0:rows].rearrange("p (n k) -> p n k", k=nk), AF.Copy, [b], [dstbuf])

        def mixer_even(st, tiles):
            Win = load_w(slots[0], w_in_even[0], 0, 1536)
            Wout = load_w(slots[1], w_out_even[0], 0, 1024)
            yT = av(YT_OFF, [128, 8, NT], BF16)
            tmp1 = av(O_T1, [128, 512], F32)
            tmp2 = av(O_T2, [128, 512], F32)
            cbs = [av(O_CB + 256 * i, [128, 512], BF16) for i in range(4)]
            for (ti, c0, w, kind, g0) in tiles:
                rmsnorm(ti, c0, w, G_MIX + 0)
            for (ti, c0, w, kind, g0) in tiles:
                if kind == "p":
                    za = av(O_ZA, [128, 4, 542], BF16)
                    ca = av(O_CA, [128, 4, 512], F32)
                    zb = av(O_ZB, [128, 4, 527], F32)
                    vcopy(za[:, :, 0:30], zaH[:], [zaH], [za])
                    vcopy(zb[:, :, 0:15], zbH[:], [zbH], [zb])
                    zcur = lambda c: za[:, c, 30:30 + w]
                    bcur = lambda c: zb[:, c, 15:15 + w]
                else:
                    za = av(O_ZA, [128, 4, NS, 31], F32)
                    ca = av(O_CA, [128, 4, NS], F32)
                    zb = av(O_ZB, [128, 4, NS, 16], F32)
                    load_hist_T(st_a, 30, lambda c: za[:, c, :, :], za)
                    load_hist_T(st_b, 15, lambda c: zb[:, c, :, :], zb)
                    zcur = lambda c: za[:, c, :, 30]
                    bcur = lambda c: zb[:, c, :, 15]
                for c in range(4):
                    ba = proj(Win, c * 128, ti, c0, w)
                    bg = proj(Win, 512 + c * 128, ti, c0, w)
                    act(tmp1[:, 0:w], bg[:, 0:w], AF.Sigmoid, [bg], [tmp1])
                    tt(zcur(c), ba[:, 0:w], tmp1[:, 0:w], ALU.mult, [ba, tmp1], [za])
                    if kind == "p":
                        tt(zaH[:, c, :], ba[:, w - 30:w], tmp1[:, w - 30:w], ALU.mult, [ba, tmp1], [zaH])
                    bz = proj(Win, 1024 + c * 128, ti, c0, w)
                    act(bcur(c), bz[:, 0:w], AF.Copy, [bz], [zb])
                for c in range(4):
                    if kind == "p":
                        for k in range(16):
                            ts(DgA[:, k, :], ident_b[:], cwv[:, c, k:k + 1], None, ALU.mult, None, [ident_b, cw], [DgA])
                        for k in range(16, 31):
                            ts(DgB[:, k - 16, :], ident_b[:], cwv[:, c, k:k + 1], None, ALU.mult, None, [ident_b, cw], [DgB])
                        bc_ = nb(pin=True)
                        mmg(bc_[:, 0:w], [(DgA[:, k, :], za[:, c, k:k + w]) for k in range(16)], True, False, [DgA, za], [bc_])
                        mmg(bc_[:, 0:w], [(DgB[:, k - 16, :], za[:, c, k:k + w]) for k in range(16, 31)], False, True, [DgB, za], [bc_])
                        act(ca[:, c, 0:w], bc_[:, 0:w], AF.Identity, [bc_, prm], [ca], bias=prm[:, C_AB + c:C_AB + c + 1], scale=1.0)
                        unpin(bc_)
                        continue
                    cac = ca[:, c, :]
                    tap = lambda k, c=c: za[:, c, :, k]
                    ts(cac, tap(30), cwv[:, c, 30:31], prm[:, C_AB + c:C_AB + c + 1], ALU.mult, ALU.add, [za, cw, prm], [ca])
                    for k in range(30):
                        stt(cac, tap(k), cwv[:, c, k:k + 1], cac, ALU.mult, ALU.add, [za, cw, ca], [ca])
                if kind == "p":
                    ln_feat(lambda c: ca[:, c, 0:w], ca, w, C_LAG, C_LAB, AF.Silu, lambda c: yT[:, c, c0:c0 + w], yT)
                else:
                    ln_feat(lambda c: ca[:, c, :], ca, w, C_LAG, C_LAB, AF.Silu, lambda c: yT[:, c, c0:c0 + w], yT)
                for g in range(4):
                    win = 2 << g
                    pooled = cbs[g]
                    if kind == "p":
                        Sa = av(O_CA, [128, 527], F32)
                        Sb = av(O_CA + 544, [128, 527], F32)
                        tt(Sa[:, 1:527], zb[:, g, 1:527], zb[:, g, 0:526], ALU.add, [zb], [Sa])
                        cur = Sa
                        if g >= 1:
                            tt(Sb[:, 3:527], Sa[:, 3:527], Sa[:, 1:525], ALU.add, [Sa], [Sb]); cur = Sb
                        if g >= 2:
                            tt(Sa[:, 7:527], Sb[:, 7:527], Sb[:, 3:523], ALU.add, [Sb], [Sa]); cur = Sa
                        if g >= 3:
                            tt(Sb[:, 15:527], Sa[:, 15:527], Sa[:, 7:519], ALU.add, [Sa], [Sb]); cur = Sb
                        stt(pooled[:, 0:w], cur[:, 15:15 + w], 1.0 / win, zb[:, g, 15:15 + w], ALU.mult, ALU.subtract, [cur, zb], [pooled])
                        if g0 == 0:
                            nfix = win - 1
                            tt(tmp2[:, 0:nfix], cur[:, 15:15 + nfix], invc[:, 0:nfix], ALU.mult, [cur, invc], [tmp2])
                            tt(pooled[:, 0:nfix], tmp2[:, 0:nfix], zb[:, g, 15:15 + nfix], ALU.subtract, [tmp2, zb], [pooled])
                    else:
                        P.op("dve", lambda e, g=g, win=win, zb=zb, tmp2=tmp2: e.tensor_reduce(out=tmp2[:, 0:NS], in_=zb[:, g, :, 16 - win:16], axis=AX.X, op=ALU.add),
                             [zb], [tmp2])
                        stt(pooled[:, 0:w], tmp2[:, 0:NS], 1.0 / win, zb[:, g, :, 15], ALU.mult, ALU.subtract, [tmp2, zb], [pooled])
                    b = nb()
                    mm(b[:, 0:w], [(poolw[:, g, :], pooled[:, 0:w])], [poolw, pooled], [b])
                    act(yT[:, 4 + g, c0:c0 + w], b[:, 0:w], AF.Identity, [b, prm], [yT], scale=prm[:, C_PBS + g:C_PBS + g + 1])
                if kind == "p":
                    vcopy(zbH[:], zb[:, :, 512:527], [zb], [zbH])
                    if g0 + 512 == SEQ:
                        to_tokmajor(lambda c: zaH[:, c, :], zaH, 30, o_pa[:, :], O_T1)
                        to_tokmajor(lambda c: zb[:, c, 512:527], zb, 15, o_pb[:, :], O_T2)
                else:
                    P.dma("sp", o_sa[:, 0:29, :], st_a[:, 1:30, :], is_output=True)
                    P.dma("sp", o_sb[:, 0:14, :], st_b[:, 1:15, :], is_output=True)
                    to_tokmajor(lambda c: za[:, c, :, 30], za, NS, o_sa[:, 29, :], O_T1)
                    to_tokmajor(lambda c: zb[:, c, :, 15], zb, NS, o_sb[:, 14, :], O_T2)
                out_proj(Wout, yT, [(ti, c0, w, kind, g0)])

        def bf(bank):
            return bank.ap[:, :].bitcast(BF16)

        def attn(st, tiles, layer):
            Wq = load_w(slots[0], wq_x[layer], 0, 1024)
            Wo = load_w(slots[1], wo_x[layer], 0, 1024)
            qT = av(T0, [128, 8, 512], BF16)
            PT = av(T0 + 2048, [128, 8, 512], BF16)
            oT = av(T0 + 4096, [128, 8, 512], BF16)
            Pun2 = [av(T0 + 6144 + i * 1024, [128, 4, 256], F32) for i in range(2)]
            Pn2 = [av(T0 + 8192 + i * 512, [128, 4, 256], BF16) for i in range(2)]
            sm2 = [av(T0 + 9216 + i * 32, [128, 16], F32) for i in range(2)]
            for (ti, c0, w, kind, g0) in tiles:
                rmsnorm(ti, c0, w, G_X + 8 * layer)

            def qproj(tile):
                (ti, c0, w, kind, g0) = tile
                for fc in range(8):
                    b = proj(Wq, fc * 128, ti, c0, w)
                    act(qT[:, fc, 0:w], b[:, 0:w], AF.Identity, [b], [qT], scale=0.0625)
            qproj(tiles[0])
            pre_scores = []
            stile = [t for t in tiles if t[3] == "s"]
            kgen = sample_k_phase(layer, stile[0][0], stile[0][1], stile[0][2]) if stile else iter(())
            for tidx, (ti, c0, w, kind, g0) in enumerate(tiles):
                nxt = tiles[tidx + 1] if tidx + 1 < len(tiles) else None
                if nxt is not None and nxt[3] == "s":
                    nxt = None
                if kind == "p":
                    def stA(sb_):
                        tk = slice(sb_ * 128, (sb_ + 1) * 128)
                        bks = [nb(pin=True), nb(pin=True)]
                        for h in range(4):
                            bk = bks[h // 2]
                            off = (h % 2) * 256
                            mm(bk[:, off:off + 256], [(qT[:, 2 * h + dc, tk], KTp[layer][:, 2 * h + dc, :]) for dc in range(2)], [qT, KTp[layer]], [bk])
                        return bks

                    def stB(sb_, bks):
                        Pun = Pun2[sb_ % 2]
                        sm = sm2[sb_ % 2]
                        Pn = Pn2[sb_ % 2]
                        for i in range(2):
                            P.op("dve", lambda e, i=i, bks=bks, sm=sm: e.tensor_reduce(out=sm[:, 2 * i:2 * i + 2], in_=bks[i][:, :].rearrange("p (h m) -> p h m", m=256),
                                                                         axis=AX.X, op=ALU.max, negate=True), [bks[i]], [sm])
                        for h in range(4):
                            bk = bks[h // 2]
                            off = (h % 2) * 256
                            act(Pun[:, h, :], bk[:, off:off + 256], AF.Exp, [bk, sm], [Pun, sm], bias=sm[:, h:h + 1], scale=1.0, accum_out=sm[:, 4 + h:5 + h])
                        unpin(bks[0])
                        unpin(bks[1])
                        P.op("dve", lambda e, sm=sm: e.reciprocal(out=sm[:, 8:12], in_=sm[:, 4:8]), [sm], [sm])
                        for h in range(4):
                            if stile:
                                act(Pn[:, h, :], Pun[:, h, :], AF.Copy, [Pun, sm], [Pn], scale=sm[:, 8 + h:9 + h])
                            else:
                                ts(Pn[:, h, :], Pun[:, h, :], sm[:, 8 + h:9 + h], None, ALU.mult, None, [Pun, sm], [Pn])

                    def stC(sb_):
                        tk = slice(sb_ * 128, (sb_ + 1) * 128)
                        Pn = Pn2[sb_ % 2]
                        bT = nb()
                        for h in range(4):
                            for mc in range(2):
                                j = h * 2 + mc
                                tr(bf(bT)[:, j * 128:(j + 1) * 128], Pn[:, h, mc * 128:(mc + 1) * 128], ident_b[:], [Pn, ident_b], [bT])
                        if stile:
                            act(PT[:, :, tk], bf(bT).rearrange("p (a t) -> p a t", t=128), AF.Copy, [bT], [PT])
                        else:
                            vcopy(PT[:, :, tk], bf(bT).rearrange("p (a t) -> p a t", t=128), [bT], [PT])

                    bq_ = {0: pre_scores.pop() if pre_scores else stA(0)}
                    for sb_ in range(4):
                        if sb_ + 1 < 4:
                            bq_[sb_ + 1] = stA(sb_ + 1)
                        elif nxt is not None:
                            qproj(nxt)
                            nxt = None
                        stB(sb_, bq_[sb_])
                        stC(sb_)
                        for _ in range(3):
                            next(kgen, None)
                    if tidx + 1 < len(tiles) and tiles[tidx + 1][3] == "p":
                        pre_scores.append(stA(0))
                    for h in range(4):
                        for dc in range(2):
                            e_ = 2 * h + dc
                            b = nb()
                            mm(b[:, 0:w], [(Vp[layer][:, mc, e_ * 128:(e_ + 1) * 128], PT[:, 2 * h + mc, 0:w]) for mc in range(2)], [Vp[layer], PT], [b])
                            act(oT[:, e_, 0:w], b[:, 0:w], AF.Copy, [b], [oT])
                else:
                    for _ in kgen:
                        pass
                    attn_sample(layer, oT)
                for oc in range(8):
                    b = nb()
                    mm(b[:, 0:w], [(Wo[:, kc, oc * 128:(oc + 1) * 128], oT[:, kc, 0:w]) for kc in range(8)], [Wo, oT], [b])
                    add_to_x(ti, c0, w, oc, b)

        def sample_k_phase(layer, ti, c0, w):
            Y0 = YT_OFF
            NK = 5
            Kr = [av(Y0 + i * 1024, [128, 2, D], BF16) for i in range(NK)]
            q_tm = av(Y0 + 5120, [NS, D], BF16)
            junk = av(Y0 + 5632, [128, 256], F32)
            STs = av(Y0 + 5888, [128, 2, NS, 4], F32)
            qTs = av(Y0 + 6016, [128, 8, NS], BF16)
            Wq = wview(slots[0], 1024)
            for fc in range(8):
                b = proj(Wq, fc * 128, ti, c0, w)
                act(qTs[:, fc, :], b[:, 0:w], AF.Identity, [b], [qTs], scale=0.0625)
            bq = nb()
            for fc in range(8):
                tr(bf(bq)[0:NS, fc * 128:(fc + 1) * 128], qTs[:, fc, :], ident_b[:], [qTs, ident_b], [bq])
            act(q_tm[:, :], bf(bq)[0:NS, :], AF.Copy, [bq], [q_tm])
            yield
            for n in range(NS):
                Kb = Kr[n % NK]
                P.dma("pool", Kb[:], ck[layer, n].rearrange("(c p) f -> p c f", p=128), writes=[Kb])
                qb = [nb(), nb()]
                for i in range(2):
                    mm(qb[i][:, :], [(sel[:, n, :], q_tm[:, i * 512:(i + 1) * 512])], [sel, q_tm], [qb[i]])
                for mc in range(2):
                    for h in range(4):
                        stt(junk[:, :], Kb[:, mc, h * 256:(h + 1) * 256], 1.0, qb[h // 2][:, (h % 2) * 256:(h % 2) * 256 + 256], ALU.mult, ALU.mult,
                            [Kb, qb[h // 2]], [junk, STs], accum_out=STs[:, mc, n, h:h + 1])
                yield

        def attn_sample(layer, oT):
            A0 = T0 + 2048
            NV = 4
            KV = [av(A0 + i * 1024, [128, 2, D], BF16) for i in range(NV)]
            q_tm = av(YT_OFF + 5120, [NS, D], BF16)
            STs = av(YT_OFF + 5888, [128, 2, NS, 4], F32)
            S_sm = av(T0 + 6144 + 128, [64, 256], F32)
            PTs = av(T0 + 6144 + 384, [128, 2, NS, 4], F32)
            Pm = av(T0 + 6144 + 512, [128, 2, 4, NS, NS], BF16)
            sm = av(T0 + 7680, [128, 16], F32)
            bS = nb()
            for mc in range(2):
                tr(bS[0:64, mc * 128:(mc + 1) * 128], STs[:, mc, :, :].rearrange("p n h -> p (n h)"), ident_f[:], [STs, ident_f], [bS])
            P.op("dve", lambda e: e.tensor_reduce(out=sm[0:64, 0:1], in_=bS[0:64, 0:256], axis=AX.X, op=ALU.max, negate=True), [bS], [sm])
            act(S_sm[:, :], bS[0:64, 0:256], AF.Exp, [bS, sm], [S_sm, sm], bias=sm[0:64, 0:1], scale=1.0, accum_out=sm[0:64, 4:5])
            P.op("dve", lambda e: e.reciprocal(out=sm[0:64, 8:9], in_=sm[0:64, 4:5]), [sm], [sm])
            ts(S_sm[:, :], S_sm[:, :], sm[0:64, 8:9], None, ALU.mult, None, [S_sm, sm], [S_sm])
            bS2 = nb()
            for mc in range(2):
                tr(bS2[:, mc * 64:(mc + 1) * 64], S_sm[:, mc * 128:(mc + 1) * 128], ident_f[0:64, 0:64], [S_sm, ident_f], [bS2])
            act(PTs[:, :, :, :].rearrange("p c n h -> p (c n h)"), bS2[:, 0:128], AF.Copy, [bS2], [PTs])
            for mc in range(2):
                for h in range(4):
                    for n in range(NS):
                        ts(Pm[:, mc, h, n, :], maskM[:, n, :], PTs[:, mc, n, h:h + 1], None, ALU.mult, None, [maskM, PTs], [Pm])
            bo = [nb(pin=True) for _ in range(4)]
            for n in range(NS):
                Vb = KV[n % NV]
                P.dma("pool", Vb[:], cv[layer, n].rearrange("(c p) f -> p c f", p=128), writes=[Vb])
                for h in range(4):
                    for mc in range(2):
                        mm1(bo[h][0:NS, 0:256], Pm[:, mc, h, n, :], Vb[:, mc, h * 256:(h + 1) * 256], n == 0 and mc == 0, n == NS - 1 and mc == 1,
                            [Pm, Vb], [bo[h]])
            o_tm = q_tm
            for h in range(4):
                act(o_tm[:, h * 256:(h + 1) * 256], bo[h][0:NS, 0:256], AF.Copy, [bo[h]], [o_tm])
                unpin(bo[h])
            bq2 = nb()
            for kc in range(8):
                tr(bf(bq2)[:, kc * NS:(kc + 1) * NS], o_tm[:, kc * 128:(kc + 1) * 128], ident_b[0:NS, 0:NS], [o_tm, ident_b], [bq2])
            act(oT[:, :, 0:NS], bf(bq2)[:, 0:8 * NS].rearrange("p (a t) -> p a t", t=NS), AF.Copy, [bq2], [oT])

        def ffn(st, tiles, layer):
            for (ti, c0, w, kind, g0) in tiles:
                rmsnorm(ti, c0, w, G_FFN + 8 * layer)
            actb = av(YT_OFF, [128, 12, NT], BF16)
            wg_v = w_gate[layer].rearrange("(k p) f -> p k f", p=128)
            wu_v = w_up[layer].rearrange("(k p) f -> p k f", p=128)
            cnt = 0
            pi = 0
            groups = [(0, 12), (12, 10)]
            for gi, (f0, G) in enumerate(groups):
                Wd = Buf(slots[gi].ap[:, 0:G * D].rearrange("p (j f) -> p j f", f=D), slots[gi].deps)
                wd_v = w_down[layer][f0 * 128:(f0 + G) * 128, :].rearrange("(j p) f -> p j f", p=128)
                hG = G // 2
                P.dma("pool", Wd[:, 0:hG, :], wd_v[:, 0:hG, :], writes=slots[gi].deps[0:2])
                P.dma("pool", Wd[:, hG:G, :], wd_v[:, hG:G, :], writes=slots[gi].deps[2:4] + slot_extra[gi])
                for jp in range(0, G, 4):
                    fc = f0 + jp
                    nq = min(4, G - jp)
                    weg = av(T0 + (pi % 2) * 4096, [128, 8, 512], BF16)
                    weu = av(T0 + (pi % 2) * 4096 + 2048, [128, 8, 512], BF16)
                    pi += 1
                    P.dma("pool", weg[:, :, 0:nq * 128], wg_v[:, :, fc * 128:(fc + nq) * 128], writes=[weg])
                    P.dma("pool", weu[:, :, 0:nq * 128], wu_v[:, :, fc * 128:(fc + nq) * 128], writes=[weu])
                    for q in range(nq):
                        j = jp + q
                        for (ti, c0, w, kind, g0) in tiles:
                            bg = nb()
                            mm(bg[:, 0:w], [(weg[:, kc, q * 128:(q + 1) * 128], hT[:, kc, c0:c0 + w]) for kc in range(8)], [weg, hdp[ti]], [bg])
                            bu = nb()
                            mm(bu[:, 0:w], [(weu[:, kc, q * 128:(q + 1) * 128], hT[:, kc, c0:c0 + w]) for kc in range(8)], [weu, hdp[ti]], [bu])
                            sg = av(T0 + 8192, [128, 512], F32)
                            act(sg[:, 0:w], bg[:, 0:w], AF.Silu, [bg], [sg])
                            tt(actb[:, j, c0:c0 + w], sg[:, 0:w], bu[:, 0:w], ALU.mult, [sg, bu], [actb])
                for (ti, c0, w, kind, g0) in tiles:
                    for oc in range(8):
                        b = nb()
                        mm(b[:, 0:w], [(Wd[:, j, oc * 128:(oc + 1) * 128], actb[:, j, c0:c0 + w]) for j in range(G)], [Wd, actb], [b])
                        add_to_x(ti, c0, w, oc, b)

        def mixer_odd(st, tiles):
            yT = av(YT_OFF, [128, 8, NT], BF16)
            tmp1 = av(O_T1, [128, 512], F32)
            tmp2 = av(O_T2, [128, 512], F32)
            WC = load_w(slots[0], w_in_odd[0], 0, 1024)
            WD = load_w(slots[1], w_in_odd[0], 1024, 2560)
            for (ti, c0, w, kind, g0) in tiles:
                rmsnorm(ti, c0, w, G_MIX + 8)
            for (ti, c0, w, kind, g0) in tiles:
                u_sb = av(T0, [128, 4, 512], F32)
                for c in range(4):
                    b = proj(WC, c * 128, ti, c0, w)
                    act(u_sb[:, c, 0:w], b[:, 0:w], AF.Copy, [b], [u_sb])
                if kind == "p":
                    vt = av(T0 + 2048, [128, 512], F32)
                    vb = av(T0 + 2560, [128, 4, 512], BF16)
                    sm = av(T0 + 3584, [128, 16], F32)
                    hb_ = [nb(pin=True) for _ in range(4)]

                    def vproj(sb_):
                        tk = slice(c0 + sb_ * 128, c0 + (sb_ + 1) * 128)
                        bv = nb(pin=True)
                        mm(bv[:, :], [(hT[:, kc, tk], WC[:, kc, 512:1024]) for kc in range(8)], [hdp[ti], WC], [bv])
                        return bv
                    bvs = {0: vproj(0)}
                    for sb_ in range(4):
                        if sb_ + 1 < 4:
                            bvs[sb_ + 1] = vproj(sb_ + 1)
                        bv = bvs[sb_]
                        P.op("dve", lambda e, bv=bv, sm=sm: e.bn_stats(out=sm[:, 0:6], in_=bv[:, :]), [bv], [sm])
                        P.op("dve", lambda e, sm=sm: e.bn_aggr(out=sm[:, 8:10], in_=sm[:, 0:6]), [sm], [sm])
                        act(sm[:, 10:11], sm[:, 9:10], AF.Ln, [sm, eps_t], [sm], bias=eps_t[:, 0:1], scale=1.0)
                        act(sm[:, 10:11], sm[:, 10:11], AF.Exp, [sm], [sm], scale=-0.5)
                        ts(vt[:, :], bv[:, :], sm[:, 8:9], sm[:, 10:11], ALU.subtract, ALU.mult, [bv, sm], [vt])
                        unpin(bv)
                        tt(vt[:, :], vt[:, :], gC_bc[:], ALU.mult, [vt, gC_bc], [vt])
                        tt(vt[:, :], vt[:, :], bC_bc[:], ALU.add, [vt, bC_bc], [vt])
                        if g0 + sb_ * 128 == SEQ - 128:
                            P.dma("sp", o_pc[:, :], vt[:, :], reads=[vt], is_output=True)
                        vcopy(vb[:, sb_, :], vt[:, :], [vt], [vb])
                        for h in range(4):
                            mm(hb_[h][:, sb_ * 128:(sb_ + 1) * 128],
                               [(vb[:, sb_, h * 128:(h + 1) * 128], wsT[:, h, :]),
                                (ones_row[0:1, :], bsH[0:1, h * 128:(h + 1) * 128]),
                                (ones_row[0:1, :], bsL[0:1, h * 128:(h + 1) * 128])], [vb, wsT, ones_row, bsH, bsL], [hb_[h]])
                    for h in range(4):
                        tt(yT[:, h, c0:c0 + w], u_sb[:, h, 0:w], hb_[h][:, 0:w], ALU.mult, [u_sb, hb_[h]], [yT])
                        unpin(hb_[h])
                else:
                    vs = av(O_CA, [128, 4, NS], F32)
                    for c in range(4):
                        b = proj(WC, 512 + c * 128, ti, c0, w)
                        act(vs[:, c, :], b[:, 0:w], AF.Copy, [b], [vs])
                    ln_feat(lambda c: vs[:, c, :], vs, w, C_LCG, C_LCB, AF.Identity, lambda c: vs[:, c, :], vs)
                    to_tokmajor(lambda c: vs[:, c, :], vs, NS, o_sc[:, :], O_T1)
                    for c in range(4):
                        ts(tmp2[:, 0:w], vs[:, c, :], ws00[:, c:c + 1], bs0[:, c:c + 1], ALU.mult, ALU.add, [vs, ws00, bs0], [tmp2])
                        tt(yT[:, c, c0:c0 + w], u_sb[:, c, 0:w], tmp2[:, 0:w], ALU.mult, [u_sb, tmp2], [yT])
            for (ti, c0, w, kind, g0) in tiles:
                if kind == "p":
                    gd = av(T0, [128, 4, 514], F32)
                    vcopy(gd[:, :, 0:2], gdH[:], [gdH], [gd])
                    cur = lambda c: gd[:, c, 2:2 + w]
                    tap = lambda c, k: gd[:, c, k:k + w]
                else:
                    gd = av(T0, [128, 4, NS, 3], F32)
                    load_hist_T(st_d, 2, lambda c: gd[:, c, :, :], gd)
                    cur = lambda c: gd[:, c, :, 2]
                    tap = lambda c, k: gd[:, c, :, k]
                for c in range(4):
                    bgc = proj(WD, 512 + c * 128, ti, c0, w)
                    bhd = proj(WD, 1024 + c * 128, ti, c0, w)
                    act(tmp1[:, 0:w], bgc[:, 0:w], AF.Copy, [bgc], [tmp1])
                    tt(cur(c), tmp1[:, 0:w], bhd[:, 0:w], ALU.mult, [tmp1, bhd], [gd])
                    ts(tmp2[:, 0:w], tap(c, 0), prm[:, C_DW + c:C_DW + c + 1], None, ALU.mult, None, [gd, prm], [tmp2])
                    stt(tmp2[:, 0:w], tap(c, 1), prm[:, C_DW + 4 + c:C_DW + 5 + c], tmp2[:, 0:w], ALU.mult, ALU.add, [gd, prm, tmp2], [tmp2])
                    stt(tmp2[:, 0:w], tap(c, 2), prm[:, C_DW + 8 + c:C_DW + 9 + c], tmp2[:, 0:w], ALU.mult, ALU.add, [gd, prm, tmp2], [tmp2])
                    bgb = proj(WD, c * 128, ti, c0, w)
                    tt(yT[:, 4 + c, c0:c0 + w], tmp2[:, 0:w], bgb[:, 0:w], ALU.mult, [tmp2, bgb], [yT])
                if kind == "p":
                    vcopy(gdH[:], gd[:, :, 512:514], [gd], [gdH])
                    if g0 + 512 == SEQ:
                        to_tokmajor(lambda c: gd[:, c, 512:514], gd, 2, o_pd[:, :], O_T1)
                else:
                    P.dma("sp", o_sd[:, 0:1, :], st_d[:, 1:2, :], is_output=True)
                    to_tokmajor(lambda c: gd[:, c, :, 2], gd, NS, o_sd[:, 1, :], O_T1)
            Wout = load_w(slots[0], w_out_odd[0], 0, 1024)
            out_proj(Wout, yT, tiles)

        ST_TILES = [
            [(0, 0, 512, "p", 0), (1, 512, 512, "p", 512)],
            [(0, 0, 512, "p", 1024), (1, 512, 512, "p", 1536), (2, 1024, NS, "s", 0)],
        ]
        if stage < 1:
            ST_TILES = []
        import os
        if os.environ.get("NOSAMPLE"):
            ST_TILES = [[t for t in tl if t[3] == "p"] for tl in ST_TILES]

        for st, tiles in enumerate(ST_TILES):
            for (ti, c0, w, kind, g0) in tiles:
                if kind == "p":
                    for sb_ in range(4):
                        xs = av(T0 + (sb_ % 2) * 1024, [128, D], F32)
                        P.dma("sp", xs[:], x_p[g0 + sb_ * 128:g0 + (sb_ + 1) * 128, :], writes=[xs])
                        for kq in range(2):
                            b = nb()
                            for k4 in range(4):
                                kc = kq * 4 + k4
                                tr(b[:, k4 * 128:(k4 + 1) * 128], xs[:, kc * 128:(kc + 1) * 128], ident_f[:], [xs, ident_f], [b])
                            act(xT[:, kq * 4:kq * 4 + 4, c0 + sb_ * 128:c0 + (sb_ + 1) * 128],
                                b[:, :].rearrange("p (a t) -> p a t", t=128), AF.Copy, [b], [xd[kq * 4 + k4][ti] for k4 in range(4)])
                else:
                    xs = av(T0, [NS, D], F32)
                    P.dma("sp", xs[:], x_s[:, :], writes=[xs])
                    b = nb()
                    for kc in range(8):
                        tr(b[:, kc * NS:(kc + 1) * NS], xs[:, kc * 128:(kc + 1) * 128], ident_f[0:NS, 0:NS], [xs, ident_f], [b])
                    act(xT[:, :, c0:c0 + NS], b[:, 0:8 * NS].rearrange("p (a t) -> p a t", t=NS), AF.Copy, [b], xdeps(ti))

            nlayers = 0 if stage < 2 else (1 if stage < 5 else 2)
            for layer in range(nlayers):
                base = 2 + 3 * layer
                if layer == 0:
                    mixer_even(st, tiles)
                else:
                    mixer_odd(st, tiles)
                if stage >= base + 1:
                    attn(st, tiles, layer)
                if stage >= base + 2:
                    ffn(st, tiles, layer)
            if os.environ.get("RMSONLY"):
                for (ti, c0, w, kind, g0) in tiles:
                    rmsnorm(ti, c0, w, G_FIN)
            for (ti, c0, w, kind, g0) in (tiles if not os.environ.get("NOFINAL") else []):
                yf = av(T0, [128, 8, 512], F32)
                rmsnorm(ti, c0, w, G_FIN, final_dst=yf)
                if kind == "p":
                    for sb_ in range(4):
                        ys = av(T0 + 4096 + (sb_ % 2) * 1024, [128, D], F32)
                        for kq in range(2):
                            b = nb()
                            for k4 in range(4):
                                kc = kq * 4 + k4
                                tr(b[:, k4 * 128:(k4 + 1) * 128], yf[:, kc, sb_ * 128:(sb_ + 1) * 128], ident_f[:], [yf, ident_f], [b])
                            act(ys[:, kq * 512:(kq + 1) * 512], b[:, :], AF.Copy, [b], [ys])
                        P.dma("sp", y_p[g0 + sb_ * 128:g0 + (sb_ + 1) * 128, :], ys[:], reads=[ys], is_output=True)
                else:
                    ys = av(T0 + 4096, [NS, D], F32)
                    for kq in range(2):
                        b = nb()
                        for k4 in range(4):
                            kc = kq * 4 + k4
                            tr(b[0:NS, k4 * 128:(k4 + 1) * 128], yf[:, kc, 0:NS], ident_f[:], [yf, ident_f], [b])
                        act(ys[:, kq * 512:(kq + 1) * 512], b[0:NS, :], AF.Copy, [b], [ys])
                    P.dma("sp", y_s[:, :], ys[:], reads=[ys], is_output=True)

        P.finish()
        P.emit_all()
    return nc


_OUT_ORDER = ["y_p", "y_s", "o_pa", "o_pb", "o_pc", "o_pd", "o_pk", "o_pv", "o_sa", "o_sb", "o_sc", "o_sd"]


def make_in_maps(inp):
    f = lambda a: np.ascontiguousarray(np.asarray(a, dtype=np.float32))
    maps = []
    shared = {k: f(inp[k]) for k in ["norm_mix", "norm_x", "norm_ffn", "norm_final", "w_in_even", "conv_a_w", "conv_a_b",
                                     "ln_a_g", "ln_a_b", "pool_b_w", "pool_b_scale", "w_out_even", "w_in_odd", "ln_c_g",
                                     "ln_c_b", "ws_c", "bs_c", "conv_d_w", "w_out_odd", "wq_x", "wk_x", "wv_x", "wo_x",
                                     "w_gate", "w_up", "w_down"]}
    for c in range(NCORE):
        s = slice(c * NS, (c + 1) * NS)
        m = dict(shared)
        m["x_p"] = f(inp["x_prompt"][c])
        m["x_s"] = f(inp["x_sample"][s, 0])
        m["mem"] = f(inp["mem_prompt"][c])
        m["st_a"] = f(inp["state_convA"][0, s])
        m["st_b"] = f(inp["state_poolB"][0, s])
        m["st_d"] = f(inp["state_convD"][0, s])
        m["ck"] = f(np.asarray(inp["cache_mem_k"])[:, s].reshape(2, NS, 256, D))
        m["cv"] = f(np.asarray(inp["cache_mem_v"])[:, s].reshape(2, NS, 256, D))
        maps.append(m)
    return maps


def assemble(results):
    r = results
    cat = lambda k: np.stack([np.asarray(r[c][k]) for c in range(NCORE)])
    y_prompt = cat("y_p")
    y_sample = np.concatenate([np.asarray(r[c]["y_s"]) for c in range(NCORE)], 0)[:, None, :]
    p_convA = cat("o_pa")[None]
    p_poolB = cat("o_pb")[None]
    p_chunkC = cat("o_pc").reshape(1, NCORE, 128, 4, 128)
    p_convD = cat("o_pd")[None]
    p_mem_k = np.stack([np.asarray(r[c]["o_pk"]) for c in range(NCORE)], 1).reshape(2, NCORE, 256, 4, 256)
    p_mem_v = np.stack([np.asarray(r[c]["o_pv"]) for c in range(NCORE)], 1).reshape(2, NCORE, 256, 4, 256)
    s_convA = np.concatenate([np.asarray(r[c]["o_sa"]) for c in range(NCORE)], 0)[None]
    s_poolB = np.concatenate([np.asarray(r[c]["o_sb"]) for c in range(NCORE)], 0)[None]
    s_chunkC = np.concatenate([np.asarray(r[c]["o_sc"]) for c in range(NCORE)], 0).reshape(1, NCORE * NS, 1, 4, 128)
    s_convD = np.concatenate([np.asarray(r[c]["o_sd"]) for c in range(NCORE)], 0)[None]
    outs = (y_prompt, y_sample, p_convA, p_poolB, p_chunkC, p_convD, p_mem_k, p_mem_v, s_convA, s_poolB, s_chunkC, s_convD)
    return tuple(np.ascontiguousarray(o, dtype=np.float32) for o in outs)


def kernel(**inputs):
    nc = build_program()
    in_maps = make_in_maps(inputs)
    res = run_bass_kernel_spmd(nc, in_maps, core_ids=list(range(NCORE)))
    return assemble(res.results)
```

```python
import numpy as np
import concourse.bass as bass
import concourse.mybir as mybir
from concourse.bass_utils import run_bass_kernel_spmd
from contextlib import ExitStack

F32 = mybir.dt.float32
BF16 = mybir.dt.bfloat16
I32 = mybir.dt.int32
ALU = mybir.AluOpType
AF = mybir.ActivationFunctionType
AX = mybir.AxisListType

NCORE = 8
D = 1024
SEQ = 2048
NS = 16
DFF = 2816
NFC = 22
EPS = 1e-6


class Dep:
    __slots__ = ("w", "r", "excl")

    def __init__(self, excl=False):
        self.w = None
        self.r = []
        self.excl = excl


class Buf:
    def __init__(self, ap, deps):
        self.ap = ap
        self.deps = list(deps)

    def __getitem__(self, idx):
        return self.ap[idx]


def _flat(xs):
    out = []
    for x in xs:
        if x is None:
            continue
        if isinstance(x, Dep):
            out.append(x)
        elif isinstance(x, Buf):
            out.extend(x.deps)
        else:
            out.extend(_flat(x))
    return out


class Prog:
    CE = ("pe", "act", "dve", "pool")
    ENGS = ("pe", "act", "dve", "pool", "sp")

    def __init__(self, nc, es, ring=16):
        self.nc = nc
        self.rec = {e: [] for e in self.ENGS}
        self.nops = {e: 0 for e in self.CE}
        self.semobj = {}
        for e in self.CE:
            self.semobj[("e", e)] = es.enter_context(nc.semaphore("s_" + e))
        self.ring = ring
        self.dq = ("sp", "pool")
        for q in self.dq:
            for i in range(ring):
                self.semobj[("d", q, i)] = es.enter_context(nc.semaphore("d_%s%d" % (q, i)))
        self.dcnt = {q: 0 for q in self.dq}
        self.out_tokens = []

    @staticmethod
    def _needs(reads, writes, extra=()):
        need = {}

        def add(tok):
            if tok is None:
                return
            k, v = tok
            if need.get(k, 0) < v:
                need[k] = v
        for d in reads:
            add(d.w)
        for d in writes:
            add(d.w)
            for t in d.r:
                add(t)
        for t in extra:
            add(t)
        return need

    @staticmethod
    def _commit(tok, reads, writes):
        for d in reads:
            d.r.append(tok)
        for d in writes:
            d.w = tok
            d.r = []

    def op(self, eng, fn, reads=(), writes=()):
        reads = _flat(reads)
        writes = _flat(writes)
        writes = writes + [d for d in reads if d.excl]
        reads = [d for d in reads if not d.excl]
        need = self._needs(reads, writes)
        self.nops[eng] += 1
        tok = (("c", eng), self.nops[eng])
        self.rec[eng].append({"kind": "op", "fn": fn, "need": need, "ord": self.nops[eng]})
        self._commit(tok, reads, writes)
        return tok

    def dma(self, q, out, in_, reads=(), writes=(), is_output=False, **kw):
        reads = _flat(reads)
        writes = _flat(writes)
        j = self.dcnt[q]
        self.dcnt[q] += 1
        slot = j % self.ring
        rnd = j // self.ring
        key = ("d", q, slot)
        extra = [(key, 16 * rnd)] if rnd > 0 else []
        need = self._needs(reads, writes, extra)
        tok = (key, 16 * (rnd + 1))
        self.rec[q].append({"kind": "dma", "out": out, "in": in_, "kw": kw, "need": need, "sem": key})
        self._commit(tok, reads, writes)
        if is_output:
            self.out_tokens.append(tok)
        return tok

    def finish(self):
        allt = list(self.out_tokens)
        for q in self.dq:
            n = self.dcnt[q]
            for slot in range(self.ring):
                k = (n - slot + self.ring - 1) // self.ring if n > slot else 0
                if k > 0:
                    allt.append((("d", q, slot), 16 * k))
        self.rec["sp"].append({"kind": "wait", "need": self._needs((), (), allt)})

    def emit_all(self):
        signal = {e: set() for e in self.CE}
        for E in self.ENGS:
            waited = {}
            for r in self.rec[E]:
                w = []
                for k, v in r["need"].items():
                    if k[0] == "c" and k[1] == "pe" and E == "pe":
                        continue
                    if waited.get(k, 0) < v:
                        waited[k] = v
                        w.append((k, v))
                        if k[0] == "c":
                            signal[k[1]].add(v)
                r["waits"] = w
        val = {}
        for e in self.CE:
            c = 0
            m = {}
            for o in range(1, self.nops[e] + 1):
                if o in signal[e]:
                    c += 1
                m[o] = c
            val[e] = m
        self.n_signals = {e: len(signal[e]) for e in self.CE}

        def run(E, e):
            for r in self.rec[E]:
                for k, v in r["waits"]:
                    if k[0] == "c":
                        e.wait_ge(self.semobj[("e", k[1])], val[k[1]][v])
                    else:
                        e.wait_ge(self.semobj[k], v)
                if r["kind"] == "op":
                    inst = r["fn"](e)
                    if r["ord"] in signal[E]:
                        inst.then_inc(self.semobj[("e", E)], 1)
                elif r["kind"] == "dma":
                    e.dma_start(out=r["out"], in_=r["in"], **r["kw"]).then_inc(self.semobj[r["sem"]], 16)

        with self.nc.Block() as block:
            @block.sync
            def _(e):
                run("sp", e)

            @block.tensor
            def _(e):
                run("pe", e)

            @block.scalar
            def _(e):
                run("act", e)

            @block.vector
            def _(e):
                run("dve", e)

            @block.gpsimd
            def _(e):
                run("pool", e)


G_MIX, G_X, G_FFN, G_FIN = 0, 16, 32, 48
C_AB, C_LAG, C_LAB, C_PBS, C_LCG, C_LCB, C_DW = 56, 60, 64, 68, 72, 76, 80
NPRM = 92


def build_program(stage=99):
    import os
    nc = bass.Bass("TRN2", target_bir_lowering=False)

    def din(name, shape):
        return nc.dram_tensor(name, list(shape), F32, kind="ExternalInput").ap()

    def dout(name, shape):
        return nc.dram_tensor(name, list(shape), F32, kind="ExternalOutput").ap()

    x_p = din("x_p", [SEQ, D]); x_s = din("x_s", [NS, D]); mem = din("mem", [256, D])
    st_a = din("st_a", [NS, 30, 512]); st_b = din("st_b", [NS, 15, 512]); st_d = din("st_d", [NS, 2, 512])
    ck = din("ck", [2, NS, 256, D]); cv = din("cv", [2, NS, 256, D])
    norm_mix = din("norm_mix", [2, D]); norm_x = din("norm_x", [2, D]); norm_ffn = din("norm_ffn", [2, D])
    norm_final = din("norm_final", [D])
    w_in_even = din("w_in_even", [1, D, 1536]); conv_a_w = din("conv_a_w", [1, 31, 512])
    conv_a_b = din("conv_a_b", [1, 512]); ln_a_g = din("ln_a_g", [1, 512]); ln_a_b = din("ln_a_b", [1, 512])
    pool_b_w = din("pool_b_w", [1, 4, 128, 128]); pool_b_scale = din("pool_b_scale", [1, 512])
    w_out_even = din("w_out_even", [1, D, D]); w_in_odd = din("w_in_odd", [1, D, 2560])
    ln_c_g = din("ln_c_g", [1, 512]); ln_c_b = din("ln_c_b", [1, 512]); ws_c = din("ws_c", [1, 4, 128, 128])
    bs_c = din("bs_c", [1, 4, 128]); conv_d_w = din("conv_d_w", [1, 3, 512]); w_out_odd = din("w_out_odd", [1, D, D])
    wq_x = din("wq_x", [2, D, D]); wk_x = din("wk_x", [2, D, D]); wv_x = din("wv_x", [2, D, D]); wo_x = din("wo_x", [2, D, D])
    w_gate = din("w_gate", [2, D, DFF]); w_up = din("w_up", [2, D, DFF]); w_down = din("w_down", [2, DFF, D])

    y_p = dout("y_p", [SEQ, D]); y_s = dout("y_s", [NS, D])
    o_pa = dout("o_pa", [30, 512]); o_pb = dout("o_pb", [15, 512]); o_pc = dout("o_pc", [128, 512]); o_pd = dout("o_pd", [2, 512])
    o_pk = dout("o_pk", [2, 256, D]); o_pv = dout("o_pv", [2, 256, D])
    o_sa = dout("o_sa", [NS, 30, 512]); o_sb = dout("o_sb", [NS, 15, 512]); o_sc = dout("o_sc", [NS, 512]); o_sd = dout("o_sd", [NS, 2, 512])

    with ExitStack() as es:
        P = Prog(nc, es)

        def sbt(name, shape, dt, ndeps=1):
            t = es.enter_context(nc.sbuf_tensor(name, list(shape), dt))
            return Buf(t, [Dep() for _ in range(ndeps)])

        NT = 1040
        xT = sbt("xT", [128, 8, NT], F32)
        xd = [[Dep() for _ in range(3)] for _ in range(8)]
        hT = sbt("hT", [128, 8, NT], BF16)
        hdp = [Dep() for _ in range(3)]
        KTp = [sbt("KTp%d" % l, [128, 8, 256], BF16) for l in range(2)]
        Vp = [sbt("Vp%d" % l, [128, 2, D], BF16) for l in range(2)]
        ident_f = sbt("ident_f", [128, 128], F32)
        ident_b = sbt("ident_b", [128, 128], BF16)
        ones_rms = sbt("ones_rms", [128, 128], BF16)
        ones_ln = sbt("ones_ln", [128, 128], BF16)
        ones_row = sbt("ones_row", [1, 128], BF16)
        prm = sbt("prm", [128, NPRM], F32)
        cw = sbt("cw", [128, 124], F32)
        eps_t = sbt("eps_t", [128, 1], F32)
        gC_bc = sbt("gC_bc", [128, 512], F32)
        bC_bc = sbt("bC_bc", [128, 512], F32)
        wsT = sbt("wsT", [128, 4, 128], BF16)
        poolw = sbt("poolw", [128, 4, 128], BF16)
        bsF = sbt("bsF", [1, 512], F32)
        bsH = sbt("bsH", [1, 512], BF16)
        bsL = sbt("bsL", [1, 512], BF16)
        ws00 = sbt("ws00", [128, 4], F32)
        bs0 = sbt("bs0", [128, 4], F32)
        invc = sbt("invc", [128, 16], F32)
        zaH = sbt("zaH", [128, 4, 30], F32)
        zbH = sbt("zbH", [128, 4, 15], F32)
        gdH = sbt("gdH", [128, 4, 2], F32)
        slots = [sbt("slot%d" % i, [128, 12288], BF16, ndeps=4) for i in range(2)]
        DgA = Buf(slots[1].ap[:, 8192:10240].rearrange("p (k f) -> p k f", f=128), [Dep()])
        DgB = Buf(slots[1].ap[:, 10240:12288].rearrange("p (k f) -> p k f", f=128), [Dep()])
        slot_extra = {1: [DgA, DgB], 0: []}
        AW = 18112
        arena_t = es.enter_context(nc.sbuf_tensor("arena", [128, AW], F32))
        CH = 32
        adeps = [Dep() for _ in range((AW + CH - 1) // CH)]

        def av(off, shape, dt):
            n = 1
            for s in shape[1:]:
                n *= s
            words = n if dt in (F32, I32) else (n + 1) // 2
            assert off + words <= AW, (off, words)
            ap = arena_t[0:shape[0], off:off + words]
            if dt != F32:
                ap = ap.bitcast(dt)
            if len(shape) >= 3:
                names = "abcdef"[:len(shape) - 1]
                kw = {names[i]: shape[i + 1] for i in range(1, len(names))}
                ap = ap.rearrange("p (%s) -> p %s" % (" ".join(names), " ".join(names)), **kw)
            assert off % CH == 0, off
            deps = adeps[off // CH:(off + words - 1) // CH + 1]
            return Buf(ap, deps)

        sq_sep = sbt("sq_sep", [128, 8, 512], BF16) if os.environ.get("SQSEP") else None
        banks = []
        for i in range(8):
            t = es.enter_context(nc.psum_tensor("pb%d" % i, [128, 512], F32))
            banks.append(Buf(t, [Dep(excl=True)]))
        bank_i = [0]
        pinned = set()

        def nb(pin=False):
            while (bank_i[0] % 8) in pinned:
                bank_i[0] += 1
            i = bank_i[0] % 8
            bank_i[0] += 1
            if pin:
                pinned.add(i)
            return banks[i]

        def unpin(b):
            pinned.discard(banks.index(b))

        def mm(out_ap, pairs, reads, writes):
            def fn(e, pairs=pairs, out_ap=out_ap):
                n = len(pairs)
                inst = None
                for i, (l, r) in enumerate(pairs):
                    inst = e.matmul(out_ap, lhsT=l, rhs=r, start=(i == 0), stop=(i == n - 1))
                return inst
            P.op("pe", fn, reads, writes)

        def mmg(out_ap, pairs, first, last, reads, writes):
            def fn(e, pairs=pairs, out_ap=out_ap, first=first, last=last):
                n = len(pairs)
                inst = None
                for i, (l, r) in enumerate(pairs):
                    inst = e.matmul(out_ap, lhsT=l, rhs=r, start=(first and i == 0), stop=(last and i == n - 1))
                return inst
            P.op("pe", fn, reads, writes)

        def mm1(out_ap, lhsT, rhs, start, stop, reads, writes):
            P.op("pe", lambda e: e.matmul(out_ap, lhsT=lhsT, rhs=rhs, start=start, stop=stop), reads, writes)

        def tr(out_ap, in_ap, ident_ap, reads, writes):
            P.op("pe", lambda e: e.transpose(out=out_ap, in_=in_ap, identity=ident_ap), reads, writes)

        def act(out, in_, func, reads, writes, **kw):
            P.op("act", lambda e: e.activation(out=out, in_=in_, func=func, **kw), reads, writes)

        def tt(out, in0, in1, op, reads, writes):
            P.op("dve", lambda e: e.tensor_tensor(out=out, in0=in0, in1=in1, op=op), reads, writes)

        def ts(out, in0, s1, s2, op0, op1, reads, writes):
            if op1 is None:
                P.op("dve", lambda e: e.tensor_scalar(out=out, in0=in0, scalar1=s1, scalar2=None, op0=op0), reads, writes)
            else:
                P.op("dve", lambda e: e.tensor_scalar(out=out, in0=in0, scalar1=s1, scalar2=s2, op0=op0, op1=op1), reads, writes)

        def stt(out, in0, scalar, in1, op0, op1, reads, writes, **kw):
            P.op("dve", lambda e: e.scalar_tensor_tensor(out=out, in0=in0, scalar=scalar, in1=in1, op0=op0, op1=op1, **kw), reads, writes)

        def vcopy(out, in_, reads, writes):
            P.op("dve", lambda e: e.tensor_copy(out=out, in_=in_), reads, writes)

        if stage < 0:
            tt_ = av(0, [NS, D], F32)
            P.dma("sp", tt_[:], x_s[:, :], writes=[tt_])
            P.dma("sp", y_s[:, :], tt_[:], reads=[tt_], is_output=True)
            P.finish()
            P.emit_all()
            return nc
        P.op("pool", lambda e: e.memset(ident_f[:], 1.0), [], [ident_f])
        P.op("pool", lambda e: e.affine_select(out=ident_f[:], in_=ident_f[:], pattern=[[-1, 128]], compare_op=ALU.is_equal,
                                               fill=0.0, base=0, channel_multiplier=1), [ident_f], [ident_f])
        vcopy(ident_b[:], ident_f[:], [ident_f], [ident_b])
        P.op("pool", lambda e: e.memset(ones_rms[:], 1.0 / 1024.0), [], [ones_rms])
        P.op("pool", lambda e: e.memset(ones_ln[:], 1.0 / 512.0), [], [ones_ln])
        P.op("pool", lambda e: e.memset(ones_row[:], 1.0), [], [ones_row])
        P.op("pool", lambda e: e.memset(eps_t[:], EPS), [], [eps_t])
        P.op("pool", lambda e: e.memset(zaH[:], 0.0), [], [zaH])
        P.op("pool", lambda e: e.memset(zbH[:], 0.0), [], [zbH])
        P.op("pool", lambda e: e.memset(gdH[:], 0.0), [], [gdH])
        ii = av(0, [128, 16], I32)
        P.op("pool", lambda e: e.iota(ii[:], pattern=[[1, 16]], base=1, channel_multiplier=0), [], [ii])
        vcopy(invc[:], ii[:], [ii], [invc])
        P.op("dve", lambda e: e.reciprocal(out=invc[:], in_=invc[:]), [invc], [invc])

        sel = sbt("sel", [NS, NS, 128], BF16)
        maskM = sbt("maskM", [128, NS, NS], BF16)
        P.op("pool", lambda e: e.memset(sel[:], 1.0), [], [sel])
        P.op("pool", lambda e: e.affine_select(out=sel[:], in_=sel[:], pattern=[[-1, NS], [0, 128]], compare_op=ALU.is_equal,
                                               fill=0.0, base=0, channel_multiplier=1), [sel], [sel])
        P.op("pool", lambda e: e.memset(maskM[:], 1.0), [], [maskM])
        P.op("pool", lambda e: e.affine_select(out=maskM[:], in_=maskM[:], pattern=[[1, NS], [-1, NS]], compare_op=ALU.is_equal,
                                               fill=0.0, base=0, channel_multiplier=0), [maskM], [maskM])
        R1 = av(512, [128, 128], F32)
        R2 = av(1024, [128, 128], F32)
        P.op("pool", lambda e: e.memset(R1[:], 0.0), [], [R1])
        P.op("pool", lambda e: e.memset(R2[:], 0.0), [], [R2])

        def rows(dst, r0, src, n):
            P.dma("sp", dst[r0:r0 + n, :], src.rearrange("(c p) -> c p", p=128), writes=[dst])
        for l in range(2):
            rows(R1, G_MIX + 8 * l, norm_mix[l], 8)
            rows(R1, G_X + 8 * l, norm_x[l], 8)
            rows(R1, G_FFN + 8 * l, norm_ffn[l], 8)
        rows(R1, G_FIN, norm_final, 8)
        rows(R1, C_AB, conv_a_b[0], 4); rows(R1, C_LAG, ln_a_g[0], 4); rows(R1, C_LAB, ln_a_b[0], 4)
        rows(R1, C_PBS, pool_b_scale[0], 4); rows(R1, C_LCG, ln_c_g[0], 4); rows(R1, C_LCB, ln_c_b[0], 4)
        for k in range(3):
            rows(R1, C_DW + 4 * k, conv_d_w[0, k], 4)
        P.dma("sp", R2[0:124, :], conv_a_w[0].rearrange("k (c p) -> (k c) p", p=128), writes=[R2])
        b = nb()
        tr(b[:, 0:128], R1[:], ident_f[:], [R1, ident_f], [b])
        vcopy(prm[:], b[:, 0:NPRM], [b], [prm])
        b = nb()
        tr(b[:, 0:128], R2[:], ident_f[:], [R2, ident_f], [b])
        vcopy(cw[:], b[:, 0:124], [b], [cw])
        P.dma("sp", gC_bc[:], ln_c_g[0].partition_broadcast(128), writes=[gC_bc])
        P.dma("sp", bC_bc[:], ln_c_b[0].partition_broadcast(128), writes=[bC_bc])
        P.dma("sp", bsF[:], bs_c[0].rearrange("h i -> (h i)").partition_broadcast(1), writes=[bsF])
        vcopy(bsH[:], bsF[:], [bsF], [bsH])
        tt(bsF[:], bsF[:], bsH[:], ALU.subtract, [bsF, bsH], [bsF])
        vcopy(bsL[:], bsF[:], [bsF], [bsL])
        if not os.environ.get("SKIPX"):
            P.dma("sp", ws00[:], ws_c[0, :, 0, 0].partition_broadcast(128), writes=[ws00], allow_slow_non_contiguous=True)
            P.dma("sp", bs0[:], bs_c[0, :, 0].partition_broadcast(128), writes=[bs0], allow_slow_non_contiguous=True)
        wsf = av(1536, [128, 4, 128], F32)
        P.dma("sp", wsf[:], ws_c[0].rearrange("h i j -> i h j"), writes=[wsf])
        P.op("pool", lambda e: e.affine_select(out=wsf[:], in_=wsf[:], pattern=[[0, 4], [-1, 128]], compare_op=ALU.is_ge,
                                               fill=0.0, base=0, channel_multiplier=1), [wsf], [wsf])
        for h in range(4):
            b = nb()
            tr(b[:, 0:128], wsf[:, h, :], ident_f[:], [wsf, ident_f], [b])
            vcopy(wsT[:, h, :], b[:, 0:128], [b], [wsT])
        P.dma("pool", poolw[:], pool_b_w[0].rearrange("g c e -> c g e"), writes=[poolw])

        memf = av(2048, [128, 2, D], F32)
        memT = av(4096, [128, 8, 256], BF16)
        stg = av(5120, [128, D], F32)
        P.dma("sp", memf[:], mem.rearrange("(c p) f -> p c f", p=128), writes=[memf])
        for kc in range(8):
            b = nb()
            for mc in range(2):
                tr(b[:, mc * 128:(mc + 1) * 128], memf[:, mc, kc * 128:(kc + 1) * 128], ident_f[:], [memf, ident_f], [b])
            act(memT[:, kc, :], b[:, 0:256], AF.Copy, [b], [memT])
        for l in range(2 if not os.environ.get("SKIPKV") else 0):
            wk = Buf(slots[0].ap[:, 0:8192].rearrange("p (k f) -> p k f", f=D), slots[0].deps)
            wv = Buf(slots[1].ap[:, 0:8192].rearrange("p (k f) -> p k f", f=D), slots[1].deps)
            P.dma("pool", wk[:], wk_x[l].rearrange("(k p) f -> p k f", p=128), writes=[wk])
            P.dma("pool", wv[:], wv_x[l].rearrange("(k p) f -> p k f", p=128), writes=[wv])
            for ec in range(8):
                b = nb()
                mm(b[:, 0:256], [(wk[:, kc, ec * 128:(ec + 1) * 128], memT[:, kc, :]) for kc in range(8)], [wk, memT], [b])
                act(KTp[l][:, ec, :], b[:, 0:256], AF.Copy, [b], [KTp[l]])
            for (w_, dst, isv) in ((wk, o_pk, False), (wv, o_pv, True)):
                for mc in range(2):
                    for hf in range(2):
                        b = nb()
                        mm(b[:, :], [(memT[:, kc, mc * 128:(mc + 1) * 128], w_[:, kc, hf * 512:(hf + 1) * 512]) for kc in range(8)],
                           [w_, memT], [b])
                        act(stg[:, hf * 512:(hf + 1) * 512], b[:, :], AF.Copy, [b], [stg])
                        if isv:
                            vcopy(Vp[l][:, mc, hf * 512:(hf + 1) * 512], b[:, :], [b], [Vp[l]])
                    P.dma("sp", dst[l, mc * 128:(mc + 1) * 128, :], stg[:], reads=[stg], is_output=True)

        def xdeps(ti):
            return [xd[k][ti] for k in range(8)]

        SQ_OFF, RS_OFF = 0, 2048

        def rmsnorm(ti, c0, w, gcol, final_dst=None):
            sq = av(SQ_OFF, [128, 8, 512], BF16)
            rstd = av(RS_OFF, [128, 512], F32)
            if os.environ.get("SQSEP"):
                sq = sq_sep
            RL = int(os.environ.get("RMS_LEVEL", "9"))
            if os.environ.get("SQ2D"):
                for k in range(8):
                    act(sq[:, k, 0:w], xT[:, k, c0:c0 + w], AF.Square, [xd[k][ti]], [sq])
            else:
                act(sq[:, :, 0:w], xT[:, :, c0:c0 + w], AF.Square, xdeps(ti), [sq])
            if RL < 2:
                return
            b = nb()
            mm(b[:, 0:w], [(ones_rms[:], sq[:, k, 0:w]) for k in range(8)], [sq, ones_rms], [b])
            if RL < 3:
                return
            act(rstd[:, 0:w], b[:, 0:w], AF.Ln, [b, eps_t], [rstd], bias=eps_t[:, 0:1], scale=1.0)
            if RL < 4:
                return
            act(rstd[:, 0:w], rstd[:, 0:w], AF.Exp, [rstd], [rstd], scale=-0.5)
            if RL < 5:
                return
            for k in range(8):
                if final_dst is None:
                    stt(hT[:, k, c0:c0 + w], xT[:, k, c0:c0 + w], prm[:, gcol + k:gcol + k + 1], rstd[:, 0:w], ALU.mult, ALU.mult,
                        [xd[k][ti], rstd, prm], [hdp[ti]])
                else:
                    stt(final_dst[:, k, 0:w], xT[:, k, c0:c0 + w], prm[:, gcol + k:gcol + k + 1], rstd[:, 0:w], ALU.mult, ALU.mult,
                        [xd[k][ti], rstd, prm], [final_dst])

        def proj(Wb, f0, ti, c0, w, src=None, srcdeps=None):
            src = hT if src is None else src
            sd = [hdp[ti]] if srcdeps is None else srcdeps
            b = nb()
            mm(b[:, 0:w], [(Wb[:, kc, f0:f0 + 128], src[:, kc, c0:c0 + w]) for kc in range(8)], [Wb] + sd, [b])
            return b

        def wview(slot, ncols):
            return Buf(slot.ap[:, 0:8 * ncols].rearrange("p (k f) -> p k f", f=ncols), slot.deps)

        def load_w(slot, src2d, c_lo, c_hi):
            n = c_hi - c_lo
            Wb = wview(slot, n)
            s = src2d.rearrange("(k p) f -> p k f", p=128)
            for k0 in range(0, 8, 2):
                P.dma("pool", Wb[:, k0:k0 + 2, :], s[:, k0:k0 + 2, c_lo:c_hi], reads=[],
                      writes=[slot.deps[k0 // 2]] + (slot_extra[slots.index(slot)] if 8 * n > 8192 else []))
            return Wb

        def add_to_x(ti, c0, w, oc, b):
            tt(xT[:, oc, c0:c0 + w], xT[:, oc, c0:c0 + w], b[:, 0:w], ALU.add, [xd[oc][ti], b], [xd[oc][ti]])

        def out_proj(Wb, yT, tiles):
            for (ti, c0, w, kind, g0) in tiles:
                for oc in range(8):
                    b = nb()
                    mm(b[:, 0:w], [(Wb[:, kc, oc * 128:(oc + 1) * 128], yT[:, kc, c0:c0 + w]) for kc in range(8)], [Wb, yT], [b])
                    add_to_x(ti, c0, w, oc, b)

        YT_OFF = 2560
        T0 = YT_OFF + 6240


        cwv = Buf(cw.ap.rearrange("p (k c) -> p c k", c=4), cw.deps)
        O_ZA, O_CA, O_ZB, O_T1, O_T2, O_CB = T0, T0 + 2176, T0 + 4224, T0 + 6336, T0 + 6848, T0 + 7360

        def ln_feat(get_src, srcbuf, w, gcol, bcol, func, get_dst, dstbuf):
            tmp1 = av(O_T1, [128, 512], F32)
            tmp2 = av(O_T2, [128, 512], F32)
            cbs = [av(O_CB + 256 * i, [128, 512], BF16) for i in range(4)]
            bm = nb()
            bq = nb()
            for c in range(4):
                cab, csq = cbs[(c % 2) * 2], cbs[(c % 2) * 2 + 1]
                act(cab[:, 0:w], get_src(c), AF.Copy, [srcbuf], [cab])
                act(csq[:, 0:w], get_src(c), AF.Square, [srcbuf], [csq])
                mm1(bm[:, 0:w], ones_ln[:], cab[:, 0:w], c == 0, c == 3, [ones_ln, cab], [bm])
                mm1(bq[:, 0:w], ones_ln[:], csq[:, 0:w], c == 0, c == 3, [ones_ln, csq], [bq])
            act(tmp1[:, 0:w], bm[:, 0:w], AF.Square, [bm], [tmp1])
            tt(tmp1[:, 0:w], bq[:, 0:w], tmp1[:, 0:w], ALU.subtract, [bq, tmp1], [tmp1])
            act(tmp1[:, 0:w], tmp1[:, 0:w], AF.Ln, [tmp1, eps_t], [tmp1], bias=eps_t[:, 0:1], scale=1.0)
            act(tmp1[:, 0:w], tmp1[:, 0:w], AF.Exp, [tmp1], [tmp1], scale=-0.5)
            act(tmp2[:, 0:w], bm[:, 0:w], AF.Copy, [bm], [tmp2])
            for c in range(4):
                tt(get_src(c), get_src(c), tmp2[:, 0:w], ALU.subtract, [srcbuf, tmp2], [srcbuf])
                tt(get_src(c), get_src(c), tmp1[:, 0:w], ALU.mult, [srcbuf, tmp1], [srcbuf])
                act(get_dst(c), get_src(c), func, [srcbuf, prm], [dstbuf], scale=prm[:, gcol + c:gcol + c + 1], bias=prm[:, bcol + c:bcol + c + 1])

        def to_tokmajor(get_src, srcbuf, nrow, dst_dram, stg_off):
            b = nb()
            for c in range(4):
                tr(b[0:nrow, c * 128:(c + 1) * 128], get_src(c), ident_f[:], [srcbuf, ident_f], [b])
            sg = av(stg_off, [32, 512], F32)
            act(sg[0:nrow, :], b[0:nrow, :], AF.Copy, [b], [sg])
            P.dma("sp", dst_dram, sg[0:nrow, :], reads=[sg], is_output=True)

        def load_hist_T(src_dram, nk, dst, dstbuf):
            per = 120 // nk if nk > 8 else 16
            per = min(per, NS)
            while NS % per:
                per -= 1
            rows = per * nk
            for j in range(NS // per):
                raw = av(O_T1, [128, 512], F32)
                P.dma("sp", raw[0:rows, :], src_dram[j * per:(j + 1) * per].rearrange("n k f -> (n k) f"), writes=[raw])
                for c in range(4):
                    b = nb()
                    tr(b[:, 0:rows], raw[0:rows, c * 128:(c + 1) * 128], ident_f[0:rows, 0:rows], [raw, ident_f], [b])
                    act(dst(c)[:, j * per:(j + 1) * per, 0:nk], b[:, 0:rows].rearrange("p (n k) -> p n k", k=nk), AF.Copy, [b], [dstbuf])

        def mixer_even(st, tiles):
            Win = load_w(slots[0], w_in_even[0], 0, 1536)
            Wout = load_w(slots[1], w_out_even[0], 0, 1024)
            yT = av(YT_OFF, [128, 8, NT], BF16)
            tmp1 = av(O_T1, [128, 512], F32)
            tmp2 = av(O_T2, [128, 512], F32)
            cbs = [av(O_CB + 256 * i, [128, 512], BF16) for i in range(4)]
            for (ti, c0, w, kind, g0) in tiles:
                rmsnorm(ti, c0, w, G_MIX + 0)
            for (ti, c0, w, kind, g0) in tiles:
                if kind == "p":
                    za = av(O_ZA, [128, 4, 542], BF16)
                    ca = av(O_CA, [128, 4, 512], F32)
                    zb = av(O_ZB, [128, 4, 527], F32)
                    vcopy(za[:, :, 0:30], zaH[:], [zaH], [za])
                    vcopy(zb[:, :, 0:15], zbH[:], [zbH], [zb])
                    zcur = lambda c: za[:, c, 30:30 + w]
                    bcur = lambda c: zb[:, c, 15:15 + w]
                else:
                    za = av(O_ZA, [128, 4, NS, 31], F32)
                    ca = av(O_CA, [128, 4, NS], F32)
                    zb = av(O_ZB, [128, 4, NS, 16], F32)
                    load_hist_T(st_a, 30, lambda c: za[:, c, :, :], za)
                    load_hist_T(st_b, 15, lambda c: zb[:, c, :, :], zb)
                    zcur = lambda c: za[:, c, :, 30]
                    bcur = lambda c: zb[:, c, :, 15]
                for c in range(4):
                    ba = proj(Win, c * 128, ti, c0, w)
                    bg = proj(Win, 512 + c * 128, ti, c0, w)
                    act(tmp1[:, 0:w], bg[:, 0:w], AF.Sigmoid, [bg], [tmp1])
                    tt(zcur(c), ba[:, 0:w], tmp1[:, 0:w], ALU.mult, [ba, tmp1], [za])
                    if kind == "p":
                        tt(zaH[:, c, :], ba[:, w - 30:w], tmp1[:, w - 30:w], ALU.mult, [ba, tmp1], [zaH])
                    bz = proj(Win, 1024 + c * 128, ti, c0, w)
                    act(bcur(c), bz[:, 0:w], AF.Copy, [bz], [zb])
                for c in range(4):
                    if kind == "p":
                        for k in range(16):
                            ts(DgA[:, k, :], ident_b[:], cwv[:, c, k:k + 1], None, ALU.mult, None, [ident_b, cw], [DgA])
                        for k in range(16, 31):
                            ts(DgB[:, k - 16, :], ident_b[:], cwv[:, c, k:k + 1], None, ALU.mult, None, [ident_b, cw], [DgB])
                        bc_ = nb(pin=True)
                        mmg(bc_[:, 0:w], [(DgA[:, k, :], za[:, c, k:k + w]) for k in range(16)], True, False, [DgA, za], [bc_])
                        mmg(bc_[:, 0:w], [(DgB[:, k - 16, :], za[:, c, k:k + w]) for k in range(16, 31)], False, True, [DgB, za], [bc_])
                        act(ca[:, c, 0:w], bc_[:, 0:w], AF.Identity, [bc_, prm], [ca], bias=prm[:, C_AB + c:C_AB + c + 1], scale=1.0)
                        unpin(bc_)
                        continue
                    cac = ca[:, c, :]
                    tap = lambda k, c=c: za[:, c, :, k]
                    ts(cac, tap(30), cwv[:, c, 30:31], prm[:, C_AB + c:C_AB + c + 1], ALU.mult, ALU.add, [za, cw, prm], [ca])
                    for k in range(30):
                        stt(cac, tap(k), cwv[:, c, k:k + 1], cac, ALU.mult, ALU.add, [za, cw, ca], [ca])
                if kind == "p":
                    ln_feat(lambda c: ca[:, c, 0:w], ca, w, C_LAG, C_LAB, AF.Silu, lambda c: yT[:, c, c0:c0 + w], yT)
                else:
                    ln_feat(lambda c: ca[:, c, :], ca, w, C_LAG, C_LAB, AF.Silu, lambda c: yT[:, c, c0:c0 + w], yT)
                for g in range(4):
                    win = 2 << g
                    pooled = cbs[g]
                    if kind == "p":
                        Sa = av(O_CA, [128, 527], F32)
                        Sb = av(O_CA + 544, [128, 527], F32)
                        tt(Sa[:, 1:527], zb[:, g, 1:527], zb[:, g, 0:526], ALU.add, [zb], [Sa])
                        cur = Sa
                        if g >= 1:
                            tt(Sb[:, 3:527], Sa[:, 3:527], Sa[:, 1:525], ALU.add, [Sa], [Sb]); cur = Sb
                        if g >= 2:
                            tt(Sa[:, 7:527], Sb[:, 7:527], Sb[:, 3:523], ALU.add, [Sb], [Sa]); cur = Sa
                        if g >= 3:
                            tt(Sb[:, 15:527], Sa[:, 15:527], Sa[:, 7:519], ALU.add, [Sa], [Sb]); cur = Sb
                        stt(pooled[:, 0:w], cur[:, 15:15 + w], 1.0 / win, zb[:, g, 15:15 + w], ALU.mult, ALU.subtract, [cur, zb], [pooled])
                        if g0 == 0:
                            nfix = win - 1
                            tt(tmp2[:, 0:nfix], cur[:, 15:15 + nfix], invc[:, 0:nfix], ALU.mult, [cur, invc], [tmp2])
                            tt(pooled[:, 0:nfix], tmp2[:, 0:nfix], zb[:, g, 15:15 + nfix], ALU.subtract, [tmp2, zb], [pooled])
                    else:
                        P.op("dve", lambda e, g=g, win=win, zb=zb, tmp2=tmp2: e.tensor_reduce(out=tmp2[:, 0:NS], in_=zb[:, g, :, 16 - win:16], axis=AX.X, op=ALU.add),
                             [zb], [tmp2])
                        stt(pooled[:, 0:w], tmp2[:, 0:NS], 1.0 / win, zb[:, g, :, 15], ALU.mult, ALU.subtract, [tmp2, zb], [pooled])
                    b = nb()
                    mm(b[:, 0:w], [(poolw[:, g, :], pooled[:, 0:w])], [poolw, pooled], [b])
                    act(yT[:, 4 + g, c0:c0 + w], b[:, 0:w], AF.Identity, [b, prm], [yT], scale=prm[:, C_PBS + g:C_PBS + g + 1])
                if kind == "p":
                    vcopy(zbH[:], zb[:, :, 512:527], [zb], [zbH])
                    if g0 + 512 == SEQ:
                        to_tokmajor(lambda c: zaH[:, c, :], zaH, 30, o_pa[:, :], O_T1)
                        to_tokmajor(lambda c: zb[:, c, 512:527], zb, 15, o_pb[:, :], O_T2)
                else:
                    P.dma("sp", o_sa[:, 0:29, :], st_a[:, 1:30, :], is_output=True)
                    P.dma("sp", o_sb[:, 0:14, :], st_b[:, 1:15, :], is_output=True)
                    to_tokmajor(lambda c: za[:, c, :, 30], za, NS, o_sa[:, 29, :], O_T1)
                    to_tokmajor(lambda c: zb[:, c, :, 15], zb, NS, o_sb[:, 14, :], O_T2)
                out_proj(Wout, yT, [(ti, c0, w, kind, g0)])

        def bf(bank):
            return bank.ap[:, :].bitcast(BF16)

        def attn(st, tiles, layer):
            Wq = load_w(slots[0], wq_x[layer], 0, 1024)
            Wo = load_w(slots[1], wo_x[layer], 0, 1024)
            qT = av(T0, [128, 8, 512], BF16)
            PT = av(T0 + 2048, [128, 8, 512], BF16)
            oT = av(T0 + 4096, [128, 8, 512], BF16)
            Pun2 = [av(T0 + 6144 + i * 1024, [128, 4, 256], F32) for i in range(2)]
            Pn2 = [av(T0 + 8192 + i * 512, [128, 4, 256], BF16) for i in range(2)]
            sm2 = [av(T0 + 9216 + i * 32, [128, 16], F32) for i in range(2)]
            for (ti, c0, w, kind, g0) in tiles:
                rmsnorm(ti, c0, w, G_X + 8 * layer)

            def qproj(tile):
                (ti, c0, w, kind, g0) = tile
                for fc in range(8):
                    b = proj(Wq, fc * 128, ti, c0, w)
                    act(qT[:, fc, 0:w], b[:, 0:w], AF.Identity, [b], [qT], scale=0.0625)
            qproj(tiles[0])
            pre_scores = []
            stile = [t for t in tiles if t[3] == "s"]
            kgen = sample_k_phase(layer, stile[0][0], stile[0][1], stile[0][2]) if stile else iter(())
            for tidx, (ti, c0, w, kind, g0) in enumerate(tiles):
                nxt = tiles[tidx + 1] if tidx + 1 < len(tiles) else None
                if nxt is not None and nxt[3] == "s":
                    nxt = None
                if kind == "p":
                    def stA(sb_):
                        tk = slice(sb_ * 128, (sb_ + 1) * 128)
                        bks = [nb(pin=True), nb(pin=True)]
                        for h in range(4):
                            bk = bks[h // 2]
                            off = (h % 2) * 256
                            mm(bk[:, off:off + 256], [(qT[:, 2 * h + dc, tk], KTp[layer][:, 2 * h + dc, :]) for dc in range(2)], [qT, KTp[layer]], [bk])
                        return bks

                    def stB(sb_, bks):
                        Pun = Pun2[sb_ % 2]
                        sm = sm2[sb_ % 2]
                        Pn = Pn2[sb_ % 2]
                        for i in range(2):
                            P.op("dve", lambda e, i=i, bks=bks, sm=sm: e.tensor_reduce(out=sm[:, 2 * i:2 * i + 2], in_=bks[i][:, :].rearrange("p (h m) -> p h m", m=256),
                                                                         axis=AX.X, op=ALU.max, negate=True), [bks[i]], [sm])
                        for h in range(4):
                            bk = bks[h // 2]
                            off = (h % 2) * 256
                            act(Pun[:, h, :], bk[:, off:off + 256], AF.Exp, [bk, sm], [Pun, sm], bias=sm[:, h:h + 1], scale=1.0, accum_out=sm[:, 4 + h:5 + h])
                        unpin(bks[0])
                        unpin(bks[1])
                        P.op("dve", lambda e, sm=sm: e.reciprocal(out=sm[:, 8:12], in_=sm[:, 4:8]), [sm], [sm])
                        for h in range(4):
                            if stile:
                                act(Pn[:, h, :], Pun[:, h, :], AF.Copy, [Pun, sm], [Pn], scale=sm[:, 8 + h:9 + h])
                            else:
                                ts(Pn[:, h, :], Pun[:, h, :], sm[:, 8 + h:9 + h], None, ALU.mult, None, [Pun, sm], [Pn])

                    def stC(sb_):
                        tk = slice(sb_ * 128, (sb_ + 1) * 128)
                        Pn = Pn2[sb_ % 2]
                        bT = nb()
                        for h in range(4):
                            for mc in range(2):
                                j = h * 2 + mc
                                tr(bf(bT)[:, j * 128:(j + 1) * 128], Pn[:, h, mc * 128:(mc + 1) * 128], ident_b[:], [Pn, ident_b], [bT])
                        if stile:
                            act(PT[:, :, tk], bf(bT).rearrange("p (a t) -> p a t", t=128), AF.Copy, [bT], [PT])
                        else:
                            vcopy(PT[:, :, tk], bf(bT).rearrange("p (a t) -> p a t", t=128), [bT], [PT])

                    bq_ = {0: pre_scores.pop() if pre_scores else stA(0)}
                    bq_[1] = stA(1)
                    stB(0, bq_[0])
                    for sb_ in range(4):
                        if sb_ + 2 < 4:
                            bq_[sb_ + 2] = stA(sb_ + 2)
                        elif sb_ + 2 == 4 and nxt is not None:
                            qproj(nxt)
                            nxt = None
                        if sb_ + 1 < 4:
                            stB(sb_ + 1, bq_[sb_ + 1])
                        stC(sb_)
                        for _ in range(3):
                            next(kgen, None)
                    if tidx + 1 < len(tiles) and tiles[tidx + 1][3] == "p":
                        pre_scores.append(stA(0))
                    for h in range(4):
                        for dc in range(2):
                            e_ = 2 * h + dc
                            b = nb()
                            mm(b[:, 0:w], [(Vp[layer][:, mc, e_ * 128:(e_ + 1) * 128], PT[:, 2 * h + mc, 0:w]) for mc in range(2)], [Vp[layer], PT], [b])
                            act(oT[:, e_, 0:w], b[:, 0:w], AF.Copy, [b], [oT])
                else:
                    for _ in kgen:
                        pass
                    attn_sample(layer, oT)
                for oc in range(8):
                    b = nb()
                    mm(b[:, 0:w], [(Wo[:, kc, oc * 128:(oc + 1) * 128], oT[:, kc, 0:w]) for kc in range(8)], [Wo, oT], [b])
                    add_to_x(ti, c0, w, oc, b)

        def sample_k_phase(layer, ti, c0, w):
            Y0 = YT_OFF
            NK = 5
            Kr = [av(Y0 + i * 1024, [128, 2, D], BF16) for i in range(NK)]
            q_tm = av(Y0 + 5120, [NS, D], BF16)
            junk = av(Y0 + 5632, [128, 256], F32)
            STs = av(Y0 + 5888, [128, 2, NS, 4], F32)
            qTs = av(Y0 + 6016, [128, 8, NS], BF16)
            Wq = wview(slots[0], 1024)
            for fc in range(8):
                b = proj(Wq, fc * 128, ti, c0, w)
                act(qTs[:, fc, :], b[:, 0:w], AF.Identity, [b], [qTs], scale=0.0625)
            bq = nb()
            for fc in range(8):
                tr(bf(bq)[0:NS, fc * 128:(fc + 1) * 128], qTs[:, fc, :], ident_b[:], [qTs, ident_b], [bq])
            act(q_tm[:, :], bf(bq)[0:NS, :], AF.Copy, [bq], [q_tm])
            yield
            for n in range(NS):
                Kb = Kr[n % NK]
                P.dma("pool", Kb[:], ck[layer, n].rearrange("(c p) f -> p c f", p=128), writes=[Kb])
                qb = [nb(), nb()]
                for i in range(2):
                    mm(qb[i][:, :], [(sel[:, n, :], q_tm[:, i * 512:(i + 1) * 512])], [sel, q_tm], [qb[i]])
                for mc in range(2):
                    for h in range(4):
                        stt(junk[:, :], Kb[:, mc, h * 256:(h + 1) * 256], 1.0, qb[h // 2][:, (h % 2) * 256:(h % 2) * 256 + 256], ALU.mult, ALU.mult,
                            [Kb, qb[h // 2]], [junk, STs], accum_out=STs[:, mc, n, h:h + 1])
                yield

        def attn_sample(layer, oT):
            A0 = T0 + 2048
            NV = 4
            KV = [av(A0 + i * 1024, [128, 2, D], BF16) for i in range(NV)]
            q_tm = av(YT_OFF + 5120, [NS, D], BF16)
            STs = av(YT_OFF + 5888, [128, 2, NS, 4], F32)
            S_sm = av(T0 + 6144 + 128, [64, 256], F32)
            PTs = av(T0 + 6144 + 384, [128, 2, NS, 4], F32)
            Pm = av(T0 + 6144 + 512, [128, 2, 4, NS, NS], BF16)
            sm = av(T0 + 7680, [128, 16], F32)
            bS = nb()
            for mc in range(2):
                tr(bS[0:64, mc * 128:(mc + 1) * 128], STs[:, mc, :, :].rearrange("p n h -> p (n h)"), ident_f[:], [STs, ident_f], [bS])
            P.op("dve", lambda e: e.tensor_reduce(out=sm[0:64, 0:1], in_=bS[0:64, 0:256], axis=AX.X, op=ALU.max, negate=True), [bS], [sm])
            act(S_sm[:, :], bS[0:64, 0:256], AF.Exp, [bS, sm], [S_sm, sm], bias=sm[0:64, 0:1], scale=1.0, accum_out=sm[0:64, 4:5])
            P.op("dve", lambda e: e.reciprocal(out=sm[0:64, 8:9], in_=sm[0:64, 4:5]), [sm], [sm])
            ts(S_sm[:, :], S_sm[:, :], sm[0:64, 8:9], None, ALU.mult, None, [S_sm, sm], [S_sm])
            bS2 = nb()
            for mc in range(2):
                tr(bS2[:, mc * 64:(mc + 1) * 64], S_sm[:, mc * 128:(mc + 1) * 128], ident_f[0:64, 0:64], [S_sm, ident_f], [bS2])
            act(PTs[:, :, :, :].rearrange("p c n h -> p (c n h)"), bS2[:, 0:128], AF.Copy, [bS2], [PTs])
            for mc in range(2):
                for h in range(4):
                    for n in range(NS):
                        ts(Pm[:, mc, h, n, :], maskM[:, n, :], PTs[:, mc, n, h:h + 1], None, ALU.mult, None, [maskM, PTs], [Pm])
            bo = [nb(pin=True) for _ in range(4)]
            for n in range(NS):
                Vb = KV[n % NV]
                P.dma("pool", Vb[:], cv[layer, n].rearrange("(c p) f -> p c f", p=128), writes=[Vb])
                for h in range(4):
                    for mc in range(2):
                        mm1(bo[h][0:NS, 0:256], Pm[:, mc, h, n, :], Vb[:, mc, h * 256:(h + 1) * 256], n == 0 and mc == 0, n == NS - 1 and mc == 1,
                            [Pm, Vb], [bo[h]])
            o_tm = q_tm
            for h in range(4):
                act(o_tm[:, h * 256:(h + 1) * 256], bo[h][0:NS, 0:256], AF.Copy, [bo[h]], [o_tm])
                unpin(bo[h])
            bq2 = nb()
            for kc in range(8):
                tr(bf(bq2)[:, kc * NS:(kc + 1) * NS], o_tm[:, kc * 128:(kc + 1) * 128], ident_b[0:NS, 0:NS], [o_tm, ident_b], [bq2])
            act(oT[:, :, 0:NS], bf(bq2)[:, 0:8 * NS].rearrange("p (a t) -> p a t", t=NS), AF.Copy, [bq2], [oT])

        def ffn(st, tiles, layer):
            for (ti, c0, w, kind, g0) in tiles:
                rmsnorm(ti, c0, w, G_FFN + 8 * layer)
            actb = av(YT_OFF, [128, 12, NT], BF16)
            wg_v = w_gate[layer].rearrange("(k p) f -> p k f", p=128)
            wu_v = w_up[layer].rearrange("(k p) f -> p k f", p=128)
            cnt = 0
            pi = 0
            groups = [(0, 12), (12, 10)]
            for gi, (f0, G) in enumerate(groups):
                Wd = Buf(slots[gi].ap[:, 0:G * D].rearrange("p (j f) -> p j f", f=D), slots[gi].deps)
                wd_v = w_down[layer][f0 * 128:(f0 + G) * 128, :].rearrange("(j p) f -> p j f", p=128)
                hG = G // 2
                P.dma("pool", Wd[:, 0:hG, :], wd_v[:, 0:hG, :], writes=slots[gi].deps[0:2])
                P.dma("pool", Wd[:, hG:G, :], wd_v[:, hG:G, :], writes=slots[gi].deps[2:4] + slot_extra[gi])
                for jp in range(0, G, 4):
                    fc = f0 + jp
                    nq = min(4, G - jp)
                    weg = av(T0 + (pi % 2) * 4096, [128, 8, 512], BF16)
                    weu = av(T0 + (pi % 2) * 4096 + 2048, [128, 8, 512], BF16)
                    pi += 1
                    P.dma("pool", weg[:, :, 0:nq * 128], wg_v[:, :, fc * 128:(fc + nq) * 128], writes=[weg])
                    P.dma("pool", weu[:, :, 0:nq * 128], wu_v[:, :, fc * 128:(fc + nq) * 128], writes=[weu])
                    for q in range(nq):
                        j = jp + q
                        for (ti, c0, w, kind, g0) in tiles:
                            bg = nb()
                            mm(bg[:, 0:w], [(weg[:, kc, q * 128:(q + 1) * 128], hT[:, kc, c0:c0 + w]) for kc in range(8)], [weg, hdp[ti]], [bg])
                            bu = nb()
                            mm(bu[:, 0:w], [(weu[:, kc, q * 128:(q + 1) * 128], hT[:, kc, c0:c0 + w]) for kc in range(8)], [weu, hdp[ti]], [bu])
                            sg = av(T0 + 8192, [128, 512], F32)
                            act(sg[:, 0:w], bg[:, 0:w], AF.Silu, [bg], [sg])
                            tt(actb[:, j, c0:c0 + w], sg[:, 0:w], bu[:, 0:w], ALU.mult, [sg, bu], [actb])
                for (ti, c0, w, kind, g0) in tiles:
                    for oc in range(8):
                        b = nb()
                        mm(b[:, 0:w], [(Wd[:, j, oc * 128:(oc + 1) * 128], actb[:, j, c0:c0 + w]) for j in range(G)], [Wd, actb], [b])
                        add_to_x(ti, c0, w, oc, b)

        def mixer_odd(st, tiles):
            yT = av(YT_OFF, [128, 8, NT], BF16)
            tmp1 = av(O_T1, [128, 512], F32)
            tmp2 = av(O_T2, [128, 512], F32)
            WC = load_w(slots[0], w_in_odd[0], 0, 1024)
            WD = load_w(slots[1], w_in_odd[0], 1024, 2560)
            for (ti, c0, w, kind, g0) in tiles:
                rmsnorm(ti, c0, w, G_MIX + 8)
            for (ti, c0, w, kind, g0) in tiles:
                u_sb = av(T0, [128, 4, 512], F32)
                for c in range(4):
                    b = proj(WC, c * 128, ti, c0, w)
                    act(u_sb[:, c, 0:w], b[:, 0:w], AF.Copy, [b], [u_sb])
                if kind == "p":
                    vt = av(T0 + 2048, [128, 512], F32)
                    vb = av(T0 + 2560, [128, 4, 512], BF16)
                    sm = av(T0 + 3584, [128, 16], F32)
                    hb_ = [nb(pin=True) for _ in range(4)]

                    def vproj(sb_):
                        tk = slice(c0 + sb_ * 128, c0 + (sb_ + 1) * 128)
                        bv = nb(pin=True)
                        mm(bv[:, :], [(hT[:, kc, tk], WC[:, kc, 512:1024]) for kc in range(8)], [hdp[ti], WC], [bv])
                        return bv
                    bvs = {0: vproj(0)}
                    for sb_ in range(4):
                        if sb_ + 1 < 4:
                            bvs[sb_ + 1] = vproj(sb_ + 1)
                        bv = bvs[sb_]
                        P.op("dve", lambda e, bv=bv, sm=sm: e.bn_stats(out=sm[:, 0:6], in_=bv[:, :]), [bv], [sm])
                        P.op("dve", lambda e, sm=sm: e.bn_aggr(out=sm[:, 8:10], in_=sm[:, 0:6]), [sm], [sm])
                        act(sm[:, 10:11], sm[:, 9:10], AF.Ln, [sm, eps_t], [sm], bias=eps_t[:, 0:1], scale=1.0)
                        act(sm[:, 10:11], sm[:, 10:11], AF.Exp, [sm], [sm], scale=-0.5)
                        ts(vt[:, :], bv[:, :], sm[:, 8:9], sm[:, 10:11], ALU.subtract, ALU.mult, [bv, sm], [vt])
                        unpin(bv)
                        tt(vt[:, :], vt[:, :], gC_bc[:], ALU.mult, [vt, gC_bc], [vt])
                        tt(vt[:, :], vt[:, :], bC_bc[:], ALU.add, [vt, bC_bc], [vt])
                        if g0 + sb_ * 128 == SEQ - 128:
                            P.dma("sp", o_pc[:, :], vt[:, :], reads=[vt], is_output=True)
                        vcopy(vb[:, sb_, :], vt[:, :], [vt], [vb])
                        for h in range(4):
                            mm(hb_[h][:, sb_ * 128:(sb_ + 1) * 128],
                               [(vb[:, sb_, h * 128:(h + 1) * 128], wsT[:, h, :]),
                                (ones_row[0:1, :], bsH[0:1, h * 128:(h + 1) * 128]),
                                (ones_row[0:1, :], bsL[0:1, h * 128:(h + 1) * 128])], [vb, wsT, ones_row, bsH, bsL], [hb_[h]])
                    for h in range(4):
                        tt(yT[:, h, c0:c0 + w], u_sb[:, h, 0:w], hb_[h][:, 0:w], ALU.mult, [u_sb, hb_[h]], [yT])
                        unpin(hb_[h])
                else:
                    vs = av(O_CA, [128, 4, NS], F32)
                    for c in range(4):
                        b = proj(WC, 512 + c * 128, ti, c0, w)
                        act(vs[:, c, :], b[:, 0:w], AF.Copy, [b], [vs])
                    ln_feat(lambda c: vs[:, c, :], vs, w, C_LCG, C_LCB, AF.Identity, lambda c: vs[:, c, :], vs)
                    to_tokmajor(lambda c: vs[:, c, :], vs, NS, o_sc[:, :], O_T1)
                    for c in range(4):
                        ts(tmp2[:, 0:w], vs[:, c, :], ws00[:, c:c + 1], bs0[:, c:c + 1], ALU.mult, ALU.add, [vs, ws00, bs0], [tmp2])
                        tt(yT[:, c, c0:c0 + w], u_sb[:, c, 0:w], tmp2[:, 0:w], ALU.mult, [u_sb, tmp2], [yT])
            for (ti, c0, w, kind, g0) in tiles:
                if kind == "p":
                    gd = av(T0, [128, 4, 514], F32)
                    vcopy(gd[:, :, 0:2], gdH[:], [gdH], [gd])
                    cur = lambda c: gd[:, c, 2:2 + w]
                    tap = lambda c, k: gd[:, c, k:k + w]
                else:
                    gd = av(T0, [128, 4, NS, 3], F32)
                    load_hist_T(st_d, 2, lambda c: gd[:, c, :, :], gd)
                    cur = lambda c: gd[:, c, :, 2]
                    tap = lambda c, k: gd[:, c, :, k]
                for c in range(4):
                    bgc = proj(WD, 512 + c * 128, ti, c0, w)
                    bhd = proj(WD, 1024 + c * 128, ti, c0, w)
                    act(tmp1[:, 0:w], bgc[:, 0:w], AF.Copy, [bgc], [tmp1])
                    tt(cur(c), tmp1[:, 0:w], bhd[:, 0:w], ALU.mult, [tmp1, bhd], [gd])
                    ts(tmp2[:, 0:w], tap(c, 0), prm[:, C_DW + c:C_DW + c + 1], None, ALU.mult, None, [gd, prm], [tmp2])
                    stt(tmp2[:, 0:w], tap(c, 1), prm[:, C_DW + 4 + c:C_DW + 5 + c], tmp2[:, 0:w], ALU.mult, ALU.add, [gd, prm, tmp2], [tmp2])
                    stt(tmp2[:, 0:w], tap(c, 2), prm[:, C_DW + 8 + c:C_DW + 9 + c], tmp2[:, 0:w], ALU.mult, ALU.add, [gd, prm, tmp2], [tmp2])
                    bgb = proj(WD, c * 128, ti, c0, w)
                    tt(yT[:, 4 + c, c0:c0 + w], tmp2[:, 0:w], bgb[:, 0:w], ALU.mult, [tmp2, bgb], [yT])
                if kind == "p":
                    vcopy(gdH[:], gd[:, :, 512:514], [gd], [gdH])
                    if g0 + 512 == SEQ:
                        to_tokmajor(lambda c: gd[:, c, 512:514], gd, 2, o_pd[:, :], O_T1)
                else:
                    P.dma("sp", o_sd[:, 0:1, :], st_d[:, 1:2, :], is_output=True)
                    to_tokmajor(lambda c: gd[:, c, :, 2], gd, NS, o_sd[:, 1, :], O_T1)
            Wout = load_w(slots[0], w_out_odd[0], 0, 1024)
            out_proj(Wout, yT, tiles)

        ST_TILES = [
            [(0, 0, 512, "p", 0), (1, 512, 512, "p", 512)],
            [(0, 0, 512, "p", 1024), (1, 512, 512, "p", 1536), (2, 1024, NS, "s", 0)],
        ]
        if stage < 1:
            ST_TILES = []
        import os
        if os.environ.get("NOSAMPLE"):
            ST_TILES = [[t for t in tl if t[3] == "p"] for tl in ST_TILES]

        for st, tiles in enumerate(ST_TILES):
            for (ti, c0, w, kind, g0) in tiles:
                if kind == "p":
                    for sb_ in range(4):
                        xs = av(T0 + (sb_ % 2) * 1024, [128, D], F32)
                        P.dma("sp", xs[:], x_p[g0 + sb_ * 128:g0 + (sb_ + 1) * 128, :], writes=[xs])
                        for kq in range(2):
                            b = nb()
                            for k4 in range(4):
                                kc = kq * 4 + k4
                                tr(b[:, k4 * 128:(k4 + 1) * 128], xs[:, kc * 128:(kc + 1) * 128], ident_f[:], [xs, ident_f], [b])
                            act(xT[:, kq * 4:kq * 4 + 4, c0 + sb_ * 128:c0 + (sb_ + 1) * 128],
                                b[:, :].rearrange("p (a t) -> p a t", t=128), AF.Copy, [b], [xd[kq * 4 + k4][ti] for k4 in range(4)])
                else:
                    xs = av(T0, [NS, D], F32)
                    P.dma("sp", xs[:], x_s[:, :], writes=[xs])
                    b = nb()
                    for kc in range(8):
                        tr(b[:, kc * NS:(kc + 1) * NS], xs[:, kc * 128:(kc + 1) * 128], ident_f[0:NS, 0:NS], [xs, ident_f], [b])
                    act(xT[:, :, c0:c0 + NS], b[:, 0:8 * NS].rearrange("p (a t) -> p a t", t=NS), AF.Copy, [b], xdeps(ti))

            nlayers = 0 if stage < 2 else (1 if stage < 5 else 2)
            for layer in range(nlayers):
                base = 2 + 3 * layer
                if layer == 0:
                    mixer_even(st, tiles)
                else:
                    mixer_odd(st, tiles)
                if stage >= base + 1:
                    attn(st, tiles, layer)
                if stage >= base + 2:
                    ffn(st, tiles, layer)
            if os.environ.get("RMSONLY"):
                for (ti, c0, w, kind, g0) in tiles:
                    rmsnorm(ti, c0, w, G_FIN)
            for (ti, c0, w, kind, g0) in (tiles if not os.environ.get("NOFINAL") else []):
                yf = av(T0, [128, 8, 512], F32)
                rmsnorm(ti, c0, w, G_FIN, final_dst=yf)
                if kind == "p":
                    for sb_ in range(4):
                        ys = av(T0 + 4096 + (sb_ % 2) * 1024, [128, D], F32)
                        for kq in range(2):
                            b = nb()
                            for k4 in range(4):
                                kc = kq * 4 + k4
                                tr(b[:, k4 * 128:(k4 + 1) * 128], yf[:, kc, sb_ * 128:(sb_ + 1) * 128], ident_f[:], [yf, ident_f], [b])
                            act(ys[:, kq * 512:(kq + 1) * 512], b[:, :], AF.Copy, [b], [ys])
                        P.dma("sp", y_p[g0 + sb_ * 128:g0 + (sb_ + 1) * 128, :], ys[:], reads=[ys], is_output=True)
                else:
                    ys = av(T0 + 4096, [NS, D], F32)
                    for kq in range(2):
                        b = nb()
                        for k4 in range(4):
                            kc = kq * 4 + k4
                            tr(b[0:NS, k4 * 128:(k4 + 1) * 128], yf[:, kc, 0:NS], ident_f[:], [yf, ident_f], [b])
                        act(ys[:, kq * 512:(kq + 1) * 512], b[0:NS, :], AF.Copy, [b], [ys])
                    P.dma("sp", y_s[:, :], ys[:], reads=[ys], is_output=True)

        P.finish()
        P.emit_all()
    return nc


_OUT_ORDER = ["y_p", "y_s", "o_pa", "o_pb", "o_pc", "o_pd", "o_pk", "o_pv", "o_sa", "o_sb", "o_sc", "o_sd"]


def make_in_maps(inp):
    f = lambda a: np.ascontiguousarray(np.asarray(a, dtype=np.float32))
    maps = []
    shared = {k: f(inp[k]) for k in ["norm_mix", "norm_x", "norm_ffn", "norm_final", "w_in_even", "conv_a_w", "conv_a_b",
                                     "ln_a_g", "ln_a_b", "pool_b_w", "pool_b_scale", "w_out_even", "w_in_odd", "ln_c_g",
                                     "ln_c_b", "ws_c", "bs_c", "conv_d_w", "w_out_odd", "wq_x", "wk_x", "wv_x", "wo_x",
                                     "w_gate", "w_up", "w_down"]}
    for c in range(NCORE):
        s = slice(c * NS, (c + 1) * NS)
        m = dict(shared)
        m["x_p"] = f(inp["x_prompt"][c])
        m["x_s"] = f(inp["x_sample"][s, 0])
        m["mem"] = f(inp["mem_prompt"][c])
        m["st_a"] = f(inp["state_convA"][0, s])
        m["st_b"] = f(inp["state_poolB"][0, s])
        m["st_d"] = f(inp["state_convD"][0, s])
        m["ck"] = f(np.asarray(inp["cache_mem_k"])[:, s].reshape(2, NS, 256, D))
        m["cv"] = f(np.asarray(inp["cache_mem_v"])[:, s].reshape(2, NS, 256, D))
        maps.append(m)
    return maps


def assemble(results):
    r = results
    cat = lambda k: np.stack([np.asarray(r[c][k]) for c in range(NCORE)])
    y_prompt = cat("y_p")
    y_sample = np.concatenate([np.asarray(r[c]["y_s"]) for c in range(NCORE)], 0)[:, None, :]
    p_convA = cat("o_pa")[None]
    p_poolB = cat("o_pb")[None]
    p_chunkC = cat("o_pc").reshape(1, NCORE, 128, 4, 128)
    p_convD = cat("o_pd")[None]
    p_mem_k = np.stack([np.asarray(r[c]["o_pk"]) for c in range(NCORE)], 1).reshape(2, NCORE, 256, 4, 256)
    p_mem_v = np.stack([np.asarray(r[c]["o_pv"]) for c in range(NCORE)], 1).reshape(2, NCORE, 256, 4, 256)
    s_convA = np.concatenate([np.asarray(r[c]["o_sa"]) for c in range(NCORE)], 0)[None]
    s_poolB = np.concatenate([np.asarray(r[c]["o_sb"]) for c in range(NCORE)], 0)[None]
    s_chunkC = np.concatenate([np.asarray(r[c]["o_sc"]) for c in range(NCORE)], 0).reshape(1, NCORE * NS, 1, 4, 128)
    s_convD = np.concatenate([np.asarray(r[c]["o_sd"]) for c in range(NCORE)], 0)[None]
    outs = (y_prompt, y_sample, p_convA, p_poolB, p_chunkC, p_convD, p_mem_k, p_mem_v, s_convA, s_poolB, s_chunkC, s_convD)
    return tuple(np.ascontiguousarray(o, dtype=np.float32) for o in outs)


def kernel(**inputs):
    nc = build_program()
    in_maps = make_in_maps(inputs)
    res = run_bass_kernel_spmd(nc, in_maps, core_ids=list(range(NCORE)))
    return assemble(res.results)
```

```python
import numpy as np
import concourse.bass as bass
import concourse.mybir as mybir
from concourse.bass_utils import run_bass_kernel_spmd
from contextlib import ExitStack

F32 = mybir.dt.float32
BF16 = mybir.dt.bfloat16
I32 = mybir.dt.int32
ALU = mybir.AluOpType
AF = mybir.ActivationFunctionType
AX = mybir.AxisListType

NCORE = 8
D = 1024
SEQ = 2048
NS = 16
DFF = 2816
NFC = 22
EPS = 1e-6


class Dep:
    __slots__ = ("w", "r", "excl")

    def __init__(self, excl=False):
        self.w = None
        self.r = []
        self.excl = excl


class Buf:
    def __init__(self, ap, deps):
        self.ap = ap
        self.deps = list(deps)

    def __getitem__(self, idx):
        return self.ap[idx]


def _flat(xs):
    out = []
    for x in xs:
        if x is None:
            continue
        if isinstance(x, Dep):
            out.append(x)
        elif isinstance(x, Buf):
            out.extend(x.deps)
        else:
            out.extend(_flat(x))
    return out


class Prog:
    CE = ("pe", "act", "dve", "pool")
    ENGS = ("pe", "act", "dve", "pool", "sp")

    def __init__(self, nc, es, ring=16):
        self.nc = nc
        self.rec = {e: [] for e in self.ENGS}
        self.nops = {e: 0 for e in self.CE}
        self.semobj = {}
        for e in self.CE:
            self.semobj[("e", e)] = es.enter_context(nc.semaphore("s_" + e))
        self.ring = ring
        self.dq = ("sp", "pool")
        for q in self.dq:
            for i in range(ring):
                self.semobj[("d", q, i)] = es.enter_context(nc.semaphore("d_%s%d" % (q, i)))
        self.dcnt = {q: 0 for q in self.dq}
        self.out_tokens = []

    @staticmethod
    def _needs(reads, writes, extra=()):
        need = {}

        def add(tok):
            if tok is None:
                return
            k, v = tok
            if need.get(k, 0) < v:
                need[k] = v
        for d in reads:
            add(d.w)
        for d in writes:
            add(d.w)
            for t in d.r:
                add(t)
        for t in extra:
            add(t)
        return need

    @staticmethod
    def _commit(tok, reads, writes):
        for d in reads:
            d.r.append(tok)
        for d in writes:
            d.w = tok
            d.r = []

    def op(self, eng, fn, reads=(), writes=()):
        reads = _flat(reads)
        writes = _flat(writes)
        writes = writes + [d for d in reads if d.excl]
        reads = [d for d in reads if not d.excl]
        need = self._needs(reads, writes)
        self.nops[eng] += 1
        tok = (("c", eng), self.nops[eng])
        self.rec[eng].append({"kind": "op", "fn": fn, "need": need, "ord": self.nops[eng]})
        self._commit(tok, reads, writes)
        return tok

    def dma(self, q, out, in_, reads=(), writes=(), is_output=False, **kw):
        reads = _flat(reads)
        writes = _flat(writes)
        j = self.dcnt[q]
        self.dcnt[q] += 1
        slot = j % self.ring
        rnd = j // self.ring
        key = ("d", q, slot)
        extra = [(key, 16 * rnd)] if rnd > 0 else []
        need = self._needs(reads, writes, extra)
        tok = (key, 16 * (rnd + 1))
        self.rec[q].append({"kind": "dma", "out": out, "in": in_, "kw": kw, "need": need, "sem": key})
        self._commit(tok, reads, writes)
        if is_output:
            self.out_tokens.append(tok)
        return tok

    def finish(self):
        allt = list(self.out_tokens)
        for q in self.dq:
            n = self.dcnt[q]
            for slot in range(self.ring):
                k = (n - slot + self.ring - 1) // self.ring if n > slot else 0
                if k > 0:
                    allt.append((("d", q, slot), 16 * k))
        self.rec["sp"].append({"kind": "wait", "need": self._needs((), (), allt)})

    def emit_all(self):
        signal = {e: set() for e in self.CE}
        for E in self.ENGS:
            waited = {}
            for r in self.rec[E]:
                w = []
                for k, v in r["need"].items():
                    if k[0] == "c" and k[1] == "pe" and E == "pe":
                        continue
                    if waited.get(k, 0) < v:
                        waited[k] = v
                        w.append((k, v))
                        if k[0] == "c":
                            signal[k[1]].add(v)
                r["waits"] = w
        val = {}
        for e in self.CE:
            c = 0
            m = {}
            for o in range(1, self.nops[e] + 1):
                if o in signal[e]:
                    c += 1
                m[o] = c
            val[e] = m
        self.n_signals = {e: len(signal[e]) for e in self.CE}

        def run(E, e):
            for r in self.rec[E]:
                for k, v in r["waits"]:
                    if k[0] == "c":
                        e.wait_ge(self.semobj[("e", k[1])], val[k[1]][v])
                    else:
                        e.wait_ge(self.semobj[k], v)
                if r["kind"] == "op":
                    inst = r["fn"](e)
                    if r["ord"] in signal[E]:
                        inst.then_inc(self.semobj[("e", E)], 1)
                elif r["kind"] == "dma":
                    e.dma_start(out=r["out"], in_=r["in"], **r["kw"]).then_inc(self.semobj[r["sem"]], 16)

        with self.nc.Block() as block:
            @block.sync
            def _(e):
                run("sp", e)

            @block.tensor
            def _(e):
                run("pe", e)

            @block.scalar
            def _(e):
                run("act", e)

            @block.vector
            def _(e):
                run("dve", e)

            @block.gpsimd
            def _(e):
                run("pool", e)


G_MIX, G_X, G_FFN, G_FIN = 0, 16, 32, 48
C_AB, C_LAG, C_LAB, C_PBS, C_LCG, C_LCB, C_DW = 56, 60, 64, 68, 72, 76, 80
NPRM = 92


def build_program(stage=99):
    import os
    nc = bass.Bass("TRN2", target_bir_lowering=False)

    def din(name, shape):
        return nc.dram_tensor(name, list(shape), F32, kind="ExternalInput").ap()

    def dout(name, shape):
        return nc.dram_tensor(name, list(shape), F32, kind="ExternalOutput").ap()

    x_p = din("x_p", [SEQ, D]); x_s = din("x_s", [NS, D]); mem = din("mem", [256, D])
    st_a = din("st_a", [NS, 30, 512]); st_b = din("st_b", [NS, 15, 512]); st_d = din("st_d", [NS, 2, 512])
    ck = din("ck", [2, NS, 256, D]); cv = din("cv", [2, NS, 256, D])
    norm_mix = din("norm_mix", [2, D]); norm_x = din("norm_x", [2, D]); norm_ffn = din("norm_ffn", [2, D])
    norm_final = din("norm_final", [D])
    w_in_even = din("w_in_even", [1, D, 1536]); conv_a_w = din("conv_a_w", [1, 31, 512])
    conv_a_b = din("conv_a_b", [1, 512]); ln_a_g = din("ln_a_g", [1, 512]); ln_a_b = din("ln_a_b", [1, 512])
    pool_b_w = din("pool_b_w", [1, 4, 128, 128]); pool_b_scale = din("pool_b_scale", [1, 512])
    w_out_even = din("w_out_even", [1, D, D]); w_in_odd = din("w_in_odd", [1, D, 2560])
    ln_c_g = din("ln_c_g", [1, 512]); ln_c_b = din("ln_c_b", [1, 512]); ws_c = din("ws_c", [1, 4, 128, 128])
    bs_c = din("bs_c", [1, 4, 128]); conv_d_w = din("conv_d_w", [1, 3, 512]); w_out_odd = din("w_out_odd", [1, D, D])
    wq_x = din("wq_x", [2, D, D]); wk_x = din("wk_x", [2, D, D]); wv_x = din("wv_x", [2, D, D]); wo_x = din("wo_x", [2, D, D])
    w_gate = din("w_gate", [2, D, DFF]); w_up = din("w_up", [2, D, DFF]); w_down = din("w_down", [2, DFF, D])

    y_p = dout("y_p", [SEQ, D]); y_s = dout("y_s", [NS, D])
    o_pa = dout("o_pa", [30, 512]); o_pb = dout("o_pb", [15, 512]); o_pc = dout("o_pc", [128, 512]); o_pd = dout("o_pd", [2, 512])
    o_pk = dout("o_pk", [2, 256, D]); o_pv = dout("o_pv", [2, 256, D])
    o_sa = dout("o_sa", [NS, 30, 512]); o_sb = dout("o_sb", [NS, 15, 512]); o_sc = dout("o_sc", [NS, 512]); o_sd = dout("o_sd", [NS, 2, 512])

    with ExitStack() as es:
        P = Prog(nc, es)

        def sbt(name, shape, dt, ndeps=1):
            t = es.enter_context(nc.sbuf_tensor(name, list(shape), dt))
            return Buf(t, [Dep() for _ in range(ndeps)])

        NT = 1040
        xT = sbt("xT", [128, 8, NT], F32)
        xd = [[Dep() for _ in range(3)] for _ in range(8)]
        hT = sbt("hT", [128, 8, NT], BF16)
        hdp = [Dep() for _ in range(3)]
        KTp = [sbt("KTp%d" % l, [128, 8, 256], BF16) for l in range(2)]
        Vp = [sbt("Vp%d" % l, [128, 2, D], BF16) for l in range(2)]
        ident_f = sbt("ident_f", [128, 128], F32)
        ident_b = sbt("ident_b", [128, 128], BF16)
        ones_rms = sbt("ones_rms", [128, 128], BF16)
        ones_ln = sbt("ones_ln", [128, 128], BF16)
        ones_row = sbt("ones_row", [1, 128], BF16)
        prm = sbt("prm", [128, NPRM], F32)
        cw = sbt("cw", [128, 124], F32)
        eps_t = sbt("eps_t", [128, 1], F32)
        gC_bc = sbt("gC_bc", [128, 512], F32)
        bC_bc = sbt("bC_bc", [128, 512], F32)
        wsT = sbt("wsT", [128, 4, 128], BF16)
        poolw = sbt("poolw", [128, 4, 128], BF16)
        bsF = sbt("bsF", [1, 512], F32)
        bsH = sbt("bsH", [1, 512], BF16)
        bsL = sbt("bsL", [1, 512], BF16)
        ws00 = sbt("ws00", [128, 4], F32)
        bs0 = sbt("bs0", [128, 4], F32)
        invc = sbt("invc", [128, 16], F32)
        zaH = sbt("zaH", [128, 4, 30], F32)
        zbH = sbt("zbH", [128, 4, 15], F32)
        gdH = sbt("gdH", [128, 4, 2], F32)
        slots = [sbt("slot%d" % i, [128, 12288], BF16, ndeps=4) for i in range(2)]
        DgA = Buf(slots[1].ap[:, 8192:10240].rearrange("p (k f) -> p k f", f=128), [Dep()])
        DgB = Buf(slots[1].ap[:, 10240:12288].rearrange("p (k f) -> p k f", f=128), [Dep()])
        slot_extra = {1: [DgA, DgB], 0: []}
        AW = 18112
        arena_t = es.enter_context(nc.sbuf_tensor("arena", [128, AW], F32))
        CH = 32
        adeps = [Dep() for _ in range((AW + CH - 1) // CH)]

        def av(off, shape, dt):
            n = 1
            for s in shape[1:]:
                n *= s
            words = n if dt in (F32, I32) else (n + 1) // 2
            assert off + words <= AW, (off, words)
            ap = arena_t[0:shape[0], off:off + words]
            if dt != F32:
                ap = ap.bitcast(dt)
            if len(shape) >= 3:
                names = "abcdef"[:len(shape) - 1]
                kw = {names[i]: shape[i + 1] for i in range(1, len(names))}
                ap = ap.rearrange("p (%s) -> p %s" % (" ".join(names), " ".join(names)), **kw)
            assert off % CH == 0, off
            deps = adeps[off // CH:(off + words - 1) // CH + 1]
            return Buf(ap, deps)

        sq_sep = sbt("sq_sep", [128, 8, 512], BF16) if os.environ.get("SQSEP") else None
        banks = []
        for i in range(8):
            t = es.enter_context(nc.psum_tensor("pb%d" % i, [128, 512], F32))
            banks.append(Buf(t, [Dep(excl=True)]))
        bank_i = [0]
        pinned = set()

        def nb(pin=False):
            while (bank_i[0] % 8) in pinned:
                bank_i[0] += 1
            i = bank_i[0] % 8
            bank_i[0] += 1
            if pin:
                pinned.add(i)
            return banks[i]

        def unpin(b):
            pinned.discard(banks.index(b))

        def mm(out_ap, pairs, reads, writes):
            def fn(e, pairs=pairs, out_ap=out_ap):
                n = len(pairs)
                inst = None
                for i, (l, r) in enumerate(pairs):
                    inst = e.matmul(out_ap, lhsT=l, rhs=r, start=(i == 0), stop=(i == n - 1))
                return inst
            P.op("pe", fn, reads, writes)

        def mmg(out_ap, pairs, first, last, reads, writes):
            def fn(e, pairs=pairs, out_ap=out_ap, first=first, last=last):
                n = len(pairs)
                inst = None
                for i, (l, r) in enumerate(pairs):
                    inst = e.matmul(out_ap, lhsT=l, rhs=r, start=(first and i == 0), stop=(last and i == n - 1))
                return inst
            P.op("pe", fn, reads, writes)

        def mm1(out_ap, lhsT, rhs, start, stop, reads, writes):
            P.op("pe", lambda e: e.matmul(out_ap, lhsT=lhsT, rhs=rhs, start=start, stop=stop), reads, writes)

        def tr(out_ap, in_ap, ident_ap, reads, writes):
            P.op("pe", lambda e: e.transpose(out=out_ap, in_=in_ap, identity=ident_ap), reads, writes)

        def act(out, in_, func, reads, writes, **kw):
            P.op("act", lambda e: e.activation(out=out, in_=in_, func=func, **kw), reads, writes)

        def tt(out, in0, in1, op, reads, writes):
            P.op("dve", lambda e: e.tensor_tensor(out=out, in0=in0, in1=in1, op=op), reads, writes)

        def ts(out, in0, s1, s2, op0, op1, reads, writes):
            if op1 is None:
                P.op("dve", lambda e: e.tensor_scalar(out=out, in0=in0, scalar1=s1, scalar2=None, op0=op0), reads, writes)
            else:
                P.op("dve", lambda e: e.tensor_scalar(out=out, in0=in0, scalar1=s1, scalar2=s2, op0=op0, op1=op1), reads, writes)

        def stt(out, in0, scalar, in1, op0, op1, reads, writes, **kw):
            P.op("dve", lambda e: e.scalar_tensor_tensor(out=out, in0=in0, scalar=scalar, in1=in1, op0=op0, op1=op1, **kw), reads, writes)

        def vcopy(out, in_, reads, writes):
            P.op("dve", lambda e: e.tensor_copy(out=out, in_=in_), reads, writes)

        if stage < 0:
            tt_ = av(0, [NS, D], F32)
            P.dma("sp", tt_[:], x_s[:, :], writes=[tt_])
            P.dma("sp", y_s[:, :], tt_[:], reads=[tt_], is_output=True)
            P.finish()
            P.emit_all()
            return nc
        P.op("pool", lambda e: e.memset(ident_f[:], 1.0), [], [ident_f])
        P.op("pool", lambda e: e.affine_select(out=ident_f[:], in_=ident_f[:], pattern=[[-1, 128]], compare_op=ALU.is_equal,
                                               fill=0.0, base=0, channel_multiplier=1), [ident_f], [ident_f])
        vcopy(ident_b[:], ident_f[:], [ident_f], [ident_b])
        P.op("pool", lambda e: e.memset(ones_rms[:], 1.0 / 1024.0), [], [ones_rms])
        P.op("pool", lambda e: e.memset(ones_ln[:], 1.0 / 512.0), [], [ones_ln])
        P.op("pool", lambda e: e.memset(ones_row[:], 1.0), [], [ones_row])
        P.op("pool", lambda e: e.memset(eps_t[:], EPS), [], [eps_t])
        P.op("pool", lambda e: e.memset(zaH[:], 0.0), [], [zaH])
        P.op("pool", lambda e: e.memset(zbH[:], 0.0), [], [zbH])
        P.op("pool", lambda e: e.memset(gdH[:], 0.0), [], [gdH])
        ii = av(0, [128, 16], I32)
        P.op("pool", lambda e: e.iota(ii[:], pattern=[[1, 16]], base=1, channel_multiplier=0), [], [ii])
        vcopy(invc[:], ii[:], [ii], [invc])
        P.op("dve", lambda e: e.reciprocal(out=invc[:], in_=invc[:]), [invc], [invc])

        sel = sbt("sel", [NS, NS, 128], BF16)
        maskM = sbt("maskM", [128, NS, NS], BF16)
        P.op("pool", lambda e: e.memset(sel[:], 1.0), [], [sel])
        P.op("pool", lambda e: e.affine_select(out=sel[:], in_=sel[:], pattern=[[-1, NS], [0, 128]], compare_op=ALU.is_equal,
                                               fill=0.0, base=0, channel_multiplier=1), [sel], [sel])
        P.op("pool", lambda e: e.memset(maskM[:], 1.0), [], [maskM])
        P.op("pool", lambda e: e.affine_select(out=maskM[:], in_=maskM[:], pattern=[[1, NS], [-1, NS]], compare_op=ALU.is_equal,
                                               fill=0.0, base=0, channel_multiplier=0), [maskM], [maskM])
        R1 = av(512, [128, 128], F32)
        R2 = av(1024, [128, 128], F32)
        P.op("pool", lambda e: e.memset(R1[:], 0.0), [], [R1])
        P.op("pool", lambda e: e.memset(R2[:], 0.0), [], [R2])

        def rows(dst, r0, src, n):
            P.dma("sp", dst[r0:r0 + n, :], src.rearrange("(c p) -> c p", p=128), writes=[dst])
        for l in range(2):
            rows(R1, G_MIX + 8 * l, norm_mix[l], 8)
            rows(R1, G_X + 8 * l, norm_x[l], 8)
            rows(R1, G_FFN + 8 * l, norm_ffn[l], 8)
        rows(R1, G_FIN, norm_final, 8)
        rows(R1, C_AB, conv_a_b[0], 4); rows(R1, C_LAG, ln_a_g[0], 4); rows(R1, C_LAB, ln_a_b[0], 4)
        rows(R1, C_PBS, pool_b_scale[0], 4); rows(R1, C_LCG, ln_c_g[0], 4); rows(R1, C_LCB, ln_c_b[0], 4)
        for k in range(3):
            rows(R1, C_DW + 4 * k, conv_d_w[0, k], 4)
        P.dma("sp", R2[0:124, :], conv_a_w[0].rearrange("k (c p) -> (k c) p", p=128), writes=[R2])
        b = nb()
        tr(b[:, 0:128], R1[:], ident_f[:], [R1, ident_f], [b])
        vcopy(prm[:], b[:, 0:NPRM], [b], [prm])
        b = nb()
        tr(b[:, 0:128], R2[:], ident_f[:], [R2, ident_f], [b])
        vcopy(cw[:], b[:, 0:124], [b], [cw])
        P.dma("sp", gC_bc[:], ln_c_g[0].partition_broadcast(128), writes=[gC_bc])
        P.dma("sp", bC_bc[:], ln_c_b[0].partition_broadcast(128), writes=[bC_bc])
        P.dma("sp", bsF[:], bs_c[0].rearrange("h i -> (h i)").partition_broadcast(1), writes=[bsF])
        vcopy(bsH[:], bsF[:], [bsF], [bsH])
        tt(bsF[:], bsF[:], bsH[:], ALU.subtract, [bsF, bsH], [bsF])
        vcopy(bsL[:], bsF[:], [bsF], [bsL])
        if not os.environ.get("SKIPX"):
            P.dma("sp", ws00[:], ws_c[0, :, 0, 0].partition_broadcast(128), writes=[ws00], allow_slow_non_contiguous=True)
            P.dma("sp", bs0[:], bs_c[0, :, 0].partition_broadcast(128), writes=[bs0], allow_slow_non_contiguous=True)
        wsf = av(1536, [128, 4, 128], F32)
        P.dma("sp", wsf[:], ws_c[0].rearrange("h i j -> i h j"), writes=[wsf])
        P.op("pool", lambda e: e.affine_select(out=wsf[:], in_=wsf[:], pattern=[[0, 4], [-1, 128]], compare_op=ALU.is_ge,
                                               fill=0.0, base=0, channel_multiplier=1), [wsf], [wsf])
        for h in range(4):
            b = nb()
            tr(b[:, 0:128], wsf[:, h, :], ident_f[:], [wsf, ident_f], [b])
            vcopy(wsT[:, h, :], b[:, 0:128], [b], [wsT])
        P.dma("pool", poolw[:], pool_b_w[0].rearrange("g c e -> c g e"), writes=[poolw])

        memf = av(2048, [128, 2, D], F32)
        memT = av(4096, [128, 8, 256], BF16)
        stg = av(5120, [128, D], F32)
        P.dma("sp", memf[:], mem.rearrange("(c p) f -> p c f", p=128), writes=[memf])
        for kc in range(8):
            b = nb()
            for mc in range(2):
                tr(b[:, mc * 128:(mc + 1) * 128], memf[:, mc, kc * 128:(kc + 1) * 128], ident_f[:], [memf, ident_f], [b])
            act(memT[:, kc, :], b[:, 0:256], AF.Copy, [b], [memT])
        for l in range(2 if not os.environ.get("SKIPKV") else 0):
            wk = Buf(slots[0].ap[:, 0:8192].rearrange("p (k f) -> p k f", f=D), slots[0].deps)
            wv = Buf(slots[1].ap[:, 0:8192].rearrange("p (k f) -> p k f", f=D), slots[1].deps)
            P.dma("pool", wk[:], wk_x[l].rearrange("(k p) f -> p k f", p=128), writes=[wk])
            P.dma("pool", wv[:], wv_x[l].rearrange("(k p) f -> p k f", p=128), writes=[wv])
            for ec in range(8):
                b = nb()
                mm(b[:, 0:256], [(wk[:, kc, ec * 128:(ec + 1) * 128], memT[:, kc, :]) for kc in range(8)], [wk, memT], [b])
                act(KTp[l][:, ec, :], b[:, 0:256], AF.Copy, [b], [KTp[l]])
            for (w_, dst, isv) in ((wk, o_pk, False), (wv, o_pv, True)):
                for mc in range(2):
                    for hf in range(2):
                        b = nb()
                        mm(b[:, :], [(memT[:, kc, mc * 128:(mc + 1) * 128], w_[:, kc, hf * 512:(hf + 1) * 512]) for kc in range(8)],
                           [w_, memT], [b])
                        act(stg[:, hf * 512:(hf + 1) * 512], b[:, :], AF.Copy, [b], [stg])
                        if isv:
                            vcopy(Vp[l][:, mc, hf * 512:(hf + 1) * 512], b[:, :], [b], [Vp[l]])
                    P.dma("sp", dst[l, mc * 128:(mc + 1) * 128, :], stg[:], reads=[stg], is_output=True)

        def xdeps(ti):
            return [xd[k][ti] for k in range(8)]

        SQ_OFF, RS_OFF = 0, 2048

        def rmsnorm(ti, c0, w, gcol, final_dst=None):
            sq = av(SQ_OFF, [128, 8, 512], BF16)
            rstd = av(RS_OFF, [128, 512], F32)
            if os.environ.get("SQSEP"):
                sq = sq_sep
            RL = int(os.environ.get("RMS_LEVEL", "9"))
            if os.environ.get("SQ2D"):
                for k in range(8):
                    act(sq[:, k, 0:w], xT[:, k, c0:c0 + w], AF.Square, [xd[k][ti]], [sq])
            else:
                act(sq[:, :, 0:w], xT[:, :, c0:c0 + w], AF.Square, xdeps(ti), [sq])
            if RL < 2:
                return
            b = nb()
            mm(b[:, 0:w], [(ones_rms[:], sq[:, k, 0:w]) for k in range(8)], [sq, ones_rms], [b])
            if RL < 3:
                return
            act(rstd[:, 0:w], b[:, 0:w], AF.Ln, [b, eps_t], [rstd], bias=eps_t[:, 0:1], scale=1.0)
            if RL < 4:
                return
            act(rstd[:, 0:w], rstd[:, 0:w], AF.Exp, [rstd], [rstd], scale=-0.5)
            if RL < 5:
                return
            for k in range(8):
                if final_dst is None:
                    stt(hT[:, k, c0:c0 + w], xT[:, k, c0:c0 + w], prm[:, gcol + k:gcol + k + 1], rstd[:, 0:w], ALU.mult, ALU.mult,
                        [xd[k][ti], rstd, prm], [hdp[ti]])
                else:
                    stt(final_dst[:, k, 0:w], xT[:, k, c0:c0 + w], prm[:, gcol + k:gcol + k + 1], rstd[:, 0:w], ALU.mult, ALU.mult,
                        [xd[k][ti], rstd, prm], [final_dst])

        def proj(Wb, f0, ti, c0, w, src=None, srcdeps=None):
            src = hT if src is None else src
            sd = [hdp[ti]] if srcdeps is None else srcdeps
            b = nb()
            mm(b[:, 0:w], [(Wb[:, kc, f0:f0 + 128], src[:, kc, c0:c0 + w]) for kc in range(8)], [Wb] + sd, [b])
            return b

        def wview(slot, ncols):
            return Buf(slot.ap[:, 0:8 * ncols].rearrange("p (k f) -> p k f", f=ncols), slot.deps)

        def load_w(slot, src2d, c_lo, c_hi):
            n = c_hi - c_lo
            Wb = wview(slot, n)
            s = src2d.rearrange("(k p) f -> p k f", p=128)
            for k0 in range(0, 8, 2):
                P.dma("pool", Wb[:, k0:k0 + 2, :], s[:, k0:k0 + 2, c_lo:c_hi], reads=[],
                      writes=[slot.deps[k0 // 2]] + (slot_extra[slots.index(slot)] if 8 * n > 8192 else []))
            return Wb

        def add_to_x(ti, c0, w, oc, b):
            tt(xT[:, oc, c0:c0 + w], xT[:, oc, c0:c0 + w], b[:, 0:w], ALU.add, [xd[oc][ti], b], [xd[oc][ti]])

        def out_proj(Wb, yT, tiles):
            for (ti, c0, w, kind, g0) in tiles:
                for oc in range(8):
                    b = nb()
                    mm(b[:, 0:w], [(Wb[:, kc, oc * 128:(oc + 1) * 128], yT[:, kc, c0:c0 + w]) for kc in range(8)], [Wb, yT], [b])
                    add_to_x(ti, c0, w, oc, b)

        YT_OFF = 2560
        T0 = YT_OFF + 6240


        cwv = Buf(cw.ap.rearrange("p (k c) -> p c k", c=4), cw.deps)
        O_ZA, O_CA, O_ZB, O_T1, O_T2, O_CB = T0, T0 + 2176, T0 + 4224, T0 + 6336, T0 + 6848, T0 + 7360

        def ln_feat(get_src, srcbuf, w, gcol, bcol, func, get_dst, dstbuf):
            tmp1 = av(O_T1, [128, 512], F32)
            tmp2 = av(O_T2, [128, 512], F32)
            cbs = [av(O_CB + 256 * i, [128, 512], BF16) for i in range(4)]
            bm = nb()
            bq = nb()
            for c in range(4):
                cab, csq = cbs[(c % 2) * 2], cbs[(c % 2) * 2 + 1]
                act(cab[:, 0:w], get_src(c), AF.Copy, [srcbuf], [cab])
                act(csq[:, 0:w], get_src(c), AF.Square, [srcbuf], [csq])
                mm1(bm[:, 0:w], ones_ln[:], cab[:, 0:w], c == 0, c == 3, [ones_ln, cab], [bm])
                mm1(bq[:, 0:w], ones_ln[:], csq[:, 0:w], c == 0, c == 3, [ones_ln, csq], [bq])
            act(tmp1[:, 0:w], bm[:, 0:w], AF.Square, [bm], [tmp1])
            tt(tmp1[:, 0:w], bq[:, 0:w], tmp1[:, 0:w], ALU.subtract, [bq, tmp1], [tmp1])
            act(tmp1[:, 0:w], tmp1[:, 0:w], AF.Ln, [tmp1, eps_t], [tmp1], bias=eps_t[:, 0:1], scale=1.0)
            act(tmp1[:, 0:w], tmp1[:, 0:w], AF.Exp, [tmp1], [tmp1], scale=-0.5)
            act(tmp2[:, 0:w], bm[:, 0:w], AF.Copy, [bm], [tmp2])
            for c in range(4):
                tt(get_src(c), get_src(c), tmp2[:, 0:w], ALU.subtract, [srcbuf, tmp2], [srcbuf])
                tt(get_src(c), get_src(c), tmp1[:, 0:w], ALU.mult, [srcbuf, tmp1], [srcbuf])
                act(get_dst(c), get_src(c), func, [srcbuf, prm], [dstbuf], scale=prm[:, gcol + c:gcol + c + 1], bias=prm[:, bcol + c:bcol + c + 1])

        def to_tokmajor(get_src, srcbuf, nrow, dst_dram, stg_off):
            b = nb()
            for c in range(4):
                tr(b[0:nrow, c * 128:(c + 1) * 128], get_src(c), ident_f[:], [srcbuf, ident_f], [b])
            sg = av(stg_off, [32, 512], F32)
            act(sg[0:nrow, :], b[0:nrow, :], AF.Copy, [b], [sg])
            P.dma("sp", dst_dram, sg[0:nrow, :], reads=[sg], is_output=True)

        def load_hist_T(src_dram, nk, dst, dstbuf):
            per = 120 // nk if nk > 8 else 16
            per = min(per, NS)
            while NS % per:
                per -= 1
            rows = per * nk
            for j in range(NS // per):
                raw = av(O_T1, [128, 512], F32)
                P.dma("sp", raw[0:rows, :], src_dram[j * per:(j + 1) * per].rearrange("n k f -> (n k) f"), writes=[raw])
                for c in range(4):
                    b = nb()
                    tr(b[:, 0:rows], raw[0:rows, c * 128:(c + 1) * 128], ident_f[0:rows, 0:rows], [raw, ident_f], [b])
                    act(dst(c)[:, j * per:(j + 1) * per, 0:nk], b[:, 0:rows].rearrange("p (n k) -> p n k", k=nk), AF.Copy, [b], [dstbuf])

        def mixer_even(st, tiles):
            Win = load_w(slots[0], w_in_even[0], 0, 1536)
            Wout = load_w(slots[1], w_out_even[0], 0, 1024)
            yT = av(YT_OFF, [128, 8, NT], BF16)
            tmp1 = av(O_T1, [128, 512], F32)
            tmp2 = av(O_T2, [128, 512], F32)
            cbs = [av(O_CB + 256 * i, [128, 512], BF16) for i in range(4)]
            for (ti, c0, w, kind, g0) in tiles:
                rmsnorm(ti, c0, w, G_MIX + 0)
            for (ti, c0, w, kind, g0) in tiles:
                if kind == "p":
                    za = av(O_ZA, [128, 4, 542], BF16)
                    ca = av(O_CA, [128, 4, 512], F32)
                    zb = av(O_ZB, [128, 4, 527], F32)
                    vcopy(za[:, :, 0:30], zaH[:], [zaH], [za])
                    vcopy(zb[:, :, 0:15], zbH[:], [zbH], [zb])
                    zcur = lambda c: za[:, c, 30:30 + w]
                    bcur = lambda c: zb[:, c, 15:15 + w]
                else:
                    za = av(O_ZA, [128, 4, NS, 31], F32)
                    ca = av(O_CA, [128, 4, NS], F32)
                    zb = av(O_ZB, [128, 4, NS, 16], F32)
                    load_hist_T(st_a, 30, lambda c: za[:, c, :, :], za)
                    load_hist_T(st_b, 15, lambda c: zb[:, c, :, :], zb)
                    zcur = lambda c: za[:, c, :, 30]
                    bcur = lambda c: zb[:, c, :, 15]
                for c in range(4):
                    ba = proj(Win, c * 128, ti, c0, w)
                    bg = proj(Win, 512 + c * 128, ti, c0, w)
                    act(tmp1[:, 0:w], bg[:, 0:w], AF.Sigmoid, [bg], [tmp1])
                    tt(zcur(c), ba[:, 0:w], tmp1[:, 0:w], ALU.mult, [ba, tmp1], [za])
                    if kind == "p":
                        tt(zaH[:, c, :], ba[:, w - 30:w], tmp1[:, w - 30:w], ALU.mult, [ba, tmp1], [zaH])
                    bz = proj(Win, 1024 + c * 128, ti, c0, w)
                    act(bcur(c), bz[:, 0:w], AF.Copy, [bz], [zb])
                for c in range(4):
                    if kind == "p":
                        for k in range(16):
                            ts(DgA[:, k, :], ident_b[:], cwv[:, c, k:k + 1], None, ALU.mult, None, [ident_b, cw], [DgA])
                        for k in range(16, 31):
                            ts(DgB[:, k - 16, :], ident_b[:], cwv[:, c, k:k + 1], None, ALU.mult, None, [ident_b, cw], [DgB])
                        bc_ = nb(pin=True)
                        mmg(bc_[:, 0:w], [(DgA[:, k, :], za[:, c, k:k + w]) for k in range(16)], True, False, [DgA, za], [bc_])
                        mmg(bc_[:, 0:w], [(DgB[:, k - 16, :], za[:, c, k:k + w]) for k in range(16, 31)], False, True, [DgB, za], [bc_])
                        act(ca[:, c, 0:w], bc_[:, 0:w], AF.Identity, [bc_, prm], [ca], bias=prm[:, C_AB + c:C_AB + c + 1], scale=1.0)
                        unpin(bc_)
                        continue
                    cac = ca[:, c, :]
                    tap = lambda k, c=c: za[:, c, :, k]
                    ts(cac, tap(30), cwv[:, c, 30:31], prm[:, C_AB + c:C_AB + c + 1], ALU.mult, ALU.add, [za, cw, prm], [ca])
                    for k in range(30):
                        stt(cac, tap(k), cwv[:, c, k:k + 1], cac, ALU.mult, ALU.add, [za, cw, ca], [ca])
                if kind == "p":
                    ln_feat(lambda c: ca[:, c, 0:w], ca, w, C_LAG, C_LAB, AF.Silu, lambda c: yT[:, c, c0:c0 + w], yT)
                else:
                    ln_feat(lambda c: ca[:, c, :], ca, w, C_LAG, C_LAB, AF.Silu, lambda c: yT[:, c, c0:c0 + w], yT)
                for g in range(4):
                    win = 2 << g
                    pooled = cbs[g]
                    if kind == "p":
                        Sa = av(O_CA, [128, 527], F32)
                        Sb = av(O_CA + 544, [128, 527], F32)
                        tt(Sa[:, 1:527], zb[:, g, 1:527], zb[:, g, 0:526], ALU.add, [zb], [Sa])
                        cur = Sa
                        if g >= 1:
                            tt(Sb[:, 3:527], Sa[:, 3:527], Sa[:, 1:525], ALU.add, [Sa], [Sb]); cur = Sb
                        if g >= 2:
                            tt(Sa[:, 7:527], Sb[:, 7:527], Sb[:, 3:523], ALU.add, [Sb], [Sa]); cur = Sa
                        if g >= 3:
                            tt(Sb[:, 15:527], Sa[:, 15:527], Sa[:, 7:519], ALU.add, [Sa], [Sb]); cur = Sb
                        stt(pooled[:, 0:w], cur[:, 15:15 + w], 1.0 / win, zb[:, g, 15:15 + w], ALU.mult, ALU.subtract, [cur, zb], [pooled])
                        if g0 == 0:
                            nfix = win - 1
                            tt(tmp2[:, 0:nfix], cur[:, 15:15 + nfix], invc[:, 0:nfix], ALU.mult, [cur, invc], [tmp2])
                            tt(pooled[:, 0:nfix], tmp2[:, 0:nfix], zb[:, g, 15:15 + nfix], ALU.subtract, [tmp2, zb], [pooled])
                    else:
                        P.op("dve", lambda e, g=g, win=win, zb=zb, tmp2=tmp2: e.tensor_reduce(out=tmp2[:, 0:NS], in_=zb[:, g, :, 16 - win:16], axis=AX.X, op=ALU.add),
                             [zb], [tmp2])
                        stt(pooled[:, 0:w], tmp2[:, 0:NS], 1.0 / win, zb[:, g, :, 15], ALU.mult, ALU.subtract, [tmp2, zb], [pooled])
                    b = nb()
                    mm(b[:, 0:w], [(poolw[:, g, :], pooled[:, 0:w])], [poolw, pooled], [b])
                    act(yT[:, 4 + g, c0:c0 + w], b[:, 0:w], AF.Identity, [b, prm], [yT], scale=prm[:, C_PBS + g:C_PBS + g + 1])
                if kind == "p":
                    vcopy(zbH[:], zb[:, :, 512:527], [zb], [zbH])
                    if g0 + 512 == SEQ:
                        to_tokmajor(lambda c: zaH[:, c, :], zaH, 30, o_pa[:, :], O_T1)
                        to_tokmajor(lambda c: zb[:, c, 512:527], zb, 15, o_pb[:, :], O_T2)
                else:
                    P.dma("sp", o_sa[:, 0:29, :], st_a[:, 1:30, :], is_output=True)
                    P.dma("sp", o_sb[:, 0:14, :], st_b[:, 1:15, :], is_output=True)
                    to_tokmajor(lambda c: za[:, c, :, 30], za, NS, o_sa[:, 29, :], O_T1)
                    to_tokmajor(lambda c: zb[:, c, :, 15], zb, NS, o_sb[:, 14, :], O_T2)
                out_proj(Wout, yT, [(ti, c0, w, kind, g0)])

        def bf(bank):
            return bank.ap[:, :].bitcast(BF16)

        def attn(st, tiles, layer):
            Wq = load_w(slots[0], wq_x[layer], 0, 1024)
            Wo = load_w(slots[1], wo_x[layer], 0, 1024)
            qT = av(T0, [128, 8, 512], BF16)
            PT = av(T0 + 2048, [128, 8, 512], BF16)
            oT = av(T0 + 4096, [128, 8, 512], BF16)
            Pun2 = [av(T0 + 6144 + i * 1024, [128, 4, 256], F32) for i in range(2)]
            Pn2 = [av(T0 + 8192 + i * 512, [128, 4, 256], BF16) for i in range(2)]
            sm2 = [av(T0 + 9216 + i * 32, [128, 16], F32) for i in range(2)]
            for (ti, c0, w, kind, g0) in tiles:
                rmsnorm(ti, c0, w, G_X + 8 * layer)

            def qproj(tile):
                (ti, c0, w, kind, g0) = tile
                for fc in range(8):
                    b = proj(Wq, fc * 128, ti, c0, w)
                    act(qT[:, fc, 0:w], b[:, 0:w], AF.Identity, [b], [qT], scale=0.0625)
            qproj(tiles[0])
            pre_scores = []
            stile = [t for t in tiles if t[3] == "s"]
            kgen = sample_k_phase(layer, stile[0][0], stile[0][1], stile[0][2]) if stile else iter(())
            for tidx, (ti, c0, w, kind, g0) in enumerate(tiles):
                nxt = tiles[tidx + 1] if tidx + 1 < len(tiles) else None
                if nxt is not None and nxt[3] == "s":
                    nxt = None
                if kind == "p":
                    def stA(sb_):
                        tk = slice(sb_ * 128, (sb_ + 1) * 128)
                        bks = [nb(pin=True), nb(pin=True)]
                        for h in range(4):
                            bk = bks[h // 2]
                            off = (h % 2) * 256
                            mm(bk[:, off:off + 256], [(qT[:, 2 * h + dc, tk], KTp[layer][:, 2 * h + dc, :]) for dc in range(2)], [qT, KTp[layer]], [bk])
                        return bks

                    def stB(sb_, bks):
                        Pun = Pun2[sb_ % 2]
                        sm = sm2[sb_ % 2]
                        Pn = Pn2[sb_ % 2]
                        for i in range(2):
                            P.op("dve", lambda e, i=i, bks=bks, sm=sm: e.tensor_reduce(out=sm[:, 2 * i:2 * i + 2], in_=bks[i][:, :].rearrange("p (h m) -> p h m", m=256),
                                                                         axis=AX.X, op=ALU.max, negate=True), [bks[i]], [sm])
                        for h in range(4):
                            bk = bks[h // 2]
                            off = (h % 2) * 256
                            act(Pun[:, h, :], bk[:, off:off + 256], AF.Exp, [bk, sm], [Pun, sm], bias=sm[:, h:h + 1], scale=1.0, accum_out=sm[:, 4 + h:5 + h])
                        unpin(bks[0])
                        unpin(bks[1])
                        P.op("dve", lambda e, sm=sm: e.reciprocal(out=sm[:, 8:12], in_=sm[:, 4:8]), [sm], [sm])
                        for h in range(4):
                            if stile:
                                act(Pn[:, h, :], Pun[:, h, :], AF.Copy, [Pun, sm], [Pn], scale=sm[:, 8 + h:9 + h])
                            else:
                                ts(Pn[:, h, :], Pun[:, h, :], sm[:, 8 + h:9 + h], None, ALU.mult, None, [Pun, sm], [Pn])

                    def stC(sb_):
                        tk = slice(sb_ * 128, (sb_ + 1) * 128)
                        Pn = Pn2[sb_ % 2]
                        bT = nb()
                        for h in range(4):
                            for mc in range(2):
                                j = h * 2 + mc
                                tr(bf(bT)[:, j * 128:(j + 1) * 128], Pn[:, h, mc * 128:(mc + 1) * 128], ident_b[:], [Pn, ident_b], [bT])
                        act(PT[:, :, tk], bf(bT).rearrange("p (a t) -> p a t", t=128), AF.Copy, [bT], [PT])

                    bq_ = {0: pre_scores.pop() if pre_scores else stA(0)}
                    bq_[1] = stA(1)
                    stB(0, bq_[0])
                    for sb_ in range(4):
                        if sb_ + 2 < 4:
                            bq_[sb_ + 2] = stA(sb_ + 2)
                        elif sb_ + 2 == 4 and nxt is not None:
                            qproj(nxt)
                            nxt = None
                        if sb_ + 1 < 4:
                            stB(sb_ + 1, bq_[sb_ + 1])
                        stC(sb_)
                        for _ in range(3):
                            next(kgen, None)
                    if tidx + 1 < len(tiles) and tiles[tidx + 1][3] == "p":
                        pre_scores.append(stA(0))
                    for h in range(4):
                        for dc in range(2):
                            e_ = 2 * h + dc
                            b = nb()
                            mm(b[:, 0:w], [(Vp[layer][:, mc, e_ * 128:(e_ + 1) * 128], PT[:, 2 * h + mc, 0:w]) for mc in range(2)], [Vp[layer], PT], [b])
                            act(oT[:, e_, 0:w], b[:, 0:w], AF.Copy, [b], [oT])
                else:
                    for _ in kgen:
                        pass
                    attn_sample(layer, oT)
                for oc in range(8):
                    b = nb()
                    mm(b[:, 0:w], [(Wo[:, kc, oc * 128:(oc + 1) * 128], oT[:, kc, 0:w]) for kc in range(8)], [Wo, oT], [b])
                    add_to_x(ti, c0, w, oc, b)

        def sample_k_phase(layer, ti, c0, w):
            Y0 = YT_OFF
            NK = 5
            Kr = [av(Y0 + i * 1024, [128, 2, D], BF16) for i in range(NK)]
            q_tm = av(Y0 + 5120, [NS, D], BF16)
            junk = av(Y0 + 5632, [128, 256], F32)
            STs = av(Y0 + 5888, [128, 2, NS, 4], F32)
            qTs = av(Y0 + 6016, [128, 8, NS], BF16)
            Wq = wview(slots[0], 1024)
            for fc in range(8):
                b = proj(Wq, fc * 128, ti, c0, w)
                act(qTs[:, fc, :], b[:, 0:w], AF.Identity, [b], [qTs], scale=0.0625)
            bq = nb()
            for fc in range(8):
                tr(bf(bq)[0:NS, fc * 128:(fc + 1) * 128], qTs[:, fc, :], ident_b[:], [qTs, ident_b], [bq])
            act(q_tm[:, :], bf(bq)[0:NS, :], AF.Copy, [bq], [q_tm])
            yield
            for n in range(NS):
                Kb = Kr[n % NK]
                P.dma("pool", Kb[:], ck[layer, n].rearrange("(c p) f -> p c f", p=128), writes=[Kb])
                qb = [nb(), nb()]
                for i in range(2):
                    mm(qb[i][:, :], [(sel[:, n, :], q_tm[:, i * 512:(i + 1) * 512])], [sel, q_tm], [qb[i]])
                for mc in range(2):
                    for h in range(4):
                        stt(junk[:, :], Kb[:, mc, h * 256:(h + 1) * 256], 1.0, qb[h // 2][:, (h % 2) * 256:(h % 2) * 256 + 256], ALU.mult, ALU.mult,
                            [Kb, qb[h // 2]], [junk, STs], accum_out=STs[:, mc, n, h:h + 1])
                yield

        def attn_sample(layer, oT):
            A0 = T0 + 2048
            NV = 4
            KV = [av(A0 + i * 1024, [128, 2, D], BF16) for i in range(NV)]
            q_tm = av(YT_OFF + 5120, [NS, D], BF16)
            STs = av(YT_OFF + 5888, [128, 2, NS, 4], F32)
            S_sm = av(T0 + 6144 + 128, [64, 256], F32)
            PTs = av(T0 + 6144 + 384, [128, 2, NS, 4], F32)
            Pm = av(T0 + 6144 + 512, [128, 2, 4, NS, NS], BF16)
            sm = av(T0 + 7680, [128, 16], F32)
            bS = nb()
            for mc in range(2):
                tr(bS[0:64, mc * 128:(mc + 1) * 128], STs[:, mc, :, :].rearrange("p n h -> p (n h)"), ident_f[:], [STs, ident_f], [bS])
            P.op("dve", lambda e: e.tensor_reduce(out=sm[0:64, 0:1], in_=bS[0:64, 0:256], axis=AX.X, op=ALU.max, negate=True), [bS], [sm])
            act(S_sm[:, :], bS[0:64, 0:256], AF.Exp, [bS, sm], [S_sm, sm], bias=sm[0:64, 0:1], scale=1.0, accum_out=sm[0:64, 4:5])
            P.op("dve", lambda e: e.reciprocal(out=sm[0:64, 8:9], in_=sm[0:64, 4:5]), [sm], [sm])
            ts(S_sm[:, :], S_sm[:, :], sm[0:64, 8:9], None, ALU.mult, None, [S_sm, sm], [S_sm])
            bS2 = nb()
            for mc in range(2):
                tr(bS2[:, mc * 64:(mc + 1) * 64], S_sm[:, mc * 128:(mc + 1) * 128], ident_f[0:64, 0:64], [S_sm, ident_f], [bS2])
            act(PTs[:, :, :, :].rearrange("p c n h -> p (c n h)"), bS2[:, 0:128], AF.Copy, [bS2], [PTs])
            for mc in range(2):
                for h in range(4):
                    for n in range(NS):
                        ts(Pm[:, mc, h, n, :], maskM[:, n, :], PTs[:, mc, n, h:h + 1], None, ALU.mult, None, [maskM, PTs], [Pm])
            bo = [nb(pin=True) for _ in range(4)]
            for n in range(NS):
                Vb = KV[n % NV]
                P.dma("pool", Vb[:], cv[layer, n].rearrange("(c p) f -> p c f", p=128), writes=[Vb])
                for h in range(4):
                    for mc in range(2):
                        mm1(bo[h][0:NS, 0:256], Pm[:, mc, h, n, :], Vb[:, mc, h * 256:(h + 1) * 256], n == 0 and mc == 0, n == NS - 1 and mc == 1,
                            [Pm, Vb], [bo[h]])
            o_tm = q_tm
            for h in range(4):
                act(o_tm[:, h * 256:(h + 1) * 256], bo[h][0:NS, 0:256], AF.Copy, [bo[h]], [o_tm])
                unpin(bo[h])
            bq2 = nb()
            for kc in range(8):
                tr(bf(bq2)[:, kc * NS:(kc + 1) * NS], o_tm[:, kc * 128:(kc + 1) * 128], ident_b[0:NS, 0:NS], [o_tm, ident_b], [bq2])
            act(oT[:, :, 0:NS], bf(bq2)[:, 0:8 * NS].rearrange("p (a t) -> p a t", t=NS), AF.Copy, [bq2], [oT])

        def ffn(st, tiles, layer):
            for (ti, c0, w, kind, g0) in tiles:
                rmsnorm(ti, c0, w, G_FFN + 8 * layer)
            actb = av(YT_OFF, [128, 12, NT], BF16)
            wg_v = w_gate[layer].rearrange("(k p) f -> p k f", p=128)
            wu_v = w_up[layer].rearrange("(k p) f -> p k f", p=128)
            cnt = 0
            pi = 0
            groups = [(0, 12), (12, 10)]
            for gi, (f0, G) in enumerate(groups):
                Wd = Buf(slots[gi].ap[:, 0:G * D].rearrange("p (j f) -> p j f", f=D), slots[gi].deps)
                wd_v = w_down[layer][f0 * 128:(f0 + G) * 128, :].rearrange("(j p) f -> p j f", p=128)
                hG = G // 2
                P.dma("pool", Wd[:, 0:hG, :], wd_v[:, 0:hG, :], writes=slots[gi].deps[0:2])
                P.dma("pool", Wd[:, hG:G, :], wd_v[:, hG:G, :], writes=slots[gi].deps[2:4] + slot_extra[gi])
                for jp in range(0, G, 4):
                    fc = f0 + jp
                    nq = min(4, G - jp)
                    weg = av(T0 + (pi % 2) * 4096, [128, 8, 512], BF16)
                    weu = av(T0 + (pi % 2) * 4096 + 2048, [128, 8, 512], BF16)
                    pi += 1
                    P.dma("pool", weg[:, :, 0:nq * 128], wg_v[:, :, fc * 128:(fc + nq) * 128], writes=[weg])
                    P.dma("pool", weu[:, :, 0:nq * 128], wu_v[:, :, fc * 128:(fc + nq) * 128], writes=[weu])
                    for q in range(nq):
                        j = jp + q
                        for (ti, c0, w, kind, g0) in tiles:
                            bg = nb()
                            mm(bg[:, 0:w], [(weg[:, kc, q * 128:(q + 1) * 128], hT[:, kc, c0:c0 + w]) for kc in range(8)], [weg, hdp[ti]], [bg])
                            bu = nb()
                            mm(bu[:, 0:w], [(weu[:, kc, q * 128:(q + 1) * 128], hT[:, kc, c0:c0 + w]) for kc in range(8)], [weu, hdp[ti]], [bu])
                            sg = av(T0 + 8192, [128, 512], F32)
                            act(sg[:, 0:w], bg[:, 0:w], AF.Silu, [bg], [sg])
                            tt(actb[:, j, c0:c0 + w], sg[:, 0:w], bu[:, 0:w], ALU.mult, [sg, bu], [actb])
                for (ti, c0, w, kind, g0) in tiles:
                    for oc in range(8):
                        b = nb()
                        mm(b[:, 0:w], [(Wd[:, j, oc * 128:(oc + 1) * 128], actb[:, j, c0:c0 + w]) for j in range(G)], [Wd, actb], [b])
                        add_to_x(ti, c0, w, oc, b)

        def mixer_odd(st, tiles):
            yT = av(YT_OFF, [128, 8, NT], BF16)
            tmp1 = av(O_T1, [128, 512], F32)
            tmp2 = av(O_T2, [128, 512], F32)
            WC = load_w(slots[0], w_in_odd[0], 0, 1024)
            WD = load_w(slots[1], w_in_odd[0], 1024, 2560)
            for (ti, c0, w, kind, g0) in tiles:
                rmsnorm(ti, c0, w, G_MIX + 8)
            for (ti, c0, w, kind, g0) in tiles:
                u_sb = av(T0, [128, 4, 512], F32)
                for c in range(4):
                    b = proj(WC, c * 128, ti, c0, w)
                    act(u_sb[:, c, 0:w], b[:, 0:w], AF.Copy, [b], [u_sb])
                if kind == "p":
                    vt = av(T0 + 2048, [128, 512], F32)
                    vb = av(T0 + 2560, [128, 4, 512], BF16)
                    sm = av(T0 + 3584, [128, 16], F32)
                    hb_ = [nb(pin=True) for _ in range(4)]

                    def vproj(sb_):
                        tk = slice(c0 + sb_ * 128, c0 + (sb_ + 1) * 128)
                        bv = nb(pin=True)
                        mm(bv[:, :], [(hT[:, kc, tk], WC[:, kc, 512:1024]) for kc in range(8)], [hdp[ti], WC], [bv])
                        return bv
                    bvs = {0: vproj(0)}
                    for sb_ in range(4):
                        if sb_ + 1 < 4:
                            bvs[sb_ + 1] = vproj(sb_ + 1)
                        bv = bvs[sb_]
                        P.op("dve", lambda e, bv=bv, sm=sm: e.bn_stats(out=sm[:, 0:6], in_=bv[:, :]), [bv], [sm])
                        P.op("dve", lambda e, sm=sm: e.bn_aggr(out=sm[:, 8:10], in_=sm[:, 0:6]), [sm], [sm])
                        act(sm[:, 10:11], sm[:, 9:10], AF.Ln, [sm, eps_t], [sm], bias=eps_t[:, 0:1], scale=1.0)
                        act(sm[:, 10:11], sm[:, 10:11], AF.Exp, [sm], [sm], scale=-0.5)
                        ts(vt[:, :], bv[:, :], sm[:, 8:9], sm[:, 10:11], ALU.subtract, ALU.mult, [bv, sm], [vt])
                        unpin(bv)
                        tt(vt[:, :], vt[:, :], gC_bc[:], ALU.mult, [vt, gC_bc], [vt])
                        tt(vt[:, :], vt[:, :], bC_bc[:], ALU.add, [vt, bC_bc], [vt])
                        if g0 + sb_ * 128 == SEQ - 128:
                            P.dma("sp", o_pc[:, :], vt[:, :], reads=[vt], is_output=True)
                        vcopy(vb[:, sb_, :], vt[:, :], [vt], [vb])
                        for h in range(4):
                            mm(hb_[h][:, sb_ * 128:(sb_ + 1) * 128],
                               [(vb[:, sb_, h * 128:(h + 1) * 128], wsT[:, h, :]),
                                (ones_row[0:1, :], bsH[0:1, h * 128:(h + 1) * 128]),
                                (ones_row[0:1, :], bsL[0:1, h * 128:(h + 1) * 128])], [vb, wsT, ones_row, bsH, bsL], [hb_[h]])
                    for h in range(4):
                        tt(yT[:, h, c0:c0 + w], u_sb[:, h, 0:w], hb_[h][:, 0:w], ALU.mult, [u_sb, hb_[h]], [yT])
                        unpin(hb_[h])
                else:
                    vs = av(O_CA, [128, 4, NS], F32)
                    for c in range(4):
                        b = proj(WC, 512 + c * 128, ti, c0, w)
                        act(vs[:, c, :], b[:, 0:w], AF.Copy, [b], [vs])
                    ln_feat(lambda c: vs[:, c, :], vs, w, C_LCG, C_LCB, AF.Identity, lambda c: vs[:, c, :], vs)
                    to_tokmajor(lambda c: vs[:, c, :], vs, NS, o_sc[:, :], O_T1)
                    for c in range(4):
                        ts(tmp2[:, 0:w], vs[:, c, :], ws00[:, c:c + 1], bs0[:, c:c + 1], ALU.mult, ALU.add, [vs, ws00, bs0], [tmp2])
                        tt(yT[:, c, c0:c0 + w], u_sb[:, c, 0:w], tmp2[:, 0:w], ALU.mult, [u_sb, tmp2], [yT])
            for (ti, c0, w, kind, g0) in tiles:
                if kind == "p":
                    gd = av(T0, [128, 4, 514], F32)
                    vcopy(gd[:, :, 0:2], gdH[:], [gdH], [gd])
                    cur = lambda c: gd[:, c, 2:2 + w]
                    tap = lambda c, k: gd[:, c, k:k + w]
                else:
                    gd = av(T0, [128, 4, NS, 3], F32)
                    load_hist_T(st_d, 2, lambda c: gd[:, c, :, :], gd)
                    cur = lambda c: gd[:, c, :, 2]
                    tap = lambda c, k: gd[:, c, :, k]
                for c in range(4):
                    bgc = proj(WD, 512 + c * 128, ti, c0, w)
                    bhd = proj(WD, 1024 + c * 128, ti, c0, w)
                    act(tmp1[:, 0:w], bgc[:, 0:w], AF.Copy, [bgc], [tmp1])
                    tt(cur(c), tmp1[:, 0:w], bhd[:, 0:w], ALU.mult, [tmp1, bhd], [gd])
                    ts(tmp2[:, 0:w], tap(c, 0), prm[:, C_DW + c:C_DW + c + 1], None, ALU.mult, None, [gd, prm], [tmp2])
                    stt(tmp2[:, 0:w], tap(c, 1), prm[:, C_DW + 4 + c:C_DW + 5 + c], tmp2[:, 0:w], ALU.mult, ALU.add, [gd, prm, tmp2], [tmp2])
                    stt(tmp2[:, 0:w], tap(c, 2), prm[:, C_DW + 8 + c:C_DW + 9 + c], tmp2[:, 0:w], ALU.mult, ALU.add, [gd, prm, tmp2], [tmp2])
                    bgb = proj(WD, c * 128, ti, c0, w)
                    tt(yT[:, 4 + c, c0:c0 + w], tmp2[:, 0:w], bgb[:, 0:w], ALU.mult, [tmp2, bgb], [yT])
                if kind == "p":
                    vcopy(gdH[:], gd[:, :, 512:514], [gd], [gdH])
                    if g0 + 512 == SEQ:
                        to_tokmajor(lambda c: gd[:, c, 512:514], gd, 2, o_pd[:, :], O_T1)
                else:
                    P.dma("sp", o_sd[:, 0:1, :], st_d[:, 1:2, :], is_output=True)
                    to_tokmajor(lambda c: gd[:, c, :, 2], gd, NS, o_sd[:, 1, :], O_T1)
            Wout = load_w(slots[0], w_out_odd[0], 0, 1024)
            out_proj(Wout, yT, tiles)

        ST_TILES = [
            [(0, 0, 512, "p", 0), (1, 512, 512, "p", 512)],
            [(0, 0, 512, "p", 1024), (1, 512, 512, "p", 1536), (2, 1024, NS, "s", 0)],
        ]
        if stage < 1:
            ST_TILES = []
        import os
        if os.environ.get("NOSAMPLE"):
            ST_TILES = [[t for t in tl if t[3] == "p"] for tl in ST_TILES]

        for st, tiles in enumerate(ST_TILES):
            for (ti, c0, w, kind, g0) in tiles:
                if kind == "p":
                    for sb_ in range(4):
                        xs = av(T0 + (sb_ % 2) * 1024, [128, D], F32)
                        P.dma("sp", xs[:], x_p[g0 + sb_ * 128:g0 + (sb_ + 1) * 128, :], writes=[xs])
                        for kq in range(2):
                            b = nb()
                            for k4 in range(4):
                                kc = kq * 4 + k4
                                tr(b[:, k4 * 128:(k4 + 1) * 128], xs[:, kc * 128:(kc + 1) * 128], ident_f[:], [xs, ident_f], [b])
                            act(xT[:, kq * 4:kq * 4 + 4, c0 + sb_ * 128:c0 + (sb_ + 1) * 128],
                                b[:, :].rearrange("p (a t) -> p a t", t=128), AF.Copy, [b], [xd[kq * 4 + k4][ti] for k4 in range(4)])
                else:
                    xs = av(T0, [NS, D], F32)
                    P.dma("sp", xs[:], x_s[:, :], writes=[xs])
                    b = nb()
                    for kc in range(8):
                        tr(b[:, kc * NS:(kc + 1) * NS], xs[:, kc * 128:(kc + 1) * 128], ident_f[0:NS, 0:NS], [xs, ident_f], [b])
                    act(xT[:, :, c0:c0 + NS], b[:, 0:8 * NS].rearrange("p (a t) -> p a t", t=NS), AF.Copy, [b], xdeps(ti))

            nlayers = 0 if stage < 2 else (1 if stage < 5 else 2)
            for layer in range(nlayers):
                base = 2 + 3 * layer
                if layer == 0:
                    mixer_even(st, tiles)
                else:
                    mixer_odd(st, tiles)
                if stage >= base + 1:
                    attn(st, tiles, layer)
                if stage >= base + 2:
                    ffn(st, tiles, layer)
            if os.environ.get("RMSONLY"):
                for (ti, c0, w, kind, g0) in tiles:
                    rmsnorm(ti, c0, w, G_FIN)
            for (ti, c0, w, kind, g0) in (tiles if not os.environ.get("NOFINAL") else []):
                yf = av(T0, [128, 8, 512], F32)
                rmsnorm(ti, c0, w, G_FIN, final_dst=yf)
                if kind == "p":
                    for sb_ in range(4):
                        ys = av(T0 + 4096 + (sb_ % 2) * 1024, [128, D], F32)
                        for kq in range(2):
                            b = nb()
                            for k4 in range(4):
                                kc = kq * 4 + k4
                                tr(b[:, k4 * 128:(k4 + 1) * 128], yf[:, kc, sb_ * 128:(sb_ + 1) * 128], ident_f[:], [yf, ident_f], [b])
                            act(ys[:, kq * 512:(kq + 1) * 512], b[:, :], AF.Copy, [b], [ys])
                        P.dma("sp", y_p[g0 + sb_ * 128:g0 + (sb_ + 1) * 128, :], ys[:], reads=[ys], is_output=True)
                else:
                    ys = av(T0 + 4096, [NS, D], F32)
                    for kq in range(2):
                        b = nb()
                        for k4 in range(4):
                            kc = kq * 4 + k4
                            tr(b[0:NS, k4 * 128:(k4 + 1) * 128], yf[:, kc, 0:NS], ident_f[:], [yf, ident_f], [b])
                        act(ys[:, kq * 512:(kq + 1) * 512], b[0:NS, :], AF.Copy, [b], [ys])
                    P.dma("sp", y_s[:, :], ys[:], reads=[ys], is_output=True)

        P.finish()
        P.emit_all()
    return nc


_OUT_ORDER = ["y_p", "y_s", "o_pa", "o_pb", "o_pc", "o_pd", "o_pk", "o_pv", "o_sa", "o_sb", "o_sc", "o_sd"]


def make_in_maps(inp):
    f = lambda a: np.ascontiguousarray(np.asarray(a, dtype=np.float32))
    maps = []
    shared = {k: f(inp[k]) for k in ["norm_mix", "norm_x", "norm_ffn", "norm_final", "w_in_even", "conv_a_w", "conv_a_b",
                                     "ln_a_g", "ln_a_b", "pool_b_w", "pool_b_scale", "w_out_even", "w_in_odd", "ln_c_g",
                                     "ln_c_b", "ws_c", "bs_c", "conv_d_w", "w_out_odd", "wq_x", "wk_x", "wv_x", "wo_x",
                                     "w_gate", "w_up", "w_down"]}
    for c in range(NCORE):
        s = slice(c * NS, (c + 1) * NS)
        m = dict(shared)
        m["x_p"] = f(inp["x_prompt"][c])
        m["x_s"] = f(inp["x_sample"][s, 0])
        m["mem"] = f(inp["mem_prompt"][c])
        m["st_a"] = f(inp["state_convA"][0, s])
        m["st_b"] = f(inp["state_poolB"][0, s])
        m["st_d"] = f(inp["state_convD"][0, s])
        m["ck"] = f(np.asarray(inp["cache_mem_k"])[:, s].reshape(2, NS, 256, D))
        m["cv"] = f(np.asarray(inp["cache_mem_v"])[:, s].reshape(2, NS, 256, D))
        maps.append(m)
    return maps


def assemble(results):
    r = results
    cat = lambda k: np.stack([np.asarray(r[c][k]) for c in range(NCORE)])
    y_prompt = cat("y_p")
    y_sample = np.concatenate([np.asarray(r[c]["y_s"]) for c in range(NCORE)], 0)[:, None, :]
    p_convA = cat("o_pa")[None]
    p_poolB = cat("o_pb")[None]
    p_chunkC = cat("o_pc").reshape(1, NCORE, 128, 4, 128)
    p_convD = cat("o_pd")[None]
    p_mem_k = np.stack([np.asarray(r[c]["o_pk"]) for c in range(NCORE)], 1).reshape(2, NCORE, 256, 4, 256)
    p_mem_v = np.stack([np.asarray(r[c]["o_pv"]) for c in range(NCORE)], 1).reshape(2, NCORE, 256, 4, 256)
    s_convA = np.concatenate([np.asarray(r[c]["o_sa"]) for c in range(NCORE)], 0)[None]
    s_poolB = np.concatenate([np.asarray(r[c]["o_sb"]) for c in range(NCORE)], 0)[None]
    s_chunkC = np.concatenate([np.asarray(r[c]["o_sc"]) for c in range(NCORE)], 0).reshape(1, NCORE * NS, 1, 4, 128)
    s_convD = np.concatenate([np.asarray(r[c]["o_sd"]) for c in range(NCORE)], 0)[None]
    outs = (y_prompt, y_sample, p_convA, p_poolB, p_chunkC, p_convD, p_mem_k, p_mem_v, s_convA, s_poolB, s_chunkC, s_convD)
    return tuple(np.ascontiguousarray(o, dtype=np.float32) for o in outs)


def kernel(**inputs):
    nc = build_program()
    in_maps = make_in_maps(inputs)
    res = run_bass_kernel_spmd(nc, in_maps, core_ids=list(range(NCORE)))
    return assemble(res.results)
```
